# Optimizing a Trainium2 kernel written in Bass

```python
import jax
import jax.numpy as jnp
from jax import lax
import numpy as np


D_MODEL = 1024
BATCH = 2
SEQ = 8192
DEPTH = 2

N_A_LAYERS = DEPTH // 2
N_B_LAYERS = DEPTH - N_A_LAYERS
Q_BLOCK = 128
NEG_INF = -1e30
NORM_EPS = 1e-6

NSA_HEADS = 16
NSA_GROUPS = 4
NSA_HPG = NSA_HEADS // NSA_GROUPS
NSA_DH = 64
CMP_BLOCK = 32
CMP_STRIDE = 16
CMP_HIDDEN = 256
SLC_BLOCK = 64
SLC_TOP = 16
WINDOW = 512
FORCE_SCORE = 1e4
NSA_IN_WIDTH = NSA_HEADS * NSA_DH + 6 * NSA_GROUPS * NSA_DH + 3 * NSA_HEADS

MLA_HEADS = 8
QK_NOPE = 128
QK_ROPE = 64
MLA_V = 128
Q_LORA = 384
KV_LORA = 256
ROPE_THETA = 10000.0

FFN_HIDDEN = -(-8 * D_MODEL // (3 * 256)) * 256

kernel_name = 'yoco_nsa_mla_hybrid'


def rms_norm(x, g):
    xf = x.astype(jnp.float32)
    y = xf * lax.rsqrt(jnp.mean(xf * xf, axis=-1, keepdims=True) + NORM_EPS)
    return (y * g.astype(jnp.float32)).astype(x.dtype)


def masked_softmax(s, mask):
    p = jax.nn.softmax(jnp.where(mask, s, NEG_INF), axis=-1)
    return jnp.where(mask, p, 0.0)


def alibi_slopes(n):
    return 2.0 ** (-8.0 * jnp.arange(1, n + 1, dtype=jnp.float32) / n)


def rope_tables(T):
    inv = ROPE_THETA ** (-jnp.arange(0, QK_ROPE, 2, dtype=jnp.float32) / QK_ROPE)
    ang = jnp.arange(T, dtype=jnp.float32)[:, None] * inv[None, :]
    return jnp.cos(ang), jnp.sin(ang)


def apply_rope(x, cos, sin):
    c = cos[None, :, None, :].astype(x.dtype)
    s = sin[None, :, None, :].astype(x.dtype)
    x1, x2 = jnp.split(x, 2, axis=-1)
    return jnp.concatenate([x1 * c - x2 * s, x1 * s + x2 * c], axis=-1)


def swiglu(h, w_gate_up, w_down):
    gate, up = jnp.split(h @ w_gate_up, 2, axis=-1)
    return (jax.nn.silu(gate) * up) @ w_down


_gather_bg = jax.vmap(jax.vmap(lambda src, idx: src[idx]))


def nsa_mixer(h, w_in, q_norm, kcmp_norm, kslc_norm, kwin_norm, pos_k, pos_v,
              k_w1, k_b1, k_w2, v_w1, v_b1, v_w2, w_out):
    B, T, _ = h.shape
    G, HG, DH = NSA_GROUPS, NSA_HPG, NSA_DH
    f32 = jnp.float32
    qw, kvw = NSA_HEADS * DH, G * DH
    cuts = [qw + i * kvw for i in range(7)]
    parts = jnp.split(h @ w_in, cuts, axis=-1)
    q = rms_norm(parts[0].reshape(B, T, G, HG, DH), q_norm)
    kc, vc, ksl, vsl, kw, vw = [a.reshape(B, T, G, DH) for a in parts[1:7]]
    gates = jax.nn.sigmoid(parts[7].astype(f32)).astype(h.dtype).reshape(B, T, G, HG, 3)

    n_cmp = (T - CMP_BLOCK) // CMP_STRIDE + 1
    cmp_j = jnp.arange(n_cmp)
    cmp_idx = cmp_j[:, None] * CMP_STRIDE + jnp.arange(CMP_BLOCK)[None, :]

    def compress(a, pe, w1, b1, w2):
        blk = a[:, cmp_idx] + pe[:, None, :].astype(a.dtype)
        blk = jnp.swapaxes(blk, 2, 3).reshape(B, n_cmp, G, CMP_BLOCK * DH)
        return jax.nn.gelu(blk @ w1 + b1) @ w2

    k_cmp = rms_norm(compress(kc, pos_k, k_w1, k_b1, k_w2), kcmp_norm)
    v_cmp = compress(vc, pos_v, v_w1, v_b1, v_w2)
    k_sel_src = jnp.swapaxes(rms_norm(ksl, kslc_norm), 1, 2)
    v_sel_src = jnp.swapaxes(vsl, 1, 2)
    pad = ((0, 0), (WINDOW, 0), (0, 0), (0, 0))
    kw_pad = jnp.pad(rms_norm(kw, kwin_norm), pad)
    vw_pad = jnp.pad(vw, pad)

    slopes = alibi_slopes(NSA_HEADS).reshape(G, HG)[:, :, None, None]
    scale = DH ** -0.5
    n_slc = T // SLC_BLOCK
    n_top = min(SLC_TOP, n_slc)
    n_sel = n_top * SLC_BLOCK
    cmp_end = cmp_j * CMP_STRIDE + CMP_BLOCK - 1
    cmp_mid = (cmp_j * CMP_STRIDE).astype(f32) + 0.5 * (CMP_BLOCK - 1)
    slc_i = jnp.arange(n_slc)
    overlap = ((cmp_j[:, None] * CMP_STRIDE <= slc_i[None, :] * SLC_BLOCK + SLC_BLOCK - 1)
               & (cmp_end[:, None] >= slc_i[None, :] * SLC_BLOCK)).astype(f32)
    win_off = jnp.arange(WINDOW + Q_BLOCK)
    in_blk = jnp.arange(SLC_BLOCK)

    def block(s0):
        t = s0 + jnp.arange(Q_BLOCK)
        tf = t.astype(f32)
        qb = lax.dynamic_slice_in_dim(q, s0, Q_BLOCK, axis=1)
        gb = lax.dynamic_slice_in_dim(gates, s0, Q_BLOCK, axis=1)
        s_c = (jnp.einsum('bqghd,bjgd->bghqj', qb, k_cmp).astype(f32) * scale
               - slopes * (tf[:, None] - cmp_mid[None, :]))
        p_c = masked_softmax(s_c, cmp_end[None, :] <= t[:, None])
        o_c = jnp.einsum('bghqj,bjgd->bqghd', p_c.astype(v_cmp.dtype), v_cmp)
        imp = jnp.einsum('bghqj,ji->bgqi', p_c, overlap)
        cur = (t // SLC_BLOCK)[:, None]
        valid = slc_i[None, :] * SLC_BLOCK <= t[:, None]
        forced = (slc_i[None, :] == 0) | (slc_i[None, :] == cur) | (slc_i[None, :] == cur - 1)
        score = jnp.where(valid, imp + jnp.where(forced, FORCE_SCORE, 0.0), -1.0)
        _, top = lax.top_k(score, n_top)
        sel_pos = (top[..., None] * SLC_BLOCK + in_blk).reshape(B, G, Q_BLOCK * n_sel)
        k_s = _gather_bg(k_sel_src, sel_pos).reshape(B, G, Q_BLOCK, n_sel, DH)
        v_s = _gather_bg(v_sel_src, sel_pos).reshape(B, G, Q_BLOCK, n_sel, DH)
        sel_pos = sel_pos.reshape(B, G, 1, Q_BLOCK, n_sel)
        s_s = (jnp.einsum('bqghd,bgqnd->bghqn', qb, k_s).astype(f32) * scale
               - slopes * (tf[:, None] - sel_pos.astype(f32)))
        p_s = masked_softmax(s_s, sel_pos <= t[:, None])
        o_s = jnp.einsum('bghqn,bgqnd->bqghd', p_s.astype(v_s.dtype), v_s)
        k_w = lax.dynamic_slice_in_dim(kw_pad, s0, WINDOW + Q_BLOCK, axis=1)
        v_w = lax.dynamic_slice_in_dim(vw_pad, s0, WINDOW + Q_BLOCK, axis=1)
        kpos = s0 - WINDOW + win_off
        dist = t[:, None] - kpos[None, :]
        s_w = (jnp.einsum('bqghd,bkgd->bghqk', qb, k_w).astype(f32) * scale
               - slopes * dist.astype(f32))
        p_w = masked_softmax(s_w, (dist >= 0) & (dist < WINDOW) & (kpos[None, :] >= 0))
        o_w = jnp.einsum('bghqk,bkgd->bqghd', p_w.astype(v_w.dtype), v_w)
        o = gb[..., 0:1] * o_c + gb[..., 1:2] * o_s + gb[..., 2:3] * o_w
        return o.reshape(B, Q_BLOCK, NSA_HEADS * DH)

    starts = jnp.arange(T // Q_BLOCK, dtype=jnp.int32) * Q_BLOCK
    o = jnp.swapaxes(lax.map(block, starts), 0, 1).reshape(B, T, NSA_HEADS * DH)
    return o @ w_out


def shared_mla_kv(x, kv_norm, kv_w_a, kv_c_norm, kv_w_b, kv_k_norm, cos, sin):
    B, T, _ = x.shape
    kv_a = rms_norm(x, kv_norm) @ kv_w_a
    c_kv = rms_norm(kv_a[..., :KV_LORA], kv_c_norm)
    k_pe = kv_a[..., KV_LORA:]
    kv = (c_kv @ kv_w_b).reshape(B, T, MLA_HEADS, QK_NOPE + MLA_V)
    k = jnp.concatenate([kv[..., :QK_NOPE],
                         jnp.broadcast_to(k_pe[:, :, None, :], (B, T, MLA_HEADS, QK_ROPE))], axis=-1)
    k = rms_norm(k, kv_k_norm)
    k = jnp.concatenate([k[..., :QK_NOPE], apply_rope(k[..., QK_NOPE:], cos, sin)], axis=-1)
    return k, kv[..., QK_NOPE:]


def mla_mixer(h, k, v, w_q_a, q_a_norm, w_q_b, q_norm, w_out, cos, sin):
    B, T, _ = h.shape
    q = (rms_norm(h @ w_q_a, q_a_norm) @ w_q_b).reshape(B, T, MLA_HEADS, QK_NOPE + QK_ROPE)
    q = rms_norm(q, q_norm)
    q = jnp.concatenate([q[..., :QK_NOPE], apply_rope(q[..., QK_NOPE:], cos, sin)], axis=-1)
    scale = (QK_NOPE + QK_ROPE) ** -0.5
    kpos = jnp.arange(T)

    def block(s0):
        t = s0 + jnp.arange(Q_BLOCK)
        qb = lax.dynamic_slice_in_dim(q, s0, Q_BLOCK, axis=1)
        s = jnp.einsum('bqhd,bkhd->bhqk', qb, k).astype(jnp.float32) * scale
        p = masked_softmax(s, kpos[None, :] <= t[:, None])
        o = jnp.einsum('bhqk,bkhd->bqhd', p.astype(v.dtype), v)
        return o.reshape(B, Q_BLOCK, MLA_HEADS * MLA_V)

    starts = jnp.arange(T // Q_BLOCK, dtype=jnp.int32) * Q_BLOCK
    o = jnp.swapaxes(lax.map(block, starts), 0, 1).reshape(B, T, MLA_HEADS * MLA_V)
    return o @ w_out


def setup_inputs(seed: int = 0) -> dict:
    key = jax.random.key(seed)
    ks = iter(jax.random.split(key, 32))
    f32 = jnp.float32

    def w(shape, fan_in):
        return jax.random.normal(next(ks), shape, f32) * (fan_in ** -0.5)

    def gain(shape):
        return 1.0 + 0.02 * jax.random.normal(next(ks), shape, f32)

    def small(shape, s):
        return s * jax.random.normal(next(ks), shape, f32)

    NA, NB, D = N_A_LAYERS, N_B_LAYERS, D_MODEL
    flat = CMP_BLOCK * NSA_DH
    return {
        'x': jax.random.normal(next(ks), (BATCH, SEQ, D), f32),
        'a_attn_norm': gain((NA, D)),
        'a_w_in': w((NA, D, NSA_IN_WIDTH), D),
        'a_q_norm': gain((NA, NSA_DH)),
        'a_kcmp_norm': gain((NA, NSA_DH)),
        'a_kslc_norm': gain((NA, NSA_DH)),
        'a_kwin_norm': gain((NA, NSA_DH)),
        'a_cmp_pos_k': small((NA, CMP_BLOCK, NSA_DH), 0.02),
        'a_cmp_pos_v': small((NA, CMP_BLOCK, NSA_DH), 0.02),
        'a_cmp_k_w1': w((NA, flat, CMP_HIDDEN), flat),
        'a_cmp_k_b1': small((NA, CMP_HIDDEN), 0.01),
        'a_cmp_k_w2': w((NA, CMP_HIDDEN, NSA_DH), CMP_HIDDEN),
        'a_cmp_v_w1': w((NA, flat, CMP_HIDDEN), flat),
        'a_cmp_v_b1': small((NA, CMP_HIDDEN), 0.01),
        'a_cmp_v_w2': w((NA, CMP_HIDDEN, NSA_DH), CMP_HIDDEN),
        'a_w_out': w((NA, NSA_HEADS * NSA_DH, D), NSA_HEADS * NSA_DH),
        'kv_norm': gain((D,)),
        'kv_w_a': w((D, KV_LORA + QK_ROPE), D),
        'kv_c_norm': gain((KV_LORA,)),
        'kv_w_b': w((KV_LORA, MLA_HEADS * (QK_NOPE + MLA_V)), KV_LORA),
        'kv_k_norm': gain((QK_NOPE + QK_ROPE,)),
        'b_attn_norm': gain((NB, D)),
        'b_w_q_a': w((NB, D, Q_LORA), D),
        'b_q_a_norm': gain((NB, Q_LORA)),
        'b_w_q_b': w((NB, Q_LORA, MLA_HEADS * (QK_NOPE + QK_ROPE)), Q_LORA),
        'b_q_norm': gain((NB, QK_NOPE + QK_ROPE)),
        'b_w_out': w((NB, MLA_HEADS * MLA_V, D), MLA_HEADS * MLA_V),
        'ffn_norm': gain((DEPTH, D)),
        'ffn_w_gate_up': w((DEPTH, D, 2 * FFN_HIDDEN), D),
        'ffn_w_down': w((DEPTH, FFN_HIDDEN, D), FFN_HIDDEN),
    }


def reference(x, a_attn_norm, a_w_in, a_q_norm, a_kcmp_norm, a_kslc_norm, a_kwin_norm,
              a_cmp_pos_k, a_cmp_pos_v, a_cmp_k_w1, a_cmp_k_b1, a_cmp_k_w2,
              a_cmp_v_w1, a_cmp_v_b1, a_cmp_v_w2, a_w_out,
              kv_norm, kv_w_a, kv_c_norm, kv_w_b, kv_k_norm,
              b_attn_norm, b_w_q_a, b_q_a_norm, b_w_q_b, b_q_norm, b_w_out,
              ffn_norm, ffn_w_gate_up, ffn_w_down):
    T = x.shape[1]
    cos, sin = rope_tables(T)
    shared_k = None
    shared_v = None
    for layer in range(DEPTH):
        if layer < N_A_LAYERS:
            i = layer
            x = x + nsa_mixer(rms_norm(x, a_attn_norm[i]), a_w_in[i], a_q_norm[i],
                              a_kcmp_norm[i], a_kslc_norm[i], a_kwin_norm[i],
                              a_cmp_pos_k[i], a_cmp_pos_v[i],
                              a_cmp_k_w1[i], a_cmp_k_b1[i], a_cmp_k_w2[i],
                              a_cmp_v_w1[i], a_cmp_v_b1[i], a_cmp_v_w2[i], a_w_out[i])
        else:
            j = layer - N_A_LAYERS
            x = x + mla_mixer(rms_norm(x, b_attn_norm[j]), shared_k, shared_v,
                              b_w_q_a[j], b_q_a_norm[j], b_w_q_b[j], b_q_norm[j],
                              b_w_out[j], cos, sin)
        x = x + swiglu(rms_norm(x, ffn_norm[layer]), ffn_w_gate_up[layer], ffn_w_down[layer])
        if layer == N_A_LAYERS - 1:
            shared_k, shared_v = shared_mla_kv(x, kv_norm, kv_w_a, kv_c_norm, kv_w_b,
                                               kv_k_norm, cos, sin)
    return x
```

```python
import numpy as np
import ml_dtypes
import concourse.bass as bass
import concourse.mybir as mybir
from concourse.bass_utils import run_bass_kernel_spmd

F32 = mybir.dt.float32
BF16 = mybir.dt.bfloat16
AF = mybir.ActivationFunctionType
ALU = mybir.AluOpType
AX = mybir.AxisListType
NPBF = ml_dtypes.bfloat16

ENGS = ["sp", "act", "dve", "pool", "pe"]


class Buf:
    __slots__ = ("w", "r", "dsem", "dcnt", "name")

    def __init__(self, name=""):
        self.w = None
        self.r = []
        self.dsem = None
        self.dcnt = 0
        self.name = name


class Prog:
    def __init__(self):
        self.nc = bass.Bass("TRN2", target_bir_lowering=False)
        nc = self.nc
        self.q = {e: [] for e in ENGS}
        self.cnt = {e: 0 for e in ENGS}
        self.sems = {}
        self.sem = {}
        for e in ("act", "dve", "pool", "pe"):
            self.sem[e] = self.new_sem("eng_" + e)
        self.wm = {e: {} for e in ENGS}
        self.all_dma_tokens = {}
        self.nsb = 0
        self.nps = 0

    def new_sem(self, name):
        h = self.nc.alloc_semaphore(name)
        sid = len(self.sems)
        self.sems[sid] = h
        return sid

    def sb(self, shape, dtype, name=None):
        self.nsb += 1
        return self.nc.alloc_sbuf_tensor(f"S{self.nsb}_" + (name or ""), list(shape), dtype)

    def ps(self, shape, dtype=F32, name=None):
        self.nps += 1
        return self.nc.alloc_psum_tensor(f"P{self.nps}_" + (name or ""), list(shape), dtype)

    def dram(self, name, shape, dtype, kind):
        return self.nc.dram_tensor(name, list(shape), dtype, kind=kind)

    def _deps(self, eng, reads, writes):
        deps = {}

        def need(tok):
            if tok is None:
                return
            s, v = tok
            if eng == "pe" and s == self.sem["pe"]:
                return
            if deps.get(s, 0) < v:
                deps[s] = v

        for b in reads:
            need(b.w)
        for b in writes:
            need(b.w)
            for t in b.r:
                need(t)
        out = []
        wm = self.wm[eng]
        for s, v in deps.items():
            if wm.get(s, 0) < v:
                wm[s] = v
                out.append((s, v))
        return out

    def op(self, eng, fn, reads=(), writes=()):
        waits = self._deps(eng, reads, writes)
        self.cnt[eng] += 1
        tok = (self.sem[eng], self.cnt[eng])
        self.q[eng].append((waits, fn, (tok[0], 1)))
        for b in reads:
            b.r.append(tok)
        for b in writes:
            b.w = tok
            b.r = []
        return tok

    def dma(self, q, out, in_, reads=(), writes=(), **kw):
        waits = self._deps(q, reads, writes)
        tb = (list(writes) + list(reads))[0]
        if tb.dsem is None:
            tb.dsem = self.new_sem("d_" + tb.name + str(len(self.sems)))
        tb.dcnt += 1
        tok = (tb.dsem, 16 * tb.dcnt)
        self.all_dma_tokens[tb.dsem] = tok[1]
        self.q[q].append((waits, lambda e: e.dma_start(out=out, in_=in_, **kw), (tok[0], 16)))
        for b in reads:
            b.r.append(tok)
        for b in writes:
            b.w = tok
            b.r = []
        return tok

    def finish(self):
        nc = self.nc
        final = [(s, v) for s, v in self.all_dma_tokens.items()]
        for e in ("act", "dve", "pool", "pe"):
            if self.cnt[e] > 0:
                final.append((self.sem[e], self.cnt[e]))
        self.q["sp"].append((final, None, None))
        emap = {"sp": "sync", "act": "scalar", "dve": "vector", "pool": "gpsimd", "pe": "tensor"}
        sems = self.sems

        def replay(lst):
            def run(e):
                for waits, fn, inc in lst:
                    for s, v in waits:
                        e.wait_ge(sems[s], v)
                    if fn is not None:
                        ins = fn(e)
                        ins.then_inc(sems[inc[0]], inc[1])
            return run

        with nc.Block() as block:
            for k in ENGS:
                if self.q[k]:
                    getattr(block, emap[k])(replay(self.q[k]))
        return nc


def mm(P, out, lhsT, rhs, start, stop, reads, writes, skip=False):
    return P.op("pe", lambda e: e.matmul(out, lhsT=lhsT, rhs=rhs, start=start, stop=stop,
                                         skip_group_check=skip), reads=reads, writes=writes)


def tr(P, out, in_, ident, reads, writes):
    return P.op("pe", lambda e: e.transpose(out, in_, ident), reads=reads, writes=writes)


def act(P, out, in_, func, reads, writes, **kw):
    return P.op("act", lambda e: e.activation(out=out, in_=in_, func=func, **kw), reads=reads, writes=writes)


def tt(P, eng, out, in0, in1, op, reads, writes):
    return P.op(eng, lambda e: e.tensor_tensor(out=out, in0=in0, in1=in1, op=op), reads=reads, writes=writes)


def ts(P, eng, out, in0, s1, s2, op0, op1, reads, writes):
    if op1 is None:
        return P.op(eng, lambda e: e.tensor_scalar(out=out, in0=in0, scalar1=s1, scalar2=None, op0=op0),
                    reads=reads, writes=writes)
    return P.op(eng, lambda e: e.tensor_scalar(out=out, in0=in0, scalar1=s1, scalar2=s2, op0=op0, op1=op1),
                reads=reads, writes=writes)


def cp(P, eng, out, in_, reads, writes):
    if eng == "act":
        return P.op("act", lambda e: e.activation(out=out, in_=in_, func=AF.Copy), reads=reads, writes=writes)
    return P.op(eng, lambda e: e.tensor_copy(out=out, in_=in_), reads=reads, writes=writes)


def memset(P, eng, ap, val, writes):
    return P.op(eng, lambda e: e.memset(ap, val), writes=writes)


class Rot:
    def __init__(self, P, n, shape, dtype, name, psum=False):
        self.items = []
        for i in range(n):
            t = P.ps(shape, dtype, f"{name}{i}") if psum else P.sb(shape, dtype, f"{name}{i}")
            self.items.append((t, Buf(f"{name}{i}")))
        self.i = 0

    def next(self):
        it = self.items[self.i % len(self.items)]
        self.i += 1
        return it


def load_const(P, dram_ap, shape, dtype, name, q="sp"):
    t = P.sb(shape, dtype, name)
    b = Buf(name)
    P.dma(q, t[:], dram_ap, writes=[b])
    return t, b


def load_weight(P, w_ap, K, N, name, stg, gain=None):
    KC = K // 128
    wt = P.sb([128, KC, N], BF16, name)
    wb = Buf(name)
    i = 0
    for kc in range(KC):
        for c0 in range(0, N, 2048):
            c1 = min(N, c0 + 2048)
            st, sbuf = stg.next()
            P.dma("sp", st[:, 0:c1 - c0], w_ap[kc * 128:(kc + 1) * 128, c0:c1], writes=[sbuf])
            eng = "dve" if i % 2 == 0 else "act"
            i += 1
            if gain is None:
                cp(P, eng, wt[:, kc, c0:c1], st[:, 0:c1 - c0], [sbuf], [wb])
            else:
                gt, gb = gain
                if eng == "dve":
                    ts(P, "dve", wt[:, kc, c0:c1], st[:, 0:c1 - c0], gt[:, kc:kc + 1], None, ALU.mult, None,
                       [sbuf, gb], [wb])
                else:
                    act(P, wt[:, kc, c0:c1], st[:, 0:c1 - c0], AF.Copy, [sbuf, gb], [wb], scale=gt[:, kc:kc + 1])
    return wt, wb


def rms_rstd(P, x_ap, D, ss, scratch, reads, SS, SCR, eps=1e-6):
    act(P, scratch, x_ap, AF.Square, reads, [SCR, SS], accum_out=ss)
    act(P, ss, ss, AF.Ln, [SS], [SS], scale=1.0 / D, bias=eps)
    act(P, ss, ss, AF.Exp, [SS], [SS], scale=-0.5)


def make_ident(P, ident_dram):
    return load_const(P, ident_dram, [128, 128], BF16, "ident_sb")


NT = 2048
NB = NT // 128


def build_outproj():
    P = Prog()
    o_d = P.dram("o", [NT, 1024], F32, "ExternalInput").ap()
    x_d = P.dram("x", [NT, 1024], F32, "ExternalInput").ap()
    w_d = P.dram("w", [1024, 1024], F32, "ExternalInput").ap()
    id_d = P.dram("ident", [128, 128], BF16, "ExternalInput").ap()
    y_d = P.dram("y", [NT, 1024], F32, "ExternalOutput").ap()
    ident, IDB = make_ident(P, id_d)
    stg = Rot(P, 2, [128, 2048], F32, "stg")
    w, WB = load_weight(P, w_d, 1024, 1024, "w", stg)
    oin = Rot(P, 2, [128, 1024], F32, "oin")
    xin = Rot(P, 2, [128, 1024], F32, "xin")
    obf = Rot(P, 2, [128, 1024], BF16, "obf")
    oT = Rot(P, 2, [128, 8, 128], BF16, "oT")
    pT = Rot(P, 2, [128, 8, 128], BF16, "pT", psum=True)
    acc = Rot(P, 4, [128, 512], F32, "acc", psum=True)
    yo = Rot(P, 2, [128, 1024], F32, "yo")
    for b in range(NB):
        rows = slice(b * 128, (b + 1) * 128)
        ot, OB = oin.next()
        xt, XB = xin.next()
        P.dma("sp", ot[:], o_d[rows, :], writes=[OB])
        P.dma("sp", xt[:], x_d[rows, :], writes=[XB])
        bt, BB = obf.next()
        cp(P, "dve", bt[:], ot[:], [OB], [BB])
        pt, PB = pT.next()
        for c in range(8):
            tr(P, pt[:, c, :], bt[:, c * 128:(c + 1) * 128], ident[:], [BB, IDB], [PB])
        tt_, TB = oT.next()
        cp(P, "act", tt_[:], pt[:], [PB], [TB])
        yt, YB = yo.next()
        for half in range(2):
            at, AB = acc.next()
            for c in range(8):
                mm(P, at[:], tt_[:, c, :], w[:, c, half * 512:(half + 1) * 512], c == 0, c == 7, [TB, WB], [AB])
            tt(P, "dve", yt[:, half * 512:(half + 1) * 512], at[:], xt[:, half * 512:(half + 1) * 512], ALU.add,
               [AB, XB], [YB])
        P.dma("pool", y_d[rows, :], yt[:], reads=[YB])
    return P.finish()


FH = 2816


def build_ffn():
    P = Prog()
    x_d = P.dram("x", [NT, 1024], F32, "ExternalInput").ap()
    g_d = P.dram("g", [128, 8], F32, "ExternalInput").ap()
    wgu_d = P.dram("wgu", [1024, 2 * FH], F32, "ExternalInput").ap()
    wd_d = P.dram("wd", [FH, 1024], F32, "ExternalInput").ap()
    id_d = P.dram("ident", [128, 128], BF16, "ExternalInput").ap()
    y_d = P.dram("y", [NT, 1024], F32, "ExternalOutput").ap()
    ident, IDB = make_ident(P, id_d)
    gain = load_const(P, g_d, [128, 8], F32, "gain")
    stg = Rot(P, 2, [128, 2048], F32, "stg")
    wgu, WGU = load_weight(P, wgu_d, 1024, 2 * FH, "wgu", stg, gain=gain)
    wd, WD = load_weight(P, wd_d, FH, 1024, "wd", stg)
    xin = Rot(P, 4, [128, 1024], F32, "xin")
    scr = P.sb([128, 1024], F32, "scr"); SCR = Buf("scr")
    ssr = Rot(P, 2, [128, 1], F32, "ss")
    xn = Rot(P, 2, [128, 1024], BF16, "xn")
    pT = Rot(P, 1, [128, 8, 128], BF16, "pT", psum=True)
    xT = Rot(P, 2, [128, 8, 256], BF16, "xT")
    gu = Rot(P, 3, [128, 512], F32, "gu", psum=True)
    accs = Rot(P, 4, [128, 512], F32, "acc", psum=True)
    sg = Rot(P, 2, [128, 256], BF16, "sg")
    aT = Rot(P, 2, [128, 256], BF16, "aT")
    yo = Rot(P, 2, [128, 1024], F32, "yo")
    NJ = FH // 128
    for sbk in range(NT // 256):
        xs = []
        xTt, XT = xT.next()
        for t in range(2):
            rows = slice(sbk * 256 + t * 128, sbk * 256 + (t + 1) * 128)
            xt, XB = xin.next()
            xs.append((xt, XB, rows))
            P.dma("sp", xt[:], x_d[rows, :], writes=[XB])
            ss, SS = ssr.next()
            rms_rstd(P, x_ap=xt[:], D=1024, ss=ss[:], scratch=scr[:], reads=[XB], SS=SS, SCR=SCR)
            xnt, XN = xn.next()
            ts(P, "dve", xnt[:], xt[:], ss[:, 0:1], None, ALU.mult, None, [XB, SS], [XN])
            pt, PB = pT.next()
            for c in range(8):
                tr(P, pt[:, c, :], xnt[:, c * 128:(c + 1) * 128], ident[:], [XN, IDB], [PB])
            cp(P, "dve", xTt[:, :, t * 128:(t + 1) * 128], pt[:], [PB], [XT])
        acc = [[accs.next() for _ in range(2)] for _ in range(2)]
        for j in range(NJ):
            gt, GB = gu.next()
            ut, UB = gu.next()
            for c in range(8):
                mm(P, gt[:, 0:256], wgu[:, c, j * 128:(j + 1) * 128], xTt[:, c, :], c == 0, c == 7, [WGU, XT], [GB])
            for c in range(8):
                mm(P, ut[:, 0:256], wgu[:, c, FH + j * 128:FH + (j + 1) * 128], xTt[:, c, :], c == 0, c == 7,
                   [WGU, XT], [UB])
            sgt, SG = sg.next()
            act(P, sgt[:], gt[:, 0:256], AF.Silu, [GB], [SG])
            at, AB = aT.next()
            tt(P, "dve", at[:], sgt[:], ut[:, 0:256], ALU.mult, [SG, UB], [AB])
            for t in range(2):
                for half in range(2):
                    a_t, A_B = acc[t][half]
                    mm(P, a_t[:], at[:, t * 128:(t + 1) * 128], wd[:, j, half * 512:(half + 1) * 512], j == 0,
                       j == NJ - 1, [AB, WD], [A_B])
        for t in range(2):
            xt, XB, rows = xs[t]
            yt, YB = yo.next()
            for half in range(2):
                a_t, A_B = acc[t][half]
                tt(P, "dve", yt[:, half * 512:(half + 1) * 512], a_t[:], xt[:, half * 512:(half + 1) * 512], ALU.add,
                   [A_B, XB], [YB])
            P.dma("pool", y_d[rows, :], yt[:], reads=[YB])
    return P.finish()


def rope_ops(P, src, SRC, dst, DST, cosb, sinb, CS, tmp, TMP, nh):
    x1 = src[:, :, 0:32]
    x2 = src[:, :, 32:64]
    t1, t2 = tmp[:, 0, 0:nh, :], tmp[:, 1, 0:nh, :]
    tt(P, "dve", t1, x1, cosb, ALU.mult, [SRC, CS], [TMP])
    tt(P, "dve", t2, x2, sinb, ALU.mult, [SRC, CS], [TMP])
    tt(P, "dve", dst[:, :, 0:32], t1, t2, ALU.subtract, [TMP], [DST])
    tt(P, "dve", t1, x1, sinb, ALU.mult, [SRC, CS], [TMP])
    tt(P, "dve", t2, x2, cosb, ALU.mult, [SRC, CS], [TMP])
    tt(P, "dve", dst[:, :, 32:64], t1, t2, ALU.add, [TMP], [DST])


def build_kvq(dbg=0):
    P = Prog()
    x_d = P.dram("x", [NT, 1024], F32, "ExternalInput").ap()
    gkv_d = P.dram("g_kv", [128, 8], F32, "ExternalInput").ap()
    gq_d = P.dram("g_q", [128, 8], F32, "ExternalInput").ap()
    gc_d = P.dram("g_c", [128, 2], F32, "ExternalInput").ap()
    gqa_d = P.dram("g_qa", [128, 3], F32, "ExternalInput").ap()
    gkn_d = P.dram("g_kn", [128, 192], F32, "ExternalInput").ap()
    gqn_d = P.dram("g_qn", [128, 192], F32, "ExternalInput").ap()
    wkva_d = P.dram("w_kva", [1024, 320], F32, "ExternalInput").ap()
    wkvb_d = P.dram("w_kvb", [256, 2048], F32, "ExternalInput").ap()
    wqa_d = P.dram("w_qa", [1024, 384], F32, "ExternalInput").ap()
    wqb_d = P.dram("w_qb", [384, 1536], F32, "ExternalInput").ap()
    cs_d = P.dram("cs", [128, NB, 64], F32, "ExternalInput").ap()
    id_d = P.dram("ident", [128, 128], BF16, "ExternalInput").ap()
    kT_d = P.dram("k", [NT, 1536], BF16, "ExternalOutput").ap()
    qT_d = P.dram("q", [NT, 1536], BF16, "ExternalOutput").ap()
    v_d = P.dram("v", [NT, 1024], BF16, "ExternalOutput").ap()
    ident, IDB = make_ident(P, id_d)
    gkv = load_const(P, gkv_d, [128, 8], F32, "gkv")
    gq = load_const(P, gq_d, [128, 8], F32, "gq")
    gc = load_const(P, gc_d, [128, 2], F32, "gc")
    gqa = load_const(P, gqa_d, [128, 3], F32, "gqa")
    gkn, GKN = load_const(P, gkn_d, [128, 192], F32, "gkn")
    gqn, GQN = load_const(P, gqn_d, [128, 192], F32, "gqn")
    cs, CS = load_const(P, cs_d, [128, NB, 64], F32, "cs")
    stg = Rot(P, 2, [128, 2048], F32, "stg")
    wkva, WKVA = load_weight(P, wkva_d, 1024, 320, "wkva", stg, gain=gkv)
    wkvb, WKVB = load_weight(P, wkvb_d, 256, 2048, "wkvb", stg, gain=gc)
    wqa, WQA = load_weight(P, wqa_d, 1024, 384, "wqa", stg, gain=gq)
    wqb, WQB = load_weight(P, wqb_d, 384, 1536, "wqb", stg, gain=gqa)
    xin = Rot(P, 2, [128, 1024], F32, "xin")
    scr = P.sb([128, 1024], F32, "scr"); SCR = Buf("scr")
    ssr = Rot(P, 4, [128, 1], F32, "ss")
    ss8 = Rot(P, 4, [128, 8], F32, "ss8")
    xn = Rot(P, 2, [128, 1024], BF16, "xn")
    pT = Rot(P, 2, [128, 8, 128], BF16, "pT", psum=True)
    xT = Rot(P, 2, [128, 8, 128], BF16, "xT")
    pf = Rot(P, 6, [128, 512], F32, "pf", psum=True)
    cn = Rot(P, 2, [128, 384], BF16, "cn")
    cT = Rot(P, 2, [128, 3, 128], BF16, "cT")
    kpe = Rot(P, 2, [128, 64], F32, "kpe")
    tmpf = Rot(P, 3, [128, 8, 192], F32, "tmpf")
    pe8 = Rot(P, 2, [128, 8, 64], F32, "pe8")
    rtmp = P.sb([128, 2, 8, 32], F32, "rtmp"); RTMP = Buf("rtmp")
    kt = Rot(P, 2, [128, 8, 192], BF16, "kt")
    vt = Rot(P, 2, [128, 8, 128], BF16, "vt")
    kTs = Rot(P, 2, [128, 2, 8, 128], BF16, "kTs")
    for b in range(NB):
        rows = slice(b * 128, (b + 1) * 128)
        cosb = cs[:, b, 0:32].unsqueeze(1).to_broadcast([128, 8, 32])
        sinb = cs[:, b, 32:64].unsqueeze(1).to_broadcast([128, 8, 32])
        xt, XB = xin.next()
        P.dma("sp", xt[:], x_d[rows, :], writes=[XB])
        ss, SS = ssr.next()
        rms_rstd(P, xt[:], 1024, ss[:], scr[:], [XB], SS, SCR)
        xnt, XN = xn.next()
        ts(P, "dve", xnt[:], xt[:], ss[:, 0:1], None, ALU.mult, None, [XB, SS], [XN])
        pt, PB = pT.next()
        for c in range(8):
            tr(P, pt[:, c, :], xnt[:, c * 128:(c + 1) * 128], ident[:], [XN, IDB], [PB])
        xTt, XT = xT.next()
        cp(P, "dve", xTt[:], pt[:], [PB], [XT])
        kva, KVA = pf.next()
        qa, QA = pf.next()
        for c in range(8):
            mm(P, kva[:, 0:320], xTt[:, c, :], wkva[:, c, :], c == 0, c == 7, [XT, WKVA], [KVA])
        for c in range(8):
            mm(P, qa[:, 0:384], xTt[:, c, :], wqa[:, c, :], c == 0, c == 7, [XT, WQA], [QA])
        ss2, SS2 = ssr.next()
        rms_rstd(P, kva[:, 0:256], 256, ss2[:], scr[:, 0:256], [KVA], SS2, SCR)
        cnt, CN = cn.next()
        ts(P, "dve", cnt[:, 0:256], kva[:, 0:256], ss2[:, 0:1], None, ALU.mult, None, [KVA, SS2], [CN])
        kp, KP = kpe.next()
        tt(P, "dve", kp[:], kva[:, 256:320], gkn[:, 128:192], ALU.mult, [KVA, GKN], [KP])
        sspe, SSPE = ssr.next()
        act(P, scr[:, 0:64], kva[:, 256:320], AF.Square, [KVA], [SCR, SSPE], accum_out=sspe[:])
        pt2, PB2 = pT.next()
        for c in range(2):
            tr(P, pt2[:, c, :], cnt[:, c * 128:(c + 1) * 128], ident[:], [CN, IDB], [PB2])
        cTt, CT = cT.next()
        cp(P, "act", cTt[:, 0:2, :], pt2[:, 0:2, :], [PB2], [CT])
        kvb = []
        for n in range(4):
            kvp, KVP = pf.next()
            kvb.append((kvp, KVP))
            for c in range(2):
                mm(P, kvp[:], cTt[:, c, :], wkvb[:, c, n * 512:(n + 1) * 512], c == 0, c == 1, [CT, WKVB], [KVP])
        s8, S8 = ss8.next()
        for h in range(8):
            kvp, KVP = kvb[h // 2]
            act(P, scr[:, 0:128], kvp[:, (h % 2) * 256:(h % 2) * 256 + 128], AF.Square, [KVP], [SCR, S8],
                accum_out=s8[:, h:h + 1])
        ts(P, "dve", s8[:], s8[:], sspe[:, 0:1], None, ALU.add, None, [S8, SSPE], [S8])
        act(P, s8[:], s8[:], AF.Ln, [S8], [S8], scale=1.0 / 192, bias=1e-6)
        act(P, s8[:], s8[:], AF.Exp, [S8], [S8], scale=-0.5)
        tf, TF = tmpf.next()
        ktt, KT = kt.next()
        vtt, VT = vt.next()
        for n in range(4):
            kvp, KVP = kvb[n]
            kv3 = kvp[:].rearrange("p (h c) -> p h c", h=2)
            tt(P, "dve", tf[:, 2 * n:2 * n + 2, 0:128], kv3[:, :, 0:128],
               gkn[:, 0:128].unsqueeze(1).to_broadcast([128, 2, 128]), ALU.mult, [KVP, GKN], [TF])
            cp(P, "act", vtt[:, 2 * n:2 * n + 2, :], kv3[:, :, 128:256], [KVP], [VT])
        tt(P, "dve", ktt[:, :, 0:128], tf[:, :, 0:128], s8[:, :].unsqueeze(2).to_broadcast([128, 8, 128]), ALU.mult,
           [TF, S8], [KT])
        p8, P8 = pe8.next()
        tt(P, "dve", p8[:], kp[:, :].unsqueeze(1).to_broadcast([128, 8, 64]),
           s8[:, :].unsqueeze(2).to_broadcast([128, 8, 64]), ALU.mult, [KP, S8], [P8])
        rope_ops(P, p8[:], P8, ktt[:, :, 128:192], KT, cosb, sinb, CS, rtmp, RTMP, 8)
        P.dma("pool", v_d[rows, :], vtt[:].rearrange("p h d -> p (h d)"), reads=[VT])

        def emit_T(src, SRC, dst_d):
            P.dma("pool", dst_d[rows, :], src[:].rearrange("p h d -> p (h d)"), reads=[SRC])

        if dbg != 1:
            emit_T(ktt, KT, kT_d)
        if dbg in (1, 2):
            continue
        ss3, SS3 = ssr.next()
        rms_rstd(P, qa[:, 0:384], 384, ss3[:], scr[:, 0:384], [QA], SS3, SCR)
        qn, QN = cn.next()
        ts(P, "dve", qn[:], qa[:, 0:384], ss3[:, 0:1], None, ALU.mult, None, [QA, SS3], [QN])
        pt3, PB3 = pT.next()
        for c in range(3):
            tr(P, pt3[:, c, :], qn[:, c * 128:(c + 1) * 128], ident[:], [QN, IDB], [PB3])
        qT_, QT = cT.next()
        cp(P, "act", qT_[:], pt3[:, 0:3, :], [PB3], [QT])
        qb = []
        for n in range(4):
            qp, QP = pf.next()
            qb.append((qp, QP))
            for c in range(3):
                mm(P, qp[:, 0:384], qT_[:, c, :], wqb[:, c, n * 384:(n + 1) * 384], c == 0, c == 2, [QT, WQB], [QP])
        s8q, S8Q = ss8.next()
        for h in range(8):
            qp, QP = qb[h // 2]
            act(P, scr[:, 0:192], qp[:, (h % 2) * 192:(h % 2) * 192 + 192], AF.Square, [QP], [SCR, S8Q],
                accum_out=s8q[:, h:h + 1])
        act(P, s8q[:], s8q[:], AF.Ln, [S8Q], [S8Q], scale=1.0 / 192, bias=1e-6)
        act(P, s8q[:], s8q[:], AF.Exp, [S8Q], [S8Q], scale=-0.5)
        if dbg == 3:
            continue
        tq, TQ = tmpf.next()
        for n in range(4):
            qp, QP = qb[n]
            cp(P, "act", tq[:, 2 * n:2 * n + 2, :].rearrange("p h c -> p (h c)"), qp[:, 0:384], [QP], [TQ])
        tq2, TQ2 = tmpf.next()
        tt(P, "dve", tq2[:], tq[:], gqn[:, :].unsqueeze(1).to_broadcast([128, 8, 192]), ALU.mult, [TQ, GQN], [TQ2])
        tq, TQ = tq2, TQ2
        if dbg == 4:
            continue
        qtt, QTT = kt.next()
        tt(P, "dve", qtt[:, :, 0:128], tq[:, :, 0:128], s8q[:, :].unsqueeze(2).to_broadcast([128, 8, 128]), ALU.mult,
           [TQ, S8Q], [QTT])
        p8q, P8Q = pe8.next()
        tt(P, "dve", p8q[:], tq[:, :, 128:192], s8q[:, :].unsqueeze(2).to_broadcast([128, 8, 64]), ALU.mult,
           [TQ, S8Q], [P8Q])
        rope_ops(P, p8q[:], P8Q, qtt[:, :, 128:192], QTT, cosb, sinb, CS, rtmp, RTMP, 8)
        emit_T(qtt, QTT, qT_d)
    return P.finish()


TT = 8192


def build_mla():
    P = Prog()
    kTa_d = P.dram("kTa", [128, 2, TT], BF16, "ExternalInput").ap()
    kTb_d = P.dram("kTb", [64, 2, TT], BF16, "ExternalInput").ap()
    qTa_d = P.dram("qTa", [128, 2, TT], BF16, "ExternalInput").ap()
    qTb_d = P.dram("qTb", [64, 2, TT], BF16, "ExternalInput").ap()
    v_d = P.dram("vaug", [128, 64, 2, 132], BF16, "ExternalInput").ap()
    m_d = P.dram("masks", [128, 4, 512], BF16, "ExternalInput").ap()
    o_d = P.dram("o", [TT, 256], F32, "ExternalOutput").ap()
    kTa = P.sb([128, 2, TT], BF16, "kTa"); KA = [Buf("ka0"), Buf("ka1")]
    kTb = P.sb([64, 2, TT], BF16, "kTb"); KB = [Buf("kb0"), Buf("kb1")]
    va = P.sb([128, 64, 2, 132], BF16, "va"); VA = Buf("va")
    for h in range(2):
        P.dma("sp", kTa[:, h, :], kTa_d[:, h, :], writes=[KA[h]])
        P.dma("sp", kTb[:, h, :], kTb_d[:, h, :], writes=[KB[h]])
    for c0 in range(0, 64, 16):
        P.dma("sp", va[:, c0:c0 + 16], v_d[:, c0:c0 + 16], writes=[VA])
    mk, MK = load_const(P, m_d, [128, 4, 512], BF16, "mk")
    qa_r = Rot(P, 2, [128, 512], BF16, "qa")
    qb_r = Rot(P, 2, [64, 512], BF16, "qb")
    s_r = Rot(P, 2, [128, 512], F32, "S", psum=True)
    acc_r = Rot(P, 4, [128, 512], F32, "acc", psum=True)
    p_r = Rot(P, 3, [128, 512], BF16, "p")
    pm_r = Rot(P, 2, [128, 512], BF16, "pm")
    rs_r = Rot(P, 4, [128, 1], F32, "rs")
    o_r = Rot(P, 2, [128, 4, 128], F32, "osb")
    scale = 192.0 ** -0.5
    for h in range(2):
        for qs in range(TT // 512):
            qa, QA = qa_r.next()
            qb, QB = qb_r.next()
            P.dma("sp", qa[:], qTa_d[:, h, qs * 512:(qs + 1) * 512], writes=[QA])
            P.dma("sp", qb[:], qTb_d[:, h, qs * 512:(qs + 1) * 512], writes=[QB])
            accs = [acc_r.next() for _ in range(4)]
            nk = 4 * qs + 4
            for kc in range(nk):
                S, SB_ = s_r.next()
                mm(P, S[:], kTa[:, h, kc * 128:(kc + 1) * 128], qa[:], True, False, [KA[h], QA], [SB_])
                mm(P, S[:], kTb[:, h, kc * 128:(kc + 1) * 128], qb[:], False, True, [KB[h], QB], [SB_])
                p, PB = p_r.next()
                act(P, p[:], S[:], AF.Exp, [SB_], [PB], scale=scale)
                j = kc - 4 * qs
                if j >= 0:
                    pm, PM = pm_r.next()
                    tt(P, "dve", pm[:], p[:], mk[:, j, :], ALU.mult, [PB, MK], [PM])
                    p, PB = pm, PM
                for t in range(4):
                    if j >= 0 and t < j:
                        continue
                    a, AB = accs[t]
                    mm(P, a[:, 0:129], p[:, t * 128:(t + 1) * 128], va[:, kc, h, 0:129], kc == 0, kc == 4 * qs + t,
                       [PB, VA], [AB])
            osb, OB = o_r.next()
            for t in range(4):
                a, AB = accs[t]
                rs, RS = rs_r.next()
                P.op("dve", lambda e, rs=rs, a=a: e.reciprocal(out=rs[:], in_=a[:, 128:129]), reads=[AB], writes=[RS])
                ts(P, "dve", osb[:, t, :], a[:, 0:128], rs[:, 0:1], None, ALU.mult, None, [AB, RS], [OB])
            P.dma("sp", o_d[qs * 512:(qs + 1) * 512, h * 128:(h + 1) * 128].rearrange("(t p) d -> p t d", p=128),
                  osb[:], reads=[OB])
    return P.finish()


def mla_host_inputs(k, q, v, b, hp):
    ks = k[b * TT:(b + 1) * TT].reshape(TT, 8, 192)[:, 2 * hp:2 * hp + 2, :]
    qs = q[b * TT:(b + 1) * TT].reshape(TT, 8, 192)[:, 2 * hp:2 * hp + 2, :]
    vs = v[b * TT:(b + 1) * TT].reshape(TT, 8, 128)[:, 2 * hp:2 * hp + 2, :]
    kT = np.ascontiguousarray(ks.transpose(2, 1, 0))
    qT = np.ascontiguousarray(qs.transpose(2, 1, 0))
    vaug = np.zeros((128, 64, 2, 132), dtype=NPBF)
    vaug[:, :, :, 0:128] = vs.reshape(64, 128, 2, 128).transpose(1, 0, 2, 3)
    vaug[:, :, :, 128] = 1.0
    kl = np.arange(128)[:, None, None]
    j = np.arange(4)[None, :, None]
    ql = np.arange(512)[None, None, :]
    masks = (j * 128 + kl <= ql).astype(np.float32).astype(NPBF)
    return {"kTa": np.ascontiguousarray(kT[0:128]), "kTb": np.ascontiguousarray(kT[128:192]),
            "qTa": np.ascontiguousarray(qT[0:128]), "qTb": np.ascontiguousarray(qT[128:192]),
            "vaug": vaug, "masks": np.ascontiguousarray(masks)}


NW = 2608


def build_nsa_a():
    P = Prog()
    x_d = P.dram("x", [NT, 1024], F32, "ExternalInput").ap()
    g_d = P.dram("g", [128, 8], F32, "ExternalInput").ap()
    w_d = P.dram("w", [1024, NW], F32, "ExternalInput").ap()
    gq_d = P.dram("gq", [128, 64], F32, "ExternalInput").ap()
    gs_d = P.dram("gs", [128, 64], F32, "ExternalInput").ap()
    gw_d = P.dram("gw", [128, 64], F32, "ExternalInput").ap()
    id_d = P.dram("ident", [128, 128], BF16, "ExternalInput").ap()
    big_d = P.dram("big", [NT, 2560], BF16, "ExternalOutput").ap()
    gates_d = P.dram("gates", [NT, 48], F32, "ExternalOutput").ap()
    ident, IDB = make_ident(P, id_d)
    gain = load_const(P, g_d, [128, 8], F32, "gain")
    gq, GQ = load_const(P, gq_d, [128, 64], F32, "gq")
    gs, GS = load_const(P, gs_d, [128, 64], F32, "gs")
    gw, GW = load_const(P, gw_d, [128, 64], F32, "gw")
    stg = Rot(P, 2, [128, 2048], F32, "stg")
    w, WB = load_weight(P, w_d, 1024, NW, "w", stg, gain=gain)
    xin = Rot(P, 2, [128, 1024], F32, "xin")
    scr = P.sb([128, 1536], F32, "scr"); SCR = Buf("scr")
    ssr = Rot(P, 2, [128, 1], F32, "ss")
    xn = Rot(P, 2, [128, 1024], BF16, "xn")
    pT = Rot(P, 1, [128, 8, 128], BF16, "pT", psum=True)
    xT = Rot(P, 2, [128, 8, 128], BF16, "xT")
    pf = Rot(P, 6, [128, 512], F32, "pf", psum=True)
    y_r = Rot(P, 2, [128, NW], F32, "y")
    ssn_r = Rot(P, 2, [128, 24], F32, "ssn")
    tmp_r = Rot(P, 2, [128, 1024], F32, "tmp")
    big_r = Rot(P, 2, [128, 2560], BF16, "bigs")
    gt_r = Rot(P, 2, [128, 48], F32, "gts")
    for b in range(NB):
        rows = slice(b * 128, (b + 1) * 128)
        xt, XB = xin.next()
        P.dma("sp", xt[:], x_d[rows, :], writes=[XB])
        ss, SS = ssr.next()
        rms_rstd(P, xt[:], 1024, ss[:], scr[:, 0:1024], [XB], SS, SCR)
        xnt, XN = xn.next()
        ts(P, "dve", xnt[:], xt[:], ss[:, 0:1], None, ALU.mult, None, [XB, SS], [XN])
        pt, PB = pT.next()
        for c in range(8):
            tr(P, pt[:, c, :], xnt[:, c * 128:(c + 1) * 128], ident[:], [XN, IDB], [PB])
        xTt, XT = xT.next()
        cp(P, "dve", xTt[:], pt[:], [PB], [XT])
        y, YB = y_r.next()
        for n in range(6):
            c0, c1 = n * 512, min(NW, (n + 1) * 512)
            pp, PP = pf.next()
            for c in range(8):
                mm(P, pp[:, 0:c1 - c0], xTt[:, c, :], w[:, c, c0:c1], c == 0, c == 7, [XT, WB], [PP])
            cp(P, "act", y[:, c0:c1], pp[:, 0:c1 - c0], [PP], [YB])
        act(P, scr[:, 0:1024], y[:, 0:1024], AF.Square, [YB], [SCR])
        act(P, scr[:, 1024:1280], y[:, 1536:1792], AF.Square, [YB], [SCR])
        act(P, scr[:, 1280:1536], y[:, 2048:2304], AF.Square, [YB], [SCR])
        ssn, SSN = ssn_r.next()
        P.op("dve", lambda e, ssn=ssn: e.tensor_reduce(out=ssn[:], in_=scr[:, 0:1536].rearrange("p (h d) -> p h d", d=64),
                                                      axis=AX.X, op=ALU.add), reads=[SCR], writes=[SSN])
        act(P, ssn[:], ssn[:], AF.Ln, [SSN], [SSN], scale=1.0 / 64, bias=1e-6)
        act(P, ssn[:], ssn[:], AF.Exp, [SSN], [SSN], scale=-0.5)
        bg, BG = big_r.next()
        tmp, TMP = tmp_r.next()

        def normed(src0, nh, h0, gt, GT, dst0):
            tv = tmp[:, 0:nh * 64].rearrange("p (h d) -> p h d", d=64)
            tt(P, "dve", tv, y[:, src0:src0 + nh * 64].rearrange("p (h d) -> p h d", d=64),
               ssn[:, h0:h0 + nh].unsqueeze(2).to_broadcast([128, nh, 64]), ALU.mult, [YB, SSN], [TMP])
            tt(P, "dve", bg[:, dst0:dst0 + nh * 64].rearrange("p (h d) -> p h d", d=64), tv,
               gt[:, :].unsqueeze(1).to_broadcast([128, nh, 64]), ALU.mult, [TMP, GT], [BG])

        normed(0, 16, 0, gq, GQ, 0)
        normed(1536, 4, 16, gs, GS, 1536)
        normed(2048, 4, 20, gw, GW, 2048)
        cp(P, "dve", bg[:, 1024:1536], y[:, 1024:1536], [YB], [BG])
        cp(P, "dve", bg[:, 1792:2048], y[:, 1792:2048], [YB], [BG])
        cp(P, "dve", bg[:, 2304:2560], y[:, 2304:2560], [YB], [BG])
        gt_, GT_ = gt_r.next()
        act(P, gt_[:], y[:, 2560:2608], AF.Sigmoid, [YB], [GT_])
        P.dma("pool", big_d[rows, :], bg[:], reads=[BG])
        P.dma("pool", gates_d[rows, :], gt_[:], reads=[GT_])
    return P.finish()


def build_nsa_b():
    P = Prog()
    ins = {}
    for nm in ("k", "v"):
        ins[nm] = dict(
            aT=P.dram(nm + "_a2T", [128, TT], BF16, "ExternalInput").ap(),
            w1=P.dram(nm + "_w1", [128, 16, 256], F32, "ExternalInput").ap(),
            pe=P.dram(nm + "_pe", [128, 16], F32, "ExternalInput").ap(),
            b1=P.dram(nm + "_b1", [128, 2], F32, "ExternalInput").ap(),
            w2=P.dram(nm + "_w2", [128, 2, 64], F32, "ExternalInput").ap(),
            out=P.dram(nm + "_cmp", [512, 64], BF16, "ExternalOutput").ap(),
        )
    gk_d = P.dram("gk", [128, 64], F32, "ExternalInput").ap()
    gk, GK = load_const(P, gk_d, [128, 64], F32, "gk")
    ps_r = Rot(P, 4, [128, 512], F32, "ps", psum=True)
    f_r = Rot(P, 6, [128, 512], F32, "f")
    for nm in ("k", "v"):
        I = ins[nm]
        aT, AT = load_const(P, I["aT"], [128, TT], BF16, nm + "aT")
        w1f, W1F = load_const(P, I["w1"], [128, 16, 256], F32, nm + "w1f")
        pef, PEF = load_const(P, I["pe"], [128, 16], F32, nm + "pef")
        b1, B1 = load_const(P, I["b1"], [128, 2], F32, nm + "b1")
        w2f, W2F = load_const(P, I["w2"], [128, 2, 64], F32, nm + "w2f")
        w1 = P.sb([128, 16, 256], BF16, nm + "w1"); W1 = Buf(nm + "w1")
        cp(P, "dve", w1[:], w1f[:], [W1F], [W1])
        peb = P.sb([128, 16], BF16, nm + "peb"); PEB = Buf(nm + "peb")
        cp(P, "dve", peb[:], pef[:], [PEF], [PEB])
        w2 = P.sb([128, 2, 64], BF16, nm + "w2"); W2 = Buf(nm + "w2")
        cp(P, "dve", w2[:], w2f[:], [W2F], [W2])
        av = aT[:, :].rearrange("p (j s) -> p j s", s=16)
        gT = P.sb([128, 2, 512], BF16, nm + "gT"); GT = Buf(nm + "gT")
        memset(P, "dve", gT[:], 0.0, [GT])
        for half in range(2):
            hp, HP = ps_r.next()
            cv, CV = ps_r.next()
            for m in range(16):
                rhs = av[:, 0:511, 2 * m] if m < 8 else av[:, 1:512, 2 * m - 16]
                mm(P, hp[:, 0:511], w1[:, m, half * 128:(half + 1) * 128], rhs, m == 0, m == 15, [W1, AT], [HP])
            for m in range(16):
                mm(P, cv[:, 0:1], w1[:, m, half * 128:(half + 1) * 128], peb[:, m:m + 1], m == 0, m == 15, [W1, PEB], [CV])
            bias, BI = f_r.next()
            tt(P, "dve", bias[:, 0:1], cv[:, 0:1], b1[:, half:half + 1], ALU.add, [CV, B1], [BI])
            u, U = f_r.next()
            act(P, u[:, 0:511], hp[:, 0:511], AF.Identity, [HP, BI], [U], bias=bias[:, 0:1])
            t1, T1 = f_r.next()
            tt(P, "dve", t1[:, 0:511], u[:, 0:511], u[:, 0:511], ALU.mult, [U], [T1])
            t2, T2 = f_r.next()
            ts(P, "dve", t2[:, 0:511], t1[:, 0:511], 0.044715, 1.0, ALU.mult, ALU.add, [T1], [T2])
            t3, T3 = f_r.next()
            tt(P, "dve", t3[:, 0:511], t2[:, 0:511], u[:, 0:511], ALU.mult, [T2, U], [T3])
            t4, T4 = f_r.next()
            act(P, t4[:, 0:511], t3[:, 0:511], AF.Tanh, [T3], [T4], scale=0.7978845608028654)
            t5, T5 = f_r.next()
            ts(P, "dve", t5[:, 0:511], t4[:, 0:511], 1.0, 0.5, ALU.add, ALU.mult, [T4], [T5])
            tt(P, "dve", gT[:, half, 0:511], t5[:, 0:511], u[:, 0:511], ALU.mult, [T5, U], [GT])
        for ch in range(4):
            op_, OP = ps_r.next()
            for half in range(2):
                mm(P, op_[:, 0:64], gT[:, half, ch * 128:(ch + 1) * 128], w2[:, half, :], half == 0, half == 1,
                   [GT, W2], [OP])
            ob = P.sb([128, 64], BF16, f"{nm}ob{ch}"); OB = Buf("ob")
            if nm == "k":
                sq, SQ = f_r.next()
                ssk = P.sb([128, 1], F32, f"ssk{ch}"); SSK = Buf("ssk")
                rms_rstd(P, op_[:, 0:64], 64, ssk[:], sq[:, 0:64], [OP], SSK, SQ)
                kn, KN = f_r.next()
                ts(P, "dve", kn[:, 0:64], op_[:, 0:64], ssk[:, 0:1], None, ALU.mult, None, [OP, SSK], [KN])
                tt(P, "dve", ob[:], kn[:, 0:64], gk[:], ALU.mult, [KN, GK], [OB])
            else:
                cp(P, "act", ob[:], op_[:, 0:64], [OP], [OB])
            P.dma("sp", I["out"][ch * 128:(ch + 1) * 128, :], ob[:], reads=[OB])
    return P.finish()


NQB = TT // 128


def build_nsa_c():
    P = Prog()
    QT_d = P.dram("QT", [70, 4, TT], BF16, "ExternalInput").ap()
    KS_d = P.dram("KS", [70, TT], BF16, "ExternalInput").ap()
    KW_d = P.dram("KW", [70, TT], BF16, "ExternalInput").ap()
    KC_d = P.dram("KC", [70, 512], BF16, "ExternalInput").ap()
    VS_d = P.dram("VS", [128, 64, 66], BF16, "ExternalInput").ap()
    VW_d = P.dram("VW", [128, 64, 66], BF16, "ExternalInput").ap()
    VC_d = P.dram("VC", [128, 4, 194], BF16, "ExternalInput").ap()
    G_d = P.dram("G", [128, 64, 12], F32, "ExternalInput").ap()
    E_d = P.dram("E", [128, TT], BF16, "ExternalInput").ap()
    CM_d = P.dram("CM", [128, 17, 512], BF16, "ExternalInput").ap()
    CA_d = P.dram("CA", [128, 512], BF16, "ExternalInput").ap()
    WM_d = P.dram("WM", [128, 512], BF16, "ExternalInput").ap()
    AT_d = P.dram("AT", [128, 256], F32, "ExternalInput").ap()
    id_d = P.dram("ident", [128, 128], BF16, "ExternalInput").ap()
    o_d = P.dram("o", [TT, 256], F32, "ExternalOutput").ap()
    ident, IDB = make_ident(P, id_d)
    QT = P.sb([70, 4, TT], BF16, "QT"); QTB = Buf("QT")
    for h in range(4):
        P.dma("sp", QT[:, h, :], QT_d[:, h, :], writes=[QTB])
    KS, KSB = load_const(P, KS_d, [70, TT], BF16, "KS")
    KW, KWB = load_const(P, KW_d, [70, TT], BF16, "KW")
    KC, KCB = load_const(P, KC_d, [70, 512], BF16, "KC")
    VS, VSB = load_const(P, VS_d, [128, 64, 66], BF16, "VS")
    VW, VWB = load_const(P, VW_d, [128, 64, 66], BF16, "VW")
    VC, VCB = load_const(P, VC_d, [128, 4, 194], BF16, "VC")
    G, GB = load_const(P, G_d, [128, 64, 12], F32, "G")
    E, EB = load_const(P, E_d, [128, TT], BF16, "E")
    CM, CMB = load_const(P, CM_d, [128, 17, 512], BF16, "CM")
    CA, CAB = load_const(P, CA_d, [128, 512], BF16, "CA")
    WM, WMB = load_const(P, WM_d, [128, 512], BF16, "WM")
    AT, ATB = load_const(P, AT_d, [128, 256], F32, "AT")
    s_r = Rot(P, 2, [128, 512], F32, "S", psum=True)
    accC_r = Rot(P, 2, [128, 512], F32, "accC", psum=True)
    accS_r = Rot(P, 1, [128, 512], F32, "accS", psum=True)
    accW_r = Rot(P, 1, [128, 512], F32, "accW", psum=True)
    pst_r = Rot(P, 1, [128, 128], BF16, "pst", psum=True)
    p_r = Rot(P, 3, [128, 512], BF16, "p")
    sm_r = Rot(P, 16, [128, 1], F32, "sm")
    imp_r = Rot(P, 3, [128, 128], F32, "imp")
    sc_r = Rot(P, 3, [128, 128], F32, "sc")
    m8_r = Rot(P, 4, [128, 8], F32, "m8")
    ns_r = Rot(P, 2, [128, 128], BF16, "ns")
    nt_r = Rot(P, 2, [128, 4, 128], BF16, "nt")
    o_r = Rot(P, 4, [128, 256], F32, "ot")
    scale = 0.125

    def qk(S, SB_, KT, KTB, c, rhsQ, extra):
        mm(P, S[:], KT[:, c * 128:(c + 1) * 128], rhsQ, True, extra is None, [KTB, QTB], [SB_])
        if extra is not None:
            lhsT, rhs, rd = extra
            mm(P, S[:], lhsT, rhs, False, True, rd, [SB_])

    for qb in range(NQB):
        rhsQ = QT[:, :, qb * 128:(qb + 1) * 128]
        ncc = qb // 16 + 1
        accC = [accC_r.next(), accC_r.next()]
        for cc in range(ncc):
            S, SB_ = s_r.next()
            extra = None
            if qb - 16 * cc <= 16:
                extra = (ident[:], CM[:, qb - 16 * cc, :], [IDB, CMB])
            qk(S, SB_, KC, KCB, cc, rhsQ, extra)
            p, PB = p_r.next()
            act(P, p[:], S[:], AF.Exp, [SB_], [PB], scale=scale)
            for h in range(4):
                a, AB = accC[h // 2]
                off = (h % 2) * 193
                mm(P, a[:, off:off + 193], p[:, h * 128:(h + 1) * 128], VC[:, cc, 0:193], cc == 0 and h % 2 == 0,
                   cc == ncc - 1, [PB, VCB], [AB], skip=True)
        ot, OT = o_r.next()
        imp, IMP = None, None
        for h in range(4):
            a, AB = accC[h // 2]
            off = (h % 2) * 193
            sm, SM = sm_r.next()
            ts(P, "dve", sm[:], a[:, off + 64:off + 65], 1e-30, None, ALU.max, None, [AB], [SM])
            rc, RC = sm_r.next()
            P.op("dve", lambda e, rc=rc, sm=sm: e.reciprocal(out=rc[:], in_=sm[:]), reads=[SM], writes=[RC])
            gf, GF = sm_r.next()
            tt(P, "dve", gf[:], rc[:], G[:, qb, h * 3:h * 3 + 1], ALU.mult, [RC, GB], [GF])
            ts(P, "dve", ot[:, h * 64:(h + 1) * 64], a[:, off:off + 64], gf[:, 0:1], None, ALU.mult, None, [AB, GF], [OT])
            ni, NI = imp_r.next()
            if imp is None:
                ts(P, "dve", ni[:], a[:, off + 65:off + 193], rc[:, 0:1], None, ALU.mult, None, [AB, RC], [NI])
            else:
                P.op("dve", lambda e, ni=ni, a=a, off=off, rc=rc, imp=imp: e.scalar_tensor_tensor(
                    out=ni[:], in0=a[:, off + 65:off + 193], scalar=rc[:, 0:1], in1=imp[:], op0=ALU.mult, op1=ALU.add),
                    reads=[AB, RC, IMP], writes=[NI])
            imp, IMP = ni, NI
        sc, SC = sc_r.next()
        tt(P, "dve", sc[:], imp[:], AT[:, 126 - 2 * qb:254 - 2 * qb], ALU.add, [IMP, ATB], [SC])
        ts(P, "dve", sc[:, 0:1], imp[:, 0:1], 1e4, None, ALU.add, None, [IMP, SC], [SC])
        m1, M1 = m8_r.next()
        P.op("dve", lambda e, m1=m1, sc=sc: e.max(out=m1[:], in_=sc[:]), reads=[SC], writes=[M1])
        sc2, SC2 = sc_r.next()
        P.op("dve", lambda e, sc2=sc2, m1=m1, sc=sc: e.match_replace(out=sc2[:], in_to_replace=m1[:], in_values=sc[:],
                                                                   imm_value=-1e9), reads=[SC, M1], writes=[SC2])
        m2, M2 = m8_r.next()
        P.op("dve", lambda e, m2=m2, sc2=sc2: e.max(out=m2[:], in_=sc2[:]), reads=[SC2], writes=[M2])
        ns, NS = ns_r.next()
        ts(P, "dve", ns[:], sc[:], m2[:, 7:8], -30000.0, ALU.is_lt, ALU.mult, [SC, M2], [NS])
        pst, PST = pst_r.next()
        tr(P, pst[:], ns[:], ident[:], [NS, IDB], [PST])
        nt, NTB = nt_r.next()
        for h in range(4):
            cp(P, "act" if h % 2 else "dve", nt[:, h, :], pst[:], [PST], [NTB])
        accS, ASB = accS_r.next()
        for c in range(qb + 1):
            S, SB_ = s_r.next()
            if c == qb:
                extra = (ident[:], CA[:], [IDB, CAB])
            else:
                extra = (E[:, c * 128:(c + 1) * 128], nt[:].rearrange("p h q -> p (h q)"), [EB, NTB])
            qk(S, SB_, KS, KSB, c, rhsQ, extra)
            p, PB = p_r.next()
            act(P, p[:], S[:], AF.Exp, [SB_], [PB], scale=scale)
            for h in range(4):
                mm(P, accS[:, h * 65:(h + 1) * 65], p[:, h * 128:(h + 1) * 128], VS[:, c, 0:65], c == 0 and h == 0,
                   c == qb, [PB, VSB], [ASB], skip=True)
        accW, AWB = accW_r.next()
        c0 = max(0, qb - 4)
        for c in range(c0, qb + 1):
            S, SB_ = s_r.next()
            extra = None
            if c == qb:
                extra = (ident[:], CA[:], [IDB, CAB])
            elif c == qb - 4:
                extra = (ident[:], WM[:], [IDB, WMB])
            qk(S, SB_, KW, KWB, c, rhsQ, extra)
            p, PB = p_r.next()
            act(P, p[:], S[:], AF.Exp, [SB_], [PB], scale=scale)
            for h in range(4):
                mm(P, accW[:, h * 65:(h + 1) * 65], p[:, h * 128:(h + 1) * 128], VW[:, c, 0:65], c == c0 and h == 0,
                   c == qb, [PB, VWB], [AWB], skip=True)
        cur, CUR = ot, OT
        for bi, (a, AB) in enumerate(((accS, ASB), (accW, AWB))):
            nxt, NXT = o_r.next()
            for h in range(4):
                sm, SM = sm_r.next()
                ts(P, "dve", sm[:], a[:, h * 65 + 64:h * 65 + 65], 1e-30, None, ALU.max, None, [AB], [SM])
                rc, RC = sm_r.next()
                P.op("dve", lambda e, rc=rc, sm=sm: e.reciprocal(out=rc[:], in_=sm[:]), reads=[SM], writes=[RC])
                gf, GF = sm_r.next()
                tt(P, "dve", gf[:], rc[:], G[:, qb, h * 3 + 1 + bi:h * 3 + 2 + bi], ALU.mult, [RC, GB], [GF])
                P.op("dve", lambda e, nxt=nxt, a=a, h=h, gf=gf, cur=cur: e.scalar_tensor_tensor(
                    out=nxt[:, h * 64:(h + 1) * 64], in0=a[:, h * 65:h * 65 + 64], scalar=gf[:, 0:1],
                    in1=cur[:, h * 64:(h + 1) * 64], op0=ALU.mult, op1=ALU.add), reads=[AB, GF, CUR], writes=[NXT])
            cur, CUR = nxt, NXT
        P.dma("sp", o_d[qb * 128:(qb + 1) * 128, :], cur[:], reads=[CUR])
    return P.finish()


_PROGS = {}


def _prog(name, fn):
    if name not in _PROGS:
        _PROGS[name] = fn()
    return _PROGS[name]


def _pl(g):
    return np.ascontiguousarray(np.asarray(g, np.float32).reshape(-1, 128).T)


def _bc(g):
    return np.ascontiguousarray(np.tile(np.asarray(g, np.float32)[None, :], (128, 1)))


def _run(nc, maps):
    res = run_bass_kernel_spmd(nc, maps, core_ids=list(range(8)))
    return res.results


def _split2(a):
    a = np.asarray(a, np.float64)
    hi = a.astype(np.float32).astype(NPBF)
    lo = (a - hi.astype(np.float64)).astype(np.float32).astype(NPBF)
    return hi, lo


def _nsa_c_consts():
    i = np.arange(128)[:, None]
    u = np.arange(TT)[None, :]
    E = (u // 64 == i).astype(np.float32).astype(NPBF)
    jl = np.arange(128)[:, None, None]
    m = np.arange(17)[None, :, None]
    ql = np.tile(np.arange(128), 4)[None, None, :]
    CM = np.where(16 * jl - ql <= 128 * m - 31, 0.0, -30000.0).astype(np.float32).astype(NPBF)
    kl = np.arange(128)[:, None]
    q4 = np.tile(np.arange(128), 4)[None, :]
    CA = np.where(kl <= q4, 0.0, -30000.0).astype(np.float32).astype(NPBF)
    WM = np.where(kl > q4, 0.0, -30000.0).astype(np.float32).astype(NPBF)
    qq = np.arange(128)[:, None]
    rel = np.arange(256)[None, :] - 126
    cur = (qq >= 64).astype(np.int64)
    AT = np.zeros((128, 256), np.float32)
    AT[(rel == cur) | (rel == cur - 1)] = 1e4
    AT[rel > cur] = -1.0
    jj = np.arange(512)[:, None]
    ii = np.arange(128)[None, :]
    Ov = ((jj >= 4 * ii - 1) & (jj <= 4 * ii + 3) & (jj <= 510)).astype(np.float32)
    return dict(E=np.ascontiguousarray(E), CM=np.ascontiguousarray(CM), CA=np.ascontiguousarray(CA),
                WM=np.ascontiguousarray(WM), AT=AT, Ov=Ov)


def _aug_k(pos):
    hi, lo = _split2(pos)
    one = np.ones_like(hi)
    return np.stack([one, one, hi, lo, hi, lo], 0)


def kernel(**inp):
    f = {k: np.asarray(v) for k, v in inp.items()}
    x = f["x"]
    B, T, D = x.shape
    xf = np.ascontiguousarray(x.reshape(B * T, D))
    ident = np.eye(128, dtype=np.float32).astype(NPBF)
    G4 = 4
    nc = _prog("nsa_a", build_nsa_a)
    maps = [{"x": xf[c * NT:(c + 1) * NT], "g": _pl(f["a_attn_norm"][0]), "w": np.ascontiguousarray(f["a_w_in"][0]),
             "gq": _bc(f["a_q_norm"][0]), "gs": _bc(f["a_kslc_norm"][0]), "gw": _bc(f["a_kwin_norm"][0]),
             "ident": ident} for c in range(8)]
    r = _run(nc, maps)
    big = np.concatenate([q["big"] for q in r], 0)
    gates = np.concatenate([q["gates"] for q in r], 0)
    nc = _prog("nsa_b", build_nsa_b)

    def a2T(col0, b):
        a = big[b * T:(b + 1) * T, col0:col0 + 64]
        o = np.zeros((128, T), NPBF)
        o[0:64] = a.T
        o[64:128, 0:T - 1] = a[1:].T
        return o

    def w1l(w):
        return np.ascontiguousarray(np.asarray(w, np.float32).reshape(16, 2, 64, 256).transpose(1, 2, 0, 3).reshape(128, 16, 256))

    def pel(p):
        return np.ascontiguousarray(np.asarray(p, np.float32).reshape(16, 2, 64).transpose(1, 2, 0).reshape(128, 16))

    maps = []
    for c in range(8):
        b, g = c // 4, c % 4
        maps.append({
            "k_a2T": a2T(1024 + g * 64, b), "v_a2T": a2T(1280 + g * 64, b),
            "k_w1": w1l(f["a_cmp_k_w1"][0]), "v_w1": w1l(f["a_cmp_v_w1"][0]),
            "k_pe": pel(f["a_cmp_pos_k"][0]), "v_pe": pel(f["a_cmp_pos_v"][0]),
            "k_b1": _pl(f["a_cmp_k_b1"][0]), "v_b1": _pl(f["a_cmp_v_b1"][0]),
            "k_w2": np.ascontiguousarray(f["a_cmp_k_w2"][0].reshape(2, 128, 64).transpose(1, 0, 2)),
            "v_w2": np.ascontiguousarray(f["a_cmp_v_w2"][0].reshape(2, 128, 64).transpose(1, 0, 2)),
            "gk": _bc(f["a_kcmp_norm"][0])})
    rB = _run(nc, maps)
    nc = _prog("nsa_c", build_nsa_c)
    C = _nsa_c_consts()
    tpos = np.arange(T, dtype=np.float64)
    kaug_tok = _aug_k(tpos)
    kaug_cmp = _aug_k(16.0 * np.arange(512) + 15.5)
    maps = []
    for c in range(8):
        b, g = c // 4, c % 4
        bb = big[b * T:(b + 1) * T]
        QT = np.zeros((70, 4, T), NPBF)
        for hg in range(4):
            h = g * 4 + hg
            QT[0:64, hg] = bb[:, h * 64:(h + 1) * 64].T
            beta = 8.0 * 2.0 ** (-8.0 * (h + 1) / 16.0)
            bh, bl = _split2(np.full(T, beta))
            th, tl = _split2(beta * tpos)
            QT[64:70, hg] = np.stack([-th.astype(np.float32), -tl.astype(np.float32), bh.astype(np.float32),
                                      bh.astype(np.float32), bl.astype(np.float32), bl.astype(np.float32)], 0).astype(NPBF)
        KS = np.concatenate([bb[:, 1536 + g * 64:1536 + (g + 1) * 64].T, kaug_tok], 0)
        KW = np.concatenate([bb[:, 2048 + g * 64:2048 + (g + 1) * 64].T, kaug_tok], 0)
        KC = np.concatenate([rB[c]["k_cmp"].T, kaug_cmp], 0)

        def vaug(v):
            o = np.zeros((128, 64, 66), NPBF)
            o[:, :, 0:64] = v.reshape(64, 128, 64).transpose(1, 0, 2)
            o[:, :, 64] = 1.0
            return o

        VC = np.zeros((128, 4, 194), NPBF)
        VC[:, :, 0:64] = rB[c]["v_cmp"].reshape(4, 128, 64).transpose(1, 0, 2)
        VC[:, :, 64] = 1.0
        VC[:, :, 65:193] = C["Ov"].reshape(4, 128, 128).transpose(1, 0, 2).astype(NPBF)
        Gt = np.ascontiguousarray(gates[b * T:(b + 1) * T, g * 12:(g + 1) * 12].reshape(64, 128, 12).transpose(1, 0, 2))
        maps.append({"QT": QT, "KS": np.ascontiguousarray(KS), "KW": np.ascontiguousarray(KW),
                     "KC": np.ascontiguousarray(KC), "VS": vaug(bb[:, 1792 + g * 64:1792 + (g + 1) * 64]),
                     "VW": vaug(bb[:, 2304 + g * 64:2304 + (g + 1) * 64]), "VC": VC, "G": Gt, "E": C["E"],
                     "CM": C["CM"], "CA": C["CA"], "WM": C["WM"], "AT": C["AT"], "ident": ident})
    rC = _run(nc, maps)
    o_nsa = np.zeros((B, T, 1024), np.float32)
    for c in range(8):
        o_nsa[c // 4, :, (c % 4) * 256:(c % 4 + 1) * 256] = rC[c]["o"]
    o_nsa = o_nsa.reshape(B * T, 1024)

    def outproj(o, xres, w):
        nc = _prog("outproj", build_outproj)
        r = _run(nc, [{"o": o[c * NT:(c + 1) * NT], "x": xres[c * NT:(c + 1) * NT], "w": np.ascontiguousarray(w),
                       "ident": ident} for c in range(8)])
        return np.concatenate([q["y"] for q in r], 0)

    def ffn(xin, g, wgu, wd):
        nc = _prog("ffn", build_ffn)
        r = _run(nc, [{"x": xin[c * NT:(c + 1) * NT], "g": _pl(g), "wgu": np.ascontiguousarray(wgu),
                       "wd": np.ascontiguousarray(wd), "ident": ident} for c in range(8)])
        return np.concatenate([q["y"] for q in r], 0)

    x1 = outproj(o_nsa, xf, f["a_w_out"][0])
    x2 = ffn(x1, f["ffn_norm"][0], f["ffn_w_gate_up"][0], f["ffn_w_down"][0])
    nc = _prog("kvq", build_kvq)
    inv = (10000.0 ** (-np.arange(0, 64, 2, dtype=np.float32) / 64)).astype(np.float32)
    maps = []
    for c in range(8):
        pos = (np.arange(NT) + (c % 4) * NT).astype(np.float32)
        ang = pos[:, None] * inv[None, :]
        cs = np.concatenate([np.cos(ang), np.sin(ang)], 1).astype(np.float32).reshape(NB, 128, 64).transpose(1, 0, 2)
        maps.append({"x": x2[c * NT:(c + 1) * NT], "g_kv": _pl(f["kv_norm"]), "g_q": _pl(f["b_attn_norm"][0]),
                     "g_c": _pl(f["kv_c_norm"]), "g_qa": _pl(f["b_q_a_norm"][0]), "g_kn": _bc(f["kv_k_norm"]),
                     "g_qn": _bc(f["b_q_norm"][0]), "w_kva": np.ascontiguousarray(f["kv_w_a"]),
                     "w_kvb": np.ascontiguousarray(f["kv_w_b"]), "w_qa": np.ascontiguousarray(f["b_w_q_a"][0]),
                     "w_qb": np.ascontiguousarray(f["b_w_q_b"][0]), "cs": np.ascontiguousarray(cs), "ident": ident})
    r = _run(nc, maps)
    kk = np.concatenate([q["k"] for q in r], 0)
    qq = np.concatenate([q["q"] for q in r], 0)
    vv = np.concatenate([q["v"] for q in r], 0)
    nc = _prog("mla", build_mla)
    r = _run(nc, [mla_host_inputs(kk, qq, vv, c // 4, c % 4) for c in range(8)])
    o_mla = np.zeros((B, T, 1024), np.float32)
    for c in range(8):
        o_mla[c // 4, :, (c % 4) * 256:(c % 4 + 1) * 256] = r[c]["o"]
    o_mla = o_mla.reshape(B * T, 1024)
    x3 = outproj(o_mla, x2, f["b_w_out"][0])
    x4 = ffn(x3, f["ffn_norm"][1], f["ffn_w_gate_up"][1], f["ffn_w_down"][1])
    return x4.reshape(B, T, D).astype(np.float32)
```

```python
import numpy as np
import ml_dtypes
import concourse.bass as bass
import concourse.mybir as mybir
from concourse.bass_utils import run_bass_kernel_spmd

F32 = mybir.dt.float32
BF16 = mybir.dt.bfloat16
AF = mybir.ActivationFunctionType
ALU = mybir.AluOpType
AX = mybir.AxisListType
NPBF = ml_dtypes.bfloat16

ENGS = ["sp", "act", "dve", "pool", "pe"]


class Buf:
    __slots__ = ("w", "r", "dsem", "dcnt", "name")

    def __init__(self, name=""):
        self.w = None
        self.r = []
        self.dsem = None
        self.dcnt = 0
        self.name = name


class Prog:
    def __init__(self):
        self.nc = bass.Bass("TRN2", target_bir_lowering=False)
        nc = self.nc
        self.q = {e: [] for e in ENGS}
        self.cnt = {e: 0 for e in ENGS}
        self.sems = {}
        self.sem = {}
        for e in ("act", "dve", "pool", "pe"):
            self.sem[e] = self.new_sem("eng_" + e)
        self.wm = {e: {} for e in ENGS}
        self.all_dma_tokens = {}
        self.nsb = 0
        self.nps = 0

    def new_sem(self, name):
        h = self.nc.alloc_semaphore(name)
        sid = len(self.sems)
        self.sems[sid] = h
        return sid

    def sb(self, shape, dtype, name=None):
        self.nsb += 1
        return self.nc.alloc_sbuf_tensor(f"S{self.nsb}_" + (name or ""), list(shape), dtype)

    def ps(self, shape, dtype=F32, name=None):
        self.nps += 1
        return self.nc.alloc_psum_tensor(f"P{self.nps}_" + (name or ""), list(shape), dtype)

    def dram(self, name, shape, dtype, kind):
        return self.nc.dram_tensor(name, list(shape), dtype, kind=kind)

    def _deps(self, eng, reads, writes):
        deps = {}

        def need(tok):
            if tok is None:
                return
            s, v = tok
            if eng == "pe" and s == self.sem["pe"]:
                return
            if deps.get(s, 0) < v:
                deps[s] = v

        for b in reads:
            need(b.w)
        for b in writes:
            need(b.w)
            for t in b.r:
                need(t)
        out = []
        wm = self.wm[eng]
        for s, v in deps.items():
            if wm.get(s, 0) < v:
                wm[s] = v
                out.append((s, v))
        return out

    def op(self, eng, fn, reads=(), writes=()):
        waits = self._deps(eng, reads, writes)
        self.cnt[eng] += 1
        tok = (self.sem[eng], self.cnt[eng])
        self.q[eng].append((waits, fn, (tok[0], 1)))
        for b in reads:
            b.r.append(tok)
        for b in writes:
            b.w = tok
            b.r = []
        return tok

    def dma(self, q, out, in_, reads=(), writes=(), **kw):
        waits = self._deps(q, reads, writes)
        tb = (list(writes) + list(reads))[0]
        if tb.dsem is None:
            tb.dsem = self.new_sem("d_" + tb.name + str(len(self.sems)))
        tb.dcnt += 1
        tok = (tb.dsem, 16 * tb.dcnt)
        self.all_dma_tokens[tb.dsem] = tok[1]
        self.q[q].append((waits, lambda e: e.dma_start(out=out, in_=in_, **kw), (tok[0], 16)))
        for b in reads:
            b.r.append(tok)
        for b in writes:
            b.w = tok
            b.r = []
        return tok

    def cc(self, kind, ins_ap, outs_ap, groups, reads=(), writes=()):
        waits = self._deps("pool", reads, writes)
        sem = self.new_sem("cc" + str(len(self.sems)))
        tok = (sem, 16)
        self.all_dma_tokens[sem] = 16
        self.q["pool"].append((waits, lambda e: e.collective_compute(
            kind, ALU.bypass, replica_groups=groups, ins=[ins_ap], outs=[outs_ap]), (sem, 16)))
        for b in reads:
            b.r.append(tok)
        for b in writes:
            b.w = tok
            b.r = []
        return tok

    def finish(self):
        nc = self.nc
        final = [(s, v) for s, v in self.all_dma_tokens.items()]
        for e in ("act", "dve", "pool", "pe"):
            if self.cnt[e] > 0:
                final.append((self.sem[e], self.cnt[e]))
        self.q["sp"].append((final, None, None))
        emap = {"sp": "sync", "act": "scalar", "dve": "vector", "pool": "gpsimd", "pe": "tensor"}
        sems = self.sems

        def replay(lst):
            def run(e):
                for waits, fn, inc in lst:
                    for s, v in waits:
                        e.wait_ge(sems[s], v)
                    if fn is not None:
                        ins = fn(e)
                        ins.then_inc(sems[inc[0]], inc[1])
            return run

        with nc.Block() as block:
            for k in ENGS:
                if self.q[k]:
                    getattr(block, emap[k])(replay(self.q[k]))
        return nc


def mm(P, out, lhsT, rhs, start, stop, reads, writes, skip=False):
    return P.op("pe", lambda e: e.matmul(out, lhsT=lhsT, rhs=rhs, start=start, stop=stop,
                                         skip_group_check=skip), reads=reads, writes=writes)


def tr(P, out, in_, ident, reads, writes):
    return P.op("pe", lambda e: e.transpose(out, in_, ident), reads=reads, writes=writes)


def act(P, out, in_, func, reads, writes, **kw):
    return P.op("act", lambda e: e.activation(out=out, in_=in_, func=func, **kw), reads=reads, writes=writes)


def tt(P, eng, out, in0, in1, op, reads, writes):
    return P.op(eng, lambda e: e.tensor_tensor(out=out, in0=in0, in1=in1, op=op), reads=reads, writes=writes)


def ts(P, eng, out, in0, s1, s2, op0, op1, reads, writes):
    if op1 is None:
        return P.op(eng, lambda e: e.tensor_scalar(out=out, in0=in0, scalar1=s1, scalar2=None, op0=op0),
                    reads=reads, writes=writes)
    return P.op(eng, lambda e: e.tensor_scalar(out=out, in0=in0, scalar1=s1, scalar2=s2, op0=op0, op1=op1),
                reads=reads, writes=writes)


def cp(P, eng, out, in_, reads, writes):
    if eng == "act":
        return P.op("act", lambda e: e.activation(out=out, in_=in_, func=AF.Copy), reads=reads, writes=writes)
    return P.op(eng, lambda e: e.tensor_copy(out=out, in_=in_), reads=reads, writes=writes)


def memset(P, eng, ap, val, writes):
    return P.op(eng, lambda e: e.memset(ap, val), writes=writes)


class Rot:
    def __init__(self, P, n, shape, dtype, name, psum=False):
        self.items = []
        for i in range(n):
            t = P.ps(shape, dtype, f"{name}{i}") if psum else P.sb(shape, dtype, f"{name}{i}")
            self.items.append((t, Buf(f"{name}{i}")))
        self.i = 0

    def next(self):
        it = self.items[self.i % len(self.items)]
        self.i += 1
        return it


def load_const(P, dram_ap, shape, dtype, name, q="sp"):
    t = P.sb(shape, dtype, name)
    b = Buf(name)
    P.dma(q, t[:], dram_ap, writes=[b])
    return t, b


def load_weight(P, w_ap, K, N, name, stg, gain=None):
    KC = K // 128
    wt = P.sb([128, KC, N], BF16, name)
    wb = Buf(name)
    i = 0
    for kc in range(KC):
        for c0 in range(0, N, 2048):
            c1 = min(N, c0 + 2048)
            st, sbuf = stg.next()
            P.dma("sp", st[:, 0:c1 - c0], w_ap[kc * 128:(kc + 1) * 128, c0:c1], writes=[sbuf])
            eng = "dve" if i % 2 == 0 else "act"
            i += 1
            if gain is None:
                cp(P, eng, wt[:, kc, c0:c1], st[:, 0:c1 - c0], [sbuf], [wb])
            else:
                gt, gb = gain
                if eng == "dve":
                    ts(P, "dve", wt[:, kc, c0:c1], st[:, 0:c1 - c0], gt[:, kc:kc + 1], None, ALU.mult, None,
                       [sbuf, gb], [wb])
                else:
                    act(P, wt[:, kc, c0:c1], st[:, 0:c1 - c0], AF.Copy, [sbuf, gb], [wb], scale=gt[:, kc:kc + 1])
    return wt, wb


def rms_rstd(P, x_ap, D, ss, scratch, reads, SS, SCR, eps=1e-6):
    act(P, scratch, x_ap, AF.Square, reads, [SCR, SS], accum_out=ss)
    act(P, ss, ss, AF.Ln, [SS], [SS], scale=1.0 / D, bias=eps)
    act(P, ss, ss, AF.Exp, [SS], [SS], scale=-0.5)


def make_ident(P, ident_dram):
    return load_const(P, ident_dram, [128, 128], BF16, "ident_sb")


NT = 2048
NB = NT // 128


def build_outproj():
    P = Prog()
    o_d = P.dram("o", [NT, 1024], F32, "ExternalInput").ap()
    x_d = P.dram("x", [NT, 1024], F32, "ExternalInput").ap()
    w_d = P.dram("w", [1024, 1024], F32, "ExternalInput").ap()
    id_d = P.dram("ident", [128, 128], BF16, "ExternalInput").ap()
    y_d = P.dram("y", [NT, 1024], F32, "ExternalOutput").ap()
    ident, IDB = make_ident(P, id_d)
    stg = Rot(P, 2, [128, 2048], F32, "stg")
    w, WB = load_weight(P, w_d, 1024, 1024, "w", stg)
    oin = Rot(P, 2, [128, 1024], F32, "oin")
    xin = Rot(P, 2, [128, 1024], F32, "xin")
    obf = Rot(P, 2, [128, 1024], BF16, "obf")
    oT = Rot(P, 2, [128, 8, 128], BF16, "oT")
    pT = Rot(P, 2, [128, 8, 128], BF16, "pT", psum=True)
    acc = Rot(P, 4, [128, 512], F32, "acc", psum=True)
    yo = Rot(P, 2, [128, 1024], F32, "yo")
    for b in range(NB):
        rows = slice(b * 128, (b + 1) * 128)
        ot, OB = oin.next()
        xt, XB = xin.next()
        P.dma("sp", ot[:], o_d[rows, :], writes=[OB])
        P.dma("sp", xt[:], x_d[rows, :], writes=[XB])
        bt, BB = obf.next()
        cp(P, "dve", bt[:], ot[:], [OB], [BB])
        pt, PB = pT.next()
        for c in range(8):
            tr(P, pt[:, c, :], bt[:, c * 128:(c + 1) * 128], ident[:], [BB, IDB], [PB])
        tt_, TB = oT.next()
        cp(P, "act", tt_[:], pt[:], [PB], [TB])
        yt, YB = yo.next()
        for half in range(2):
            at, AB = acc.next()
            for c in range(8):
                mm(P, at[:], tt_[:, c, :], w[:, c, half * 512:(half + 1) * 512], c == 0, c == 7, [TB, WB], [AB])
            tt(P, "dve", yt[:, half * 512:(half + 1) * 512], at[:], xt[:, half * 512:(half + 1) * 512], ALU.add,
               [AB, XB], [YB])
        P.dma("pool", y_d[rows, :], yt[:], reads=[YB])
    return P.finish()


FH = 2816


def build_ffn():
    P = Prog()
    x_d = P.dram("x", [NT, 1024], F32, "ExternalInput").ap()
    g_d = P.dram("g", [128, 8], F32, "ExternalInput").ap()
    wgu_d = P.dram("wgu", [1024, 2 * FH], F32, "ExternalInput").ap()
    wd_d = P.dram("wd", [FH, 1024], F32, "ExternalInput").ap()
    id_d = P.dram("ident", [128, 128], BF16, "ExternalInput").ap()
    y_d = P.dram("y", [NT, 1024], F32, "ExternalOutput").ap()
    ident, IDB = make_ident(P, id_d)
    gain = load_const(P, g_d, [128, 8], F32, "gain")
    stg = Rot(P, 2, [128, 2048], F32, "stg")
    wgu, WGU = load_weight(P, wgu_d, 1024, 2 * FH, "wgu", stg, gain=gain)
    wd, WD = load_weight(P, wd_d, FH, 1024, "wd", stg)
    xin = Rot(P, 4, [128, 1024], F32, "xin")
    scr = P.sb([128, 1024], F32, "scr"); SCR = Buf("scr")
    ssr = Rot(P, 2, [128, 1], F32, "ss")
    xn = Rot(P, 2, [128, 1024], BF16, "xn")
    pT = Rot(P, 1, [128, 8, 128], BF16, "pT", psum=True)
    xT = Rot(P, 2, [128, 8, 256], BF16, "xT")
    gu = Rot(P, 3, [128, 512], F32, "gu", psum=True)
    accs = Rot(P, 4, [128, 512], F32, "acc", psum=True)
    sg = Rot(P, 2, [128, 256], BF16, "sg")
    aT = Rot(P, 2, [128, 256], BF16, "aT")
    yo = Rot(P, 2, [128, 1024], F32, "yo")
    NJ = FH // 128
    for sbk in range(NT // 256):
        xs = []
        xTt, XT = xT.next()
        for t in range(2):
            rows = slice(sbk * 256 + t * 128, sbk * 256 + (t + 1) * 128)
            xt, XB = xin.next()
            xs.append((xt, XB, rows))
            P.dma("sp", xt[:], x_d[rows, :], writes=[XB])
            ss, SS = ssr.next()
            rms_rstd(P, x_ap=xt[:], D=1024, ss=ss[:], scratch=scr[:], reads=[XB], SS=SS, SCR=SCR)
            xnt, XN = xn.next()
            ts(P, "dve", xnt[:], xt[:], ss[:, 0:1], None, ALU.mult, None, [XB, SS], [XN])
            pt, PB = pT.next()
            for c in range(8):
                tr(P, pt[:, c, :], xnt[:, c * 128:(c + 1) * 128], ident[:], [XN, IDB], [PB])
            cp(P, "dve", xTt[:, :, t * 128:(t + 1) * 128], pt[:], [PB], [XT])
        acc = [[accs.next() for _ in range(2)] for _ in range(2)]
        for j in range(NJ):
            gt, GB = gu.next()
            ut, UB = gu.next()
            for c in range(8):
                mm(P, gt[:, 0:256], wgu[:, c, j * 128:(j + 1) * 128], xTt[:, c, :], c == 0, c == 7, [WGU, XT], [GB])
            for c in range(8):
                mm(P, ut[:, 0:256], wgu[:, c, FH + j * 128:FH + (j + 1) * 128], xTt[:, c, :], c == 0, c == 7,
                   [WGU, XT], [UB])
            sgt, SG = sg.next()
            act(P, sgt[:], gt[:, 0:256], AF.Silu, [GB], [SG])
            at, AB = aT.next()
            tt(P, "dve", at[:], sgt[:], ut[:, 0:256], ALU.mult, [SG, UB], [AB])
            for t in range(2):
                for half in range(2):
                    a_t, A_B = acc[t][half]
                    mm(P, a_t[:], at[:, t * 128:(t + 1) * 128], wd[:, j, half * 512:(half + 1) * 512], j == 0,
                       j == NJ - 1, [AB, WD], [A_B])
        for t in range(2):
            xt, XB, rows = xs[t]
            yt, YB = yo.next()
            for half in range(2):
                a_t, A_B = acc[t][half]
                tt(P, "dve", yt[:, half * 512:(half + 1) * 512], a_t[:], xt[:, half * 512:(half + 1) * 512], ALU.add,
                   [A_B, XB], [YB])
            P.dma("pool", y_d[rows, :], yt[:], reads=[YB])
    return P.finish()


def rope_ops(P, src, SRC, dst, DST, cosb, sinb, CS, tmp, TMP, nh):
    x1 = src[:, :, 0:32]
    x2 = src[:, :, 32:64]
    t1, t2 = tmp[:, 0, 0:nh, :], tmp[:, 1, 0:nh, :]
    tt(P, "dve", t1, x1, cosb, ALU.mult, [SRC, CS], [TMP])
    tt(P, "dve", t2, x2, sinb, ALU.mult, [SRC, CS], [TMP])
    tt(P, "dve", dst[:, :, 0:32], t1, t2, ALU.subtract, [TMP], [DST])
    tt(P, "dve", t1, x1, sinb, ALU.mult, [SRC, CS], [TMP])
    tt(P, "dve", t2, x2, cosb, ALU.mult, [SRC, CS], [TMP])
    tt(P, "dve", dst[:, :, 32:64], t1, t2, ALU.add, [TMP], [DST])


def build_kvq(dbg=0):
    P = Prog()
    x_d = P.dram("x", [NT, 1024], F32, "ExternalInput").ap()
    gkv_d = P.dram("g_kv", [128, 8], F32, "ExternalInput").ap()
    gq_d = P.dram("g_q", [128, 8], F32, "ExternalInput").ap()
    gc_d = P.dram("g_c", [128, 2], F32, "ExternalInput").ap()
    gqa_d = P.dram("g_qa", [128, 3], F32, "ExternalInput").ap()
    gkn_d = P.dram("g_kn", [128, 192], F32, "ExternalInput").ap()
    gqn_d = P.dram("g_qn", [128, 192], F32, "ExternalInput").ap()
    wkva_d = P.dram("w_kva", [1024, 320], F32, "ExternalInput").ap()
    wkvb_d = P.dram("w_kvb", [256, 2048], F32, "ExternalInput").ap()
    wqa_d = P.dram("w_qa", [1024, 384], F32, "ExternalInput").ap()
    wqb_d = P.dram("w_qb", [384, 1536], F32, "ExternalInput").ap()
    cs_d = P.dram("cs", [128, NB, 64], F32, "ExternalInput").ap()
    id_d = P.dram("ident", [128, 128], BF16, "ExternalInput").ap()
    kT_d = P.dram("k", [NT, 1536], BF16, "ExternalOutput").ap()
    qT_d = P.dram("q", [NT, 1536], BF16, "ExternalOutput").ap()
    v_d = P.dram("v", [NT, 1024], BF16, "ExternalOutput").ap()
    ident, IDB = make_ident(P, id_d)
    gkv = load_const(P, gkv_d, [128, 8], F32, "gkv")
    gq = load_const(P, gq_d, [128, 8], F32, "gq")
    gc = load_const(P, gc_d, [128, 2], F32, "gc")
    gqa = load_const(P, gqa_d, [128, 3], F32, "gqa")
    gkn, GKN = load_const(P, gkn_d, [128, 192], F32, "gkn")
    gqn, GQN = load_const(P, gqn_d, [128, 192], F32, "gqn")
    cs, CS = load_const(P, cs_d, [128, NB, 64], F32, "cs")
    stg = Rot(P, 2, [128, 2048], F32, "stg")
    wkva, WKVA = load_weight(P, wkva_d, 1024, 320, "wkva", stg, gain=gkv)
    wkvb, WKVB = load_weight(P, wkvb_d, 256, 2048, "wkvb", stg, gain=gc)
    wqa, WQA = load_weight(P, wqa_d, 1024, 384, "wqa", stg, gain=gq)
    wqb, WQB = load_weight(P, wqb_d, 384, 1536, "wqb", stg, gain=gqa)
    xin = Rot(P, 2, [128, 1024], F32, "xin")
    scr = P.sb([128, 1024], F32, "scr"); SCR = Buf("scr")
    ssr = Rot(P, 4, [128, 1], F32, "ss")
    ss8 = Rot(P, 4, [128, 8], F32, "ss8")
    xn = Rot(P, 2, [128, 1024], BF16, "xn")
    pT = Rot(P, 2, [128, 8, 128], BF16, "pT", psum=True)
    xT = Rot(P, 2, [128, 8, 128], BF16, "xT")
    pf = Rot(P, 6, [128, 512], F32, "pf", psum=True)
    cn = Rot(P, 2, [128, 384], BF16, "cn")
    cT = Rot(P, 2, [128, 3, 128], BF16, "cT")
    kpe = Rot(P, 2, [128, 64], F32, "kpe")
    tmpf = Rot(P, 3, [128, 8, 192], F32, "tmpf")
    pe8 = Rot(P, 2, [128, 8, 64], F32, "pe8")
    rtmp = P.sb([128, 2, 8, 32], F32, "rtmp"); RTMP = Buf("rtmp")
    kt = Rot(P, 2, [128, 8, 192], BF16, "kt")
    vt = Rot(P, 2, [128, 8, 128], BF16, "vt")
    kTs = Rot(P, 2, [128, 2, 8, 128], BF16, "kTs")
    for b in range(NB):
        rows = slice(b * 128, (b + 1) * 128)
        cosb = cs[:, b, 0:32].unsqueeze(1).to_broadcast([128, 8, 32])
        sinb = cs[:, b, 32:64].unsqueeze(1).to_broadcast([128, 8, 32])
        xt, XB = xin.next()
        P.dma("sp", xt[:], x_d[rows, :], writes=[XB])
        ss, SS = ssr.next()
        rms_rstd(P, xt[:], 1024, ss[:], scr[:], [XB], SS, SCR)
        xnt, XN = xn.next()
        ts(P, "dve", xnt[:], xt[:], ss[:, 0:1], None, ALU.mult, None, [XB, SS], [XN])
        pt, PB = pT.next()
        for c in range(8):
            tr(P, pt[:, c, :], xnt[:, c * 128:(c + 1) * 128], ident[:], [XN, IDB], [PB])
        xTt, XT = xT.next()
        cp(P, "dve", xTt[:], pt[:], [PB], [XT])
        kva, KVA = pf.next()
        qa, QA = pf.next()
        for c in range(8):
            mm(P, kva[:, 0:320], xTt[:, c, :], wkva[:, c, :], c == 0, c == 7, [XT, WKVA], [KVA])
        for c in range(8):
            mm(P, qa[:, 0:384], xTt[:, c, :], wqa[:, c, :], c == 0, c == 7, [XT, WQA], [QA])
        ss2, SS2 = ssr.next()
        rms_rstd(P, kva[:, 0:256], 256, ss2[:], scr[:, 0:256], [KVA], SS2, SCR)
        cnt, CN = cn.next()
        ts(P, "dve", cnt[:, 0:256], kva[:, 0:256], ss2[:, 0:1], None, ALU.mult, None, [KVA, SS2], [CN])
        kp, KP = kpe.next()
        tt(P, "dve", kp[:], kva[:, 256:320], gkn[:, 128:192], ALU.mult, [KVA, GKN], [KP])
        sspe, SSPE = ssr.next()
        act(P, scr[:, 0:64], kva[:, 256:320], AF.Square, [KVA], [SCR, SSPE], accum_out=sspe[:])
        pt2, PB2 = pT.next()
        for c in range(2):
            tr(P, pt2[:, c, :], cnt[:, c * 128:(c + 1) * 128], ident[:], [CN, IDB], [PB2])
        cTt, CT = cT.next()
        cp(P, "act", cTt[:, 0:2, :], pt2[:, 0:2, :], [PB2], [CT])
        kvb = []
        for n in range(4):
            kvp, KVP = pf.next()
            kvb.append((kvp, KVP))
            for c in range(2):
                mm(P, kvp[:], cTt[:, c, :], wkvb[:, c, n * 512:(n + 1) * 512], c == 0, c == 1, [CT, WKVB], [KVP])
        s8, S8 = ss8.next()
        for h in range(8):
            kvp, KVP = kvb[h // 2]
            act(P, scr[:, 0:128], kvp[:, (h % 2) * 256:(h % 2) * 256 + 128], AF.Square, [KVP], [SCR, S8],
                accum_out=s8[:, h:h + 1])
        ts(P, "dve", s8[:], s8[:], sspe[:, 0:1], None, ALU.add, None, [S8, SSPE], [S8])
        act(P, s8[:], s8[:], AF.Ln, [S8], [S8], scale=1.0 / 192, bias=1e-6)
        act(P, s8[:], s8[:], AF.Exp, [S8], [S8], scale=-0.5)
        tf, TF = tmpf.next()
        ktt, KT = kt.next()
        vtt, VT = vt.next()
        for n in range(4):
            kvp, KVP = kvb[n]
            kv3 = kvp[:].rearrange("p (h c) -> p h c", h=2)
            tt(P, "dve", tf[:, 2 * n:2 * n + 2, 0:128], kv3[:, :, 0:128],
               gkn[:, 0:128].unsqueeze(1).to_broadcast([128, 2, 128]), ALU.mult, [KVP, GKN], [TF])
            cp(P, "act", vtt[:, 2 * n:2 * n + 2, :], kv3[:, :, 128:256], [KVP], [VT])
        tt(P, "dve", ktt[:, :, 0:128], tf[:, :, 0:128], s8[:, :].unsqueeze(2).to_broadcast([128, 8, 128]), ALU.mult,
           [TF, S8], [KT])
        p8, P8 = pe8.next()
        tt(P, "dve", p8[:], kp[:, :].unsqueeze(1).to_broadcast([128, 8, 64]),
           s8[:, :].unsqueeze(2).to_broadcast([128, 8, 64]), ALU.mult, [KP, S8], [P8])
        rope_ops(P, p8[:], P8, ktt[:, :, 128:192], KT, cosb, sinb, CS, rtmp, RTMP, 8)
        P.dma("pool", v_d[rows, :], vtt[:].rearrange("p h d -> p (h d)"), reads=[VT])

        def emit_T(src, SRC, dst_d):
            P.dma("pool", dst_d[rows, :], src[:].rearrange("p h d -> p (h d)"), reads=[SRC])

        if dbg != 1:
            emit_T(ktt, KT, kT_d)
        if dbg in (1, 2):
            continue
        ss3, SS3 = ssr.next()
        rms_rstd(P, qa[:, 0:384], 384, ss3[:], scr[:, 0:384], [QA], SS3, SCR)
        qn, QN = cn.next()
        ts(P, "dve", qn[:], qa[:, 0:384], ss3[:, 0:1], None, ALU.mult, None, [QA, SS3], [QN])
        pt3, PB3 = pT.next()
        for c in range(3):
            tr(P, pt3[:, c, :], qn[:, c * 128:(c + 1) * 128], ident[:], [QN, IDB], [PB3])
        qT_, QT = cT.next()
        cp(P, "act", qT_[:], pt3[:, 0:3, :], [PB3], [QT])
        qb = []
        for n in range(4):
            qp, QP = pf.next()
            qb.append((qp, QP))
            for c in range(3):
                mm(P, qp[:, 0:384], qT_[:, c, :], wqb[:, c, n * 384:(n + 1) * 384], c == 0, c == 2, [QT, WQB], [QP])
        s8q, S8Q = ss8.next()
        for h in range(8):
            qp, QP = qb[h // 2]
            act(P, scr[:, 0:192], qp[:, (h % 2) * 192:(h % 2) * 192 + 192], AF.Square, [QP], [SCR, S8Q],
                accum_out=s8q[:, h:h + 1])
        act(P, s8q[:], s8q[:], AF.Ln, [S8Q], [S8Q], scale=1.0 / 192, bias=1e-6)
        act(P, s8q[:], s8q[:], AF.Exp, [S8Q], [S8Q], scale=-0.5)
        if dbg == 3:
            continue
        tq, TQ = tmpf.next()
        for n in range(4):
            qp, QP = qb[n]
            cp(P, "act", tq[:, 2 * n:2 * n + 2, :].rearrange("p h c -> p (h c)"), qp[:, 0:384], [QP], [TQ])
        tq2, TQ2 = tmpf.next()
        tt(P, "dve", tq2[:], tq[:], gqn[:, :].unsqueeze(1).to_broadcast([128, 8, 192]), ALU.mult, [TQ, GQN], [TQ2])
        tq, TQ = tq2, TQ2
        if dbg == 4:
            continue
        qtt, QTT = kt.next()
        tt(P, "dve", qtt[:, :, 0:128], tq[:, :, 0:128], s8q[:, :].unsqueeze(2).to_broadcast([128, 8, 128]), ALU.mult,
           [TQ, S8Q], [QTT])
        p8q, P8Q = pe8.next()
        tt(P, "dve", p8q[:], tq[:, :, 128:192], s8q[:, :].unsqueeze(2).to_broadcast([128, 8, 64]), ALU.mult,
           [TQ, S8Q], [P8Q])
        rope_ops(P, p8q[:], P8Q, qtt[:, :, 128:192], QTT, cosb, sinb, CS, rtmp, RTMP, 8)
        emit_T(qtt, QTT, qT_d)
    return P.finish()


def run_pipeline(P, steps, s_r, p_r, scale):
    n = len(steps)
    Sb = [None] * n

    def do_qk(i):
        S, SB_ = s_r.next()
        Sb[i] = (S, SB_)
        steps[i][0](S, SB_)

    do_qk(0)
    for i in range(n):
        if i + 1 < n:
            do_qk(i + 1)
        S, SB_ = Sb[i]
        p, PB = p_r.next()
        act(P, p[:], S[:], AF.Exp, [SB_], [PB], scale=scale)
        steps[i][1](p, PB)
        if steps[i][2] is not None:
            steps[i][2]()


TT = 8192


def build_mla():
    P = Prog()
    kTa_d = P.dram("kTa", [128, 2, TT], BF16, "ExternalInput").ap()
    kTb_d = P.dram("kTb", [64, 2, TT], BF16, "ExternalInput").ap()
    qTa_d = P.dram("qTa", [128, 2, TT], BF16, "ExternalInput").ap()
    qTb_d = P.dram("qTb", [64, 2, TT], BF16, "ExternalInput").ap()
    v_d = P.dram("vaug", [128, 64, 2, 132], BF16, "ExternalInput").ap()
    m_d = P.dram("masks", [128, 4, 512], BF16, "ExternalInput").ap()
    o_d = P.dram("o", [TT, 256], F32, "ExternalOutput").ap()
    kTa = P.sb([128, 2, TT], BF16, "kTa"); KA = [Buf("ka0"), Buf("ka1")]
    kTb = P.sb([64, 2, TT], BF16, "kTb"); KB = [Buf("kb0"), Buf("kb1")]
    va = P.sb([128, 64, 2, 132], BF16, "va"); VA = Buf("va")
    for h in range(2):
        P.dma("sp", kTa[:, h, :], kTa_d[:, h, :], writes=[KA[h]])
        P.dma("sp", kTb[:, h, :], kTb_d[:, h, :], writes=[KB[h]])
    for c0 in range(0, 64, 16):
        P.dma("sp", va[:, c0:c0 + 16], v_d[:, c0:c0 + 16], writes=[VA])
    mk, MK = load_const(P, m_d, [128, 4, 512], BF16, "mk")
    qa_r = Rot(P, 2, [128, 512], BF16, "qa")
    qb_r = Rot(P, 2, [64, 512], BF16, "qb")
    s_r = Rot(P, 3, [128, 512], F32, "S", psum=True)
    acc_r = Rot(P, 4, [128, 512], F32, "acc", psum=True)
    p_r = Rot(P, 4, [128, 512], BF16, "p")
    pm_r = Rot(P, 3, [128, 512], BF16, "pm")
    rs_r = Rot(P, 4, [128, 1], F32, "rs")
    o_r = Rot(P, 2, [128, 4, 128], F32, "osb")
    scale = 192.0 ** -0.5
    steps = []
    for h in range(2):
        for qs in range(TT // 512):
            ctx = {}

            def mk_qk(h=h, qs=qs, ctx=ctx, kc=0):
                def f(S, SB_):
                    if kc == 0:
                        qa, QA = qa_r.next()
                        qb, QB = qb_r.next()
                        P.dma("sp", qa[:], qTa_d[:, h, qs * 512:(qs + 1) * 512], writes=[QA])
                        P.dma("sp", qb[:], qTb_d[:, h, qs * 512:(qs + 1) * 512], writes=[QB])
                        ctx["q"] = (qa, QA, qb, QB)
                        ctx["accs"] = [acc_r.next() for _ in range(4)]
                    qa, QA, qb, QB = ctx["q"]
                    mm(P, S[:], kTa[:, h, kc * 128:(kc + 1) * 128], qa[:], True, False, [KA[h], QA], [SB_])
                    mm(P, S[:], kTb[:, h, kc * 128:(kc + 1) * 128], qb[:], False, True, [KB[h], QB], [SB_])
                return f

            def mk_pv(h=h, qs=qs, ctx=ctx, kc=0):
                def f(p, PB):
                    j = kc - 4 * qs
                    if j >= 0:
                        pm, PM = pm_r.next()
                        tt(P, "dve", pm[:], p[:], mk[:, j, :], ALU.mult, [PB, MK], [PM])
                        p, PB = pm, PM
                    for t in range(4):
                        if j >= 0 and t < j:
                            continue
                        a, AB = ctx["accs"][t]
                        mm(P, a[:, 0:129], p[:, t * 128:(t + 1) * 128], va[:, kc, h, 0:129], kc == 0,
                           kc == 4 * qs + t, [PB, VA], [AB])
                return f

            def mk_after(h=h, qs=qs, ctx=ctx):
                def f():
                    osb, OB = o_r.next()
                    for t in range(4):
                        a, AB = ctx["accs"][t]
                        rs, RS = rs_r.next()
                        P.op("dve", lambda e, rs=rs, a=a: e.reciprocal(out=rs[:], in_=a[:, 128:129]), reads=[AB],
                             writes=[RS])
                        ts(P, "dve", osb[:, t, :], a[:, 0:128], rs[:, 0:1], None, ALU.mult, None, [AB, RS], [OB])
                    P.dma("sp", o_d[qs * 512:(qs + 1) * 512, h * 128:(h + 1) * 128].rearrange("(t p) d -> p t d", p=128),
                          osb[:], reads=[OB])
                return f

            nk = 4 * qs + 4
            for kc in range(nk):
                steps.append((mk_qk(kc=kc), mk_pv(kc=kc), mk_after() if kc == nk - 1 else None))
    run_pipeline(P, steps, s_r, p_r, scale)
    return P.finish()


def mla_host_inputs(k, q, v, b, hp):
    ks = k[b * TT:(b + 1) * TT].reshape(TT, 8, 192)[:, 2 * hp:2 * hp + 2, :]
    qs = q[b * TT:(b + 1) * TT].reshape(TT, 8, 192)[:, 2 * hp:2 * hp + 2, :]
    vs = v[b * TT:(b + 1) * TT].reshape(TT, 8, 128)[:, 2 * hp:2 * hp + 2, :]
    kT = np.ascontiguousarray(ks.transpose(2, 1, 0))
    qT = np.ascontiguousarray(qs.transpose(2, 1, 0))
    vaug = np.zeros((128, 64, 2, 132), dtype=NPBF)
    vaug[:, :, :, 0:128] = vs.reshape(64, 128, 2, 128).transpose(1, 0, 2, 3)
    vaug[:, :, :, 128] = 1.0
    kl = np.arange(128)[:, None, None]
    j = np.arange(4)[None, :, None]
    ql = np.arange(512)[None, None, :]
    masks = (j * 128 + kl <= ql).astype(np.float32).astype(NPBF)
    return {"kTa": np.ascontiguousarray(kT[0:128]), "kTb": np.ascontiguousarray(kT[128:192]),
            "qTa": np.ascontiguousarray(qT[0:128]), "qTb": np.ascontiguousarray(qT[128:192]),
            "vaug": vaug, "masks": np.ascontiguousarray(masks)}


NW = 2608


def build_nsa_a():
    P = Prog()
    x_d = P.dram("x", [NT, 1024], F32, "ExternalInput").ap()
    g_d = P.dram("g", [128, 8], F32, "ExternalInput").ap()
    w_d = P.dram("w", [1024, NW], F32, "ExternalInput").ap()
    gq_d = P.dram("gq", [128, 64], F32, "ExternalInput").ap()
    gs_d = P.dram("gs", [128, 64], F32, "ExternalInput").ap()
    gw_d = P.dram("gw", [128, 64], F32, "ExternalInput").ap()
    id_d = P.dram("ident", [128, 128], BF16, "ExternalInput").ap()
    big_d = P.dram("big", [NT, 2560], BF16, "ExternalOutput").ap()
    gates_d = P.dram("gates", [NT, 48], F32, "ExternalOutput").ap()
    ident, IDB = make_ident(P, id_d)
    gain = load_const(P, g_d, [128, 8], F32, "gain")
    gq, GQ = load_const(P, gq_d, [128, 64], F32, "gq")
    gs, GS = load_const(P, gs_d, [128, 64], F32, "gs")
    gw, GW = load_const(P, gw_d, [128, 64], F32, "gw")
    stg = Rot(P, 2, [128, 2048], F32, "stg")
    w, WB = load_weight(P, w_d, 1024, NW, "w", stg, gain=gain)
    xin = Rot(P, 2, [128, 1024], F32, "xin")
    scr = P.sb([128, 1536], F32, "scr"); SCR = Buf("scr")
    ssr = Rot(P, 2, [128, 1], F32, "ss")
    xn = Rot(P, 2, [128, 1024], BF16, "xn")
    pT = Rot(P, 1, [128, 8, 128], BF16, "pT", psum=True)
    xT = Rot(P, 2, [128, 8, 128], BF16, "xT")
    pf = Rot(P, 6, [128, 512], F32, "pf", psum=True)
    y_r = Rot(P, 2, [128, NW], F32, "y")
    ssn_r = Rot(P, 2, [128, 24], F32, "ssn")
    tmp_r = Rot(P, 2, [128, 1024], F32, "tmp")
    big_r = Rot(P, 2, [128, 2560], BF16, "bigs")
    gt_r = Rot(P, 2, [128, 48], F32, "gts")
    for b in range(NB):
        rows = slice(b * 128, (b + 1) * 128)
        xt, XB = xin.next()
        P.dma("sp", xt[:], x_d[rows, :], writes=[XB])
        ss, SS = ssr.next()
        rms_rstd(P, xt[:], 1024, ss[:], scr[:, 0:1024], [XB], SS, SCR)
        xnt, XN = xn.next()
        ts(P, "dve", xnt[:], xt[:], ss[:, 0:1], None, ALU.mult, None, [XB, SS], [XN])
        pt, PB = pT.next()
        for c in range(8):
            tr(P, pt[:, c, :], xnt[:, c * 128:(c + 1) * 128], ident[:], [XN, IDB], [PB])
        xTt, XT = xT.next()
        cp(P, "dve", xTt[:], pt[:], [PB], [XT])
        y, YB = y_r.next()
        for n in range(6):
            c0, c1 = n * 512, min(NW, (n + 1) * 512)
            pp, PP = pf.next()
            for c in range(8):
                mm(P, pp[:, 0:c1 - c0], xTt[:, c, :], w[:, c, c0:c1], c == 0, c == 7, [XT, WB], [PP])
            cp(P, "act", y[:, c0:c1], pp[:, 0:c1 - c0], [PP], [YB])
        act(P, scr[:, 0:1024], y[:, 0:1024], AF.Square, [YB], [SCR])
        act(P, scr[:, 1024:1280], y[:, 1536:1792], AF.Square, [YB], [SCR])
        act(P, scr[:, 1280:1536], y[:, 2048:2304], AF.Square, [YB], [SCR])
        ssn, SSN = ssn_r.next()
        P.op("dve", lambda e, ssn=ssn: e.tensor_reduce(out=ssn[:], in_=scr[:, 0:1536].rearrange("p (h d) -> p h d", d=64),
                                                      axis=AX.X, op=ALU.add), reads=[SCR], writes=[SSN])
        act(P, ssn[:], ssn[:], AF.Ln, [SSN], [SSN], scale=1.0 / 64, bias=1e-6)
        act(P, ssn[:], ssn[:], AF.Exp, [SSN], [SSN], scale=-0.5)
        bg, BG = big_r.next()
        tmp, TMP = tmp_r.next()

        def normed(src0, nh, h0, gt, GT, dst0):
            tv = tmp[:, 0:nh * 64].rearrange("p (h d) -> p h d", d=64)
            tt(P, "dve", tv, y[:, src0:src0 + nh * 64].rearrange("p (h d) -> p h d", d=64),
               ssn[:, h0:h0 + nh].unsqueeze(2).to_broadcast([128, nh, 64]), ALU.mult, [YB, SSN], [TMP])
            tt(P, "dve", bg[:, dst0:dst0 + nh * 64].rearrange("p (h d) -> p h d", d=64), tv,
               gt[:, :].unsqueeze(1).to_broadcast([128, nh, 64]), ALU.mult, [TMP, GT], [BG])

        normed(0, 16, 0, gq, GQ, 0)
        normed(1536, 4, 16, gs, GS, 1536)
        normed(2048, 4, 20, gw, GW, 2048)
        cp(P, "dve", bg[:, 1024:1536], y[:, 1024:1536], [YB], [BG])
        cp(P, "dve", bg[:, 1792:2048], y[:, 1792:2048], [YB], [BG])
        cp(P, "dve", bg[:, 2304:2560], y[:, 2304:2560], [YB], [BG])
        gt_, GT_ = gt_r.next()
        act(P, gt_[:], y[:, 2560:2608], AF.Sigmoid, [YB], [GT_])
        P.dma("pool", big_d[rows, :], bg[:], reads=[BG])
        P.dma("pool", gates_d[rows, :], gt_[:], reads=[GT_])
    return P.finish()


def build_nsa_b():
    P = Prog()
    ins = {}
    for nm in ("k", "v"):
        ins[nm] = dict(
            aT=P.dram(nm + "_a2T", [128, TT], BF16, "ExternalInput").ap(),
            w1=P.dram(nm + "_w1", [128, 16, 256], F32, "ExternalInput").ap(),
            pe=P.dram(nm + "_pe", [128, 16], F32, "ExternalInput").ap(),
            b1=P.dram(nm + "_b1", [128, 2], F32, "ExternalInput").ap(),
            w2=P.dram(nm + "_w2", [128, 2, 64], F32, "ExternalInput").ap(),
            out=P.dram(nm + "_cmp", [512, 64], BF16, "ExternalOutput").ap(),
        )
    gk_d = P.dram("gk", [128, 64], F32, "ExternalInput").ap()
    gk, GK = load_const(P, gk_d, [128, 64], F32, "gk")
    ps_r = Rot(P, 4, [128, 512], F32, "ps", psum=True)
    f_r = Rot(P, 6, [128, 512], F32, "f")
    for nm in ("k", "v"):
        I = ins[nm]
        aT, AT = load_const(P, I["aT"], [128, TT], BF16, nm + "aT")
        w1f, W1F = load_const(P, I["w1"], [128, 16, 256], F32, nm + "w1f")
        pef, PEF = load_const(P, I["pe"], [128, 16], F32, nm + "pef")
        b1, B1 = load_const(P, I["b1"], [128, 2], F32, nm + "b1")
        w2f, W2F = load_const(P, I["w2"], [128, 2, 64], F32, nm + "w2f")
        w1 = P.sb([128, 16, 256], BF16, nm + "w1"); W1 = Buf(nm + "w1")
        cp(P, "dve", w1[:], w1f[:], [W1F], [W1])
        peb = P.sb([128, 16], BF16, nm + "peb"); PEB = Buf(nm + "peb")
        cp(P, "dve", peb[:], pef[:], [PEF], [PEB])
        w2 = P.sb([128, 2, 64], BF16, nm + "w2"); W2 = Buf(nm + "w2")
        cp(P, "dve", w2[:], w2f[:], [W2F], [W2])
        av = aT[:, :].rearrange("p (j s) -> p j s", s=16)
        gT = P.sb([128, 2, 512], BF16, nm + "gT"); GT = Buf(nm + "gT")
        memset(P, "dve", gT[:], 0.0, [GT])
        for half in range(2):
            hp, HP = ps_r.next()
            cv, CV = ps_r.next()
            for m in range(16):
                rhs = av[:, 0:511, 2 * m] if m < 8 else av[:, 1:512, 2 * m - 16]
                mm(P, hp[:, 0:511], w1[:, m, half * 128:(half + 1) * 128], rhs, m == 0, m == 15, [W1, AT], [HP])
            for m in range(16):
                mm(P, cv[:, 0:1], w1[:, m, half * 128:(half + 1) * 128], peb[:, m:m + 1], m == 0, m == 15, [W1, PEB], [CV])
            bias, BI = f_r.next()
            tt(P, "dve", bias[:, 0:1], cv[:, 0:1], b1[:, half:half + 1], ALU.add, [CV, B1], [BI])
            u, U = f_r.next()
            act(P, u[:, 0:511], hp[:, 0:511], AF.Identity, [HP, BI], [U], bias=bias[:, 0:1])
            t1, T1 = f_r.next()
            tt(P, "dve", t1[:, 0:511], u[:, 0:511], u[:, 0:511], ALU.mult, [U], [T1])
            t2, T2 = f_r.next()
            ts(P, "dve", t2[:, 0:511], t1[:, 0:511], 0.044715, 1.0, ALU.mult, ALU.add, [T1], [T2])
            t3, T3 = f_r.next()
            tt(P, "dve", t3[:, 0:511], t2[:, 0:511], u[:, 0:511], ALU.mult, [T2, U], [T3])
            t4, T4 = f_r.next()
            act(P, t4[:, 0:511], t3[:, 0:511], AF.Tanh, [T3], [T4], scale=0.7978845608028654)
            t5, T5 = f_r.next()
            ts(P, "dve", t5[:, 0:511], t4[:, 0:511], 1.0, 0.5, ALU.add, ALU.mult, [T4], [T5])
            tt(P, "dve", gT[:, half, 0:511], t5[:, 0:511], u[:, 0:511], ALU.mult, [T5, U], [GT])
        for ch in range(4):
            op_, OP = ps_r.next()
            for half in range(2):
                mm(P, op_[:, 0:64], gT[:, half, ch * 128:(ch + 1) * 128], w2[:, half, :], half == 0, half == 1,
                   [GT, W2], [OP])
            ob = P.sb([128, 64], BF16, f"{nm}ob{ch}"); OB = Buf("ob")
            if nm == "k":
                sq, SQ = f_r.next()
                ssk = P.sb([128, 1], F32, f"ssk{ch}"); SSK = Buf("ssk")
                rms_rstd(P, op_[:, 0:64], 64, ssk[:], sq[:, 0:64], [OP], SSK, SQ)
                kn, KN = f_r.next()
                ts(P, "dve", kn[:, 0:64], op_[:, 0:64], ssk[:, 0:1], None, ALU.mult, None, [OP, SSK], [KN])
                tt(P, "dve", ob[:], kn[:, 0:64], gk[:], ALU.mult, [KN, GK], [OB])
            else:
                cp(P, "act", ob[:], op_[:, 0:64], [OP], [OB])
            P.dma("sp", I["out"][ch * 128:(ch + 1) * 128, :], ob[:], reads=[OB])
    return P.finish()


NQB = TT // 128


def build_nsa_c():
    P = Prog()
    QT_d = P.dram("QT", [70, 4, TT], BF16, "ExternalInput").ap()
    KS_d = P.dram("KS", [70, TT], BF16, "ExternalInput").ap()
    KW_d = P.dram("KW", [70, TT], BF16, "ExternalInput").ap()
    KC_d = P.dram("KC", [70, 512], BF16, "ExternalInput").ap()
    VS_d = P.dram("VS", [128, 64, 66], BF16, "ExternalInput").ap()
    VW_d = P.dram("VW", [128, 64, 66], BF16, "ExternalInput").ap()
    VC_d = P.dram("VC", [128, 4, 194], BF16, "ExternalInput").ap()
    G_d = P.dram("G", [128, 64, 12], F32, "ExternalInput").ap()
    E_d = P.dram("E", [128, TT], BF16, "ExternalInput").ap()
    CM_d = P.dram("CM", [128, 17, 512], BF16, "ExternalInput").ap()
    CA_d = P.dram("CA", [128, 512], BF16, "ExternalInput").ap()
    WM_d = P.dram("WM", [128, 512], BF16, "ExternalInput").ap()
    AT_d = P.dram("AT", [128, 256], F32, "ExternalInput").ap()
    id_d = P.dram("ident", [128, 128], BF16, "ExternalInput").ap()
    o_d = P.dram("o", [TT, 256], F32, "ExternalOutput").ap()
    ident, IDB = make_ident(P, id_d)
    QT = P.sb([70, 4, TT], BF16, "QT"); QTB = Buf("QT")
    for h in range(4):
        P.dma("sp", QT[:, h, :], QT_d[:, h, :], writes=[QTB])
    KS, KSB = load_const(P, KS_d, [70, TT], BF16, "KS")
    KW, KWB = load_const(P, KW_d, [70, TT], BF16, "KW")
    KC, KCB = load_const(P, KC_d, [70, 512], BF16, "KC")
    VS, VSB = load_const(P, VS_d, [128, 64, 66], BF16, "VS")
    VW, VWB = load_const(P, VW_d, [128, 64, 66], BF16, "VW")
    VC, VCB = load_const(P, VC_d, [128, 4, 194], BF16, "VC")
    G, GB = load_const(P, G_d, [128, 64, 12], F32, "G")
    E, EB = load_const(P, E_d, [128, TT], BF16, "E")
    CM, CMB = load_const(P, CM_d, [128, 17, 512], BF16, "CM")
    CA, CAB = load_const(P, CA_d, [128, 512], BF16, "CA")
    WM, WMB = load_const(P, WM_d, [128, 512], BF16, "WM")
    AT, ATB = load_const(P, AT_d, [128, 256], F32, "AT")
    s_r = Rot(P, 3, [128, 512], F32, "S", psum=True)
    accC_r = Rot(P, 2, [128, 512], F32, "accC", psum=True)
    accS_r = Rot(P, 1, [128, 512], F32, "accS", psum=True)
    accW_r = Rot(P, 1, [128, 512], F32, "accW", psum=True)
    pst_r = Rot(P, 1, [128, 128], BF16, "pst", psum=True)
    p_r = Rot(P, 4, [128, 512], BF16, "p")
    sm_r = Rot(P, 48, [128, 1], F32, "sm")
    imp_r = Rot(P, 4, [128, 128], F32, "imp")
    sc_r = Rot(P, 4, [128, 128], F32, "sc")
    m8_r = Rot(P, 4, [128, 8], F32, "m8")
    ns_r = Rot(P, 2, [128, 128], BF16, "ns")
    nt_r = Rot(P, 3, [128, 4, 128], BF16, "nt")
    o_r = Rot(P, 8, [128, 256], F32, "ot")
    scale = 0.125

    def qk(S, SB_, KT, KTB, c, rhsQ, extra):
        mm(P, S[:], KT[:, c * 128:(c + 1) * 128], rhsQ, True, extra is None, [KTB, QTB], [SB_])
        if extra is not None:
            lhsT, rhs, rd = extra
            mm(P, S[:], lhsT, rhs, False, True, rd, [SB_])

    def rq(qb):
        return QT[:, :, qb * 128:(qb + 1) * 128]

    st = {}

    def finalize(qb, a, AB, width, col, first):
        nxt, NXT = o_r.next()
        cur = st.get(("o", qb))
        rcs = []
        for h in range(4):
            off = (h % 2) * 193 if width == 193 else h * 65
            aa, AAB = (a[h // 2], AB[h // 2]) if width == 193 else (a, AB)
            sm, SM = sm_r.next()
            ts(P, "dve", sm[:], aa[:, off + 64:off + 65], 1e-30, None, ALU.max, None, [AAB], [SM])
            rc, RC = sm_r.next()
            P.op("dve", lambda e, rc=rc, sm=sm: e.reciprocal(out=rc[:], in_=sm[:]), reads=[SM], writes=[RC])
            gf, GF = sm_r.next()
            tt(P, "dve", gf[:], rc[:], G[:, qb, h * 3 + col:h * 3 + col + 1], ALU.mult, [RC, GB], [GF])
            if cur is None:
                ts(P, "dve", nxt[:, h * 64:(h + 1) * 64], aa[:, off:off + 64], gf[:, 0:1], None, ALU.mult, None,
                   [AAB, GF], [NXT])
            else:
                c_t, C_B = cur
                P.op("dve", lambda e, nxt=nxt, aa=aa, off=off, h=h, gf=gf, c_t=c_t: e.scalar_tensor_tensor(
                    out=nxt[:, h * 64:(h + 1) * 64], in0=aa[:, off:off + 64], scalar=gf[:, 0:1],
                    in1=c_t[:, h * 64:(h + 1) * 64], op0=ALU.mult, op1=ALU.add), reads=[AAB, GF, C_B], writes=[NXT])
            rcs.append((rc, RC))
        st[("o", qb)] = (nxt, NXT)
        return rcs

    def cmp_steps(qb):
        rhsQ = rq(qb)
        ncc = qb // 16 + 1
        ctx = {}

        def mk_qk(cc):
            def f(S, SB_):
                if cc == 0:
                    ctx["acc"] = [accC_r.next(), accC_r.next()]
                extra = None
                if qb - 16 * cc <= 16:
                    extra = (ident[:], CM[:, qb - 16 * cc, :], [IDB, CMB])
                qk(S, SB_, KC, KCB, cc, rhsQ, extra)
            return f

        def mk_pv(cc):
            def f(p, PB):
                for h in range(4):
                    a, AB = ctx["acc"][h // 2]
                    off = (h % 2) * 193
                    mm(P, a[:, off:off + 193], p[:, h * 128:(h + 1) * 128], VC[:, cc, 0:193],
                       cc == 0 and h % 2 == 0, cc == ncc - 1, [PB, VCB], [AB], skip=True)
            return f

        def after():
            acc = ctx["acc"]
            rcs = finalize(qb, [acc[0][0], acc[1][0]], [acc[0][1], acc[1][1]], 193, 0, True)
            imp, IMP = None, None
            for h in range(4):
                a, AB = acc[h // 2]
                off = (h % 2) * 193
                rc, RC = rcs[h]
                ni, NI = imp_r.next()
                if imp is None:
                    ts(P, "dve", ni[:], a[:, off + 65:off + 193], rc[:, 0:1], None, ALU.mult, None, [AB, RC], [NI])
                else:
                    P.op("dve", lambda e, ni=ni, a=a, off=off, rc=rc, imp=imp: e.scalar_tensor_tensor(
                        out=ni[:], in0=a[:, off + 65:off + 193], scalar=rc[:, 0:1], in1=imp[:], op0=ALU.mult,
                        op1=ALU.add), reads=[AB, RC, IMP], writes=[NI])
                imp, IMP = ni, NI
            sc, SC = sc_r.next()
            tt(P, "dve", sc[:], imp[:], AT[:, 126 - 2 * qb:254 - 2 * qb], ALU.add, [IMP, ATB], [SC])
            ts(P, "dve", sc[:, 0:1], imp[:, 0:1], 1e4, None, ALU.add, None, [IMP, SC], [SC])
            m1, M1 = m8_r.next()
            P.op("dve", lambda e, m1=m1, sc=sc: e.max(out=m1[:], in_=sc[:]), reads=[SC], writes=[M1])
            sc2, SC2 = sc_r.next()
            P.op("dve", lambda e, sc2=sc2, m1=m1, sc=sc: e.match_replace(
                out=sc2[:], in_to_replace=m1[:], in_values=sc[:], imm_value=-1e9), reads=[SC, M1], writes=[SC2])
            m2, M2 = m8_r.next()
            P.op("dve", lambda e, m2=m2, sc2=sc2: e.max(out=m2[:], in_=sc2[:]), reads=[SC2], writes=[M2])
            ns, NS = ns_r.next()
            ts(P, "dve", ns[:], sc[:], m2[:, 7:8], -30000.0, ALU.is_lt, ALU.mult, [SC, M2], [NS])
            pst, PST = pst_r.next()
            tr(P, pst[:], ns[:], ident[:], [NS, IDB], [PST])
            nt, NTB = nt_r.next()
            for h in range(4):
                cp(P, "act" if h % 2 else "dve", nt[:, h, :], pst[:], [PST], [NTB])
            st[("nt", qb)] = (nt, NTB)

        return [(mk_qk(cc), mk_pv(cc), after if cc == ncc - 1 else None) for cc in range(ncc)]

    def branch_steps(qb, which):
        rhsQ = rq(qb)
        ctx = {}
        if which == "sel":
            cs = list(range(qb + 1))
            KT, KTB, V, VB, accr, col = KS, KSB, VS, VSB, accS_r, 1
        else:
            cs = list(range(max(0, qb - 4), qb + 1))
            KT, KTB, V, VB, accr, col = KW, KWB, VW, VWB, accW_r, 2

        def mk_qk(c):
            def f(S, SB_):
                if c == cs[0]:
                    ctx["acc"] = accr.next()
                extra = None
                if c == qb:
                    extra = (ident[:], CA[:], [IDB, CAB])
                elif which == "sel":
                    nt, NTB = st[("nt", qb)]
                    extra = (E[:, c * 128:(c + 1) * 128], nt[:].rearrange("p h q -> p (h q)"), [EB, NTB])
                elif c == qb - 4:
                    extra = (ident[:], WM[:], [IDB, WMB])
                qk(S, SB_, KT, KTB, c, rhsQ, extra)
            return f

        def mk_pv(c):
            def f(p, PB):
                a, AB = ctx["acc"]
                for h in range(4):
                    mm(P, a[:, h * 65:(h + 1) * 65], p[:, h * 128:(h + 1) * 128], V[:, c, 0:65],
                       c == cs[0] and h == 0, c == qb, [PB, VB], [AB], skip=True)
            return f

        def after():
            a, AB = ctx["acc"]
            finalize(qb, a, AB, 65, col, False)
            if which == "sel":
                cur, CUR = st[("o", qb)]
                P.dma("sp", o_d[qb * 128:(qb + 1) * 128, :], cur[:], reads=[CUR])
                st.pop(("o", qb)); st.pop(("nt", qb))

        return [(mk_qk(c), mk_pv(c), after if c == qb else None) for c in cs]

    steps = list(cmp_steps(0))
    for qb in range(NQB):
        steps += branch_steps(qb, "win")
        if qb + 1 < NQB:
            steps += cmp_steps(qb + 1)
        steps += branch_steps(qb, "sel")
    run_pipeline(P, steps, s_r, p_r, scale)
    return P.finish()


_PROGS = {}


def _prog(name, fn):
    if name not in _PROGS:
        _PROGS[name] = fn()
    return _PROGS[name]


def _pl(g):
    return np.ascontiguousarray(np.asarray(g, np.float32).reshape(-1, 128).T)


def _bc(g):
    return np.ascontiguousarray(np.tile(np.asarray(g, np.float32)[None, :], (128, 1)))


def _run(nc, maps):
    res = run_bass_kernel_spmd(nc, maps, core_ids=list(range(8)))
    return res.results


def _split2(a):
    a = np.asarray(a, np.float64)
    hi = a.astype(np.float32).astype(NPBF)
    lo = (a - hi.astype(np.float64)).astype(np.float32).astype(NPBF)
    return hi, lo


def _nsa_c_consts():
    i = np.arange(128)[:, None]
    u = np.arange(TT)[None, :]
    E = (u // 64 == i).astype(np.float32).astype(NPBF)
    jl = np.arange(128)[:, None, None]
    m = np.arange(17)[None, :, None]
    ql = np.tile(np.arange(128), 4)[None, None, :]
    CM = np.where(16 * jl - ql <= 128 * m - 31, 0.0, -30000.0).astype(np.float32).astype(NPBF)
    kl = np.arange(128)[:, None]
    q4 = np.tile(np.arange(128), 4)[None, :]
    CA = np.where(kl <= q4, 0.0, -30000.0).astype(np.float32).astype(NPBF)
    WM = np.where(kl > q4, 0.0, -30000.0).astype(np.float32).astype(NPBF)
    qq = np.arange(128)[:, None]
    rel = np.arange(256)[None, :] - 126
    cur = (qq >= 64).astype(np.int64)
    AT = np.zeros((128, 256), np.float32)
    AT[(rel == cur) | (rel == cur - 1)] = 1e4
    AT[rel > cur] = -1.0
    jj = np.arange(512)[:, None]
    ii = np.arange(128)[None, :]
    Ov = ((jj >= 4 * ii - 1) & (jj <= 4 * ii + 3) & (jj <= 510)).astype(np.float32)
    return dict(E=np.ascontiguousarray(E), CM=np.ascontiguousarray(CM), CA=np.ascontiguousarray(CA),
                WM=np.ascontiguousarray(WM), AT=AT, Ov=Ov)


def _aug_k(pos):
    hi, lo = _split2(pos)
    one = np.ones_like(hi)
    return np.stack([one, one, hi, lo, hi, lo], 0)


def kernel(**inp):
    f = {k: np.asarray(v) for k, v in inp.items()}
    x = f["x"]
    B, T, D = x.shape
    xf = np.ascontiguousarray(x.reshape(B * T, D))
    ident = np.eye(128, dtype=np.float32).astype(NPBF)
    G4 = 4
    nc = _prog("nsa_a", build_nsa_a)
    maps = [{"x": xf[c * NT:(c + 1) * NT], "g": _pl(f["a_attn_norm"][0]), "w": np.ascontiguousarray(f["a_w_in"][0]),
             "gq": _bc(f["a_q_norm"][0]), "gs": _bc(f["a_kslc_norm"][0]), "gw": _bc(f["a_kwin_norm"][0]),
             "ident": ident} for c in range(8)]
    r = _run(nc, maps)
    big = np.concatenate([q["big"] for q in r], 0)
    gates = np.concatenate([q["gates"] for q in r], 0)
    nc = _prog("nsa_b", build_nsa_b)

    def a2T(col0, b):
        a = big[b * T:(b + 1) * T, col0:col0 + 64]
        o = np.zeros((128, T), NPBF)
        o[0:64] = a.T
        o[64:128, 0:T - 1] = a[1:].T
        return o

    def w1l(w):
        return np.ascontiguousarray(np.asarray(w, np.float32).reshape(16, 2, 64, 256).transpose(1, 2, 0, 3).reshape(128, 16, 256))

    def pel(p):
        return np.ascontiguousarray(np.asarray(p, np.float32).reshape(16, 2, 64).transpose(1, 2, 0).reshape(128, 16))

    maps = []
    for c in range(8):
        b, g = c // 4, c % 4
        maps.append({
            "k_a2T": a2T(1024 + g * 64, b), "v_a2T": a2T(1280 + g * 64, b),
            "k_w1": w1l(f["a_cmp_k_w1"][0]), "v_w1": w1l(f["a_cmp_v_w1"][0]),
            "k_pe": pel(f["a_cmp_pos_k"][0]), "v_pe": pel(f["a_cmp_pos_v"][0]),
            "k_b1": _pl(f["a_cmp_k_b1"][0]), "v_b1": _pl(f["a_cmp_v_b1"][0]),
            "k_w2": np.ascontiguousarray(f["a_cmp_k_w2"][0].reshape(2, 128, 64).transpose(1, 0, 2)),
            "v_w2": np.ascontiguousarray(f["a_cmp_v_w2"][0].reshape(2, 128, 64).transpose(1, 0, 2)),
            "gk": _bc(f["a_kcmp_norm"][0])})
    rB = _run(nc, maps)
    nc = _prog("nsa_c", build_nsa_c)
    C = _nsa_c_consts()
    tpos = np.arange(T, dtype=np.float64)
    kaug_tok = _aug_k(tpos)
    kaug_cmp = _aug_k(16.0 * np.arange(512) + 15.5)
    maps = []
    for c in range(8):
        b, g = c // 4, c % 4
        bb = big[b * T:(b + 1) * T]
        QT = np.zeros((70, 4, T), NPBF)
        for hg in range(4):
            h = g * 4 + hg
            QT[0:64, hg] = bb[:, h * 64:(h + 1) * 64].T
            beta = 8.0 * 2.0 ** (-8.0 * (h + 1) / 16.0)
            bh, bl = _split2(np.full(T, beta))
            th, tl = _split2(beta * tpos)
            QT[64:70, hg] = np.stack([-th.astype(np.float32), -tl.astype(np.float32), bh.astype(np.float32),
                                      bh.astype(np.float32), bl.astype(np.float32), bl.astype(np.float32)], 0).astype(NPBF)
        KS = np.concatenate([bb[:, 1536 + g * 64:1536 + (g + 1) * 64].T, kaug_tok], 0)
        KW = np.concatenate([bb[:, 2048 + g * 64:2048 + (g + 1) * 64].T, kaug_tok], 0)
        KC = np.concatenate([rB[c]["k_cmp"].T, kaug_cmp], 0)

        def vaug(v):
            o = np.zeros((128, 64, 66), NPBF)
            o[:, :, 0:64] = v.reshape(64, 128, 64).transpose(1, 0, 2)
            o[:, :, 64] = 1.0
            return o

        VC = np.zeros((128, 4, 194), NPBF)
        VC[:, :, 0:64] = rB[c]["v_cmp"].reshape(4, 128, 64).transpose(1, 0, 2)
        VC[:, :, 64] = 1.0
        VC[:, :, 65:193] = C["Ov"].reshape(4, 128, 128).transpose(1, 0, 2).astype(NPBF)
        Gt = np.ascontiguousarray(gates[b * T:(b + 1) * T, g * 12:(g + 1) * 12].reshape(64, 128, 12).transpose(1, 0, 2))
        maps.append({"QT": QT, "KS": np.ascontiguousarray(KS), "KW": np.ascontiguousarray(KW),
                     "KC": np.ascontiguousarray(KC), "VS": vaug(bb[:, 1792 + g * 64:1792 + (g + 1) * 64]),
                     "VW": vaug(bb[:, 2304 + g * 64:2304 + (g + 1) * 64]), "VC": VC, "G": Gt, "E": C["E"],
                     "CM": C["CM"], "CA": C["CA"], "WM": C["WM"], "AT": C["AT"], "ident": ident})
    rC = _run(nc, maps)
    o_nsa = np.zeros((B, T, 1024), np.float32)
    for c in range(8):
        o_nsa[c // 4, :, (c % 4) * 256:(c % 4 + 1) * 256] = rC[c]["o"]
    o_nsa = o_nsa.reshape(B * T, 1024)

    def outproj(o, xres, w):
        nc = _prog("outproj", build_outproj)
        r = _run(nc, [{"o": o[c * NT:(c + 1) * NT], "x": xres[c * NT:(c + 1) * NT], "w": np.ascontiguousarray(w),
                       "ident": ident} for c in range(8)])
        return np.concatenate([q["y"] for q in r], 0)

    def ffn(xin, g, wgu, wd):
        nc = _prog("ffn", build_ffn)
        r = _run(nc, [{"x": xin[c * NT:(c + 1) * NT], "g": _pl(g), "wgu": np.ascontiguousarray(wgu),
                       "wd": np.ascontiguousarray(wd), "ident": ident} for c in range(8)])
        return np.concatenate([q["y"] for q in r], 0)

    x1 = outproj(o_nsa, xf, f["a_w_out"][0])
    x2 = ffn(x1, f["ffn_norm"][0], f["ffn_w_gate_up"][0], f["ffn_w_down"][0])
    nc = _prog("kvq", build_kvq)
    inv = (10000.0 ** (-np.arange(0, 64, 2, dtype=np.float32) / 64)).astype(np.float32)
    maps = []
    for c in range(8):
        pos = (np.arange(NT) + (c % 4) * NT).astype(np.float32)
        ang = pos[:, None] * inv[None, :]
        cs = np.concatenate([np.cos(ang), np.sin(ang)], 1).astype(np.float32).reshape(NB, 128, 64).transpose(1, 0, 2)
        maps.append({"x": x2[c * NT:(c + 1) * NT], "g_kv": _pl(f["kv_norm"]), "g_q": _pl(f["b_attn_norm"][0]),
                     "g_c": _pl(f["kv_c_norm"]), "g_qa": _pl(f["b_q_a_norm"][0]), "g_kn": _bc(f["kv_k_norm"]),
                     "g_qn": _bc(f["b_q_norm"][0]), "w_kva": np.ascontiguousarray(f["kv_w_a"]),
                     "w_kvb": np.ascontiguousarray(f["kv_w_b"]), "w_qa": np.ascontiguousarray(f["b_w_q_a"][0]),
                     "w_qb": np.ascontiguousarray(f["b_w_q_b"][0]), "cs": np.ascontiguousarray(cs), "ident": ident})
    r = _run(nc, maps)
    kk = np.concatenate([q["k"] for q in r], 0)
    qq = np.concatenate([q["q"] for q in r], 0)
    vv = np.concatenate([q["v"] for q in r], 0)
    nc = _prog("mla", build_mla)
    r = _run(nc, [mla_host_inputs(kk, qq, vv, c // 4, c % 4) for c in range(8)])
    o_mla = np.zeros((B, T, 1024), np.float32)
    for c in range(8):
        o_mla[c // 4, :, (c % 4) * 256:(c % 4 + 1) * 256] = r[c]["o"]
    o_mla = o_mla.reshape(B * T, 1024)
    x3 = outproj(o_mla, x2, f["b_w_out"][0])
    x4 = ffn(x3, f["ffn_norm"][1], f["ffn_w_gate_up"][1], f["ffn_w_down"][1])
    return x4.reshape(B, T, D).astype(np.float32)
```

```python
import numpy as np
import ml_dtypes
import concourse.bass as bass
import concourse.mybir as mybir
from concourse.bass_utils import run_bass_kernel_spmd

F32 = mybir.dt.float32
BF16 = mybir.dt.bfloat16
AF = mybir.ActivationFunctionType
ALU = mybir.AluOpType
AX = mybir.AxisListType
NPBF = ml_dtypes.bfloat16

ENGS = ["sp", "act", "dve", "pool", "pe"]


class Buf:
    __slots__ = ("w", "r", "dsem", "dcnt", "name")

    def __init__(self, name=""):
        self.w = None
        self.r = []
        self.dsem = None
        self.dcnt = 0
        self.name = name


class Prog:
    def __init__(self):
        self.nc = bass.Bass("TRN2", target_bir_lowering=False)
        nc = self.nc
        self.q = {e: [] for e in ENGS}
        self.cnt = {e: 0 for e in ENGS}
        self.sems = {}
        self.sem = {}
        for e in ("act", "dve", "pool", "pe"):
            self.sem[e] = self.new_sem("eng_" + e)
        self.wm = {e: {} for e in ENGS}
        self.all_dma_tokens = {}
        self.nsb = 0
        self.nps = 0

    def new_sem(self, name):
        h = self.nc.alloc_semaphore(name)
        sid = len(self.sems)
        self.sems[sid] = h
        return sid

    def sb(self, shape, dtype, name=None):
        self.nsb += 1
        return self.nc.alloc_sbuf_tensor(f"S{self.nsb}_" + (name or ""), list(shape), dtype)

    def ps(self, shape, dtype=F32, name=None):
        self.nps += 1
        return self.nc.alloc_psum_tensor(f"P{self.nps}_" + (name or ""), list(shape), dtype)

    def dram(self, name, shape, dtype, kind):
        return self.nc.dram_tensor(name, list(shape), dtype, kind=kind)

    def _deps(self, eng, reads, writes):
        deps = {}

        def need(tok):
            if tok is None:
                return
            s, v = tok
            if eng == "pe" and s == self.sem["pe"]:
                return
            if deps.get(s, 0) < v:
                deps[s] = v

        for b in reads:
            need(b.w)
        for b in writes:
            need(b.w)
            for t in b.r:
                need(t)
        out = []
        wm = self.wm[eng]
        for s, v in deps.items():
            if wm.get(s, 0) < v:
                wm[s] = v
                out.append((s, v))
        return out

    def op(self, eng, fn, reads=(), writes=()):
        waits = self._deps(eng, reads, writes)
        self.cnt[eng] += 1
        tok = (self.sem[eng], self.cnt[eng])
        self.q[eng].append((waits, fn, (tok[0], 1)))
        for b in reads:
            b.r.append(tok)
        for b in writes:
            b.w = tok
            b.r = []
        return tok

    def dma(self, q, out, in_, reads=(), writes=(), **kw):
        waits = self._deps(q, reads, writes)
        tb = (list(writes) + list(reads))[0]
        if tb.dsem is None:
            tb.dsem = self.new_sem("d_" + tb.name + str(len(self.sems)))
        tb.dcnt += 1
        tok = (tb.dsem, 16 * tb.dcnt)
        self.all_dma_tokens[tb.dsem] = tok[1]
        self.q[q].append((waits, lambda e: e.dma_start(out=out, in_=in_, **kw), (tok[0], 16)))
        for b in reads:
            b.r.append(tok)
        for b in writes:
            b.w = tok
            b.r = []
        return tok

    def cc(self, kind, ins_ap, outs_ap, groups, reads=(), writes=()):
        waits = self._deps("pool", reads, writes)
        sem = self.new_sem("cc" + str(len(self.sems)))
        tok = (sem, 16)
        self.all_dma_tokens[sem] = 16
        self.q["pool"].append((waits, lambda e: e.collective_compute(
            kind, ALU.bypass, replica_groups=groups, ins=[ins_ap], outs=[outs_ap]), (sem, 16)))
        for b in reads:
            b.r.append(tok)
        for b in writes:
            b.w = tok
            b.r = []
        return tok

    def finish(self):
        nc = self.nc
        final = [(s, v) for s, v in self.all_dma_tokens.items()]
        for e in ("act", "dve", "pool", "pe"):
            if self.cnt[e] > 0:
                final.append((self.sem[e], self.cnt[e]))
        self.q["sp"].append((final, None, None))
        emap = {"sp": "sync", "act": "scalar", "dve": "vector", "pool": "gpsimd", "pe": "tensor"}
        sems = self.sems

        def replay(lst):
            def run(e):
                for waits, fn, inc in lst:
                    for s, v in waits:
                        e.wait_ge(sems[s], v)
                    if fn is not None:
                        ins = fn(e)
                        ins.then_inc(sems[inc[0]], inc[1])
            return run

        with nc.Block() as block:
            for k in ENGS:
                if self.q[k]:
                    getattr(block, emap[k])(replay(self.q[k]))
        return nc


def mm(P, out, lhsT, rhs, start, stop, reads, writes, skip=False):
    return P.op("pe", lambda e: e.matmul(out, lhsT=lhsT, rhs=rhs, start=start, stop=stop,
                                         skip_group_check=skip), reads=reads, writes=writes)


def tr(P, out, in_, ident, reads, writes):
    return P.op("pe", lambda e: e.transpose(out, in_, ident), reads=reads, writes=writes)


def act(P, out, in_, func, reads, writes, **kw):
    return P.op("act", lambda e: e.activation(out=out, in_=in_, func=func, **kw), reads=reads, writes=writes)


def tt(P, eng, out, in0, in1, op, reads, writes):
    return P.op(eng, lambda e: e.tensor_tensor(out=out, in0=in0, in1=in1, op=op), reads=reads, writes=writes)


def ts(P, eng, out, in0, s1, s2, op0, op1, reads, writes):
    if op1 is None:
        return P.op(eng, lambda e: e.tensor_scalar(out=out, in0=in0, scalar1=s1, scalar2=None, op0=op0),
                    reads=reads, writes=writes)
    return P.op(eng, lambda e: e.tensor_scalar(out=out, in0=in0, scalar1=s1, scalar2=s2, op0=op0, op1=op1),
                reads=reads, writes=writes)


def cp(P, eng, out, in_, reads, writes):
    if eng == "act":
        return P.op("act", lambda e: e.activation(out=out, in_=in_, func=AF.Copy), reads=reads, writes=writes)
    return P.op(eng, lambda e: e.tensor_copy(out=out, in_=in_), reads=reads, writes=writes)


def memset(P, eng, ap, val, writes):
    return P.op(eng, lambda e: e.memset(ap, val), writes=writes)


class Rot:
    def __init__(self, P, n, shape, dtype, name, psum=False):
        self.items = []
        for i in range(n):
            t = P.ps(shape, dtype, f"{name}{i}") if psum else P.sb(shape, dtype, f"{name}{i}")
            self.items.append((t, Buf(f"{name}{i}")))
        self.i = 0

    def next(self):
        it = self.items[self.i % len(self.items)]
        self.i += 1
        return it


def load_const(P, dram_ap, shape, dtype, name, q="sp"):
    t = P.sb(shape, dtype, name)
    b = Buf(name)
    P.dma(q, t[:], dram_ap, writes=[b])
    return t, b


def load_weight(P, w_ap, K, N, name, stg, gain=None):
    KC = K // 128
    wt = P.sb([128, KC, N], BF16, name)
    wb = Buf(name)
    i = 0
    for kc in range(KC):
        for c0 in range(0, N, 2048):
            c1 = min(N, c0 + 2048)
            st, sbuf = stg.next()
            P.dma("sp", st[:, 0:c1 - c0], w_ap[kc * 128:(kc + 1) * 128, c0:c1], writes=[sbuf])
            eng = "dve" if i % 2 == 0 else "act"
            i += 1
            if gain is None:
                cp(P, eng, wt[:, kc, c0:c1], st[:, 0:c1 - c0], [sbuf], [wb])
            else:
                gt, gb = gain
                if eng == "dve":
                    ts(P, "dve", wt[:, kc, c0:c1], st[:, 0:c1 - c0], gt[:, kc:kc + 1], None, ALU.mult, None,
                       [sbuf, gb], [wb])
                else:
                    act(P, wt[:, kc, c0:c1], st[:, 0:c1 - c0], AF.Copy, [sbuf, gb], [wb], scale=gt[:, kc:kc + 1])
    return wt, wb


def rms_rstd(P, x_ap, D, ss, scratch, reads, SS, SCR, eps=1e-6):
    act(P, scratch, x_ap, AF.Square, reads, [SCR, SS], accum_out=ss)
    act(P, ss, ss, AF.Ln, [SS], [SS], scale=1.0 / D, bias=eps)
    act(P, ss, ss, AF.Exp, [SS], [SS], scale=-0.5)


def make_ident(P, ident_dram):
    return load_const(P, ident_dram, [128, 128], BF16, "ident_sb")


NT = 2048
NB = NT // 128


def build_outproj():
    P = Prog()
    o_d = P.dram("o", [NT, 1024], F32, "ExternalInput").ap()
    x_d = P.dram("x", [NT, 1024], F32, "ExternalInput").ap()
    w_d = P.dram("w", [1024, 1024], F32, "ExternalInput").ap()
    id_d = P.dram("ident", [128, 128], BF16, "ExternalInput").ap()
    y_d = P.dram("y", [NT, 1024], F32, "ExternalOutput").ap()
    ident, IDB = make_ident(P, id_d)
    stg = Rot(P, 2, [128, 2048], F32, "stg")
    w, WB = load_weight(P, w_d, 1024, 1024, "w", stg)
    oin = Rot(P, 2, [128, 1024], F32, "oin")
    xin = Rot(P, 2, [128, 1024], F32, "xin")
    obf = Rot(P, 2, [128, 1024], BF16, "obf")
    oT = Rot(P, 2, [128, 8, 128], BF16, "oT")
    pT = Rot(P, 2, [128, 8, 128], BF16, "pT", psum=True)
    acc = Rot(P, 4, [128, 512], F32, "acc", psum=True)
    yo = Rot(P, 2, [128, 1024], F32, "yo")
    for b in range(NB):
        rows = slice(b * 128, (b + 1) * 128)
        ot, OB = oin.next()
        xt, XB = xin.next()
        P.dma("sp", ot[:], o_d[rows, :], writes=[OB])
        P.dma("sp", xt[:], x_d[rows, :], writes=[XB])
        bt, BB = obf.next()
        cp(P, "dve", bt[:], ot[:], [OB], [BB])
        pt, PB = pT.next()
        for c in range(8):
            tr(P, pt[:, c, :], bt[:, c * 128:(c + 1) * 128], ident[:], [BB, IDB], [PB])
        tt_, TB = oT.next()
        cp(P, "act", tt_[:], pt[:], [PB], [TB])
        yt, YB = yo.next()
        for half in range(2):
            at, AB = acc.next()
            for c in range(8):
                mm(P, at[:], tt_[:, c, :], w[:, c, half * 512:(half + 1) * 512], c == 0, c == 7, [TB, WB], [AB])
            tt(P, "dve", yt[:, half * 512:(half + 1) * 512], at[:], xt[:, half * 512:(half + 1) * 512], ALU.add,
               [AB, XB], [YB])
        P.dma("pool", y_d[rows, :], yt[:], reads=[YB])
    return P.finish()


FH = 2816


def build_ffn():
    P = Prog()
    x_d = P.dram("x", [NT, 1024], F32, "ExternalInput").ap()
    g_d = P.dram("g", [128, 8], F32, "ExternalInput").ap()
    wgu_d = P.dram("wgu", [1024, 2 * FH], F32, "ExternalInput").ap()
    wd_d = P.dram("wd", [FH, 1024], F32, "ExternalInput").ap()
    id_d = P.dram("ident", [128, 128], BF16, "ExternalInput").ap()
    y_d = P.dram("y", [NT, 1024], F32, "ExternalOutput").ap()
    ident, IDB = make_ident(P, id_d)
    gain = load_const(P, g_d, [128, 8], F32, "gain")
    stg = Rot(P, 2, [128, 2048], F32, "stg")
    wgu, WGU = load_weight(P, wgu_d, 1024, 2 * FH, "wgu", stg, gain=gain)
    wd, WD = load_weight(P, wd_d, FH, 1024, "wd", stg)
    xin = Rot(P, 4, [128, 1024], F32, "xin")
    scr = P.sb([128, 1024], F32, "scr"); SCR = Buf("scr")
    ssr = Rot(P, 2, [128, 1], F32, "ss")
    xn = Rot(P, 2, [128, 1024], BF16, "xn")
    pT = Rot(P, 1, [128, 8, 128], BF16, "pT", psum=True)
    xT = Rot(P, 2, [128, 8, 256], BF16, "xT")
    gu = Rot(P, 3, [128, 512], F32, "gu", psum=True)
    accs = Rot(P, 4, [128, 512], F32, "acc", psum=True)
    sg = Rot(P, 2, [128, 256], BF16, "sg")
    aT = Rot(P, 2, [128, 256], BF16, "aT")
    yo = Rot(P, 2, [128, 1024], F32, "yo")
    NJ = FH // 128
    for sbk in range(NT // 256):
        xs = []
        xTt, XT = xT.next()
        for t in range(2):
            rows = slice(sbk * 256 + t * 128, sbk * 256 + (t + 1) * 128)
            xt, XB = xin.next()
            xs.append((xt, XB, rows))
            P.dma("sp", xt[:], x_d[rows, :], writes=[XB])
            ss, SS = ssr.next()
            rms_rstd(P, x_ap=xt[:], D=1024, ss=ss[:], scratch=scr[:], reads=[XB], SS=SS, SCR=SCR)
            xnt, XN = xn.next()
            ts(P, "dve", xnt[:], xt[:], ss[:, 0:1], None, ALU.mult, None, [XB, SS], [XN])
            pt, PB = pT.next()
            for c in range(8):
                tr(P, pt[:, c, :], xnt[:, c * 128:(c + 1) * 128], ident[:], [XN, IDB], [PB])
            cp(P, "dve", xTt[:, :, t * 128:(t + 1) * 128], pt[:], [PB], [XT])
        acc = [[accs.next() for _ in range(2)] for _ in range(2)]
        for j in range(NJ):
            gt, GB = gu.next()
            ut, UB = gu.next()
            for c in range(8):
                mm(P, gt[:, 0:256], wgu[:, c, j * 128:(j + 1) * 128], xTt[:, c, :], c == 0, c == 7, [WGU, XT], [GB])
            for c in range(8):
                mm(P, ut[:, 0:256], wgu[:, c, FH + j * 128:FH + (j + 1) * 128], xTt[:, c, :], c == 0, c == 7,
                   [WGU, XT], [UB])
            sgt, SG = sg.next()
            act(P, sgt[:], gt[:, 0:256], AF.Silu, [GB], [SG])
            at, AB = aT.next()
            tt(P, "dve", at[:], sgt[:], ut[:, 0:256], ALU.mult, [SG, UB], [AB])
            for t in range(2):
                for half in range(2):
                    a_t, A_B = acc[t][half]
                    mm(P, a_t[:], at[:, t * 128:(t + 1) * 128], wd[:, j, half * 512:(half + 1) * 512], j == 0,
                       j == NJ - 1, [AB, WD], [A_B])
        for t in range(2):
            xt, XB, rows = xs[t]
            yt, YB = yo.next()
            for half in range(2):
                a_t, A_B = acc[t][half]
                tt(P, "dve", yt[:, half * 512:(half + 1) * 512], a_t[:], xt[:, half * 512:(half + 1) * 512], ALU.add,
                   [A_B, XB], [YB])
            P.dma("pool", y_d[rows, :], yt[:], reads=[YB])
    return P.finish()


def rope_ops(P, src, SRC, dst, DST, cosb, sinb, CS, tmp, TMP, nh):
    x1 = src[:, :, 0:32]
    x2 = src[:, :, 32:64]
    t1, t2 = tmp[:, 0, 0:nh, :], tmp[:, 1, 0:nh, :]
    tt(P, "dve", t1, x1, cosb, ALU.mult, [SRC, CS], [TMP])
    tt(P, "dve", t2, x2, sinb, ALU.mult, [SRC, CS], [TMP])
    tt(P, "dve", dst[:, :, 0:32], t1, t2, ALU.subtract, [TMP], [DST])
    tt(P, "dve", t1, x1, sinb, ALU.mult, [SRC, CS], [TMP])
    tt(P, "dve", t2, x2, cosb, ALU.mult, [SRC, CS], [TMP])
    tt(P, "dve", dst[:, :, 32:64], t1, t2, ALU.add, [TMP], [DST])


def build_kvq(dbg=0):
    P = Prog()
    x_d = P.dram("x", [NT, 1024], F32, "ExternalInput").ap()
    gkv_d = P.dram("g_kv", [128, 8], F32, "ExternalInput").ap()
    gq_d = P.dram("g_q", [128, 8], F32, "ExternalInput").ap()
    gc_d = P.dram("g_c", [128, 2], F32, "ExternalInput").ap()
    gqa_d = P.dram("g_qa", [128, 3], F32, "ExternalInput").ap()
    gkn_d = P.dram("g_kn", [128, 192], F32, "ExternalInput").ap()
    gqn_d = P.dram("g_qn", [128, 192], F32, "ExternalInput").ap()
    wkva_d = P.dram("w_kva", [1024, 320], F32, "ExternalInput").ap()
    wkvb_d = P.dram("w_kvb", [256, 2048], F32, "ExternalInput").ap()
    wqa_d = P.dram("w_qa", [1024, 384], F32, "ExternalInput").ap()
    wqb_d = P.dram("w_qb", [384, 1536], F32, "ExternalInput").ap()
    cs_d = P.dram("cs", [128, NB, 64], F32, "ExternalInput").ap()
    id_d = P.dram("ident", [128, 128], BF16, "ExternalInput").ap()
    kT_d = P.dram("k", [NT, 1536], BF16, "ExternalOutput").ap()
    qT_d = P.dram("q", [NT, 1536], BF16, "ExternalOutput").ap()
    v_d = P.dram("v", [NT, 1024], BF16, "ExternalOutput").ap()
    ident, IDB = make_ident(P, id_d)
    gkv = load_const(P, gkv_d, [128, 8], F32, "gkv")
    gq = load_const(P, gq_d, [128, 8], F32, "gq")
    gc = load_const(P, gc_d, [128, 2], F32, "gc")
    gqa = load_const(P, gqa_d, [128, 3], F32, "gqa")
    gkn, GKN = load_const(P, gkn_d, [128, 192], F32, "gkn")
    gqn, GQN = load_const(P, gqn_d, [128, 192], F32, "gqn")
    cs, CS = load_const(P, cs_d, [128, NB, 64], F32, "cs")
    stg = Rot(P, 2, [128, 2048], F32, "stg")
    wkva, WKVA = load_weight(P, wkva_d, 1024, 320, "wkva", stg, gain=gkv)
    wkvb, WKVB = load_weight(P, wkvb_d, 256, 2048, "wkvb", stg, gain=gc)
    wqa, WQA = load_weight(P, wqa_d, 1024, 384, "wqa", stg, gain=gq)
    wqb, WQB = load_weight(P, wqb_d, 384, 1536, "wqb", stg, gain=gqa)
    xin = Rot(P, 2, [128, 1024], F32, "xin")
    scr = P.sb([128, 1024], F32, "scr"); SCR = Buf("scr")
    ssr = Rot(P, 4, [128, 1], F32, "ss")
    ss8 = Rot(P, 4, [128, 8], F32, "ss8")
    xn = Rot(P, 2, [128, 1024], BF16, "xn")
    pT = Rot(P, 2, [128, 8, 128], BF16, "pT", psum=True)
    xT = Rot(P, 2, [128, 8, 128], BF16, "xT")
    pf = Rot(P, 6, [128, 512], F32, "pf", psum=True)
    cn = Rot(P, 2, [128, 384], BF16, "cn")
    cT = Rot(P, 2, [128, 3, 128], BF16, "cT")
    kpe = Rot(P, 2, [128, 64], F32, "kpe")
    tmpf = Rot(P, 3, [128, 8, 192], F32, "tmpf")
    pe8 = Rot(P, 2, [128, 8, 64], F32, "pe8")
    rtmp = P.sb([128, 2, 8, 32], F32, "rtmp"); RTMP = Buf("rtmp")
    kt = Rot(P, 2, [128, 8, 192], BF16, "kt")
    vt = Rot(P, 2, [128, 8, 128], BF16, "vt")
    kTs = Rot(P, 2, [128, 2, 8, 128], BF16, "kTs")
    for b in range(NB):
        rows = slice(b * 128, (b + 1) * 128)
        cosb = cs[:, b, 0:32].unsqueeze(1).to_broadcast([128, 8, 32])
        sinb = cs[:, b, 32:64].unsqueeze(1).to_broadcast([128, 8, 32])
        xt, XB = xin.next()
        P.dma("sp", xt[:], x_d[rows, :], writes=[XB])
        ss, SS = ssr.next()
        rms_rstd(P, xt[:], 1024, ss[:], scr[:], [XB], SS, SCR)
        xnt, XN = xn.next()
        ts(P, "dve", xnt[:], xt[:], ss[:, 0:1], None, ALU.mult, None, [XB, SS], [XN])
        pt, PB = pT.next()
        for c in range(8):
            tr(P, pt[:, c, :], xnt[:, c * 128:(c + 1) * 128], ident[:], [XN, IDB], [PB])
        xTt, XT = xT.next()
        cp(P, "dve", xTt[:], pt[:], [PB], [XT])
        kva, KVA = pf.next()
        qa, QA = pf.next()
        for c in range(8):
            mm(P, kva[:, 0:320], xTt[:, c, :], wkva[:, c, :], c == 0, c == 7, [XT, WKVA], [KVA])
        for c in range(8):
            mm(P, qa[:, 0:384], xTt[:, c, :], wqa[:, c, :], c == 0, c == 7, [XT, WQA], [QA])
        ss2, SS2 = ssr.next()
        rms_rstd(P, kva[:, 0:256], 256, ss2[:], scr[:, 0:256], [KVA], SS2, SCR)
        cnt, CN = cn.next()
        ts(P, "dve", cnt[:, 0:256], kva[:, 0:256], ss2[:, 0:1], None, ALU.mult, None, [KVA, SS2], [CN])
        kp, KP = kpe.next()
        tt(P, "dve", kp[:], kva[:, 256:320], gkn[:, 128:192], ALU.mult, [KVA, GKN], [KP])
        sspe, SSPE = ssr.next()
        act(P, scr[:, 0:64], kva[:, 256:320], AF.Square, [KVA], [SCR, SSPE], accum_out=sspe[:])
        pt2, PB2 = pT.next()
        for c in range(2):
            tr(P, pt2[:, c, :], cnt[:, c * 128:(c + 1) * 128], ident[:], [CN, IDB], [PB2])
        cTt, CT = cT.next()
        cp(P, "act", cTt[:, 0:2, :], pt2[:, 0:2, :], [PB2], [CT])
        kvb = []
        for n in range(4):
            kvp, KVP = pf.next()
            kvb.append((kvp, KVP))
            for c in range(2):
                mm(P, kvp[:], cTt[:, c, :], wkvb[:, c, n * 512:(n + 1) * 512], c == 0, c == 1, [CT, WKVB], [KVP])
        s8, S8 = ss8.next()
        for h in range(8):
            kvp, KVP = kvb[h // 2]
            act(P, scr[:, 0:128], kvp[:, (h % 2) * 256:(h % 2) * 256 + 128], AF.Square, [KVP], [SCR, S8],
                accum_out=s8[:, h:h + 1])
        ts(P, "dve", s8[:], s8[:], sspe[:, 0:1], None, ALU.add, None, [S8, SSPE], [S8])
        act(P, s8[:], s8[:], AF.Ln, [S8], [S8], scale=1.0 / 192, bias=1e-6)
        act(P, s8[:], s8[:], AF.Exp, [S8], [S8], scale=-0.5)
        tf, TF = tmpf.next()
        ktt, KT = kt.next()
        vtt, VT = vt.next()
        for n in range(4):
            kvp, KVP = kvb[n]
            kv3 = kvp[:].rearrange("p (h c) -> p h c", h=2)
            tt(P, "dve", tf[:, 2 * n:2 * n + 2, 0:128], kv3[:, :, 0:128],
               gkn[:, 0:128].unsqueeze(1).to_broadcast([128, 2, 128]), ALU.mult, [KVP, GKN], [TF])
            cp(P, "act", vtt[:, 2 * n:2 * n + 2, :], kv3[:, :, 128:256], [KVP], [VT])
        tt(P, "dve", ktt[:, :, 0:128], tf[:, :, 0:128], s8[:, :].unsqueeze(2).to_broadcast([128, 8, 128]), ALU.mult,
           [TF, S8], [KT])
        p8, P8 = pe8.next()
        tt(P, "dve", p8[:], kp[:, :].unsqueeze(1).to_broadcast([128, 8, 64]),
           s8[:, :].unsqueeze(2).to_broadcast([128, 8, 64]), ALU.mult, [KP, S8], [P8])
        rope_ops(P, p8[:], P8, ktt[:, :, 128:192], KT, cosb, sinb, CS, rtmp, RTMP, 8)
        P.dma("pool", v_d[rows, :], vtt[:].rearrange("p h d -> p (h d)"), reads=[VT])

        def emit_T(src, SRC, dst_d):
            P.dma("pool", dst_d[rows, :], src[:].rearrange("p h d -> p (h d)"), reads=[SRC])

        if dbg != 1:
            emit_T(ktt, KT, kT_d)
        if dbg in (1, 2):
            continue
        ss3, SS3 = ssr.next()
        rms_rstd(P, qa[:, 0:384], 384, ss3[:], scr[:, 0:384], [QA], SS3, SCR)
        qn, QN = cn.next()
        ts(P, "dve", qn[:], qa[:, 0:384], ss3[:, 0:1], None, ALU.mult, None, [QA, SS3], [QN])
        pt3, PB3 = pT.next()
        for c in range(3):
            tr(P, pt3[:, c, :], qn[:, c * 128:(c + 1) * 128], ident[:], [QN, IDB], [PB3])
        qT_, QT = cT.next()
        cp(P, "act", qT_[:], pt3[:, 0:3, :], [PB3], [QT])
        qb = []
        for n in range(4):
            qp, QP = pf.next()
            qb.append((qp, QP))
            for c in range(3):
                mm(P, qp[:, 0:384], qT_[:, c, :], wqb[:, c, n * 384:(n + 1) * 384], c == 0, c == 2, [QT, WQB], [QP])
        s8q, S8Q = ss8.next()
        for h in range(8):
            qp, QP = qb[h // 2]
            act(P, scr[:, 0:192], qp[:, (h % 2) * 192:(h % 2) * 192 + 192], AF.Square, [QP], [SCR, S8Q],
                accum_out=s8q[:, h:h + 1])
        act(P, s8q[:], s8q[:], AF.Ln, [S8Q], [S8Q], scale=1.0 / 192, bias=1e-6)
        act(P, s8q[:], s8q[:], AF.Exp, [S8Q], [S8Q], scale=-0.5)
        if dbg == 3:
            continue
        tq, TQ = tmpf.next()
        for n in range(4):
            qp, QP = qb[n]
            cp(P, "act", tq[:, 2 * n:2 * n + 2, :].rearrange("p h c -> p (h c)"), qp[:, 0:384], [QP], [TQ])
        tq2, TQ2 = tmpf.next()
        tt(P, "dve", tq2[:], tq[:], gqn[:, :].unsqueeze(1).to_broadcast([128, 8, 192]), ALU.mult, [TQ, GQN], [TQ2])
        tq, TQ = tq2, TQ2
        if dbg == 4:
            continue
        qtt, QTT = kt.next()
        tt(P, "dve", qtt[:, :, 0:128], tq[:, :, 0:128], s8q[:, :].unsqueeze(2).to_broadcast([128, 8, 128]), ALU.mult,
           [TQ, S8Q], [QTT])
        p8q, P8Q = pe8.next()
        tt(P, "dve", p8q[:], tq[:, :, 128:192], s8q[:, :].unsqueeze(2).to_broadcast([128, 8, 64]), ALU.mult,
           [TQ, S8Q], [P8Q])
        rope_ops(P, p8q[:], P8Q, qtt[:, :, 128:192], QTT, cosb, sinb, CS, rtmp, RTMP, 8)
        emit_T(qtt, QTT, qT_d)
    return P.finish()


def run_pipeline(P, steps, s_r, p_r, scale, look=2):
    n = len(steps)
    Sb = [None] * n

    def do_qk(i):
        S, SB_ = s_r.next()
        Sb[i] = (S, SB_)
        steps[i][0](S, SB_)

    for i in range(min(look, n)):
        do_qk(i)
    for i in range(n):
        if i + look < n:
            do_qk(i + look)
        S, SB_ = Sb[i]
        p, PB = p_r.next()
        act(P, p[:], S[:], AF.Exp, [SB_], [PB], scale=scale)
        steps[i][1](p, PB)
        if steps[i][2] is not None:
            steps[i][2]()


TT = 8192


def build_mla():
    P = Prog()
    kTa_d = P.dram("kTa", [128, 2, TT], BF16, "ExternalInput").ap()
    kTb_d = P.dram("kTb", [128, 2, TT], BF16, "ExternalInput").ap()
    qTa_d = P.dram("qTa", [128, 2, TT], BF16, "ExternalInput").ap()
    qTb_d = P.dram("qTb", [128, 2, TT], BF16, "ExternalInput").ap()
    v_d = P.dram("vaug", [128, 64, 2, 132], BF16, "ExternalInput").ap()
    m_d = P.dram("masks", [128, 4, 512], BF16, "ExternalInput").ap()
    o_d = P.dram("o", [TT, 256], F32, "ExternalOutput").ap()
    kTa = P.sb([128, 2, TT], BF16, "kTa"); KA = [Buf("ka0"), Buf("ka1")]
    kTb = P.sb([128, 2, TT], BF16, "kTb"); KB = [Buf("kb0"), Buf("kb1")]
    va = P.sb([128, 64, 2, 132], BF16, "va"); VA = Buf("va")
    for h in range(2):
        P.dma("sp", kTa[:, h, :], kTa_d[:, h, :], writes=[KA[h]])
        P.dma("sp", kTb[:, h, :], kTb_d[:, h, :], writes=[KB[h]])
    for c0 in range(0, 64, 16):
        P.dma("sp", va[:, c0:c0 + 16], v_d[:, c0:c0 + 16], writes=[VA])
    mk, MK = load_const(P, m_d, [128, 4, 512], BF16, "mk")
    qa_r = Rot(P, 2, [128, 512], BF16, "qa")
    qb_r = Rot(P, 2, [128, 512], BF16, "qb")
    s_r = Rot(P, 3, [128, 512], F32, "S", psum=True)
    acc_r = Rot(P, 4, [128, 512], F32, "acc", psum=True)
    p_r = Rot(P, 4, [128, 512], BF16, "p")
    pm_r = Rot(P, 3, [128, 512], BF16, "pm")
    rs_r = Rot(P, 4, [128, 1], F32, "rs")
    o_r = Rot(P, 2, [128, 4, 128], F32, "osb")
    scale = 192.0 ** -0.5
    steps = []
    for h in range(2):
        for qs in range(TT // 512):
            ctx = {}

            def mk_qk(h=h, qs=qs, ctx=ctx, kc=0):
                def f(S, SB_):
                    if kc == 0:
                        qa, QA = qa_r.next()
                        qb, QB = qb_r.next()
                        P.dma("sp", qa[:], qTa_d[:, h, qs * 512:(qs + 1) * 512], writes=[QA])
                        P.dma("sp", qb[:], qTb_d[:, h, qs * 512:(qs + 1) * 512], writes=[QB])
                        ctx["q"] = (qa, QA, qb, QB)
                        ctx["accs"] = [acc_r.next() for _ in range(4)]
                    qa, QA, qb, QB = ctx["q"]
                    mm(P, S[:], kTa[:, h, kc * 128:(kc + 1) * 128], qa[:], True, False, [KA[h], QA], [SB_])
                    mm(P, S[:], kTb[:, h, kc * 128:(kc + 1) * 128], qb[:], False, True, [KB[h], QB], [SB_])
                return f

            def mk_pv(h=h, qs=qs, ctx=ctx, kc=0):
                def f(p, PB):
                    j = kc - 4 * qs
                    if j >= 0:
                        pm, PM = pm_r.next()
                        tt(P, "dve", pm[:], p[:], mk[:, j, :], ALU.mult, [PB, MK], [PM])
                        p, PB = pm, PM
                    for t in range(4):
                        if j >= 0 and t < j:
                            continue
                        a, AB = ctx["accs"][t]
                        mm(P, a[:, 0:129], p[:, t * 128:(t + 1) * 128], va[:, kc, h, 0:129], kc == 0,
                           kc == 4 * qs + t, [PB, VA], [AB])
                return f

            def mk_after(h=h, qs=qs, ctx=ctx):
                def f():
                    osb, OB = o_r.next()
                    for t in range(4):
                        a, AB = ctx["accs"][t]
                        rs, RS = rs_r.next()
                        P.op("dve", lambda e, rs=rs, a=a: e.reciprocal(out=rs[:], in_=a[:, 128:129]), reads=[AB],
                             writes=[RS])
                        ts(P, "dve", osb[:, t, :], a[:, 0:128], rs[:, 0:1], None, ALU.mult, None, [AB, RS], [OB])
                    P.dma("sp", o_d[qs * 512:(qs + 1) * 512, h * 128:(h + 1) * 128].rearrange("(t p) d -> p t d", p=128),
                          osb[:], reads=[OB])
                return f

            nk = 4 * qs + 4
            for kc in range(nk):
                steps.append((mk_qk(kc=kc), mk_pv(kc=kc), mk_after() if kc == nk - 1 else None))
    run_pipeline(P, steps, s_r, p_r, scale)
    return P.finish()


def mla_host_inputs(k, q, v, b, hp):
    ks = k[b * TT:(b + 1) * TT].reshape(TT, 8, 192)[:, 2 * hp:2 * hp + 2, :]
    qs = q[b * TT:(b + 1) * TT].reshape(TT, 8, 192)[:, 2 * hp:2 * hp + 2, :]
    vs = v[b * TT:(b + 1) * TT].reshape(TT, 8, 128)[:, 2 * hp:2 * hp + 2, :]
    kT = np.ascontiguousarray(ks.transpose(2, 1, 0))
    qT = np.ascontiguousarray(qs.transpose(2, 1, 0))
    vaug = np.zeros((128, 64, 2, 132), dtype=NPBF)
    vaug[:, :, :, 0:128] = vs.reshape(64, 128, 2, 128).transpose(1, 0, 2, 3)
    vaug[:, :, :, 128] = 1.0
    kl = np.arange(128)[:, None, None]
    j = np.arange(4)[None, :, None]
    ql = np.arange(512)[None, None, :]
    masks = (j * 128 + kl <= ql).astype(np.float32).astype(NPBF)
    kTb = np.zeros((128, 2, TT), NPBF)
    kTb[0:64] = kT[128:192]
    qTb = np.zeros((128, 2, TT), NPBF)
    qTb[0:64] = qT[128:192]
    return {"kTa": np.ascontiguousarray(kT[0:128]), "kTb": kTb,
            "qTa": np.ascontiguousarray(qT[0:128]), "qTb": qTb,
            "vaug": vaug, "masks": np.ascontiguousarray(masks)}


NW = 2608


def build_nsa_a():
    P = Prog()
    x_d = P.dram("x", [NT, 1024], F32, "ExternalInput").ap()
    g_d = P.dram("g", [128, 8], F32, "ExternalInput").ap()
    w_d = P.dram("w", [1024, NW], F32, "ExternalInput").ap()
    gq_d = P.dram("gq", [128, 64], F32, "ExternalInput").ap()
    gs_d = P.dram("gs", [128, 64], F32, "ExternalInput").ap()
    gw_d = P.dram("gw", [128, 64], F32, "ExternalInput").ap()
    id_d = P.dram("ident", [128, 128], BF16, "ExternalInput").ap()
    big_d = P.dram("big", [NT, 2560], BF16, "ExternalOutput").ap()
    gates_d = P.dram("gates", [NT, 48], F32, "ExternalOutput").ap()
    ident, IDB = make_ident(P, id_d)
    gain = load_const(P, g_d, [128, 8], F32, "gain")
    gq, GQ = load_const(P, gq_d, [128, 64], F32, "gq")
    gs, GS = load_const(P, gs_d, [128, 64], F32, "gs")
    gw, GW = load_const(P, gw_d, [128, 64], F32, "gw")
    stg = Rot(P, 2, [128, 2048], F32, "stg")
    w, WB = load_weight(P, w_d, 1024, NW, "w", stg, gain=gain)
    xin = Rot(P, 2, [128, 1024], F32, "xin")
    scr = P.sb([128, 1536], F32, "scr"); SCR = Buf("scr")
    ssr = Rot(P, 2, [128, 1], F32, "ss")
    xn = Rot(P, 2, [128, 1024], BF16, "xn")
    pT = Rot(P, 1, [128, 8, 128], BF16, "pT", psum=True)
    xT = Rot(P, 2, [128, 8, 128], BF16, "xT")
    pf = Rot(P, 6, [128, 512], F32, "pf", psum=True)
    y_r = Rot(P, 2, [128, NW], F32, "y")
    ssn_r = Rot(P, 2, [128, 24], F32, "ssn")
    tmp_r = Rot(P, 2, [128, 1024], F32, "tmp")
    big_r = Rot(P, 2, [128, 2560], BF16, "bigs")
    gt_r = Rot(P, 2, [128, 48], F32, "gts")
    for b in range(NB):
        rows = slice(b * 128, (b + 1) * 128)
        xt, XB = xin.next()
        P.dma("sp", xt[:], x_d[rows, :], writes=[XB])
        ss, SS = ssr.next()
        rms_rstd(P, xt[:], 1024, ss[:], scr[:, 0:1024], [XB], SS, SCR)
        xnt, XN = xn.next()
        ts(P, "dve", xnt[:], xt[:], ss[:, 0:1], None, ALU.mult, None, [XB, SS], [XN])
        pt, PB = pT.next()
        for c in range(8):
            tr(P, pt[:, c, :], xnt[:, c * 128:(c + 1) * 128], ident[:], [XN, IDB], [PB])
        xTt, XT = xT.next()
        cp(P, "dve", xTt[:], pt[:], [PB], [XT])
        y, YB = y_r.next()
        for n in range(6):
            c0, c1 = n * 512, min(NW, (n + 1) * 512)
            pp, PP = pf.next()
            for c in range(8):
                mm(P, pp[:, 0:c1 - c0], xTt[:, c, :], w[:, c, c0:c1], c == 0, c == 7, [XT, WB], [PP])
            cp(P, "act", y[:, c0:c1], pp[:, 0:c1 - c0], [PP], [YB])
        act(P, scr[:, 0:1024], y[:, 0:1024], AF.Square, [YB], [SCR])
        act(P, scr[:, 1024:1280], y[:, 1536:1792], AF.Square, [YB], [SCR])
        act(P, scr[:, 1280:1536], y[:, 2048:2304], AF.Square, [YB], [SCR])
        ssn, SSN = ssn_r.next()
        P.op("dve", lambda e, ssn=ssn: e.tensor_reduce(out=ssn[:], in_=scr[:, 0:1536].rearrange("p (h d) -> p h d", d=64),
                                                      axis=AX.X, op=ALU.add), reads=[SCR], writes=[SSN])
        act(P, ssn[:], ssn[:], AF.Ln, [SSN], [SSN], scale=1.0 / 64, bias=1e-6)
        act(P, ssn[:], ssn[:], AF.Exp, [SSN], [SSN], scale=-0.5)
        bg, BG = big_r.next()
        tmp, TMP = tmp_r.next()

        def normed(src0, nh, h0, gt, GT, dst0):
            tv = tmp[:, 0:nh * 64].rearrange("p (h d) -> p h d", d=64)
            tt(P, "dve", tv, y[:, src0:src0 + nh * 64].rearrange("p (h d) -> p h d", d=64),
               ssn[:, h0:h0 + nh].unsqueeze(2).to_broadcast([128, nh, 64]), ALU.mult, [YB, SSN], [TMP])
            tt(P, "dve", bg[:, dst0:dst0 + nh * 64].rearrange("p (h d) -> p h d", d=64), tv,
               gt[:, :].unsqueeze(1).to_broadcast([128, nh, 64]), ALU.mult, [TMP, GT], [BG])

        normed(0, 16, 0, gq, GQ, 0)
        normed(1536, 4, 16, gs, GS, 1536)
        normed(2048, 4, 20, gw, GW, 2048)
        cp(P, "dve", bg[:, 1024:1536], y[:, 1024:1536], [YB], [BG])
        cp(P, "dve", bg[:, 1792:2048], y[:, 1792:2048], [YB], [BG])
        cp(P, "dve", bg[:, 2304:2560], y[:, 2304:2560], [YB], [BG])
        gt_, GT_ = gt_r.next()
        act(P, gt_[:], y[:, 2560:2608], AF.Sigmoid, [YB], [GT_])
        P.dma("pool", big_d[rows, :], bg[:], reads=[BG])
        P.dma("pool", gates_d[rows, :], gt_[:], reads=[GT_])
    return P.finish()


def build_nsa_b():
    P = Prog()
    ins = {}
    for nm in ("k", "v"):
        ins[nm] = dict(
            aT=P.dram(nm + "_a2T", [128, TT], BF16, "ExternalInput").ap(),
            w1=P.dram(nm + "_w1", [128, 16, 256], F32, "ExternalInput").ap(),
            pe=P.dram(nm + "_pe", [128, 16], F32, "ExternalInput").ap(),
            b1=P.dram(nm + "_b1", [128, 2], F32, "ExternalInput").ap(),
            w2=P.dram(nm + "_w2", [128, 2, 64], F32, "ExternalInput").ap(),
            out=P.dram(nm + "_cmp", [512, 64], BF16, "ExternalOutput").ap(),
        )
    gk_d = P.dram("gk", [128, 64], F32, "ExternalInput").ap()
    gk, GK = load_const(P, gk_d, [128, 64], F32, "gk")
    ps_r = Rot(P, 4, [128, 512], F32, "ps", psum=True)
    f_r = Rot(P, 6, [128, 512], F32, "f")
    for nm in ("k", "v"):
        I = ins[nm]
        aT, AT = load_const(P, I["aT"], [128, TT], BF16, nm + "aT")
        w1f, W1F = load_const(P, I["w1"], [128, 16, 256], F32, nm + "w1f")
        pef, PEF = load_const(P, I["pe"], [128, 16], F32, nm + "pef")
        b1, B1 = load_const(P, I["b1"], [128, 2], F32, nm + "b1")
        w2f, W2F = load_const(P, I["w2"], [128, 2, 64], F32, nm + "w2f")
        w1 = P.sb([128, 16, 256], BF16, nm + "w1"); W1 = Buf(nm + "w1")
        cp(P, "dve", w1[:], w1f[:], [W1F], [W1])
        peb = P.sb([128, 16], BF16, nm + "peb"); PEB = Buf(nm + "peb")
        cp(P, "dve", peb[:], pef[:], [PEF], [PEB])
        w2 = P.sb([128, 2, 64], BF16, nm + "w2"); W2 = Buf(nm + "w2")
        cp(P, "dve", w2[:], w2f[:], [W2F], [W2])
        av = aT[:, :].rearrange("p (j s) -> p j s", s=16)
        gT = P.sb([128, 2, 512], BF16, nm + "gT"); GT = Buf(nm + "gT")
        memset(P, "dve", gT[:], 0.0, [GT])
        for half in range(2):
            hp, HP = ps_r.next()
            cv, CV = ps_r.next()
            for m in range(16):
                rhs = av[:, 0:511, 2 * m] if m < 8 else av[:, 1:512, 2 * m - 16]
                mm(P, hp[:, 0:511], w1[:, m, half * 128:(half + 1) * 128], rhs, m == 0, m == 15, [W1, AT], [HP])
            for m in range(16):
                mm(P, cv[:, 0:1], w1[:, m, half * 128:(half + 1) * 128], peb[:, m:m + 1], m == 0, m == 15, [W1, PEB], [CV])
            bias, BI = f_r.next()
            tt(P, "dve", bias[:, 0:1], cv[:, 0:1], b1[:, half:half + 1], ALU.add, [CV, B1], [BI])
            u, U = f_r.next()
            act(P, u[:, 0:511], hp[:, 0:511], AF.Identity, [HP, BI], [U], bias=bias[:, 0:1])
            t1, T1 = f_r.next()
            tt(P, "dve", t1[:, 0:511], u[:, 0:511], u[:, 0:511], ALU.mult, [U], [T1])
            t2, T2 = f_r.next()
            ts(P, "dve", t2[:, 0:511], t1[:, 0:511], 0.044715, 1.0, ALU.mult, ALU.add, [T1], [T2])
            t3, T3 = f_r.next()
            tt(P, "dve", t3[:, 0:511], t2[:, 0:511], u[:, 0:511], ALU.mult, [T2, U], [T3])
            t4, T4 = f_r.next()
            act(P, t4[:, 0:511], t3[:, 0:511], AF.Tanh, [T3], [T4], scale=0.7978845608028654)
            t5, T5 = f_r.next()
            ts(P, "dve", t5[:, 0:511], t4[:, 0:511], 1.0, 0.5, ALU.add, ALU.mult, [T4], [T5])
            tt(P, "dve", gT[:, half, 0:511], t5[:, 0:511], u[:, 0:511], ALU.mult, [T5, U], [GT])
        for ch in range(4):
            op_, OP = ps_r.next()
            for half in range(2):
                mm(P, op_[:, 0:64], gT[:, half, ch * 128:(ch + 1) * 128], w2[:, half, :], half == 0, half == 1,
                   [GT, W2], [OP])
            ob = P.sb([128, 64], BF16, f"{nm}ob{ch}"); OB = Buf("ob")
            if nm == "k":
                sq, SQ = f_r.next()
                ssk = P.sb([128, 1], F32, f"ssk{ch}"); SSK = Buf("ssk")
                rms_rstd(P, op_[:, 0:64], 64, ssk[:], sq[:, 0:64], [OP], SSK, SQ)
                kn, KN = f_r.next()
                ts(P, "dve", kn[:, 0:64], op_[:, 0:64], ssk[:, 0:1], None, ALU.mult, None, [OP, SSK], [KN])
                tt(P, "dve", ob[:], kn[:, 0:64], gk[:], ALU.mult, [KN, GK], [OB])
            else:
                cp(P, "act", ob[:], op_[:, 0:64], [OP], [OB])
            P.dma("sp", I["out"][ch * 128:(ch + 1) * 128, :], ob[:], reads=[OB])
    return P.finish()


NQB = TT // 128


def build_nsa_c():
    P = Prog()
    QT_d = P.dram("QT", [70, 4, TT], BF16, "ExternalInput").ap()
    KS_d = P.dram("KS", [70, TT], BF16, "ExternalInput").ap()
    KW_d = P.dram("KW", [70, TT], BF16, "ExternalInput").ap()
    KC_d = P.dram("KC", [70, 512], BF16, "ExternalInput").ap()
    VS_d = P.dram("VS", [128, 64, 66], BF16, "ExternalInput").ap()
    VW_d = P.dram("VW", [128, 64, 66], BF16, "ExternalInput").ap()
    VC_d = P.dram("VC", [128, 4, 194], BF16, "ExternalInput").ap()
    G_d = P.dram("G", [128, 64, 12], F32, "ExternalInput").ap()
    E_d = P.dram("E", [128, TT], BF16, "ExternalInput").ap()
    CM_d = P.dram("CM", [128, 17, 512], BF16, "ExternalInput").ap()
    CA_d = P.dram("CA", [128, 512], BF16, "ExternalInput").ap()
    WM_d = P.dram("WM", [128, 512], BF16, "ExternalInput").ap()
    AT_d = P.dram("AT", [128, 256], F32, "ExternalInput").ap()
    id_d = P.dram("ident", [128, 128], BF16, "ExternalInput").ap()
    o_d = P.dram("o", [TT, 256], F32, "ExternalOutput").ap()
    ident, IDB = make_ident(P, id_d)
    QT = P.sb([70, 4, TT], BF16, "QT"); QTB = Buf("QT")
    for h in range(4):
        P.dma("sp", QT[:, h, :], QT_d[:, h, :], writes=[QTB])
    KS, KSB = load_const(P, KS_d, [70, TT], BF16, "KS")
    KW, KWB = load_const(P, KW_d, [70, TT], BF16, "KW")
    KC, KCB = load_const(P, KC_d, [70, 512], BF16, "KC")
    VS, VSB = load_const(P, VS_d, [128, 64, 66], BF16, "VS")
    VW, VWB = load_const(P, VW_d, [128, 64, 66], BF16, "VW")
    VC, VCB = load_const(P, VC_d, [128, 4, 194], BF16, "VC")
    G, GB = load_const(P, G_d, [128, 64, 12], F32, "G")
    E, EB = load_const(P, E_d, [128, TT], BF16, "E")
    CM, CMB = load_const(P, CM_d, [128, 17, 512], BF16, "CM")
    CA, CAB = load_const(P, CA_d, [128, 512], BF16, "CA")
    WM, WMB = load_const(P, WM_d, [128, 512], BF16, "WM")
    AT, ATB = load_const(P, AT_d, [128, 256], F32, "AT")
    s_r = Rot(P, 3, [128, 512], F32, "S", psum=True)
    accC_r = Rot(P, 2, [128, 512], F32, "accC", psum=True)
    accS_r = Rot(P, 1, [128, 512], F32, "accS", psum=True)
    accW_r = Rot(P, 1, [128, 512], F32, "accW", psum=True)
    pst_r = Rot(P, 1, [128, 128], BF16, "pst", psum=True)
    p_r = Rot(P, 4, [128, 512], BF16, "p")
    sm_r = Rot(P, 48, [128, 1], F32, "sm")
    imp_r = Rot(P, 4, [128, 128], F32, "imp")
    sc_r = Rot(P, 4, [128, 128], F32, "sc")
    m8_r = Rot(P, 4, [128, 8], F32, "m8")
    ns_r = Rot(P, 2, [128, 128], BF16, "ns")
    nt_r = Rot(P, 3, [128, 4, 128], BF16, "nt")
    o_r = Rot(P, 8, [128, 256], F32, "ot")
    scale = 0.125

    def qk(S, SB_, KT, KTB, c, rhsQ, extra):
        mm(P, S[:], KT[:, c * 128:(c + 1) * 128], rhsQ, True, extra is None, [KTB, QTB], [SB_])
        if extra is not None:
            lhsT, rhs, rd = extra
            mm(P, S[:], lhsT, rhs, False, True, rd, [SB_])

    def rq(qb):
        return QT[:, :, qb * 128:(qb + 1) * 128]

    st = {}

    def finalize(qb, a, AB, width, col, first):
        nxt, NXT = o_r.next()
        cur = st.get(("o", qb))
        rcs = []
        for h in range(4):
            off = (h % 2) * 193 if width == 193 else h * 65
            aa, AAB = (a[h // 2], AB[h // 2]) if width == 193 else (a, AB)
            sm, SM = sm_r.next()
            ts(P, "dve", sm[:], aa[:, off + 64:off + 65], 1e-30, None, ALU.max, None, [AAB], [SM])
            rc, RC = sm_r.next()
            P.op("dve", lambda e, rc=rc, sm=sm: e.reciprocal(out=rc[:], in_=sm[:]), reads=[SM], writes=[RC])
            gf, GF = sm_r.next()
            tt(P, "dve", gf[:], rc[:], G[:, qb, h * 3 + col:h * 3 + col + 1], ALU.mult, [RC, GB], [GF])
            if cur is None:
                ts(P, "dve", nxt[:, h * 64:(h + 1) * 64], aa[:, off:off + 64], gf[:, 0:1], None, ALU.mult, None,
                   [AAB, GF], [NXT])
            else:
                c_t, C_B = cur
                P.op("dve", lambda e, nxt=nxt, aa=aa, off=off, h=h, gf=gf, c_t=c_t: e.scalar_tensor_tensor(
                    out=nxt[:, h * 64:(h + 1) * 64], in0=aa[:, off:off + 64], scalar=gf[:, 0:1],
                    in1=c_t[:, h * 64:(h + 1) * 64], op0=ALU.mult, op1=ALU.add), reads=[AAB, GF, C_B], writes=[NXT])
            rcs.append((rc, RC))
        st[("o", qb)] = (nxt, NXT)
        return rcs

    def cmp_steps(qb):
        rhsQ = rq(qb)
        ncc = qb // 16 + 1
        ctx = {}

        def mk_qk(cc):
            def f(S, SB_):
                if cc == 0:
                    ctx["acc"] = [accC_r.next(), accC_r.next()]
                extra = None
                if qb - 16 * cc <= 16:
                    extra = (ident[:], CM[:, qb - 16 * cc, :], [IDB, CMB])
                qk(S, SB_, KC, KCB, cc, rhsQ, extra)
            return f

        def mk_pv(cc):
            def f(p, PB):
                for h in range(4):
                    a, AB = ctx["acc"][h // 2]
                    off = (h % 2) * 193
                    mm(P, a[:, off:off + 193], p[:, h * 128:(h + 1) * 128], VC[:, cc, 0:193],
                       cc == 0 and h % 2 == 0, cc == ncc - 1, [PB, VCB], [AB], skip=True)
            return f

        def after():
            acc = ctx["acc"]
            rcs = finalize(qb, [acc[0][0], acc[1][0]], [acc[0][1], acc[1][1]], 193, 0, True)
            imp, IMP = None, None
            for h in range(4):
                a, AB = acc[h // 2]
                off = (h % 2) * 193
                rc, RC = rcs[h]
                ni, NI = imp_r.next()
                if imp is None:
                    ts(P, "dve", ni[:], a[:, off + 65:off + 193], rc[:, 0:1], None, ALU.mult, None, [AB, RC], [NI])
                else:
                    P.op("dve", lambda e, ni=ni, a=a, off=off, rc=rc, imp=imp: e.scalar_tensor_tensor(
                        out=ni[:], in0=a[:, off + 65:off + 193], scalar=rc[:, 0:1], in1=imp[:], op0=ALU.mult,
                        op1=ALU.add), reads=[AB, RC, IMP], writes=[NI])
                imp, IMP = ni, NI
            sc, SC = sc_r.next()
            tt(P, "dve", sc[:], imp[:], AT[:, 126 - 2 * qb:254 - 2 * qb], ALU.add, [IMP, ATB], [SC])
            ts(P, "dve", sc[:, 0:1], imp[:, 0:1], 1e4, None, ALU.add, None, [IMP, SC], [SC])
            m1, M1 = m8_r.next()
            P.op("dve", lambda e, m1=m1, sc=sc: e.max(out=m1[:], in_=sc[:]), reads=[SC], writes=[M1])
            sc2, SC2 = sc_r.next()
            P.op("dve", lambda e, sc2=sc2, m1=m1, sc=sc: e.match_replace(
                out=sc2[:], in_to_replace=m1[:], in_values=sc[:], imm_value=-1e9), reads=[SC, M1], writes=[SC2])
            m2, M2 = m8_r.next()
            P.op("dve", lambda e, m2=m2, sc2=sc2: e.max(out=m2[:], in_=sc2[:]), reads=[SC2], writes=[M2])
            ns, NS = ns_r.next()
            ts(P, "dve", ns[:], sc[:], m2[:, 7:8], -30000.0, ALU.is_lt, ALU.mult, [SC, M2], [NS])
            pst, PST = pst_r.next()
            tr(P, pst[:], ns[:], ident[:], [NS, IDB], [PST])
            nt, NTB = nt_r.next()
            for h in range(4):
                cp(P, "act" if h % 2 else "dve", nt[:, h, :], pst[:], [PST], [NTB])
            st[("nt", qb)] = (nt, NTB)

        return [(mk_qk(cc), mk_pv(cc), after if cc == ncc - 1 else None) for cc in range(ncc)]

    def branch_steps(qb, which):
        rhsQ = rq(qb)
        ctx = {}
        if which == "sel":
            cs = list(range(qb + 1))
            KT, KTB, V, VB, accr, col = KS, KSB, VS, VSB, accS_r, 1
        else:
            cs = list(range(max(0, qb - 4), qb + 1))
            KT, KTB, V, VB, accr, col = KW, KWB, VW, VWB, accW_r, 2

        def mk_qk(c):
            def f(S, SB_):
                if c == cs[0]:
                    ctx["acc"] = accr.next()
                extra = None
                if c == qb:
                    extra = (ident[:], CA[:], [IDB, CAB])
                elif which == "sel":
                    nt, NTB = st[("nt", qb)]
                    extra = (E[:, c * 128:(c + 1) * 128], nt[:].rearrange("p h q -> p (h q)"), [EB, NTB])
                elif c == qb - 4:
                    extra = (ident[:], WM[:], [IDB, WMB])
                qk(S, SB_, KT, KTB, c, rhsQ, extra)
            return f

        def mk_pv(c):
            def f(p, PB):
                a, AB = ctx["acc"]
                for h in range(4):
                    mm(P, a[:, h * 65:(h + 1) * 65], p[:, h * 128:(h + 1) * 128], V[:, c, 0:65],
                       c == cs[0] and h == 0, c == qb, [PB, VB], [AB], skip=True)
            return f

        def after():
            a, AB = ctx["acc"]
            finalize(qb, a, AB, 65, col, False)
            if which == "sel":
                cur, CUR = st[("o", qb)]
                P.dma("sp", o_d[qb * 128:(qb + 1) * 128, :], cur[:], reads=[CUR])
                st.pop(("o", qb)); st.pop(("nt", qb))

        return [(mk_qk(c), mk_pv(c), after if c == qb else None) for c in cs]

    steps = list(cmp_steps(0))
    for qb in range(NQB):
        steps += branch_steps(qb, "win")
        if qb + 1 < NQB:
            steps += cmp_steps(qb + 1)
        steps += branch_steps(qb, "sel")
    run_pipeline(P, steps, s_r, p_r, scale)
    return P.finish()


_PROGS = {}


def _prog(name, fn):
    if name not in _PROGS:
        _PROGS[name] = fn()
    return _PROGS[name]


def _pl(g):
    return np.ascontiguousarray(np.asarray(g, np.float32).reshape(-1, 128).T)


def _bc(g):
    return np.ascontiguousarray(np.tile(np.asarray(g, np.float32)[None, :], (128, 1)))


def _run(nc, maps):
    res = run_bass_kernel_spmd(nc, maps, core_ids=list(range(8)))
    return res.results


def _split2(a):
    a = np.asarray(a, np.float64)
    hi = a.astype(np.float32).astype(NPBF)
    lo = (a - hi.astype(np.float64)).astype(np.float32).astype(NPBF)
    return hi, lo


def _nsa_c_consts():
    i = np.arange(128)[:, None]
    u = np.arange(TT)[None, :]
    E = (u // 64 == i).astype(np.float32).astype(NPBF)
    jl = np.arange(128)[:, None, None]
    m = np.arange(17)[None, :, None]
    ql = np.tile(np.arange(128), 4)[None, None, :]
    CM = np.where(16 * jl - ql <= 128 * m - 31, 0.0, -30000.0).astype(np.float32).astype(NPBF)
    kl = np.arange(128)[:, None]
    q4 = np.tile(np.arange(128), 4)[None, :]
    CA = np.where(kl <= q4, 0.0, -30000.0).astype(np.float32).astype(NPBF)
    WM = np.where(kl > q4, 0.0, -30000.0).astype(np.float32).astype(NPBF)
    qq = np.arange(128)[:, None]
    rel = np.arange(256)[None, :] - 126
    cur = (qq >= 64).astype(np.int64)
    AT = np.zeros((128, 256), np.float32)
    AT[(rel == cur) | (rel == cur - 1)] = 1e4
    AT[rel > cur] = -1.0
    jj = np.arange(512)[:, None]
    ii = np.arange(128)[None, :]
    Ov = ((jj >= 4 * ii - 1) & (jj <= 4 * ii + 3) & (jj <= 510)).astype(np.float32)
    return dict(E=np.ascontiguousarray(E), CM=np.ascontiguousarray(CM), CA=np.ascontiguousarray(CA),
                WM=np.ascontiguousarray(WM), AT=AT, Ov=Ov)


def _aug_k(pos):
    hi, lo = _split2(pos)
    one = np.ones_like(hi)
    return np.stack([one, one, hi, lo, hi, lo], 0)


def kernel(**inp):
    f = {k: np.asarray(v) for k, v in inp.items()}
    x = f["x"]
    B, T, D = x.shape
    xf = np.ascontiguousarray(x.reshape(B * T, D))
    ident = np.eye(128, dtype=np.float32).astype(NPBF)
    G4 = 4
    nc = _prog("nsa_a", build_nsa_a)
    maps = [{"x": xf[c * NT:(c + 1) * NT], "g": _pl(f["a_attn_norm"][0]), "w": np.ascontiguousarray(f["a_w_in"][0]),
             "gq": _bc(f["a_q_norm"][0]), "gs": _bc(f["a_kslc_norm"][0]), "gw": _bc(f["a_kwin_norm"][0]),
             "ident": ident} for c in range(8)]
    r = _run(nc, maps)
    big = np.concatenate([q["big"] for q in r], 0)
    gates = np.concatenate([q["gates"] for q in r], 0)
    nc = _prog("nsa_b", build_nsa_b)

    def a2T(col0, b):
        a = big[b * T:(b + 1) * T, col0:col0 + 64]
        o = np.zeros((128, T), NPBF)
        o[0:64] = a.T
        o[64:128, 0:T - 1] = a[1:].T
        return o

    def w1l(w):
        return np.ascontiguousarray(np.asarray(w, np.float32).reshape(16, 2, 64, 256).transpose(1, 2, 0, 3).reshape(128, 16, 256))

    def pel(p):
        return np.ascontiguousarray(np.asarray(p, np.float32).reshape(16, 2, 64).transpose(1, 2, 0).reshape(128, 16))

    maps = []
    for c in range(8):
        b, g = c // 4, c % 4
        maps.append({
            "k_a2T": a2T(1024 + g * 64, b), "v_a2T": a2T(1280 + g * 64, b),
            "k_w1": w1l(f["a_cmp_k_w1"][0]), "v_w1": w1l(f["a_cmp_v_w1"][0]),
            "k_pe": pel(f["a_cmp_pos_k"][0]), "v_pe": pel(f["a_cmp_pos_v"][0]),
            "k_b1": _pl(f["a_cmp_k_b1"][0]), "v_b1": _pl(f["a_cmp_v_b1"][0]),
            "k_w2": np.ascontiguousarray(f["a_cmp_k_w2"][0].reshape(2, 128, 64).transpose(1, 0, 2)),
            "v_w2": np.ascontiguousarray(f["a_cmp_v_w2"][0].reshape(2, 128, 64).transpose(1, 0, 2)),
            "gk": _bc(f["a_kcmp_norm"][0])})
    rB = _run(nc, maps)
    nc = _prog("nsa_c", build_nsa_c)
    C = _nsa_c_consts()
    tpos = np.arange(T, dtype=np.float64)
    kaug_tok = _aug_k(tpos)
    kaug_cmp = _aug_k(16.0 * np.arange(512) + 15.5)
    maps = []
    for c in range(8):
        b, g = c // 4, c % 4
        bb = big[b * T:(b + 1) * T]
        QT = np.zeros((70, 4, T), NPBF)
        for hg in range(4):
            h = g * 4 + hg
            QT[0:64, hg] = bb[:, h * 64:(h + 1) * 64].T
            beta = 8.0 * 2.0 ** (-8.0 * (h + 1) / 16.0)
            bh, bl = _split2(np.full(T, beta))
            th, tl = _split2(beta * tpos)
            QT[64:70, hg] = np.stack([-th.astype(np.float32), -tl.astype(np.float32), bh.astype(np.float32),
                                      bh.astype(np.float32), bl.astype(np.float32), bl.astype(np.float32)], 0).astype(NPBF)
        KS = np.concatenate([bb[:, 1536 + g * 64:1536 + (g + 1) * 64].T, kaug_tok], 0)
        KW = np.concatenate([bb[:, 2048 + g * 64:2048 + (g + 1) * 64].T, kaug_tok], 0)
        KC = np.concatenate([rB[c]["k_cmp"].T, kaug_cmp], 0)

        def vaug(v):
            o = np.zeros((128, 64, 66), NPBF)
            o[:, :, 0:64] = v.reshape(64, 128, 64).transpose(1, 0, 2)
            o[:, :, 64] = 1.0
            return o

        VC = np.zeros((128, 4, 194), NPBF)
        VC[:, :, 0:64] = rB[c]["v_cmp"].reshape(4, 128, 64).transpose(1, 0, 2)
        VC[:, :, 64] = 1.0
        VC[:, :, 65:193] = C["Ov"].reshape(4, 128, 128).transpose(1, 0, 2).astype(NPBF)
        Gt = np.ascontiguousarray(gates[b * T:(b + 1) * T, g * 12:(g + 1) * 12].reshape(64, 128, 12).transpose(1, 0, 2))
        maps.append({"QT": QT, "KS": np.ascontiguousarray(KS), "KW": np.ascontiguousarray(KW),
                     "KC": np.ascontiguousarray(KC), "VS": vaug(bb[:, 1792 + g * 64:1792 + (g + 1) * 64]),
                     "VW": vaug(bb[:, 2304 + g * 64:2304 + (g + 1) * 64]), "VC": VC, "G": Gt, "E": C["E"],
                     "CM": C["CM"], "CA": C["CA"], "WM": C["WM"], "AT": C["AT"], "ident": ident})
    rC = _run(nc, maps)
    o_nsa = np.zeros((B, T, 1024), np.float32)
    for c in range(8):
        o_nsa[c // 4, :, (c % 4) * 256:(c % 4 + 1) * 256] = rC[c]["o"]
    o_nsa = o_nsa.reshape(B * T, 1024)

    def outproj(o, xres, w):
        nc = _prog("outproj", build_outproj)
        r = _run(nc, [{"o": o[c * NT:(c + 1) * NT], "x": xres[c * NT:(c + 1) * NT], "w": np.ascontiguousarray(w),
                       "ident": ident} for c in range(8)])
        return np.concatenate([q["y"] for q in r], 0)

    def ffn(xin, g, wgu, wd):
        nc = _prog("ffn", build_ffn)
        r = _run(nc, [{"x": xin[c * NT:(c + 1) * NT], "g": _pl(g), "wgu": np.ascontiguousarray(wgu),
                       "wd": np.ascontiguousarray(wd), "ident": ident} for c in range(8)])
        return np.concatenate([q["y"] for q in r], 0)

    x1 = outproj(o_nsa, xf, f["a_w_out"][0])
    x2 = ffn(x1, f["ffn_norm"][0], f["ffn_w_gate_up"][0], f["ffn_w_down"][0])
    nc = _prog("kvq", build_kvq)
    inv = (10000.0 ** (-np.arange(0, 64, 2, dtype=np.float32) / 64)).astype(np.float32)
    maps = []
    for c in range(8):
        pos = (np.arange(NT) + (c % 4) * NT).astype(np.float32)
        ang = pos[:, None] * inv[None, :]
        cs = np.concatenate([np.cos(ang), np.sin(ang)], 1).astype(np.float32).reshape(NB, 128, 64).transpose(1, 0, 2)
        maps.append({"x": x2[c * NT:(c + 1) * NT], "g_kv": _pl(f["kv_norm"]), "g_q": _pl(f["b_attn_norm"][0]),
                     "g_c": _pl(f["kv_c_norm"]), "g_qa": _pl(f["b_q_a_norm"][0]), "g_kn": _bc(f["kv_k_norm"]),
                     "g_qn": _bc(f["b_q_norm"][0]), "w_kva": np.ascontiguousarray(f["kv_w_a"]),
                     "w_kvb": np.ascontiguousarray(f["kv_w_b"]), "w_qa": np.ascontiguousarray(f["b_w_q_a"][0]),
                     "w_qb": np.ascontiguousarray(f["b_w_q_b"][0]), "cs": np.ascontiguousarray(cs), "ident": ident})
    r = _run(nc, maps)
    kk = np.concatenate([q["k"] for q in r], 0)
    qq = np.concatenate([q["q"] for q in r], 0)
    vv = np.concatenate([q["v"] for q in r], 0)
    nc = _prog("mla", build_mla)
    r = _run(nc, [mla_host_inputs(kk, qq, vv, c // 4, c % 4) for c in range(8)])
    o_mla = np.zeros((B, T, 1024), np.float32)
    for c in range(8):
        o_mla[c // 4, :, (c % 4) * 256:(c % 4 + 1) * 256] = r[c]["o"]
    o_mla = o_mla.reshape(B * T, 1024)
    x3 = outproj(o_mla, x2, f["b_w_out"][0])
    x4 = ffn(x3, f["ffn_norm"][1], f["ffn_w_gate_up"][1], f["ffn_w_down"][1])
    return x4.reshape(B, T, D).astype(np.float32)
```

```python
import numpy as np
import ml_dtypes
import concourse.bass as bass
import concourse.mybir as mybir
from concourse.bass_utils import run_bass_kernel_spmd

F32 = mybir.dt.float32
BF16 = mybir.dt.bfloat16
AF = mybir.ActivationFunctionType
ALU = mybir.AluOpType
AX = mybir.AxisListType
NPBF = ml_dtypes.bfloat16

ENGS = ["sp", "act", "dve", "pool", "pe"]


class Buf:
    __slots__ = ("w", "r", "dsem", "dcnt", "name")

    def __init__(self, name=""):
        self.w = None
        self.r = []
        self.dsem = None
        self.dcnt = 0
        self.name = name


class Prog:
    def __init__(self):
        self.nc = bass.Bass("TRN2", target_bir_lowering=False)
        nc = self.nc
        self.q = {e: [] for e in ENGS}
        self.cnt = {e: 0 for e in ENGS}
        self.sems = {}
        self.sem = {}
        for e in ("act", "dve", "pool", "pe"):
            self.sem[e] = self.new_sem("eng_" + e)
        self.wm = {e: {} for e in ENGS}
        self.all_dma_tokens = {}
        self.nsb = 0
        self.nps = 0

    def new_sem(self, name):
        h = self.nc.alloc_semaphore(name)
        sid = len(self.sems)
        self.sems[sid] = h
        return sid

    def sb(self, shape, dtype, name=None):
        self.nsb += 1
        return self.nc.alloc_sbuf_tensor(f"S{self.nsb}_" + (name or ""), list(shape), dtype)

    def ps(self, shape, dtype=F32, name=None):
        self.nps += 1
        return self.nc.alloc_psum_tensor(f"P{self.nps}_" + (name or ""), list(shape), dtype)

    def dram(self, name, shape, dtype, kind):
        return self.nc.dram_tensor(name, list(shape), dtype, kind=kind)

    def _deps(self, eng, reads, writes):
        deps = {}

        def need(tok):
            if tok is None:
                return
            s, v = tok
            if eng == "pe" and s == self.sem["pe"]:
                return
            if deps.get(s, 0) < v:
                deps[s] = v

        for b in reads:
            need(b.w)
        for b in writes:
            need(b.w)
            for t in b.r:
                need(t)
        out = []
        wm = self.wm[eng]
        for s, v in deps.items():
            if wm.get(s, 0) < v:
                wm[s] = v
                out.append((s, v))
        return out

    def op(self, eng, fn, reads=(), writes=()):
        waits = self._deps(eng, reads, writes)
        self.cnt[eng] += 1
        tok = (self.sem[eng], self.cnt[eng])
        self.q[eng].append((waits, fn, (tok[0], 1)))
        for b in reads:
            b.r.append(tok)
        for b in writes:
            b.w = tok
            b.r = []
        return tok

    def dma(self, q, out, in_, reads=(), writes=(), **kw):
        waits = self._deps(q, reads, writes)
        tb = (list(writes) + list(reads))[0]
        if tb.dsem is None:
            tb.dsem = self.new_sem("d_" + tb.name + str(len(self.sems)))
        tb.dcnt += 1
        tok = (tb.dsem, 16 * tb.dcnt)
        self.all_dma_tokens[tb.dsem] = tok[1]
        self.q[q].append((waits, lambda e: e.dma_start(out=out, in_=in_, **kw), (tok[0], 16)))
        for b in reads:
            b.r.append(tok)
        for b in writes:
            b.w = tok
            b.r = []
        return tok

    def cc(self, kind, ins_ap, outs_ap, groups, reads=(), writes=()):
        waits = self._deps("pool", reads, writes)
        sem = self.new_sem("cc" + str(len(self.sems)))
        tok = (sem, 16)
        self.all_dma_tokens[sem] = 16
        self.q["pool"].append((waits, lambda e: e.collective_compute(
            kind, ALU.bypass, replica_groups=groups, ins=[ins_ap], outs=[outs_ap]), (sem, 16)))
        for b in reads:
            b.r.append(tok)
        for b in writes:
            b.w = tok
            b.r = []
        return tok

    def finish(self):
        nc = self.nc
        final = [(s, v) for s, v in self.all_dma_tokens.items()]
        for e in ("act", "dve", "pool", "pe"):
            if self.cnt[e] > 0:
                final.append((self.sem[e], self.cnt[e]))
        self.q["sp"].append((final, None, None))
        emap = {"sp": "sync", "act": "scalar", "dve": "vector", "pool": "gpsimd", "pe": "tensor"}
        sems = self.sems

        def replay(lst):
            def run(e):
                for waits, fn, inc in lst:
                    for s, v in waits:
                        e.wait_ge(sems[s], v)
                    if fn is not None:
                        ins = fn(e)
                        ins.then_inc(sems[inc[0]], inc[1])
            return run

        with nc.Block() as block:
            for k in ENGS:
                if self.q[k]:
                    getattr(block, emap[k])(replay(self.q[k]))
        return nc


def mm(P, out, lhsT, rhs, start, stop, reads, writes, skip=False):
    return P.op("pe", lambda e: e.matmul(out, lhsT=lhsT, rhs=rhs, start=start, stop=stop,
                                         skip_group_check=skip), reads=reads, writes=writes)


def tr(P, out, in_, ident, reads, writes):
    return P.op("pe", lambda e: e.transpose(out, in_, ident), reads=reads, writes=writes)


def act(P, out, in_, func, reads, writes, **kw):
    return P.op("act", lambda e: e.activation(out=out, in_=in_, func=func, **kw), reads=reads, writes=writes)


def tt(P, eng, out, in0, in1, op, reads, writes):
    return P.op(eng, lambda e: e.tensor_tensor(out=out, in0=in0, in1=in1, op=op), reads=reads, writes=writes)


def ts(P, eng, out, in0, s1, s2, op0, op1, reads, writes):
    if op1 is None:
        return P.op(eng, lambda e: e.tensor_scalar(out=out, in0=in0, scalar1=s1, scalar2=None, op0=op0),
                    reads=reads, writes=writes)
    return P.op(eng, lambda e: e.tensor_scalar(out=out, in0=in0, scalar1=s1, scalar2=s2, op0=op0, op1=op1),
                reads=reads, writes=writes)


def cp(P, eng, out, in_, reads, writes):
    if eng == "act":
        return P.op("act", lambda e: e.activation(out=out, in_=in_, func=AF.Copy), reads=reads, writes=writes)
    return P.op(eng, lambda e: e.tensor_copy(out=out, in_=in_), reads=reads, writes=writes)


def memset(P, eng, ap, val, writes):
    return P.op(eng, lambda e: e.memset(ap, val), writes=writes)


class Rot:
    def __init__(self, P, n, shape, dtype, name, psum=False):
        self.items = []
        for i in range(n):
            t = P.ps(shape, dtype, f"{name}{i}") if psum else P.sb(shape, dtype, f"{name}{i}")
            self.items.append((t, Buf(f"{name}{i}")))
        self.i = 0

    def next(self):
        it = self.items[self.i % len(self.items)]
        self.i += 1
        return it


def load_const(P, dram_ap, shape, dtype, name, q="sp"):
    t = P.sb(shape, dtype, name)
    b = Buf(name)
    P.dma(q, t[:], dram_ap, writes=[b])
    return t, b


class WBufs:
    def __init__(self):
        self.items = []

    def cols(self, c0, c1):
        return [b for lo, hi, b in self.items if lo < c1 and hi > c0]

    def kc(self, j):
        return [b for lo, hi, b in self.items if lo <= j < hi]


def load_weight(P, w_ap, K, N, name, stg, gain=None, by="col", cw=1024, defer=False):
    KC = K // 128
    wt = P.sb([128, KC, N], BF16, name)
    wb = WBufs()
    cnt = [0]

    def piece(kc, c0, c1, buf):
        st, sbuf = stg.next()
        P.dma("sp", st[:, 0:c1 - c0], w_ap[kc * 128:(kc + 1) * 128, c0:c1], writes=[sbuf])
        eng = "dve" if cnt[0] % 2 == 0 else "act"
        cnt[0] += 1
        if gain is None:
            cp(P, eng, wt[:, kc, c0:c1], st[:, 0:c1 - c0], [sbuf], [buf])
        else:
            gt, gb = gain
            if eng == "dve":
                ts(P, "dve", wt[:, kc, c0:c1], st[:, 0:c1 - c0], gt[:, kc:kc + 1], None, ALU.mult, None,
                   [sbuf, gb], [buf])
            else:
                act(P, wt[:, kc, c0:c1], st[:, 0:c1 - c0], AF.Copy, [sbuf, gb], [buf], scale=gt[:, kc:kc + 1])

    chunks = {}
    keys = []
    if by == "col":
        for c0 in range(0, N, cw):
            c1 = min(N, c0 + cw)
            buf = Buf(f"{name}_{c0}")
            wb.items.append((c0, c1, buf))
            chunks[c0] = [(kc, c0, c1, buf) for kc in range(KC)]
            keys.append(c0)
    else:
        for kc in range(KC):
            buf = Buf(f"{name}_k{kc}")
            wb.items.append((kc, kc + 1, buf))
            chunks[kc] = [(kc, c0, min(N, c0 + cw), buf) for c0 in range(0, N, cw)]
            keys.append(kc)
    done = set()

    def load(key):
        if key in done:
            return
        done.add(key)
        for a in chunks[key]:
            piece(*a)

    if defer:
        return wt, wb, load
    for k in keys:
        load(k)
    return wt, wb


def rms_rstd(P, x_ap, D, ss, scratch, reads, SS, SCR, eps=1e-6):
    act(P, scratch, x_ap, AF.Square, reads, [SCR, SS], accum_out=ss)
    act(P, ss, ss, AF.Ln, [SS], [SS], scale=1.0 / D, bias=eps)
    act(P, ss, ss, AF.Exp, [SS], [SS], scale=-0.5)


def make_ident(P, ident_dram):
    return load_const(P, ident_dram, [128, 128], BF16, "ident_sb")


NT = 2048
NB = NT // 128


def build_outproj():
    P = Prog()
    o_d = P.dram("o", [NT, 1024], F32, "ExternalInput").ap()
    x_d = P.dram("x", [NT, 1024], F32, "ExternalInput").ap()
    w_d = P.dram("w", [1024, 1024], F32, "ExternalInput").ap()
    id_d = P.dram("ident", [128, 128], BF16, "ExternalInput").ap()
    y_d = P.dram("y", [NT, 1024], F32, "ExternalOutput").ap()
    ident, IDB = make_ident(P, id_d)
    stg = Rot(P, 4, [128, 1024], F32, "stg")
    w, WB = load_weight(P, w_d, 1024, 1024, "w", stg)
    oin = Rot(P, 2, [128, 1024], F32, "oin")
    xin = Rot(P, 2, [128, 1024], F32, "xin")
    obf = Rot(P, 2, [128, 1024], BF16, "obf")
    oT = Rot(P, 2, [128, 8, 128], BF16, "oT")
    pT = Rot(P, 2, [128, 8, 128], BF16, "pT", psum=True)
    acc = Rot(P, 4, [128, 512], F32, "acc", psum=True)
    yo = Rot(P, 2, [128, 1024], F32, "yo")
    for b in range(NB):
        rows = slice(b * 128, (b + 1) * 128)
        ot, OB = oin.next()
        xt, XB = xin.next()
        P.dma("sp", ot[:], o_d[rows, :], writes=[OB])
        P.dma("sp", xt[:], x_d[rows, :], writes=[XB])
        bt, BB = obf.next()
        cp(P, "dve", bt[:], ot[:], [OB], [BB])
        pt, PB = pT.next()
        for c in range(8):
            tr(P, pt[:, c, :], bt[:, c * 128:(c + 1) * 128], ident[:], [BB, IDB], [PB])
        tt_, TB = oT.next()
        cp(P, "act", tt_[:], pt[:], [PB], [TB])
        yt, YB = yo.next()
        for half in range(2):
            at, AB = acc.next()
            for c in range(8):
                mm(P, at[:], tt_[:, c, :], w[:, c, half * 512:(half + 1) * 512], c == 0, c == 7,
                   [TB] + WB.cols(half * 512, half * 512 + 512), [AB])
            tt(P, "dve", yt[:, half * 512:(half + 1) * 512], at[:], xt[:, half * 512:(half + 1) * 512], ALU.add,
               [AB, XB], [YB])
        P.dma("pool", y_d[rows, :], yt[:], reads=[YB])
    return P.finish()


FH = 2816


def build_ffn():
    P = Prog()
    x_d = P.dram("x", [NT, 1024], F32, "ExternalInput").ap()
    g_d = P.dram("g", [128, 8], F32, "ExternalInput").ap()
    wgu_d = P.dram("wgu", [1024, 2 * FH], F32, "ExternalInput").ap()
    wd_d = P.dram("wd", [FH, 1024], F32, "ExternalInput").ap()
    id_d = P.dram("ident", [128, 128], BF16, "ExternalInput").ap()
    y_d = P.dram("y", [NT, 1024], F32, "ExternalOutput").ap()
    ident, IDB = make_ident(P, id_d)
    gain = load_const(P, g_d, [128, 8], F32, "gain")
    stg = Rot(P, 4, [128, 1024], F32, "stg")
    wgu, WGU, ld_gu = load_weight(P, wgu_d, 1024, 2 * FH, "wgu", stg, gain=gain, cw=512, defer=True)
    wd, WD, ld_d = load_weight(P, wd_d, FH, 1024, "wd", stg, by="kc", defer=True)
    for j in range(FH // 128):
        ld_gu((j * 128) // 512 * 512)
        ld_gu((FH + j * 128) // 512 * 512)
        ld_d(j)
    xin = Rot(P, 4, [128, 1024], F32, "xin")
    scr = P.sb([128, 1024], F32, "scr"); SCR = Buf("scr")
    ssr = Rot(P, 2, [128, 1], F32, "ss")
    xn = Rot(P, 2, [128, 1024], BF16, "xn")
    pT = Rot(P, 1, [128, 8, 128], BF16, "pT", psum=True)
    xT = Rot(P, 2, [128, 8, 256], BF16, "xT")
    gu = Rot(P, 3, [128, 512], F32, "gu", psum=True)
    accs = Rot(P, 4, [128, 512], F32, "acc", psum=True)
    sg = Rot(P, 2, [128, 256], BF16, "sg")
    aT = Rot(P, 2, [128, 256], BF16, "aT")
    yo = Rot(P, 2, [128, 1024], F32, "yo")
    NJ = FH // 128
    for sbk in range(NT // 256):
        xs = []
        xTt, XT = xT.next()
        for t in range(2):
            rows = slice(sbk * 256 + t * 128, sbk * 256 + (t + 1) * 128)
            xt, XB = xin.next()
            xs.append((xt, XB, rows))
            P.dma("sp", xt[:], x_d[rows, :], writes=[XB])
            ss, SS = ssr.next()
            rms_rstd(P, x_ap=xt[:], D=1024, ss=ss[:], scratch=scr[:], reads=[XB], SS=SS, SCR=SCR)
            xnt, XN = xn.next()
            ts(P, "dve", xnt[:], xt[:], ss[:, 0:1], None, ALU.mult, None, [XB, SS], [XN])
            pt, PB = pT.next()
            for c in range(8):
                tr(P, pt[:, c, :], xnt[:, c * 128:(c + 1) * 128], ident[:], [XN, IDB], [PB])
            cp(P, "dve", xTt[:, :, t * 128:(t + 1) * 128], pt[:], [PB], [XT])
        acc = [[accs.next() for _ in range(2)] for _ in range(2)]
        for j in range(NJ):
            gt, GB = gu.next()
            ut, UB = gu.next()
            for c in range(8):
                mm(P, gt[:, 0:256], wgu[:, c, j * 128:(j + 1) * 128], xTt[:, c, :], c == 0, c == 7,
                   WGU.cols(j * 128, j * 128 + 128) + [XT], [GB])
            for c in range(8):
                mm(P, ut[:, 0:256], wgu[:, c, FH + j * 128:FH + (j + 1) * 128], xTt[:, c, :], c == 0, c == 7,
                   WGU.cols(FH + j * 128, FH + j * 128 + 128) + [XT], [UB])
            sgt, SG = sg.next()
            act(P, sgt[:], gt[:, 0:256], AF.Silu, [GB], [SG])
            at, AB = aT.next()
            tt(P, "dve", at[:], sgt[:], ut[:, 0:256], ALU.mult, [SG, UB], [AB])
            for t in range(2):
                for half in range(2):
                    a_t, A_B = acc[t][half]
                    mm(P, a_t[:], at[:, t * 128:(t + 1) * 128], wd[:, j, half * 512:(half + 1) * 512], j == 0,
                       j == NJ - 1, [AB] + WD.kc(j), [A_B])
        for t in range(2):
            xt, XB, rows = xs[t]
            yt, YB = yo.next()
            for half in range(2):
                a_t, A_B = acc[t][half]
                tt(P, "dve", yt[:, half * 512:(half + 1) * 512], a_t[:], xt[:, half * 512:(half + 1) * 512], ALU.add,
                   [A_B, XB], [YB])
            P.dma("pool", y_d[rows, :], yt[:], reads=[YB])
    return P.finish()


def rope_ops(P, src, SRC, dst, DST, cosb, sinb, CS, tmp, TMP, nh):
    x1 = src[:, :, 0:32]
    x2 = src[:, :, 32:64]
    t1, t2 = tmp[:, 0, 0:nh, :], tmp[:, 1, 0:nh, :]
    tt(P, "dve", t1, x1, cosb, ALU.mult, [SRC, CS], [TMP])
    tt(P, "dve", t2, x2, sinb, ALU.mult, [SRC, CS], [TMP])
    tt(P, "dve", dst[:, :, 0:32], t1, t2, ALU.subtract, [TMP], [DST])
    tt(P, "dve", t1, x1, sinb, ALU.mult, [SRC, CS], [TMP])
    tt(P, "dve", t2, x2, cosb, ALU.mult, [SRC, CS], [TMP])
    tt(P, "dve", dst[:, :, 32:64], t1, t2, ALU.add, [TMP], [DST])


def build_kvq(dbg=0):
    P = Prog()
    x_d = P.dram("x", [NT, 1024], F32, "ExternalInput").ap()
    gkv_d = P.dram("g_kv", [128, 8], F32, "ExternalInput").ap()
    gq_d = P.dram("g_q", [128, 8], F32, "ExternalInput").ap()
    gc_d = P.dram("g_c", [128, 2], F32, "ExternalInput").ap()
    gqa_d = P.dram("g_qa", [128, 3], F32, "ExternalInput").ap()
    gkn_d = P.dram("g_kn", [128, 192], F32, "ExternalInput").ap()
    gqn_d = P.dram("g_qn", [128, 192], F32, "ExternalInput").ap()
    wkva_d = P.dram("w_kva", [1024, 320], F32, "ExternalInput").ap()
    wkvb_d = P.dram("w_kvb", [256, 2048], F32, "ExternalInput").ap()
    wqa_d = P.dram("w_qa", [1024, 384], F32, "ExternalInput").ap()
    wqb_d = P.dram("w_qb", [384, 1536], F32, "ExternalInput").ap()
    cs_d = P.dram("cs", [128, NB, 64], F32, "ExternalInput").ap()
    id_d = P.dram("ident", [128, 128], BF16, "ExternalInput").ap()
    kT_d = P.dram("k", [NT, 1536], BF16, "ExternalOutput").ap()
    qT_d = P.dram("q", [NT, 1536], BF16, "ExternalOutput").ap()
    v_d = P.dram("v", [NT, 1024], BF16, "ExternalOutput").ap()
    ident, IDB = make_ident(P, id_d)
    gkv = load_const(P, gkv_d, [128, 8], F32, "gkv")
    gq = load_const(P, gq_d, [128, 8], F32, "gq")
    gc = load_const(P, gc_d, [128, 2], F32, "gc")
    gqa = load_const(P, gqa_d, [128, 3], F32, "gqa")
    gkn, GKN = load_const(P, gkn_d, [128, 192], F32, "gkn")
    gqn, GQN = load_const(P, gqn_d, [128, 192], F32, "gqn")
    cs, CS = load_const(P, cs_d, [128, NB, 64], F32, "cs")
    stg = Rot(P, 4, [128, 1024], F32, "stg")
    wkva, WKVA = load_weight(P, wkva_d, 1024, 320, "wkva", stg, gain=gkv)
    wkvb, WKVB = load_weight(P, wkvb_d, 256, 2048, "wkvb", stg, gain=gc)
    wqa, WQA = load_weight(P, wqa_d, 1024, 384, "wqa", stg, gain=gq)
    wqb, WQB = load_weight(P, wqb_d, 384, 1536, "wqb", stg, gain=gqa)
    xin = Rot(P, 2, [128, 1024], F32, "xin")
    scr = P.sb([128, 1024], F32, "scr"); SCR = Buf("scr")
    ssr = Rot(P, 4, [128, 1], F32, "ss")
    ss8 = Rot(P, 4, [128, 8], F32, "ss8")
    xn = Rot(P, 2, [128, 1024], BF16, "xn")
    pT = Rot(P, 2, [128, 8, 128], BF16, "pT", psum=True)
    xT = Rot(P, 2, [128, 8, 128], BF16, "xT")
    pf = Rot(P, 6, [128, 512], F32, "pf", psum=True)
    cn = Rot(P, 2, [128, 384], BF16, "cn")
    cT = Rot(P, 2, [128, 3, 128], BF16, "cT")
    kpe = Rot(P, 2, [128, 64], F32, "kpe")
    tmpf = Rot(P, 3, [128, 8, 192], F32, "tmpf")
    pe8 = Rot(P, 2, [128, 8, 64], F32, "pe8")
    rtmp = P.sb([128, 2, 8, 32], F32, "rtmp"); RTMP = Buf("rtmp")
    kt = Rot(P, 2, [128, 8, 192], BF16, "kt")
    vt = Rot(P, 2, [128, 8, 128], BF16, "vt")
    kTs = Rot(P, 2, [128, 2, 8, 128], BF16, "kTs")
    for b in range(NB):
        rows = slice(b * 128, (b + 1) * 128)
        cosb = cs[:, b, 0:32].unsqueeze(1).to_broadcast([128, 8, 32])
        sinb = cs[:, b, 32:64].unsqueeze(1).to_broadcast([128, 8, 32])
        xt, XB = xin.next()
        P.dma("sp", xt[:], x_d[rows, :], writes=[XB])
        ss, SS = ssr.next()
        rms_rstd(P, xt[:], 1024, ss[:], scr[:], [XB], SS, SCR)
        xnt, XN = xn.next()
        ts(P, "dve", xnt[:], xt[:], ss[:, 0:1], None, ALU.mult, None, [XB, SS], [XN])
        pt, PB = pT.next()
        for c in range(8):
            tr(P, pt[:, c, :], xnt[:, c * 128:(c + 1) * 128], ident[:], [XN, IDB], [PB])
        xTt, XT = xT.next()
        cp(P, "dve", xTt[:], pt[:], [PB], [XT])
        kva, KVA = pf.next()
        qa, QA = pf.next()
        for c in range(8):
            mm(P, kva[:, 0:320], xTt[:, c, :], wkva[:, c, :], c == 0, c == 7, [XT] + WKVA.cols(0, 320), [KVA])
        for c in range(8):
            mm(P, qa[:, 0:384], xTt[:, c, :], wqa[:, c, :], c == 0, c == 7, [XT] + WQA.cols(0, 384), [QA])
        ss2, SS2 = ssr.next()
        rms_rstd(P, kva[:, 0:256], 256, ss2[:], scr[:, 0:256], [KVA], SS2, SCR)
        cnt, CN = cn.next()
        ts(P, "dve", cnt[:, 0:256], kva[:, 0:256], ss2[:, 0:1], None, ALU.mult, None, [KVA, SS2], [CN])
        kp, KP = kpe.next()
        tt(P, "dve", kp[:], kva[:, 256:320], gkn[:, 128:192], ALU.mult, [KVA, GKN], [KP])
        sspe, SSPE = ssr.next()
        act(P, scr[:, 0:64], kva[:, 256:320], AF.Square, [KVA], [SCR, SSPE], accum_out=sspe[:])
        pt2, PB2 = pT.next()
        for c in range(2):
            tr(P, pt2[:, c, :], cnt[:, c * 128:(c + 1) * 128], ident[:], [CN, IDB], [PB2])
        cTt, CT = cT.next()
        cp(P, "act", cTt[:, 0:2, :], pt2[:, 0:2, :], [PB2], [CT])
        kvb = []
        for n in range(4):
            kvp, KVP = pf.next()
            kvb.append((kvp, KVP))
            for c in range(2):
                mm(P, kvp[:], cTt[:, c, :], wkvb[:, c, n * 512:(n + 1) * 512], c == 0, c == 1,
                   [CT] + WKVB.cols(n * 512, n * 512 + 512), [KVP])
        s8, S8 = ss8.next()
        for h in range(8):
            kvp, KVP = kvb[h // 2]
            act(P, scr[:, 0:128], kvp[:, (h % 2) * 256:(h % 2) * 256 + 128], AF.Square, [KVP], [SCR, S8],
                accum_out=s8[:, h:h + 1])
        ts(P, "dve", s8[:], s8[:], sspe[:, 0:1], None, ALU.add, None, [S8, SSPE], [S8])
        act(P, s8[:], s8[:], AF.Ln, [S8], [S8], scale=1.0 / 192, bias=1e-6)
        act(P, s8[:], s8[:], AF.Exp, [S8], [S8], scale=-0.5)
        tf, TF = tmpf.next()
        ktt, KT = kt.next()
        vtt, VT = vt.next()
        for n in range(4):
            kvp, KVP = kvb[n]
            kv3 = kvp[:].rearrange("p (h c) -> p h c", h=2)
            tt(P, "dve", tf[:, 2 * n:2 * n + 2, 0:128], kv3[:, :, 0:128],
               gkn[:, 0:128].unsqueeze(1).to_broadcast([128, 2, 128]), ALU.mult, [KVP, GKN], [TF])
            cp(P, "act", vtt[:, 2 * n:2 * n + 2, :], kv3[:, :, 128:256], [KVP], [VT])
        tt(P, "dve", ktt[:, :, 0:128], tf[:, :, 0:128], s8[:, :].unsqueeze(2).to_broadcast([128, 8, 128]), ALU.mult,
           [TF, S8], [KT])
        p8, P8 = pe8.next()
        tt(P, "dve", p8[:], kp[:, :].unsqueeze(1).to_broadcast([128, 8, 64]),
           s8[:, :].unsqueeze(2).to_broadcast([128, 8, 64]), ALU.mult, [KP, S8], [P8])
        rope_ops(P, p8[:], P8, ktt[:, :, 128:192], KT, cosb, sinb, CS, rtmp, RTMP, 8)
        P.dma("pool", v_d[rows, :], vtt[:].rearrange("p h d -> p (h d)"), reads=[VT])

        def emit_T(src, SRC, dst_d):
            P.dma("pool", dst_d[rows, :], src[:].rearrange("p h d -> p (h d)"), reads=[SRC])

        if dbg != 1:
            emit_T(ktt, KT, kT_d)
        if dbg in (1, 2):
            continue
        ss3, SS3 = ssr.next()
        rms_rstd(P, qa[:, 0:384], 384, ss3[:], scr[:, 0:384], [QA], SS3, SCR)
        qn, QN = cn.next()
        ts(P, "dve", qn[:], qa[:, 0:384], ss3[:, 0:1], None, ALU.mult, None, [QA, SS3], [QN])
        pt3, PB3 = pT.next()
        for c in range(3):
            tr(P, pt3[:, c, :], qn[:, c * 128:(c + 1) * 128], ident[:], [QN, IDB], [PB3])
        qT_, QT = cT.next()
        cp(P, "act", qT_[:], pt3[:, 0:3, :], [PB3], [QT])
        qb = []
        for n in range(4):
            qp, QP = pf.next()
            qb.append((qp, QP))
            for c in range(3):
                mm(P, qp[:, 0:384], qT_[:, c, :], wqb[:, c, n * 384:(n + 1) * 384], c == 0, c == 2,
                   [QT] + WQB.cols(n * 384, n * 384 + 384), [QP])
        s8q, S8Q = ss8.next()
        for h in range(8):
            qp, QP = qb[h // 2]
            act(P, scr[:, 0:192], qp[:, (h % 2) * 192:(h % 2) * 192 + 192], AF.Square, [QP], [SCR, S8Q],
                accum_out=s8q[:, h:h + 1])
        act(P, s8q[:], s8q[:], AF.Ln, [S8Q], [S8Q], scale=1.0 / 192, bias=1e-6)
        act(P, s8q[:], s8q[:], AF.Exp, [S8Q], [S8Q], scale=-0.5)
        if dbg == 3:
            continue
        tq, TQ = tmpf.next()
        for n in range(4):
            qp, QP = qb[n]
            cp(P, "act", tq[:, 2 * n:2 * n + 2, :].rearrange("p h c -> p (h c)"), qp[:, 0:384], [QP], [TQ])
        tq2, TQ2 = tmpf.next()
        tt(P, "dve", tq2[:], tq[:], gqn[:, :].unsqueeze(1).to_broadcast([128, 8, 192]), ALU.mult, [TQ, GQN], [TQ2])
        tq, TQ = tq2, TQ2
        if dbg == 4:
            continue
        qtt, QTT = kt.next()
        tt(P, "dve", qtt[:, :, 0:128], tq[:, :, 0:128], s8q[:, :].unsqueeze(2).to_broadcast([128, 8, 128]), ALU.mult,
           [TQ, S8Q], [QTT])
        p8q, P8Q = pe8.next()
        tt(P, "dve", p8q[:], tq[:, :, 128:192], s8q[:, :].unsqueeze(2).to_broadcast([128, 8, 64]), ALU.mult,
           [TQ, S8Q], [P8Q])
        rope_ops(P, p8q[:], P8Q, qtt[:, :, 128:192], QTT, cosb, sinb, CS, rtmp, RTMP, 8)
        emit_T(qtt, QTT, qT_d)
    return P.finish()


def run_pipeline(P, steps, s_r, p_r, scale, look=2):
    n = len(steps)
    Sb = [None] * n

    def do_qk(i):
        S, SB_ = s_r.next()
        Sb[i] = (S, SB_)
        steps[i][0](S, SB_)

    for i in range(min(look, n)):
        do_qk(i)
    for i in range(n):
        if i + look < n:
            do_qk(i + look)
        S, SB_ = Sb[i]
        p, PB = p_r.next()
        act(P, p[:], S[:], AF.Exp, [SB_], [PB], scale=scale)
        steps[i][1](p, PB)
        if steps[i][2] is not None:
            steps[i][2]()


TT = 8192


def build_mla():
    P = Prog()
    kTa_d = P.dram("kTa", [128, 2, TT], BF16, "ExternalInput").ap()
    kTb_d = P.dram("kTb", [128, 2, TT], BF16, "ExternalInput").ap()
    qTa_d = P.dram("qTa", [128, 2, TT], BF16, "ExternalInput").ap()
    qTb_d = P.dram("qTb", [128, 2, TT], BF16, "ExternalInput").ap()
    v_d = P.dram("vaug", [128, 64, 2, 132], BF16, "ExternalInput").ap()
    m_d = P.dram("masks", [128, 4, 512], BF16, "ExternalInput").ap()
    o_d = P.dram("o", [TT, 256], F32, "ExternalOutput").ap()
    kTa = P.sb([128, 2, TT], BF16, "kTa"); KA = [Buf("ka0"), Buf("ka1")]
    kTb = P.sb([128, 2, TT], BF16, "kTb"); KB = [Buf("kb0"), Buf("kb1")]
    va = P.sb([128, 64, 2, 132], BF16, "va"); VA = Buf("va")
    for h in range(2):
        P.dma("sp", kTa[:, h, :], kTa_d[:, h, :], writes=[KA[h]])
        P.dma("sp", kTb[:, h, :], kTb_d[:, h, :], writes=[KB[h]])
    for c0 in range(0, 64, 16):
        P.dma("sp", va[:, c0:c0 + 16], v_d[:, c0:c0 + 16], writes=[VA])
    mk, MK = load_const(P, m_d, [128, 4, 512], BF16, "mk")
    qa_r = Rot(P, 2, [128, 512], BF16, "qa")
    qb_r = Rot(P, 2, [128, 512], BF16, "qb")
    s_r = Rot(P, 3, [128, 512], F32, "S", psum=True)
    acc_r = Rot(P, 4, [128, 512], F32, "acc", psum=True)
    p_r = Rot(P, 4, [128, 512], BF16, "p")
    pm_r = Rot(P, 3, [128, 512], BF16, "pm")
    rs_r = Rot(P, 4, [128, 1], F32, "rs")
    o_r = Rot(P, 2, [128, 4, 128], F32, "osb")
    scale = 192.0 ** -0.5
    steps = []
    for h in range(2):
        for qs in range(TT // 512):
            ctx = {}

            def mk_qk(h=h, qs=qs, ctx=ctx, kc=0):
                def f(S, SB_):
                    if kc == 0:
                        qa, QA = qa_r.next()
                        qb, QB = qb_r.next()
                        P.dma("sp", qa[:], qTa_d[:, h, qs * 512:(qs + 1) * 512], writes=[QA])
                        P.dma("sp", qb[:], qTb_d[:, h, qs * 512:(qs + 1) * 512], writes=[QB])
                        ctx["q"] = (qa, QA, qb, QB)
                        ctx["accs"] = [acc_r.next() for _ in range(4)]
                    qa, QA, qb, QB = ctx["q"]
                    mm(P, S[:], kTa[:, h, kc * 128:(kc + 1) * 128], qa[:], True, False, [KA[h], QA], [SB_])
                    mm(P, S[:], kTb[:, h, kc * 128:(kc + 1) * 128], qb[:], False, True, [KB[h], QB], [SB_])
                return f

            def mk_pv(h=h, qs=qs, ctx=ctx, kc=0):
                def f(p, PB):
                    j = kc - 4 * qs
                    if j >= 0:
                        pm, PM = pm_r.next()
                        tt(P, "dve", pm[:], p[:], mk[:, j, :], ALU.mult, [PB, MK], [PM])
                        p, PB = pm, PM
                    for t in range(4):
                        if j >= 0 and t < j:
                            continue
                        a, AB = ctx["accs"][t]
                        mm(P, a[:, 0:129], p[:, t * 128:(t + 1) * 128], va[:, kc, h, 0:129], kc == 0,
                           kc == 4 * qs + t, [PB, VA], [AB])
                return f

            def mk_after(h=h, qs=qs, ctx=ctx):
                def f():
                    osb, OB = o_r.next()
                    for t in range(4):
                        a, AB = ctx["accs"][t]
                        rs, RS = rs_r.next()
                        P.op("dve", lambda e, rs=rs, a=a: e.reciprocal(out=rs[:], in_=a[:, 128:129]), reads=[AB],
                             writes=[RS])
                        ts(P, "dve", osb[:, t, :], a[:, 0:128], rs[:, 0:1], None, ALU.mult, None, [AB, RS], [OB])
                    P.dma("sp", o_d[qs * 512:(qs + 1) * 512, h * 128:(h + 1) * 128].rearrange("(t p) d -> p t d", p=128),
                          osb[:], reads=[OB])
                return f

            nk = 4 * qs + 4
            for kc in range(nk):
                steps.append((mk_qk(kc=kc), mk_pv(kc=kc), mk_after() if kc == nk - 1 else None))
    run_pipeline(P, steps, s_r, p_r, scale)
    return P.finish()


def mla_host_inputs(k, q, v, b, hp):
    ks = k[b * TT:(b + 1) * TT].reshape(TT, 8, 192)[:, 2 * hp:2 * hp + 2, :]
    qs = q[b * TT:(b + 1) * TT].reshape(TT, 8, 192)[:, 2 * hp:2 * hp + 2, :]
    vs = v[b * TT:(b + 1) * TT].reshape(TT, 8, 128)[:, 2 * hp:2 * hp + 2, :]
    kT = np.ascontiguousarray(ks.transpose(2, 1, 0))
    qT = np.ascontiguousarray(qs.transpose(2, 1, 0))
    vaug = np.zeros((128, 64, 2, 132), dtype=NPBF)
    vaug[:, :, :, 0:128] = vs.reshape(64, 128, 2, 128).transpose(1, 0, 2, 3)
    vaug[:, :, :, 128] = 1.0
    kl = np.arange(128)[:, None, None]
    j = np.arange(4)[None, :, None]
    ql = np.arange(512)[None, None, :]
    masks = (j * 128 + kl <= ql).astype(np.float32).astype(NPBF)
    kTb = np.zeros((128, 2, TT), NPBF)
    kTb[0:64] = kT[128:192]
    qTb = np.zeros((128, 2, TT), NPBF)
    qTb[0:64] = qT[128:192]
    return {"kTa": np.ascontiguousarray(kT[0:128]), "kTb": kTb,
            "qTa": np.ascontiguousarray(qT[0:128]), "qTb": qTb,
            "vaug": vaug, "masks": np.ascontiguousarray(masks)}


NW = 2608


def build_nsa_a():
    P = Prog()
    x_d = P.dram("x", [NT, 1024], F32, "ExternalInput").ap()
    g_d = P.dram("g", [128, 8], F32, "ExternalInput").ap()
    w_d = P.dram("w", [1024, NW], F32, "ExternalInput").ap()
    gq_d = P.dram("gq", [128, 64], F32, "ExternalInput").ap()
    gs_d = P.dram("gs", [128, 64], F32, "ExternalInput").ap()
    gw_d = P.dram("gw", [128, 64], F32, "ExternalInput").ap()
    id_d = P.dram("ident", [128, 128], BF16, "ExternalInput").ap()
    big_d = P.dram("big", [NT, 2560], BF16, "ExternalOutput").ap()
    gates_d = P.dram("gates", [NT, 48], F32, "ExternalOutput").ap()
    ident, IDB = make_ident(P, id_d)
    gain = load_const(P, g_d, [128, 8], F32, "gain")
    gq, GQ = load_const(P, gq_d, [128, 64], F32, "gq")
    gs, GS = load_const(P, gs_d, [128, 64], F32, "gs")
    gw, GW = load_const(P, gw_d, [128, 64], F32, "gw")
    stg = Rot(P, 4, [128, 1024], F32, "stg")
    w, WB = load_weight(P, w_d, 1024, NW, "w", stg, gain=gain)
    xin = Rot(P, 2, [128, 1024], F32, "xin")
    scr = P.sb([128, 1536], F32, "scr"); SCR = Buf("scr")
    ssr = Rot(P, 2, [128, 1], F32, "ss")
    xn = Rot(P, 2, [128, 1024], BF16, "xn")
    pT = Rot(P, 1, [128, 8, 128], BF16, "pT", psum=True)
    xT = Rot(P, 2, [128, 8, 128], BF16, "xT")
    pf = Rot(P, 6, [128, 512], F32, "pf", psum=True)
    y_r = Rot(P, 2, [128, NW], F32, "y")
    ssn_r = Rot(P, 2, [128, 24], F32, "ssn")
    tmp_r = Rot(P, 2, [128, 1024], F32, "tmp")
    big_r = Rot(P, 2, [128, 2560], BF16, "bigs")
    gt_r = Rot(P, 2, [128, 48], F32, "gts")
    for b in range(NB):
        rows = slice(b * 128, (b + 1) * 128)
        xt, XB = xin.next()
        P.dma("sp", xt[:], x_d[rows, :], writes=[XB])
        ss, SS = ssr.next()
        rms_rstd(P, xt[:], 1024, ss[:], scr[:, 0:1024], [XB], SS, SCR)
        xnt, XN = xn.next()
        ts(P, "dve", xnt[:], xt[:], ss[:, 0:1], None, ALU.mult, None, [XB, SS], [XN])
        pt, PB = pT.next()
        for c in range(8):
            tr(P, pt[:, c, :], xnt[:, c * 128:(c + 1) * 128], ident[:], [XN, IDB], [PB])
        xTt, XT = xT.next()
        cp(P, "dve", xTt[:], pt[:], [PB], [XT])
        y, YB = y_r.next()
        for n in range(6):
            c0, c1 = n * 512, min(NW, (n + 1) * 512)
            pp, PP = pf.next()
            for c in range(8):
                mm(P, pp[:, 0:c1 - c0], xTt[:, c, :], w[:, c, c0:c1], c == 0, c == 7, [XT] + WB.cols(c0, c1), [PP])
            cp(P, "act", y[:, c0:c1], pp[:, 0:c1 - c0], [PP], [YB])
        act(P, scr[:, 0:1024], y[:, 0:1024], AF.Square, [YB], [SCR])
        act(P, scr[:, 1024:1280], y[:, 1536:1792], AF.Square, [YB], [SCR])
        act(P, scr[:, 1280:1536], y[:, 2048:2304], AF.Square, [YB], [SCR])
        ssn, SSN = ssn_r.next()
        P.op("dve", lambda e, ssn=ssn: e.tensor_reduce(out=ssn[:], in_=scr[:, 0:1536].rearrange("p (h d) -> p h d", d=64),
                                                      axis=AX.X, op=ALU.add), reads=[SCR], writes=[SSN])
        act(P, ssn[:], ssn[:], AF.Ln, [SSN], [SSN], scale=1.0 / 64, bias=1e-6)
        act(P, ssn[:], ssn[:], AF.Exp, [SSN], [SSN], scale=-0.5)
        bg, BG = big_r.next()
        tmp, TMP = tmp_r.next()

        def normed(src0, nh, h0, gt, GT, dst0):
            tv = tmp[:, 0:nh * 64].rearrange("p (h d) -> p h d", d=64)
            tt(P, "dve", tv, y[:, src0:src0 + nh * 64].rearrange("p (h d) -> p h d", d=64),
               ssn[:, h0:h0 + nh].unsqueeze(2).to_broadcast([128, nh, 64]), ALU.mult, [YB, SSN], [TMP])
            tt(P, "dve", bg[:, dst0:dst0 + nh * 64].rearrange("p (h d) -> p h d", d=64), tv,
               gt[:, :].unsqueeze(1).to_broadcast([128, nh, 64]), ALU.mult, [TMP, GT], [BG])

        normed(0, 16, 0, gq, GQ, 0)
        normed(1536, 4, 16, gs, GS, 1536)
        normed(2048, 4, 20, gw, GW, 2048)
        cp(P, "dve", bg[:, 1024:1536], y[:, 1024:1536], [YB], [BG])
        cp(P, "dve", bg[:, 1792:2048], y[:, 1792:2048], [YB], [BG])
        cp(P, "dve", bg[:, 2304:2560], y[:, 2304:2560], [YB], [BG])
        gt_, GT_ = gt_r.next()
        act(P, gt_[:], y[:, 2560:2608], AF.Sigmoid, [YB], [GT_])
        P.dma("pool", big_d[rows, :], bg[:], reads=[BG])
        P.dma("pool", gates_d[rows, :], gt_[:], reads=[GT_])
    return P.finish()


def build_nsa_b():
    P = Prog()
    ins = {}
    for nm in ("k", "v"):
        ins[nm] = dict(
            aT=P.dram(nm + "_a2T", [128, TT], BF16, "ExternalInput").ap(),
            w1=P.dram(nm + "_w1", [128, 16, 256], F32, "ExternalInput").ap(),
            pe=P.dram(nm + "_pe", [128, 16], F32, "ExternalInput").ap(),
            b1=P.dram(nm + "_b1", [128, 2], F32, "ExternalInput").ap(),
            w2=P.dram(nm + "_w2", [128, 2, 64], F32, "ExternalInput").ap(),
            out=P.dram(nm + "_cmp", [512, 64], BF16, "ExternalOutput").ap(),
        )
    gk_d = P.dram("gk", [128, 64], F32, "ExternalInput").ap()
    gk, GK = load_const(P, gk_d, [128, 64], F32, "gk")
    ps_r = Rot(P, 4, [128, 512], F32, "ps", psum=True)
    f_r = Rot(P, 6, [128, 512], F32, "f")
    for nm in ("k", "v"):
        I = ins[nm]
        aT, AT = load_const(P, I["aT"], [128, TT], BF16, nm + "aT")
        w1f, W1F = load_const(P, I["w1"], [128, 16, 256], F32, nm + "w1f")
        pef, PEF = load_const(P, I["pe"], [128, 16], F32, nm + "pef")
        b1, B1 = load_const(P, I["b1"], [128, 2], F32, nm + "b1")
        w2f, W2F = load_const(P, I["w2"], [128, 2, 64], F32, nm + "w2f")
        w1 = P.sb([128, 16, 256], BF16, nm + "w1"); W1 = Buf(nm + "w1")
        cp(P, "dve", w1[:], w1f[:], [W1F], [W1])
        peb = P.sb([128, 16], BF16, nm + "peb"); PEB = Buf(nm + "peb")
        cp(P, "dve", peb[:], pef[:], [PEF], [PEB])
        w2 = P.sb([128, 2, 64], BF16, nm + "w2"); W2 = Buf(nm + "w2")
        cp(P, "dve", w2[:], w2f[:], [W2F], [W2])
        av = aT[:, :].rearrange("p (j s) -> p j s", s=16)
        gT = P.sb([128, 2, 512], BF16, nm + "gT"); GT = Buf(nm + "gT")
        memset(P, "dve", gT[:], 0.0, [GT])
        for half in range(2):
            hp, HP = ps_r.next()
            cv, CV = ps_r.next()
            for m in range(16):
                rhs = av[:, 0:511, 2 * m] if m < 8 else av[:, 1:512, 2 * m - 16]
                mm(P, hp[:, 0:511], w1[:, m, half * 128:(half + 1) * 128], rhs, m == 0, m == 15, [W1, AT], [HP])
            for m in range(16):
                mm(P, cv[:, 0:1], w1[:, m, half * 128:(half + 1) * 128], peb[:, m:m + 1], m == 0, m == 15, [W1, PEB], [CV])
            bias, BI = f_r.next()
            tt(P, "dve", bias[:, 0:1], cv[:, 0:1], b1[:, half:half + 1], ALU.add, [CV, B1], [BI])
            u, U = f_r.next()
            act(P, u[:, 0:511], hp[:, 0:511], AF.Identity, [HP, BI], [U], bias=bias[:, 0:1])
            t1, T1 = f_r.next()
            tt(P, "dve", t1[:, 0:511], u[:, 0:511], u[:, 0:511], ALU.mult, [U], [T1])
            t2, T2 = f_r.next()
            ts(P, "dve", t2[:, 0:511], t1[:, 0:511], 0.044715, 1.0, ALU.mult, ALU.add, [T1], [T2])
            t3, T3 = f_r.next()
            tt(P, "dve", t3[:, 0:511], t2[:, 0:511], u[:, 0:511], ALU.mult, [T2, U], [T3])
            t4, T4 = f_r.next()
            act(P, t4[:, 0:511], t3[:, 0:511], AF.Tanh, [T3], [T4], scale=0.7978845608028654)
            t5, T5 = f_r.next()
            ts(P, "dve", t5[:, 0:511], t4[:, 0:511], 1.0, 0.5, ALU.add, ALU.mult, [T4], [T5])
            tt(P, "dve", gT[:, half, 0:511], t5[:, 0:511], u[:, 0:511], ALU.mult, [T5, U], [GT])
        for ch in range(4):
            op_, OP = ps_r.next()
            for half in range(2):
                mm(P, op_[:, 0:64], gT[:, half, ch * 128:(ch + 1) * 128], w2[:, half, :], half == 0, half == 1,
                   [GT, W2], [OP])
            ob = P.sb([128, 64], BF16, f"{nm}ob{ch}"); OB = Buf("ob")
            if nm == "k":
                sq, SQ = f_r.next()
                ssk = P.sb([128, 1], F32, f"ssk{ch}"); SSK = Buf("ssk")
                rms_rstd(P, op_[:, 0:64], 64, ssk[:], sq[:, 0:64], [OP], SSK, SQ)
                kn, KN = f_r.next()
                ts(P, "dve", kn[:, 0:64], op_[:, 0:64], ssk[:, 0:1], None, ALU.mult, None, [OP, SSK], [KN])
                tt(P, "dve", ob[:], kn[:, 0:64], gk[:], ALU.mult, [KN, GK], [OB])
            else:
                cp(P, "act", ob[:], op_[:, 0:64], [OP], [OB])
            P.dma("sp", I["out"][ch * 128:(ch + 1) * 128, :], ob[:], reads=[OB])
    return P.finish()


NQB = TT // 128


def build_nsa_c():
    P = Prog()
    QT_d = P.dram("QT", [70, 4, TT], BF16, "ExternalInput").ap()
    KS_d = P.dram("KS", [70, TT], BF16, "ExternalInput").ap()
    KW_d = P.dram("KW", [70, TT], BF16, "ExternalInput").ap()
    KC_d = P.dram("KC", [70, 512], BF16, "ExternalInput").ap()
    VS_d = P.dram("VS", [128, 64, 66], BF16, "ExternalInput").ap()
    VW_d = P.dram("VW", [128, 64, 66], BF16, "ExternalInput").ap()
    VC_d = P.dram("VC", [128, 4, 194], BF16, "ExternalInput").ap()
    G_d = P.dram("G", [128, 64, 12], F32, "ExternalInput").ap()
    E_d = P.dram("E", [128, TT], BF16, "ExternalInput").ap()
    CM_d = P.dram("CM", [128, 17, 512], BF16, "ExternalInput").ap()
    CA_d = P.dram("CA", [128, 512], BF16, "ExternalInput").ap()
    WM_d = P.dram("WM", [128, 512], BF16, "ExternalInput").ap()
    AT_d = P.dram("AT", [128, 256], F32, "ExternalInput").ap()
    id_d = P.dram("ident", [128, 128], BF16, "ExternalInput").ap()
    o_d = P.dram("o", [TT, 256], F32, "ExternalOutput").ap()
    ident, IDB = make_ident(P, id_d)
    QT = P.sb([70, 4, TT], BF16, "QT"); QTB = Buf("QT")
    for h in range(4):
        P.dma("sp", QT[:, h, :], QT_d[:, h, :], writes=[QTB])
    KS, KSB = load_const(P, KS_d, [70, TT], BF16, "KS")
    KW, KWB = load_const(P, KW_d, [70, TT], BF16, "KW")
    KC, KCB = load_const(P, KC_d, [70, 512], BF16, "KC")
    VS, VSB = load_const(P, VS_d, [128, 64, 66], BF16, "VS")
    VW, VWB = load_const(P, VW_d, [128, 64, 66], BF16, "VW")
    VC, VCB = load_const(P, VC_d, [128, 4, 194], BF16, "VC")
    G, GB = load_const(P, G_d, [128, 64, 12], F32, "G")
    E, EB = load_const(P, E_d, [128, TT], BF16, "E")
    CM, CMB = load_const(P, CM_d, [128, 17, 512], BF16, "CM")
    CA, CAB = load_const(P, CA_d, [128, 512], BF16, "CA")
    WM, WMB = load_const(P, WM_d, [128, 512], BF16, "WM")
    AT, ATB = load_const(P, AT_d, [128, 256], F32, "AT")
    s_r = Rot(P, 3, [128, 512], F32, "S", psum=True)
    accC_r = Rot(P, 2, [128, 512], F32, "accC", psum=True)
    accS_r = Rot(P, 1, [128, 512], F32, "accS", psum=True)
    accW_r = Rot(P, 1, [128, 512], F32, "accW", psum=True)
    pst_r = Rot(P, 1, [128, 128], BF16, "pst", psum=True)
    p_r = Rot(P, 4, [128, 512], BF16, "p")
    sm_r = Rot(P, 48, [128, 1], F32, "sm")
    imp_r = Rot(P, 4, [128, 128], F32, "imp")
    sc_r = Rot(P, 4, [128, 128], F32, "sc")
    m8_r = Rot(P, 4, [128, 8], F32, "m8")
    ns_r = Rot(P, 2, [128, 128], BF16, "ns")
    nt_r = Rot(P, 3, [128, 4, 128], BF16, "nt")
    o_r = Rot(P, 8, [128, 256], F32, "ot")
    scale = 0.125

    def qk(S, SB_, KT, KTB, c, rhsQ, extra):
        mm(P, S[:], KT[:, c * 128:(c + 1) * 128], rhsQ, True, extra is None, [KTB, QTB], [SB_])
        if extra is not None:
            lhsT, rhs, rd = extra
            mm(P, S[:], lhsT, rhs, False, True, rd, [SB_])

    def rq(qb):
        return QT[:, :, qb * 128:(qb + 1) * 128]

    st = {}

    def finalize(qb, a, AB, width, col, first):
        nxt, NXT = o_r.next()
        cur = st.get(("o", qb))
        rcs = []
        for h in range(4):
            off = (h % 2) * 193 if width == 193 else h * 65
            aa, AAB = (a[h // 2], AB[h // 2]) if width == 193 else (a, AB)
            sm, SM = sm_r.next()
            ts(P, "dve", sm[:], aa[:, off + 64:off + 65], 1e-30, None, ALU.max, None, [AAB], [SM])
            rc, RC = sm_r.next()
            P.op("dve", lambda e, rc=rc, sm=sm: e.reciprocal(out=rc[:], in_=sm[:]), reads=[SM], writes=[RC])
            gf, GF = sm_r.next()
            tt(P, "dve", gf[:], rc[:], G[:, qb, h * 3 + col:h * 3 + col + 1], ALU.mult, [RC, GB], [GF])
            if cur is None:
                ts(P, "dve", nxt[:, h * 64:(h + 1) * 64], aa[:, off:off + 64], gf[:, 0:1], None, ALU.mult, None,
                   [AAB, GF], [NXT])
            else:
                c_t, C_B = cur
                P.op("dve", lambda e, nxt=nxt, aa=aa, off=off, h=h, gf=gf, c_t=c_t: e.scalar_tensor_tensor(
                    out=nxt[:, h * 64:(h + 1) * 64], in0=aa[:, off:off + 64], scalar=gf[:, 0:1],
                    in1=c_t[:, h * 64:(h + 1) * 64], op0=ALU.mult, op1=ALU.add), reads=[AAB, GF, C_B], writes=[NXT])
            rcs.append((rc, RC))
        st[("o", qb)] = (nxt, NXT)
        return rcs

    def cmp_steps(qb):
        rhsQ = rq(qb)
        ncc = qb // 16 + 1
        ctx = {}

        def mk_qk(cc):
            def f(S, SB_):
                if cc == 0:
                    ctx["acc"] = [accC_r.next(), accC_r.next()]
                extra = None
                if qb - 16 * cc <= 16:
                    extra = (ident[:], CM[:, qb - 16 * cc, :], [IDB, CMB])
                qk(S, SB_, KC, KCB, cc, rhsQ, extra)
            return f

        def mk_pv(cc):
            def f(p, PB):
                for h in range(4):
                    a, AB = ctx["acc"][h // 2]
                    off = (h % 2) * 193
                    mm(P, a[:, off:off + 193], p[:, h * 128:(h + 1) * 128], VC[:, cc, 0:193],
                       cc == 0 and h % 2 == 0, cc == ncc - 1, [PB, VCB], [AB], skip=True)
            return f

        def after():
            acc = ctx["acc"]
            rcs = finalize(qb, [acc[0][0], acc[1][0]], [acc[0][1], acc[1][1]], 193, 0, True)
            imp, IMP = None, None
            for h in range(4):
                a, AB = acc[h // 2]
                off = (h % 2) * 193
                rc, RC = rcs[h]
                ni, NI = imp_r.next()
                if imp is None:
                    ts(P, "dve", ni[:], a[:, off + 65:off + 193], rc[:, 0:1], None, ALU.mult, None, [AB, RC], [NI])
                else:
                    P.op("dve", lambda e, ni=ni, a=a, off=off, rc=rc, imp=imp: e.scalar_tensor_tensor(
                        out=ni[:], in0=a[:, off + 65:off + 193], scalar=rc[:, 0:1], in1=imp[:], op0=ALU.mult,
                        op1=ALU.add), reads=[AB, RC, IMP], writes=[NI])
                imp, IMP = ni, NI
            sc, SC = sc_r.next()
            tt(P, "dve", sc[:], imp[:], AT[:, 126 - 2 * qb:254 - 2 * qb], ALU.add, [IMP, ATB], [SC])
            ts(P, "dve", sc[:, 0:1], imp[:, 0:1], 1e4, None, ALU.add, None, [IMP, SC], [SC])
            m1, M1 = m8_r.next()
            P.op("dve", lambda e, m1=m1, sc=sc: e.max(out=m1[:], in_=sc[:]), reads=[SC], writes=[M1])
            sc2, SC2 = sc_r.next()
            P.op("dve", lambda e, sc2=sc2, m1=m1, sc=sc: e.match_replace(
                out=sc2[:], in_to_replace=m1[:], in_values=sc[:], imm_value=-1e9), reads=[SC, M1], writes=[SC2])
            m2, M2 = m8_r.next()
            P.op("dve", lambda e, m2=m2, sc2=sc2: e.max(out=m2[:], in_=sc2[:]), reads=[SC2], writes=[M2])
            ns, NS = ns_r.next()
            ts(P, "dve", ns[:], sc[:], m2[:, 7:8], -30000.0, ALU.is_lt, ALU.mult, [SC, M2], [NS])
            pst, PST = pst_r.next()
            tr(P, pst[:], ns[:], ident[:], [NS, IDB], [PST])
            nt, NTB = nt_r.next()
            for h in range(4):
                cp(P, "act" if h % 2 else "dve", nt[:, h, :], pst[:], [PST], [NTB])
            st[("nt", qb)] = (nt, NTB)

        return [(mk_qk(cc), mk_pv(cc), after if cc == ncc - 1 else None) for cc in range(ncc)]

    def branch_steps(qb, which):
        rhsQ = rq(qb)
        ctx = {}
        if which == "sel":
            cs = list(range(qb + 1))
            KT, KTB, V, VB, accr, col = KS, KSB, VS, VSB, accS_r, 1
        else:
            cs = list(range(max(0, qb - 4), qb + 1))
            KT, KTB, V, VB, accr, col = KW, KWB, VW, VWB, accW_r, 2

        def mk_qk(c):
            def f(S, SB_):
                if c == cs[0]:
                    ctx["acc"] = accr.next()
                extra = None
                if c == qb:
                    extra = (ident[:], CA[:], [IDB, CAB])
                elif which == "sel":
                    nt, NTB = st[("nt", qb)]
                    extra = (E[:, c * 128:(c + 1) * 128], nt[:].rearrange("p h q -> p (h q)"), [EB, NTB])
                elif c == qb - 4:
                    extra = (ident[:], WM[:], [IDB, WMB])
                qk(S, SB_, KT, KTB, c, rhsQ, extra)
            return f

        def mk_pv(c):
            def f(p, PB):
                a, AB = ctx["acc"]
                for h in range(4):
                    mm(P, a[:, h * 65:(h + 1) * 65], p[:, h * 128:(h + 1) * 128], V[:, c, 0:65],
                       c == cs[0] and h == 0, c == qb, [PB, VB], [AB], skip=True)
            return f

        def after():
            a, AB = ctx["acc"]
            finalize(qb, a, AB, 65, col, False)
            if which == "sel":
                cur, CUR = st[("o", qb)]
                P.dma("sp", o_d[qb * 128:(qb + 1) * 128, :], cur[:], reads=[CUR])
                st.pop(("o", qb)); st.pop(("nt", qb))

        return [(mk_qk(c), mk_pv(c), after if c == qb else None) for c in cs]

    steps = list(cmp_steps(0))
    for qb in range(NQB):
        steps += branch_steps(qb, "win")
        if qb + 1 < NQB:
            steps += cmp_steps(qb + 1)
        steps += branch_steps(qb, "sel")
    run_pipeline(P, steps, s_r, p_r, scale)
    return P.finish()


_PROGS = {}


def _prog(name, fn):
    if name not in _PROGS:
        _PROGS[name] = fn()
    return _PROGS[name]


def _pl(g):
    return np.ascontiguousarray(np.asarray(g, np.float32).reshape(-1, 128).T)


def _bc(g):
    return np.ascontiguousarray(np.tile(np.asarray(g, np.float32)[None, :], (128, 1)))


def _run(nc, maps):
    res = run_bass_kernel_spmd(nc, maps, core_ids=list(range(8)))
    return res.results


def _split2(a):
    a = np.asarray(a, np.float64)
    hi = a.astype(np.float32).astype(NPBF)
    lo = (a - hi.astype(np.float64)).astype(np.float32).astype(NPBF)
    return hi, lo


def _nsa_c_consts():
    i = np.arange(128)[:, None]
    u = np.arange(TT)[None, :]
    E = (u // 64 == i).astype(np.float32).astype(NPBF)
    jl = np.arange(128)[:, None, None]
    m = np.arange(17)[None, :, None]
    ql = np.tile(np.arange(128), 4)[None, None, :]
    CM = np.where(16 * jl - ql <= 128 * m - 31, 0.0, -30000.0).astype(np.float32).astype(NPBF)
    kl = np.arange(128)[:, None]
    q4 = np.tile(np.arange(128), 4)[None, :]
    CA = np.where(kl <= q4, 0.0, -30000.0).astype(np.float32).astype(NPBF)
    WM = np.where(kl > q4, 0.0, -30000.0).astype(np.float32).astype(NPBF)
    qq = np.arange(128)[:, None]
    rel = np.arange(256)[None, :] - 126
    cur = (qq >= 64).astype(np.int64)
    AT = np.zeros((128, 256), np.float32)
    AT[(rel == cur) | (rel == cur - 1)] = 1e4
    AT[rel > cur] = -1.0
    jj = np.arange(512)[:, None]
    ii = np.arange(128)[None, :]
    Ov = ((jj >= 4 * ii - 1) & (jj <= 4 * ii + 3) & (jj <= 510)).astype(np.float32)
    return dict(E=np.ascontiguousarray(E), CM=np.ascontiguousarray(CM), CA=np.ascontiguousarray(CA),
                WM=np.ascontiguousarray(WM), AT=AT, Ov=Ov)


def _aug_k(pos):
    hi, lo = _split2(pos)
    one = np.ones_like(hi)
    return np.stack([one, one, hi, lo, hi, lo], 0)


def kernel(**inp):
    f = {k: np.asarray(v) for k, v in inp.items()}
    x = f["x"]
    B, T, D = x.shape
    xf = np.ascontiguousarray(x.reshape(B * T, D))
    ident = np.eye(128, dtype=np.float32).astype(NPBF)
    G4 = 4
    nc = _prog("nsa_a", build_nsa_a)
    maps = [{"x": xf[c * NT:(c + 1) * NT], "g": _pl(f["a_attn_norm"][0]), "w": np.ascontiguousarray(f["a_w_in"][0]),
             "gq": _bc(f["a_q_norm"][0]), "gs": _bc(f["a_kslc_norm"][0]), "gw": _bc(f["a_kwin_norm"][0]),
             "ident": ident} for c in range(8)]
    r = _run(nc, maps)
    big = np.concatenate([q["big"] for q in r], 0)
    gates = np.concatenate([q["gates"] for q in r], 0)
    nc = _prog("nsa_b", build_nsa_b)

    def a2T(col0, b):
        a = big[b * T:(b + 1) * T, col0:col0 + 64]
        o = np.zeros((128, T), NPBF)
        o[0:64] = a.T
        o[64:128, 0:T - 1] = a[1:].T
        return o

    def w1l(w):
        return np.ascontiguousarray(np.asarray(w, np.float32).reshape(16, 2, 64, 256).transpose(1, 2, 0, 3).reshape(128, 16, 256))

    def pel(p):
        return np.ascontiguousarray(np.asarray(p, np.float32).reshape(16, 2, 64).transpose(1, 2, 0).reshape(128, 16))

    maps = []
    for c in range(8):
        b, g = c // 4, c % 4
        maps.append({
            "k_a2T": a2T(1024 + g * 64, b), "v_a2T": a2T(1280 + g * 64, b),
            "k_w1": w1l(f["a_cmp_k_w1"][0]), "v_w1": w1l(f["a_cmp_v_w1"][0]),
            "k_pe": pel(f["a_cmp_pos_k"][0]), "v_pe": pel(f["a_cmp_pos_v"][0]),
            "k_b1": _pl(f["a_cmp_k_b1"][0]), "v_b1": _pl(f["a_cmp_v_b1"][0]),
            "k_w2": np.ascontiguousarray(f["a_cmp_k_w2"][0].reshape(2, 128, 64).transpose(1, 0, 2)),
            "v_w2": np.ascontiguousarray(f["a_cmp_v_w2"][0].reshape(2, 128, 64).transpose(1, 0, 2)),
            "gk": _bc(f["a_kcmp_norm"][0])})
    rB = _run(nc, maps)
    nc = _prog("nsa_c", build_nsa_c)
    C = _nsa_c_consts()
    tpos = np.arange(T, dtype=np.float64)
    kaug_tok = _aug_k(tpos)
    kaug_cmp = _aug_k(16.0 * np.arange(512) + 15.5)
    maps = []
    for c in range(8):
        b, g = c // 4, c % 4
        bb = big[b * T:(b + 1) * T]
        QT = np.zeros((70, 4, T), NPBF)
        for hg in range(4):
            h = g * 4 + hg
            QT[0:64, hg] = bb[:, h * 64:(h + 1) * 64].T
            beta = 8.0 * 2.0 ** (-8.0 * (h + 1) / 16.0)
            bh, bl = _split2(np.full(T, beta))
            th, tl = _split2(beta * tpos)
            QT[64:70, hg] = np.stack([-th.astype(np.float32), -tl.astype(np.float32), bh.astype(np.float32),
                                      bh.astype(np.float32), bl.astype(np.float32), bl.astype(np.float32)], 0).astype(NPBF)
        KS = np.concatenate([bb[:, 1536 + g * 64:1536 + (g + 1) * 64].T, kaug_tok], 0)
        KW = np.concatenate([bb[:, 2048 + g * 64:2048 + (g + 1) * 64].T, kaug_tok], 0)
        KC = np.concatenate([rB[c]["k_cmp"].T, kaug_cmp], 0)

        def vaug(v):
            o = np.zeros((128, 64, 66), NPBF)
            o[:, :, 0:64] = v.reshape(64, 128, 64).transpose(1, 0, 2)
            o[:, :, 64] = 1.0
            return o

        VC = np.zeros((128, 4, 194), NPBF)
        VC[:, :, 0:64] = rB[c]["v_cmp"].reshape(4, 128, 64).transpose(1, 0, 2)
        VC[:, :, 64] = 1.0
        VC[:, :, 65:193] = C["Ov"].reshape(4, 128, 128).transpose(1, 0, 2).astype(NPBF)
        Gt = np.ascontiguousarray(gates[b * T:(b + 1) * T, g * 12:(g + 1) * 12].reshape(64, 128, 12).transpose(1, 0, 2))
        maps.append({"QT": QT, "KS": np.ascontiguousarray(KS), "KW": np.ascontiguousarray(KW),
                     "KC": np.ascontiguousarray(KC), "VS": vaug(bb[:, 1792 + g * 64:1792 + (g + 1) * 64]),
                     "VW": vaug(bb[:, 2304 + g * 64:2304 + (g + 1) * 64]), "VC": VC, "G": Gt, "E": C["E"],
                     "CM": C["CM"], "CA": C["CA"], "WM": C["WM"], "AT": C["AT"], "ident": ident})
    rC = _run(nc, maps)
    o_nsa = np.zeros((B, T, 1024), np.float32)
    for c in range(8):
        o_nsa[c // 4, :, (c % 4) * 256:(c % 4 + 1) * 256] = rC[c]["o"]
    o_nsa = o_nsa.reshape(B * T, 1024)

    def outproj(o, xres, w):
        nc = _prog("outproj", build_outproj)
        r = _run(nc, [{"o": o[c * NT:(c + 1) * NT], "x": xres[c * NT:(c + 1) * NT], "w": np.ascontiguousarray(w),
                       "ident": ident} for c in range(8)])
        return np.concatenate([q["y"] for q in r], 0)

    def ffn(xin, g, wgu, wd):
        nc = _prog("ffn", build_ffn)
        r = _run(nc, [{"x": xin[c * NT:(c + 1) * NT], "g": _pl(g), "wgu": np.ascontiguousarray(wgu),
                       "wd": np.ascontiguousarray(wd), "ident": ident} for c in range(8)])
        return np.concatenate([q["y"] for q in r], 0)

    x1 = outproj(o_nsa, xf, f["a_w_out"][0])
    x2 = ffn(x1, f["ffn_norm"][0], f["ffn_w_gate_up"][0], f["ffn_w_down"][0])
    nc = _prog("kvq", build_kvq)
    inv = (10000.0 ** (-np.arange(0, 64, 2, dtype=np.float32) / 64)).astype(np.float32)
    maps = []
    for c in range(8):
        pos = (np.arange(NT) + (c % 4) * NT).astype(np.float32)
        ang = pos[:, None] * inv[None, :]
        cs = np.concatenate([np.cos(ang), np.sin(ang)], 1).astype(np.float32).reshape(NB, 128, 64).transpose(1, 0, 2)
        maps.append({"x": x2[c * NT:(c + 1) * NT], "g_kv": _pl(f["kv_norm"]), "g_q": _pl(f["b_attn_norm"][0]),
                     "g_c": _pl(f["kv_c_norm"]), "g_qa": _pl(f["b_q_a_norm"][0]), "g_kn": _bc(f["kv_k_norm"]),
                     "g_qn": _bc(f["b_q_norm"][0]), "w_kva": np.ascontiguousarray(f["kv_w_a"]),
                     "w_kvb": np.ascontiguousarray(f["kv_w_b"]), "w_qa": np.ascontiguousarray(f["b_w_q_a"][0]),
                     "w_qb": np.ascontiguousarray(f["b_w_q_b"][0]), "cs": np.ascontiguousarray(cs), "ident": ident})
    r = _run(nc, maps)
    kk = np.concatenate([q["k"] for q in r], 0)
    qq = np.concatenate([q["q"] for q in r], 0)
    vv = np.concatenate([q["v"] for q in r], 0)
    nc = _prog("mla", build_mla)
    r = _run(nc, [mla_host_inputs(kk, qq, vv, c // 4, c % 4) for c in range(8)])
    o_mla = np.zeros((B, T, 1024), np.float32)
    for c in range(8):
        o_mla[c // 4, :, (c % 4) * 256:(c % 4 + 1) * 256] = r[c]["o"]
    o_mla = o_mla.reshape(B * T, 1024)
    x3 = outproj(o_mla, x2, f["b_w_out"][0])
    x4 = ffn(x3, f["ffn_norm"][1], f["ffn_w_gate_up"][1], f["ffn_w_down"][1])
    return x4.reshape(B, T, D).astype(np.float32)
```

```python
import numpy as np
import ml_dtypes
import concourse.bass as bass
import concourse.mybir as mybir
from concourse.bass_utils import run_bass_kernel_spmd

F32 = mybir.dt.float32
BF16 = mybir.dt.bfloat16
AF = mybir.ActivationFunctionType
ALU = mybir.AluOpType
AX = mybir.AxisListType
NPBF = ml_dtypes.bfloat16

ENGS = ["sp", "act", "dve", "pool", "pe"]


class Buf:
    __slots__ = ("w", "r", "dsem", "dcnt", "name")

    def __init__(self, name=""):
        self.w = None
        self.r = []
        self.dsem = None
        self.dcnt = 0
        self.name = name


class Prog:
    def __init__(self):
        self.nc = bass.Bass("TRN2", target_bir_lowering=False)
        nc = self.nc
        self.q = {e: [] for e in ENGS}
        self.cnt = {e: 0 for e in ENGS}
        self.sems = {}
        self.sem = {}
        for e in ("act", "dve", "pool", "pe"):
            self.sem[e] = self.new_sem("eng_" + e)
        self.wm = {e: {} for e in ENGS}
        self.all_dma_tokens = {}
        self.nsb = 0
        self.nps = 0

    def new_sem(self, name):
        h = self.nc.alloc_semaphore(name)
        sid = len(self.sems)
        self.sems[sid] = h
        return sid

    def sb(self, shape, dtype, name=None):
        self.nsb += 1
        return self.nc.alloc_sbuf_tensor(f"S{self.nsb}_" + (name or ""), list(shape), dtype)

    def ps(self, shape, dtype=F32, name=None):
        self.nps += 1
        return self.nc.alloc_psum_tensor(f"P{self.nps}_" + (name or ""), list(shape), dtype)

    def dram(self, name, shape, dtype, kind):
        return self.nc.dram_tensor(name, list(shape), dtype, kind=kind)

    def _deps(self, eng, reads, writes):
        deps = {}

        def need(tok):
            if tok is None:
                return
            s, v = tok
            if eng == "pe" and s == self.sem["pe"]:
                return
            if deps.get(s, 0) < v:
                deps[s] = v

        for b in reads:
            need(b.w)
        for b in writes:
            need(b.w)
            for t in b.r:
                need(t)
        out = []
        wm = self.wm[eng]
        for s, v in deps.items():
            if wm.get(s, 0) < v:
                wm[s] = v
                out.append((s, v))
        return out

    def op(self, eng, fn, reads=(), writes=()):
        waits = self._deps(eng, reads, writes)
        self.cnt[eng] += 1
        tok = (self.sem[eng], self.cnt[eng])
        self.q[eng].append((waits, fn, (tok[0], 1)))
        for b in reads:
            b.r.append(tok)
        for b in writes:
            b.w = tok
            b.r = []
        return tok

    def dma(self, q, out, in_, reads=(), writes=(), **kw):
        waits = self._deps(q, reads, writes)
        tb = (list(writes) + list(reads))[0]
        if tb.dsem is None:
            tb.dsem = self.new_sem("d_" + tb.name + str(len(self.sems)))
        tb.dcnt += 1
        tok = (tb.dsem, 16 * tb.dcnt)
        self.all_dma_tokens[tb.dsem] = tok[1]
        self.q[q].append((waits, lambda e: e.dma_start(out=out, in_=in_, **kw), (tok[0], 16)))
        for b in reads:
            b.r.append(tok)
        for b in writes:
            b.w = tok
            b.r = []
        return tok

    def cc(self, kind, ins_ap, outs_ap, groups, reads=(), writes=()):
        waits = self._deps("pool", reads, writes)
        sem = self.new_sem("cc" + str(len(self.sems)))
        tok = (sem, 16)
        self.all_dma_tokens[sem] = 16
        self.q["pool"].append((waits, lambda e: e.collective_compute(
            kind, ALU.bypass, replica_groups=groups, ins=[ins_ap], outs=[outs_ap]), (sem, 16)))
        for b in reads:
            b.r.append(tok)
        for b in writes:
            b.w = tok
            b.r = []
        return tok

    def finish(self):
        nc = self.nc
        final = [(s, v) for s, v in self.all_dma_tokens.items()]
        for e in ("act", "dve", "pool", "pe"):
            if self.cnt[e] > 0:
                final.append((self.sem[e], self.cnt[e]))
        self.q["sp"].append((final, None, None))
        emap = {"sp": "sync", "act": "scalar", "dve": "vector", "pool": "gpsimd", "pe": "tensor"}
        sems = self.sems

        def replay(lst):
            def run(e):
                for waits, fn, inc in lst:
                    for s, v in waits:
                        e.wait_ge(sems[s], v)
                    if fn is not None:
                        ins = fn(e)
                        ins.then_inc(sems[inc[0]], inc[1])
            return run

        with nc.Block() as block:
            for k in ENGS:
                if self.q[k]:
                    getattr(block, emap[k])(replay(self.q[k]))
        return nc


def mm(P, out, lhsT, rhs, start, stop, reads, writes, skip=False):
    return P.op("pe", lambda e: e.matmul(out, lhsT=lhsT, rhs=rhs, start=start, stop=stop,
                                         skip_group_check=skip), reads=reads, writes=writes)


def tr(P, out, in_, ident, reads, writes):
    return P.op("pe", lambda e: e.transpose(out, in_, ident), reads=reads, writes=writes)


def act(P, out, in_, func, reads, writes, **kw):
    return P.op("act", lambda e: e.activation(out=out, in_=in_, func=func, **kw), reads=reads, writes=writes)


def tt(P, eng, out, in0, in1, op, reads, writes):
    return P.op(eng, lambda e: e.tensor_tensor(out=out, in0=in0, in1=in1, op=op), reads=reads, writes=writes)


def ts(P, eng, out, in0, s1, s2, op0, op1, reads, writes):
    if op1 is None:
        return P.op(eng, lambda e: e.tensor_scalar(out=out, in0=in0, scalar1=s1, scalar2=None, op0=op0),
                    reads=reads, writes=writes)
    return P.op(eng, lambda e: e.tensor_scalar(out=out, in0=in0, scalar1=s1, scalar2=s2, op0=op0, op1=op1),
                reads=reads, writes=writes)


def cp(P, eng, out, in_, reads, writes):
    if eng == "act":
        return P.op("act", lambda e: e.activation(out=out, in_=in_, func=AF.Copy), reads=reads, writes=writes)
    return P.op(eng, lambda e: e.tensor_copy(out=out, in_=in_), reads=reads, writes=writes)


def memset(P, eng, ap, val, writes):
    return P.op(eng, lambda e: e.memset(ap, val), writes=writes)


class Rot:
    def __init__(self, P, n, shape, dtype, name, psum=False):
        self.items = []
        for i in range(n):
            t = P.ps(shape, dtype, f"{name}{i}") if psum else P.sb(shape, dtype, f"{name}{i}")
            self.items.append((t, Buf(f"{name}{i}")))
        self.i = 0

    def next(self):
        it = self.items[self.i % len(self.items)]
        self.i += 1
        return it


def load_const(P, dram_ap, shape, dtype, name, q="sp"):
    t = P.sb(shape, dtype, name)
    b = Buf(name)
    P.dma(q, t[:], dram_ap, writes=[b])
    return t, b


class WBufs:
    def __init__(self):
        self.items = []

    def cols(self, c0, c1):
        return [b for lo, hi, b in self.items if lo < c1 and hi > c0]

    def kc(self, j):
        return [b for lo, hi, b in self.items if lo <= j < hi]


def load_weight(P, w_ap, K, N, name, stg, gain=None, by="col", cw=1024, defer=False):
    KC = K // 128
    wt = P.sb([128, KC, N], BF16, name)
    wb = WBufs()
    cnt = [0]

    def piece(kc, c0, c1, buf):
        st, sbuf = stg.next()
        P.dma("sp", st[:, 0:c1 - c0], w_ap[kc * 128:(kc + 1) * 128, c0:c1], writes=[sbuf])
        eng = "dve" if cnt[0] % 2 == 0 else "act"
        cnt[0] += 1
        if gain is None:
            cp(P, eng, wt[:, kc, c0:c1], st[:, 0:c1 - c0], [sbuf], [buf])
        else:
            gt, gb = gain
            if eng == "dve":
                ts(P, "dve", wt[:, kc, c0:c1], st[:, 0:c1 - c0], gt[:, kc:kc + 1], None, ALU.mult, None,
                   [sbuf, gb], [buf])
            else:
                act(P, wt[:, kc, c0:c1], st[:, 0:c1 - c0], AF.Copy, [sbuf, gb], [buf], scale=gt[:, kc:kc + 1])

    chunks = {}
    keys = []
    if by == "col":
        for c0 in range(0, N, cw):
            c1 = min(N, c0 + cw)
            buf = Buf(f"{name}_{c0}")
            wb.items.append((c0, c1, buf))
            chunks[c0] = [(kc, c0, c1, buf) for kc in range(KC)]
            keys.append(c0)
    else:
        for kc in range(KC):
            buf = Buf(f"{name}_k{kc}")
            wb.items.append((kc, kc + 1, buf))
            chunks[kc] = [(kc, c0, min(N, c0 + cw), buf) for c0 in range(0, N, cw)]
            keys.append(kc)
    done = set()

    def load(key):
        if key in done:
            return
        done.add(key)
        for a in chunks[key]:
            piece(*a)

    if defer:
        return wt, wb, load
    for k in keys:
        load(k)
    return wt, wb


def rms_rstd(P, x_ap, D, ss, scratch, reads, SS, SCR, eps=1e-6):
    act(P, scratch, x_ap, AF.Square, reads, [SCR, SS], accum_out=ss)
    act(P, ss, ss, AF.Ln, [SS], [SS], scale=1.0 / D, bias=eps)
    act(P, ss, ss, AF.Exp, [SS], [SS], scale=-0.5)


def make_ident(P, ident_dram):
    return load_const(P, ident_dram, [128, 128], BF16, "ident_sb")


NT = 2048
NB = NT // 128


def build_outproj():
    P = Prog()
    o_d = P.dram("o", [NT, 1024], F32, "ExternalInput").ap()
    x_d = P.dram("x", [NT, 1024], F32, "ExternalInput").ap()
    w_d = P.dram("w", [1024, 1024], F32, "ExternalInput").ap()
    id_d = P.dram("ident", [128, 128], BF16, "ExternalInput").ap()
    y_d = P.dram("y", [NT, 1024], F32, "ExternalOutput").ap()
    ident, IDB = make_ident(P, id_d)
    stg = Rot(P, 4, [128, 1024], F32, "stg")
    w, WB = load_weight(P, w_d, 1024, 1024, "w", stg)
    oin = Rot(P, 2, [128, 1024], F32, "oin")
    xin = Rot(P, 2, [128, 1024], F32, "xin")
    obf = Rot(P, 2, [128, 1024], BF16, "obf")
    oT = Rot(P, 2, [128, 8, 128], BF16, "oT")
    pT = Rot(P, 2, [128, 8, 128], BF16, "pT", psum=True)
    acc = Rot(P, 4, [128, 512], F32, "acc", psum=True)
    yo = Rot(P, 2, [128, 1024], F32, "yo")
    for b in range(NB):
        rows = slice(b * 128, (b + 1) * 128)
        ot, OB = oin.next()
        xt, XB = xin.next()
        P.dma("sp", ot[:], o_d[rows, :], writes=[OB])
        P.dma("sp", xt[:], x_d[rows, :], writes=[XB])
        bt, BB = obf.next()
        cp(P, "dve", bt[:], ot[:], [OB], [BB])
        pt, PB = pT.next()
        for c in range(8):
            tr(P, pt[:, c, :], bt[:, c * 128:(c + 1) * 128], ident[:], [BB, IDB], [PB])
        tt_, TB = oT.next()
        cp(P, "act", tt_[:], pt[:], [PB], [TB])
        yt, YB = yo.next()
        for half in range(2):
            at, AB = acc.next()
            for c in range(8):
                mm(P, at[:], tt_[:, c, :], w[:, c, half * 512:(half + 1) * 512], c == 0, c == 7,
                   [TB] + WB.cols(half * 512, half * 512 + 512), [AB])
            tt(P, "dve", yt[:, half * 512:(half + 1) * 512], at[:], xt[:, half * 512:(half + 1) * 512], ALU.add,
               [AB, XB], [YB])
        P.dma("pool", y_d[rows, :], yt[:], reads=[YB])
    return P.finish()


FH = 2816


def build_ffn():
    P = Prog()
    x_d = P.dram("x", [NT, 1024], F32, "ExternalInput").ap()
    g_d = P.dram("g", [128, 8], F32, "ExternalInput").ap()
    wgu_d = P.dram("wgu", [1024, 2 * FH], F32, "ExternalInput").ap()
    wd_d = P.dram("wd", [FH, 1024], F32, "ExternalInput").ap()
    id_d = P.dram("ident", [128, 128], BF16, "ExternalInput").ap()
    y_d = P.dram("y", [NT, 1024], F32, "ExternalOutput").ap()
    ident, IDB = make_ident(P, id_d)
    gain = load_const(P, g_d, [128, 8], F32, "gain")
    stg = Rot(P, 4, [128, 1024], F32, "stg")
    wgu, WGU, ld_gu = load_weight(P, wgu_d, 1024, 2 * FH, "wgu", stg, gain=gain, cw=512, defer=True)
    wd, WD, ld_d = load_weight(P, wd_d, FH, 1024, "wd", stg, by="kc", defer=True)
    for j in range(FH // 128):
        ld_gu((j * 128) // 512 * 512)
        ld_gu((FH + j * 128) // 512 * 512)
        ld_d(j)
    xin = Rot(P, 4, [128, 1024], F32, "xin")
    scr = P.sb([128, 1024], F32, "scr"); SCR = Buf("scr")
    ssr = Rot(P, 2, [128, 1], F32, "ss")
    xn = Rot(P, 2, [128, 1024], BF16, "xn")
    pT = Rot(P, 1, [128, 8, 128], BF16, "pT", psum=True)
    xT = Rot(P, 2, [128, 8, 256], BF16, "xT")
    gu = Rot(P, 3, [128, 512], F32, "gu", psum=True)
    accs = Rot(P, 4, [128, 512], F32, "acc", psum=True)
    sg = Rot(P, 2, [128, 256], BF16, "sg")
    aT = Rot(P, 2, [128, 256], BF16, "aT")
    yo = Rot(P, 2, [128, 1024], F32, "yo")
    NJ = FH // 128
    for sbk in range(NT // 256):
        xs = []
        xTt, XT = xT.next()
        for t in range(2):
            rows = slice(sbk * 256 + t * 128, sbk * 256 + (t + 1) * 128)
            xt, XB = xin.next()
            xs.append((xt, XB, rows))
            P.dma("sp", xt[:], x_d[rows, :], writes=[XB])
            ss, SS = ssr.next()
            rms_rstd(P, x_ap=xt[:], D=1024, ss=ss[:], scratch=scr[:], reads=[XB], SS=SS, SCR=SCR)
            xnt, XN = xn.next()
            ts(P, "dve", xnt[:], xt[:], ss[:, 0:1], None, ALU.mult, None, [XB, SS], [XN])
            pt, PB = pT.next()
            for c in range(8):
                tr(P, pt[:, c, :], xnt[:, c * 128:(c + 1) * 128], ident[:], [XN, IDB], [PB])
            cp(P, "dve", xTt[:, :, t * 128:(t + 1) * 128], pt[:], [PB], [XT])
        acc = [[accs.next() for _ in range(2)] for _ in range(2)]
        for j in range(NJ):
            gt, GB = gu.next()
            ut, UB = gu.next()
            for c in range(8):
                mm(P, gt[:, 0:256], wgu[:, c, j * 128:(j + 1) * 128], xTt[:, c, :], c == 0, c == 7,
                   WGU.cols(j * 128, j * 128 + 128) + [XT], [GB])
            for c in range(8):
                mm(P, ut[:, 0:256], wgu[:, c, FH + j * 128:FH + (j + 1) * 128], xTt[:, c, :], c == 0, c == 7,
                   WGU.cols(FH + j * 128, FH + j * 128 + 128) + [XT], [UB])
            sgt, SG = sg.next()
            act(P, sgt[:], gt[:, 0:256], AF.Silu, [GB], [SG])
            at, AB = aT.next()
            tt(P, "dve", at[:], sgt[:], ut[:, 0:256], ALU.mult, [SG, UB], [AB])
            for t in range(2):
                for half in range(2):
                    a_t, A_B = acc[t][half]
                    mm(P, a_t[:], at[:, t * 128:(t + 1) * 128], wd[:, j, half * 512:(half + 1) * 512], j == 0,
                       j == NJ - 1, [AB] + WD.kc(j), [A_B])
        for t in range(2):
            xt, XB, rows = xs[t]
            yt, YB = yo.next()
            for half in range(2):
                a_t, A_B = acc[t][half]
                tt(P, "dve", yt[:, half * 512:(half + 1) * 512], a_t[:], xt[:, half * 512:(half + 1) * 512], ALU.add,
                   [A_B, XB], [YB])
            P.dma("pool", y_d[rows, :], yt[:], reads=[YB])
    return P.finish()


def rope_ops(P, src, SRC, dst, DST, cosb, sinb, CS, tmp, TMP, nh):
    x1 = src[:, :, 0:32]
    x2 = src[:, :, 32:64]
    t1, t2 = tmp[:, 0, 0:nh, :], tmp[:, 1, 0:nh, :]
    tt(P, "dve", t1, x1, cosb, ALU.mult, [SRC, CS], [TMP])
    tt(P, "dve", t2, x2, sinb, ALU.mult, [SRC, CS], [TMP])
    tt(P, "dve", dst[:, :, 0:32], t1, t2, ALU.subtract, [TMP], [DST])
    tt(P, "dve", t1, x1, sinb, ALU.mult, [SRC, CS], [TMP])
    tt(P, "dve", t2, x2, cosb, ALU.mult, [SRC, CS], [TMP])
    tt(P, "dve", dst[:, :, 32:64], t1, t2, ALU.add, [TMP], [DST])


def build_kvq(dbg=0):
    P = Prog()
    x_d = P.dram("x", [NT, 1024], F32, "ExternalInput").ap()
    gkv_d = P.dram("g_kv", [128, 8], F32, "ExternalInput").ap()
    gq_d = P.dram("g_q", [128, 8], F32, "ExternalInput").ap()
    gc_d = P.dram("g_c", [128, 2], F32, "ExternalInput").ap()
    gqa_d = P.dram("g_qa", [128, 3], F32, "ExternalInput").ap()
    gkn_d = P.dram("g_kn", [128, 192], F32, "ExternalInput").ap()
    gqn_d = P.dram("g_qn", [128, 192], F32, "ExternalInput").ap()
    wkva_d = P.dram("w_kva", [1024, 320], F32, "ExternalInput").ap()
    wkvb_d = P.dram("w_kvb", [256, 2048], F32, "ExternalInput").ap()
    wqa_d = P.dram("w_qa", [1024, 384], F32, "ExternalInput").ap()
    wqb_d = P.dram("w_qb", [384, 1536], F32, "ExternalInput").ap()
    cs_d = P.dram("cs", [128, NB, 64], F32, "ExternalInput").ap()
    id_d = P.dram("ident", [128, 128], BF16, "ExternalInput").ap()
    kT_d = P.dram("k", [NT, 1536], BF16, "ExternalOutput").ap()
    qT_d = P.dram("q", [NT, 1536], BF16, "ExternalOutput").ap()
    v_d = P.dram("v", [NT, 1024], BF16, "ExternalOutput").ap()
    ident, IDB = make_ident(P, id_d)
    gkv = load_const(P, gkv_d, [128, 8], F32, "gkv")
    gq = load_const(P, gq_d, [128, 8], F32, "gq")
    gc = load_const(P, gc_d, [128, 2], F32, "gc")
    gqa = load_const(P, gqa_d, [128, 3], F32, "gqa")
    gkn, GKN = load_const(P, gkn_d, [128, 192], F32, "gkn")
    gqn, GQN = load_const(P, gqn_d, [128, 192], F32, "gqn")
    cs, CS = load_const(P, cs_d, [128, NB, 64], F32, "cs")
    stg = Rot(P, 4, [128, 1024], F32, "stg")
    wkva, WKVA = load_weight(P, wkva_d, 1024, 320, "wkva", stg, gain=gkv)
    wkvb, WKVB = load_weight(P, wkvb_d, 256, 2048, "wkvb", stg, gain=gc)
    wqa, WQA = load_weight(P, wqa_d, 1024, 384, "wqa", stg, gain=gq)
    wqb, WQB = load_weight(P, wqb_d, 384, 1536, "wqb", stg, gain=gqa)
    xin = Rot(P, 2, [128, 1024], F32, "xin")
    scr = P.sb([128, 1024], F32, "scr"); SCR = Buf("scr")
    ssr = Rot(P, 4, [128, 1], F32, "ss")
    ss8 = Rot(P, 4, [128, 8], F32, "ss8")
    xn = Rot(P, 2, [128, 1024], BF16, "xn")
    pT = Rot(P, 2, [128, 8, 128], BF16, "pT", psum=True)
    xT = Rot(P, 2, [128, 8, 128], BF16, "xT")
    pf = Rot(P, 6, [128, 512], F32, "pf", psum=True)
    cn = Rot(P, 2, [128, 384], BF16, "cn")
    cT = Rot(P, 2, [128, 3, 128], BF16, "cT")
    kpe = Rot(P, 2, [128, 64], F32, "kpe")
    tmpf = Rot(P, 3, [128, 8, 192], F32, "tmpf")
    pe8 = Rot(P, 2, [128, 8, 64], F32, "pe8")
    rtmp = P.sb([128, 2, 8, 32], F32, "rtmp"); RTMP = Buf("rtmp")
    kt = Rot(P, 2, [128, 8, 192], BF16, "kt")
    vt = Rot(P, 2, [128, 8, 128], BF16, "vt")
    kTs = Rot(P, 2, [128, 2, 8, 128], BF16, "kTs")
    for b in range(NB):
        rows = slice(b * 128, (b + 1) * 128)
        cosb = cs[:, b, 0:32].unsqueeze(1).to_broadcast([128, 8, 32])
        sinb = cs[:, b, 32:64].unsqueeze(1).to_broadcast([128, 8, 32])
        xt, XB = xin.next()
        P.dma("sp", xt[:], x_d[rows, :], writes=[XB])
        ss, SS = ssr.next()
        rms_rstd(P, xt[:], 1024, ss[:], scr[:], [XB], SS, SCR)
        xnt, XN = xn.next()
        ts(P, "dve", xnt[:], xt[:], ss[:, 0:1], None, ALU.mult, None, [XB, SS], [XN])
        pt, PB = pT.next()
        for c in range(8):
            tr(P, pt[:, c, :], xnt[:, c * 128:(c + 1) * 128], ident[:], [XN, IDB], [PB])
        xTt, XT = xT.next()
        cp(P, "dve", xTt[:], pt[:], [PB], [XT])
        kva, KVA = pf.next()
        qa, QA = pf.next()
        for c in range(8):
            mm(P, kva[:, 0:320], xTt[:, c, :], wkva[:, c, :], c == 0, c == 7, [XT] + WKVA.cols(0, 320), [KVA])
        for c in range(8):
            mm(P, qa[:, 0:384], xTt[:, c, :], wqa[:, c, :], c == 0, c == 7, [XT] + WQA.cols(0, 384), [QA])
        ss2, SS2 = ssr.next()
        rms_rstd(P, kva[:, 0:256], 256, ss2[:], scr[:, 0:256], [KVA], SS2, SCR)
        cnt, CN = cn.next()
        ts(P, "dve", cnt[:, 0:256], kva[:, 0:256], ss2[:, 0:1], None, ALU.mult, None, [KVA, SS2], [CN])
        kp, KP = kpe.next()
        tt(P, "dve", kp[:], kva[:, 256:320], gkn[:, 128:192], ALU.mult, [KVA, GKN], [KP])
        sspe, SSPE = ssr.next()
        act(P, scr[:, 0:64], kva[:, 256:320], AF.Square, [KVA], [SCR, SSPE], accum_out=sspe[:])
        pt2, PB2 = pT.next()
        for c in range(2):
            tr(P, pt2[:, c, :], cnt[:, c * 128:(c + 1) * 128], ident[:], [CN, IDB], [PB2])
        cTt, CT = cT.next()
        cp(P, "act", cTt[:, 0:2, :], pt2[:, 0:2, :], [PB2], [CT])
        kvb = []
        for n in range(4):
            kvp, KVP = pf.next()
            kvb.append((kvp, KVP))
            for c in range(2):
                mm(P, kvp[:], cTt[:, c, :], wkvb[:, c, n * 512:(n + 1) * 512], c == 0, c == 1,
                   [CT] + WKVB.cols(n * 512, n * 512 + 512), [KVP])
        s8, S8 = ss8.next()
        for h in range(8):
            kvp, KVP = kvb[h // 2]
            act(P, scr[:, 0:128], kvp[:, (h % 2) * 256:(h % 2) * 256 + 128], AF.Square, [KVP], [SCR, S8],
                accum_out=s8[:, h:h + 1])
        ts(P, "dve", s8[:], s8[:], sspe[:, 0:1], None, ALU.add, None, [S8, SSPE], [S8])
        act(P, s8[:], s8[:], AF.Ln, [S8], [S8], scale=1.0 / 192, bias=1e-6)
        act(P, s8[:], s8[:], AF.Exp, [S8], [S8], scale=-0.5)
        tf, TF = tmpf.next()
        ktt, KT = kt.next()
        vtt, VT = vt.next()
        for n in range(4):
            kvp, KVP = kvb[n]
            kv3 = kvp[:].rearrange("p (h c) -> p h c", h=2)
            tt(P, "dve", tf[:, 2 * n:2 * n + 2, 0:128], kv3[:, :, 0:128],
               gkn[:, 0:128].unsqueeze(1).to_broadcast([128, 2, 128]), ALU.mult, [KVP, GKN], [TF])
            cp(P, "act", vtt[:, 2 * n:2 * n + 2, :], kv3[:, :, 128:256], [KVP], [VT])
        tt(P, "dve", ktt[:, :, 0:128], tf[:, :, 0:128], s8[:, :].unsqueeze(2).to_broadcast([128, 8, 128]), ALU.mult,
           [TF, S8], [KT])
        p8, P8 = pe8.next()
        tt(P, "dve", p8[:], kp[:, :].unsqueeze(1).to_broadcast([128, 8, 64]),
           s8[:, :].unsqueeze(2).to_broadcast([128, 8, 64]), ALU.mult, [KP, S8], [P8])
        rope_ops(P, p8[:], P8, ktt[:, :, 128:192], KT, cosb, sinb, CS, rtmp, RTMP, 8)
        P.dma("pool", v_d[rows, :], vtt[:].rearrange("p h d -> p (h d)"), reads=[VT])

        def emit_T(src, SRC, dst_d):
            P.dma("pool", dst_d[rows, :], src[:].rearrange("p h d -> p (h d)"), reads=[SRC])

        if dbg != 1:
            emit_T(ktt, KT, kT_d)
        if dbg in (1, 2):
            continue
        ss3, SS3 = ssr.next()
        rms_rstd(P, qa[:, 0:384], 384, ss3[:], scr[:, 0:384], [QA], SS3, SCR)
        qn, QN = cn.next()
        ts(P, "dve", qn[:], qa[:, 0:384], ss3[:, 0:1], None, ALU.mult, None, [QA, SS3], [QN])
        pt3, PB3 = pT.next()
        for c in range(3):
            tr(P, pt3[:, c, :], qn[:, c * 128:(c + 1) * 128], ident[:], [QN, IDB], [PB3])
        qT_, QT = cT.next()
        cp(P, "act", qT_[:], pt3[:, 0:3, :], [PB3], [QT])
        qb = []
        for n in range(4):
            qp, QP = pf.next()
            qb.append((qp, QP))
            for c in range(3):
                mm(P, qp[:, 0:384], qT_[:, c, :], wqb[:, c, n * 384:(n + 1) * 384], c == 0, c == 2,
                   [QT] + WQB.cols(n * 384, n * 384 + 384), [QP])
        s8q, S8Q = ss8.next()
        for h in range(8):
            qp, QP = qb[h // 2]
            act(P, scr[:, 0:192], qp[:, (h % 2) * 192:(h % 2) * 192 + 192], AF.Square, [QP], [SCR, S8Q],
                accum_out=s8q[:, h:h + 1])
        act(P, s8q[:], s8q[:], AF.Ln, [S8Q], [S8Q], scale=1.0 / 192, bias=1e-6)
        act(P, s8q[:], s8q[:], AF.Exp, [S8Q], [S8Q], scale=-0.5)
        if dbg == 3:
            continue
        tq, TQ = tmpf.next()
        for n in range(4):
            qp, QP = qb[n]
            cp(P, "act", tq[:, 2 * n:2 * n + 2, :].rearrange("p h c -> p (h c)"), qp[:, 0:384], [QP], [TQ])
        tq2, TQ2 = tmpf.next()
        tt(P, "dve", tq2[:], tq[:], gqn[:, :].unsqueeze(1).to_broadcast([128, 8, 192]), ALU.mult, [TQ, GQN], [TQ2])
        tq, TQ = tq2, TQ2
        if dbg == 4:
            continue
        qtt, QTT = kt.next()
        tt(P, "dve", qtt[:, :, 0:128], tq[:, :, 0:128], s8q[:, :].unsqueeze(2).to_broadcast([128, 8, 128]), ALU.mult,
           [TQ, S8Q], [QTT])
        p8q, P8Q = pe8.next()
        tt(P, "dve", p8q[:], tq[:, :, 128:192], s8q[:, :].unsqueeze(2).to_broadcast([128, 8, 64]), ALU.mult,
           [TQ, S8Q], [P8Q])
        rope_ops(P, p8q[:], P8Q, qtt[:, :, 128:192], QTT, cosb, sinb, CS, rtmp, RTMP, 8)
        emit_T(qtt, QTT, qT_d)
    return P.finish()


def run_pipeline(P, steps, s_r, p_r, scale, look=2):
    n = len(steps)
    Sb = [None] * n

    def do_qk(i):
        S, SB_ = s_r.next()
        Sb[i] = (S, SB_)
        steps[i][0](S, SB_)

    for i in range(min(look, n)):
        do_qk(i)
    for i in range(n):
        if i + look < n:
            do_qk(i + look)
        S, SB_ = Sb[i]
        p, PB = p_r.next()
        act(P, p[:], S[:], AF.Exp, [SB_], [PB], scale=scale)
        steps[i][1](p, PB)
        if steps[i][2] is not None:
            steps[i][2]()


TT = 8192


def build_mla():
    P = Prog()
    kTa_d = P.dram("kTa", [128, 2, TT], BF16, "ExternalInput").ap()
    kTb_d = P.dram("kTb", [128, 2, TT], BF16, "ExternalInput").ap()
    qTa_d = P.dram("qTa", [128, 2, TT], BF16, "ExternalInput").ap()
    qTb_d = P.dram("qTb", [128, 2, TT], BF16, "ExternalInput").ap()
    v_d = P.dram("vaug", [128, 64, 2, 132], BF16, "ExternalInput").ap()
    m_d = P.dram("masks", [128, 4, 512], BF16, "ExternalInput").ap()
    o_d = P.dram("o", [TT, 256], F32, "ExternalOutput").ap()
    kTa = P.sb([128, 2, TT], BF16, "kTa"); KA = [Buf("ka0"), Buf("ka1")]
    kTb = P.sb([128, 2, TT], BF16, "kTb"); KB = [Buf("kb0"), Buf("kb1")]
    va = P.sb([128, 64, 2, 132], BF16, "va"); VA = Buf("va")
    for h in range(2):
        P.dma("sp", kTa[:, h, :], kTa_d[:, h, :], writes=[KA[h]])
        P.dma("sp", kTb[:, h, :], kTb_d[:, h, :], writes=[KB[h]])
    for c0 in range(0, 64, 16):
        P.dma("sp", va[:, c0:c0 + 16], v_d[:, c0:c0 + 16], writes=[VA])
    mk, MK = load_const(P, m_d, [128, 4, 512], BF16, "mk")
    qa_r = Rot(P, 2, [128, 512], BF16, "qa")
    qb_r = Rot(P, 2, [128, 512], BF16, "qb")
    s_r = Rot(P, 3, [128, 512], F32, "S", psum=True)
    acc_r = Rot(P, 4, [128, 512], F32, "acc", psum=True)
    p_r = Rot(P, 4, [128, 512], BF16, "p")
    pm_r = Rot(P, 3, [128, 512], BF16, "pm")
    rs_r = Rot(P, 4, [128, 1], F32, "rs")
    o_r = Rot(P, 2, [128, 4, 128], F32, "osb")
    scale = 192.0 ** -0.5
    steps = []
    for h in range(2):
        for qs in range(TT // 512):
            ctx = {}

            def mk_qk(h=h, qs=qs, ctx=ctx, kc=0):
                def f(S, SB_):
                    if kc == 0:
                        qa, QA = qa_r.next()
                        qb, QB = qb_r.next()
                        P.dma("sp", qa[:], qTa_d[:, h, qs * 512:(qs + 1) * 512], writes=[QA])
                        P.dma("sp", qb[:], qTb_d[:, h, qs * 512:(qs + 1) * 512], writes=[QB])
                        ctx["q"] = (qa, QA, qb, QB)
                        ctx["accs"] = [acc_r.next() for _ in range(4)]
                    qa, QA, qb, QB = ctx["q"]
                    mm(P, S[:], kTa[:, h, kc * 128:(kc + 1) * 128], qa[:], True, False, [KA[h], QA], [SB_])
                    mm(P, S[:], kTb[:, h, kc * 128:(kc + 1) * 128], qb[:], False, True, [KB[h], QB], [SB_])
                return f

            def mk_pv(h=h, qs=qs, ctx=ctx, kc=0):
                def f(p, PB):
                    j = kc - 4 * qs
                    if j >= 0:
                        pm, PM = pm_r.next()
                        tt(P, "dve", pm[:], p[:], mk[:, j, :], ALU.mult, [PB, MK], [PM])
                        p, PB = pm, PM
                    for t in range(4):
                        if j >= 0 and t < j:
                            continue
                        a, AB = ctx["accs"][t]
                        mm(P, a[:, 0:129], p[:, t * 128:(t + 1) * 128], va[:, kc, h, 0:129], kc == 0,
                           kc == 4 * qs + t, [PB, VA], [AB])
                return f

            def mk_after(h=h, qs=qs, ctx=ctx):
                def f():
                    osb, OB = o_r.next()
                    for t in range(4):
                        a, AB = ctx["accs"][t]
                        rs, RS = rs_r.next()
                        P.op("dve", lambda e, rs=rs, a=a: e.reciprocal(out=rs[:], in_=a[:, 128:129]), reads=[AB],
                             writes=[RS])
                        ts(P, "dve", osb[:, t, :], a[:, 0:128], rs[:, 0:1], None, ALU.mult, None, [AB, RS], [OB])
                    P.dma("sp", o_d[qs * 512:(qs + 1) * 512, h * 128:(h + 1) * 128].rearrange("(t p) d -> p t d", p=128),
                          osb[:], reads=[OB])
                return f

            nk = 4 * qs + 4
            for kc in range(nk):
                steps.append((mk_qk(kc=kc), mk_pv(kc=kc), mk_after() if kc == nk - 1 else None))
    run_pipeline(P, steps, s_r, p_r, scale)
    return P.finish()


def mla_host_inputs(k, q, v, b, hp):
    ks = k[b * TT:(b + 1) * TT].reshape(TT, 8, 192)[:, 2 * hp:2 * hp + 2, :]
    qs = q[b * TT:(b + 1) * TT].reshape(TT, 8, 192)[:, 2 * hp:2 * hp + 2, :]
    vs = v[b * TT:(b + 1) * TT].reshape(TT, 8, 128)[:, 2 * hp:2 * hp + 2, :]
    kT = np.ascontiguousarray(ks.transpose(2, 1, 0))
    qT = np.ascontiguousarray(qs.transpose(2, 1, 0))
    vaug = np.zeros((128, 64, 2, 132), dtype=NPBF)
    vaug[:, :, :, 0:128] = vs.reshape(64, 128, 2, 128).transpose(1, 0, 2, 3)
    vaug[:, :, :, 128] = 1.0
    kl = np.arange(128)[:, None, None]
    j = np.arange(4)[None, :, None]
    ql = np.arange(512)[None, None, :]
    masks = (j * 128 + kl <= ql).astype(np.float32).astype(NPBF)
    kTb = np.zeros((128, 2, TT), NPBF)
    kTb[0:64] = kT[128:192]
    qTb = np.zeros((128, 2, TT), NPBF)
    qTb[0:64] = qT[128:192]
    return {"kTa": np.ascontiguousarray(kT[0:128]), "kTb": kTb,
            "qTa": np.ascontiguousarray(qT[0:128]), "qTb": qTb,
            "vaug": vaug, "masks": np.ascontiguousarray(masks)}


NW = 2608


def build_nsa_a():
    P = Prog()
    x_d = P.dram("x", [NT, 1024], F32, "ExternalInput").ap()
    g_d = P.dram("g", [128, 8], F32, "ExternalInput").ap()
    w_d = P.dram("w", [1024, NW], F32, "ExternalInput").ap()
    gq_d = P.dram("gq", [128, 64], F32, "ExternalInput").ap()
    gs_d = P.dram("gs", [128, 64], F32, "ExternalInput").ap()
    gw_d = P.dram("gw", [128, 64], F32, "ExternalInput").ap()
    id_d = P.dram("ident", [128, 128], BF16, "ExternalInput").ap()
    big_d = P.dram("big", [NT, 2560], BF16, "ExternalOutput").ap()
    gates_d = P.dram("gates", [NT, 48], F32, "ExternalOutput").ap()
    ident, IDB = make_ident(P, id_d)
    gain = load_const(P, g_d, [128, 8], F32, "gain")
    gq, GQ = load_const(P, gq_d, [128, 64], F32, "gq")
    gs, GS = load_const(P, gs_d, [128, 64], F32, "gs")
    gw, GW = load_const(P, gw_d, [128, 64], F32, "gw")
    stg = Rot(P, 4, [128, 1024], F32, "stg")
    w, WB = load_weight(P, w_d, 1024, NW, "w", stg, gain=gain)
    xin = Rot(P, 2, [128, 1024], F32, "xin")
    scr = P.sb([128, 1536], F32, "scr"); SCR = Buf("scr")
    ssr = Rot(P, 2, [128, 1], F32, "ss")
    xn = Rot(P, 2, [128, 1024], BF16, "xn")
    pT = Rot(P, 1, [128, 8, 128], BF16, "pT", psum=True)
    xT = Rot(P, 2, [128, 8, 128], BF16, "xT")
    pf = Rot(P, 6, [128, 512], F32, "pf", psum=True)
    y_r = Rot(P, 2, [128, NW], F32, "y")
    ssn_r = Rot(P, 2, [128, 24], F32, "ssn")
    tmp_r = Rot(P, 2, [128, 1024], F32, "tmp")
    big_r = Rot(P, 2, [128, 2560], BF16, "bigs")
    gt_r = Rot(P, 2, [128, 48], F32, "gts")
    for b in range(NB):
        rows = slice(b * 128, (b + 1) * 128)
        xt, XB = xin.next()
        P.dma("sp", xt[:], x_d[rows, :], writes=[XB])
        ss, SS = ssr.next()
        rms_rstd(P, xt[:], 1024, ss[:], scr[:, 0:1024], [XB], SS, SCR)
        xnt, XN = xn.next()
        ts(P, "dve", xnt[:], xt[:], ss[:, 0:1], None, ALU.mult, None, [XB, SS], [XN])
        pt, PB = pT.next()
        for c in range(8):
            tr(P, pt[:, c, :], xnt[:, c * 128:(c + 1) * 128], ident[:], [XN, IDB], [PB])
        xTt, XT = xT.next()
        cp(P, "dve", xTt[:], pt[:], [PB], [XT])
        y, YB = y_r.next()
        for n in range(6):
            c0, c1 = n * 512, min(NW, (n + 1) * 512)
            pp, PP = pf.next()
            for c in range(8):
                mm(P, pp[:, 0:c1 - c0], xTt[:, c, :], w[:, c, c0:c1], c == 0, c == 7, [XT] + WB.cols(c0, c1), [PP])
            cp(P, "act", y[:, c0:c1], pp[:, 0:c1 - c0], [PP], [YB])
        act(P, scr[:, 0:1024], y[:, 0:1024], AF.Square, [YB], [SCR])
        act(P, scr[:, 1024:1280], y[:, 1536:1792], AF.Square, [YB], [SCR])
        act(P, scr[:, 1280:1536], y[:, 2048:2304], AF.Square, [YB], [SCR])
        ssn, SSN = ssn_r.next()
        P.op("dve", lambda e, ssn=ssn: e.tensor_reduce(out=ssn[:], in_=scr[:, 0:1536].rearrange("p (h d) -> p h d", d=64),
                                                      axis=AX.X, op=ALU.add), reads=[SCR], writes=[SSN])
        act(P, ssn[:], ssn[:], AF.Ln, [SSN], [SSN], scale=1.0 / 64, bias=1e-6)
        act(P, ssn[:], ssn[:], AF.Exp, [SSN], [SSN], scale=-0.5)
        bg, BG = big_r.next()
        tmp, TMP = tmp_r.next()

        def normed(src0, nh, h0, gt, GT, dst0):
            tv = tmp[:, 0:nh * 64].rearrange("p (h d) -> p h d", d=64)
            tt(P, "dve", tv, y[:, src0:src0 + nh * 64].rearrange("p (h d) -> p h d", d=64),
               ssn[:, h0:h0 + nh].unsqueeze(2).to_broadcast([128, nh, 64]), ALU.mult, [YB, SSN], [TMP])
            tt(P, "dve", bg[:, dst0:dst0 + nh * 64].rearrange("p (h d) -> p h d", d=64), tv,
               gt[:, :].unsqueeze(1).to_broadcast([128, nh, 64]), ALU.mult, [TMP, GT], [BG])

        normed(0, 16, 0, gq, GQ, 0)
        normed(1536, 4, 16, gs, GS, 1536)
        normed(2048, 4, 20, gw, GW, 2048)
        cp(P, "dve", bg[:, 1024:1536], y[:, 1024:1536], [YB], [BG])
        cp(P, "dve", bg[:, 1792:2048], y[:, 1792:2048], [YB], [BG])
        cp(P, "dve", bg[:, 2304:2560], y[:, 2304:2560], [YB], [BG])
        gt_, GT_ = gt_r.next()
        act(P, gt_[:], y[:, 2560:2608], AF.Sigmoid, [YB], [GT_])
        P.dma("pool", big_d[rows, :], bg[:], reads=[BG])
        P.dma("pool", gates_d[rows, :], gt_[:], reads=[GT_])
    return P.finish()


def build_nsa_b():
    P = Prog()
    ins = {}
    for nm in ("k", "v"):
        ins[nm] = dict(
            aT=P.dram(nm + "_a2T", [128, TT], BF16, "ExternalInput").ap(),
            w1=P.dram(nm + "_w1", [128, 16, 256], F32, "ExternalInput").ap(),
            pe=P.dram(nm + "_pe", [128, 16], F32, "ExternalInput").ap(),
            b1=P.dram(nm + "_b1", [128, 2], F32, "ExternalInput").ap(),
            w2=P.dram(nm + "_w2", [128, 2, 64], F32, "ExternalInput").ap(),
            out=P.dram(nm + "_cmp", [512, 64], BF16, "ExternalOutput").ap(),
        )
    gk_d = P.dram("gk", [128, 64], F32, "ExternalInput").ap()
    gk, GK = load_const(P, gk_d, [128, 64], F32, "gk")
    ps_r = Rot(P, 4, [128, 512], F32, "ps", psum=True)
    f_r = Rot(P, 6, [128, 512], F32, "f")
    for nm in ("k", "v"):
        I = ins[nm]
        aT, AT = load_const(P, I["aT"], [128, TT], BF16, nm + "aT")
        w1f, W1F = load_const(P, I["w1"], [128, 16, 256], F32, nm + "w1f")
        pef, PEF = load_const(P, I["pe"], [128, 16], F32, nm + "pef")
        b1, B1 = load_const(P, I["b1"], [128, 2], F32, nm + "b1")
        w2f, W2F = load_const(P, I["w2"], [128, 2, 64], F32, nm + "w2f")
        w1 = P.sb([128, 16, 256], BF16, nm + "w1"); W1 = Buf(nm + "w1")
        cp(P, "dve", w1[:], w1f[:], [W1F], [W1])
        peb = P.sb([128, 16], BF16, nm + "peb"); PEB = Buf(nm + "peb")
        cp(P, "dve", peb[:], pef[:], [PEF], [PEB])
        w2 = P.sb([128, 2, 64], BF16, nm + "w2"); W2 = Buf(nm + "w2")
        cp(P, "dve", w2[:], w2f[:], [W2F], [W2])
        av = aT[:, :].rearrange("p (j s) -> p j s", s=16)
        gT = P.sb([128, 2, 512], BF16, nm + "gT"); GT = Buf(nm + "gT")
        memset(P, "dve", gT[:], 0.0, [GT])
        for half in range(2):
            hp, HP = ps_r.next()
            cv, CV = ps_r.next()
            for m in range(16):
                rhs = av[:, 0:511, 2 * m] if m < 8 else av[:, 1:512, 2 * m - 16]
                mm(P, hp[:, 0:511], w1[:, m, half * 128:(half + 1) * 128], rhs, m == 0, m == 15, [W1, AT], [HP])
            for m in range(16):
                mm(P, cv[:, 0:1], w1[:, m, half * 128:(half + 1) * 128], peb[:, m:m + 1], m == 0, m == 15, [W1, PEB], [CV])
            bias, BI = f_r.next()
            tt(P, "dve", bias[:, 0:1], cv[:, 0:1], b1[:, half:half + 1], ALU.add, [CV, B1], [BI])
            u, U = f_r.next()
            act(P, u[:, 0:511], hp[:, 0:511], AF.Identity, [HP, BI], [U], bias=bias[:, 0:1])
            t1, T1 = f_r.next()
            tt(P, "dve", t1[:, 0:511], u[:, 0:511], u[:, 0:511], ALU.mult, [U], [T1])
            t2, T2 = f_r.next()
            ts(P, "dve", t2[:, 0:511], t1[:, 0:511], 0.044715, 1.0, ALU.mult, ALU.add, [T1], [T2])
            t3, T3 = f_r.next()
            tt(P, "dve", t3[:, 0:511], t2[:, 0:511], u[:, 0:511], ALU.mult, [T2, U], [T3])
            t4, T4 = f_r.next()
            act(P, t4[:, 0:511], t3[:, 0:511], AF.Tanh, [T3], [T4], scale=0.7978845608028654)
            t5, T5 = f_r.next()
            ts(P, "dve", t5[:, 0:511], t4[:, 0:511], 1.0, 0.5, ALU.add, ALU.mult, [T4], [T5])
            tt(P, "dve", gT[:, half, 0:511], t5[:, 0:511], u[:, 0:511], ALU.mult, [T5, U], [GT])
        for ch in range(4):
            op_, OP = ps_r.next()
            for half in range(2):
                mm(P, op_[:, 0:64], gT[:, half, ch * 128:(ch + 1) * 128], w2[:, half, :], half == 0, half == 1,
                   [GT, W2], [OP])
            ob = P.sb([128, 64], BF16, f"{nm}ob{ch}"); OB = Buf("ob")
            if nm == "k":
                sq, SQ = f_r.next()
                ssk = P.sb([128, 1], F32, f"ssk{ch}"); SSK = Buf("ssk")
                rms_rstd(P, op_[:, 0:64], 64, ssk[:], sq[:, 0:64], [OP], SSK, SQ)
                kn, KN = f_r.next()
                ts(P, "dve", kn[:, 0:64], op_[:, 0:64], ssk[:, 0:1], None, ALU.mult, None, [OP, SSK], [KN])
                tt(P, "dve", ob[:], kn[:, 0:64], gk[:], ALU.mult, [KN, GK], [OB])
            else:
                cp(P, "act", ob[:], op_[:, 0:64], [OP], [OB])
            P.dma("sp", I["out"][ch * 128:(ch + 1) * 128, :], ob[:], reads=[OB])
    return P.finish()


NQB = TT // 128


def build_nsa_c():
    P = Prog()
    QT_d = P.dram("QT", [70, 4, TT], BF16, "ExternalInput").ap()
    KS_d = P.dram("KS", [128, TT], BF16, "ExternalInput").ap()
    KW_d = P.dram("KW", [70, TT], BF16, "ExternalInput").ap()
    KC_d = P.dram("KC", [70, 512], BF16, "ExternalInput").ap()
    VS_d = P.dram("VS", [128, 64, 66], BF16, "ExternalInput").ap()
    VW_d = P.dram("VW", [128, 64, 66], BF16, "ExternalInput").ap()
    VC_d = P.dram("VC", [128, 4, 194], BF16, "ExternalInput").ap()
    G_d = P.dram("G", [128, 64, 12], F32, "ExternalInput").ap()
    CM_d = P.dram("CM", [128, 17, 512], BF16, "ExternalInput").ap()
    CA_d = P.dram("CA", [128, 512], BF16, "ExternalInput").ap()
    WM_d = P.dram("WM", [128, 512], BF16, "ExternalInput").ap()
    AT_d = P.dram("AT", [128, 256], F32, "ExternalInput").ap()
    id_d = P.dram("ident", [128, 128], BF16, "ExternalInput").ap()
    o_d = P.dram("o", [TT, 256], F32, "ExternalOutput").ap()
    ident, IDB = make_ident(P, id_d)
    QT = P.sb([70, 4, TT], BF16, "QT"); QTB = Buf("QT")
    for h in range(4):
        P.dma("sp", QT[:, h, :], QT_d[:, h, :], writes=[QTB])
    KS, KSB = load_const(P, KS_d, [128, TT], BF16, "KS")
    KW, KWB = load_const(P, KW_d, [70, TT], BF16, "KW")
    KC, KCB = load_const(P, KC_d, [70, 512], BF16, "KC")
    VS, VSB = load_const(P, VS_d, [128, 64, 66], BF16, "VS")
    VW, VWB = load_const(P, VW_d, [128, 64, 66], BF16, "VW")
    VC, VCB = load_const(P, VC_d, [128, 4, 194], BF16, "VC")
    G, GB = load_const(P, G_d, [128, 64, 12], F32, "G")
    CM, CMB = load_const(P, CM_d, [128, 17, 512], BF16, "CM")
    CA, CAB = load_const(P, CA_d, [128, 512], BF16, "CA")
    WM, WMB = load_const(P, WM_d, [128, 512], BF16, "WM")
    AT, ATB = load_const(P, AT_d, [128, 256], F32, "AT")
    s_r = Rot(P, 3, [128, 512], F32, "S", psum=True)
    accC_r = Rot(P, 2, [128, 512], F32, "accC", psum=True)
    accS_r = Rot(P, 1, [128, 512], F32, "accS", psum=True)
    accW_r = Rot(P, 1, [128, 512], F32, "accW", psum=True)
    pst_r = Rot(P, 1, [128, 128], BF16, "pst", psum=True)
    p_r = Rot(P, 4, [128, 512], BF16, "p")
    sm_r = Rot(P, 48, [128, 1], F32, "sm")
    imp_r = Rot(P, 4, [128, 128], F32, "imp")
    sc_r = Rot(P, 4, [128, 128], F32, "sc")
    m8_r = Rot(P, 4, [128, 8], F32, "m8")
    ns_r = Rot(P, 2, [128, 128], BF16, "ns")
    nt_r = Rot(P, 3, [128, 4, 128], BF16, "nt")
    o_r = Rot(P, 8, [128, 256], F32, "ot")
    qv_r = Rot(P, 7, [128, 4, 128], BF16, "qv")
    for qvt, QVB in qv_r.items:
        memset(P, "pool", qvt[64:128], 0.0, [QVB])
    scale = 0.125

    def qk(S, SB_, KT, KTB, c, rhsQ, extra):
        mm(P, S[:], KT[0:70, c * 128:(c + 1) * 128], rhsQ, True, extra is None, [KTB, QTB], [SB_])
        if extra is not None:
            lhsT, rhs, rd = extra
            mm(P, S[:], lhsT, rhs, False, True, rd, [SB_])

    def rq(qb):
        return QT[:, :, qb * 128:(qb + 1) * 128]

    st = {}

    def finalize(qb, a, AB, width, col, first):
        nxt, NXT = o_r.next()
        cur = st.get(("o", qb))
        rcs = []
        for h in range(4):
            off = (h % 2) * 193 if width == 193 else h * 65
            aa, AAB = (a[h // 2], AB[h // 2]) if width == 193 else (a, AB)
            sm, SM = sm_r.next()
            ts(P, "dve", sm[:], aa[:, off + 64:off + 65], 1e-30, None, ALU.max, None, [AAB], [SM])
            rc, RC = sm_r.next()
            P.op("dve", lambda e, rc=rc, sm=sm: e.reciprocal(out=rc[:], in_=sm[:]), reads=[SM], writes=[RC])
            gf, GF = sm_r.next()
            tt(P, "dve", gf[:], rc[:], G[:, qb, h * 3 + col:h * 3 + col + 1], ALU.mult, [RC, GB], [GF])
            if cur is None:
                ts(P, "dve", nxt[:, h * 64:(h + 1) * 64], aa[:, off:off + 64], gf[:, 0:1], None, ALU.mult, None,
                   [AAB, GF], [NXT])
            else:
                c_t, C_B = cur
                P.op("dve", lambda e, nxt=nxt, aa=aa, off=off, h=h, gf=gf, c_t=c_t: e.scalar_tensor_tensor(
                    out=nxt[:, h * 64:(h + 1) * 64], in0=aa[:, off:off + 64], scalar=gf[:, 0:1],
                    in1=c_t[:, h * 64:(h + 1) * 64], op0=ALU.mult, op1=ALU.add), reads=[AAB, GF, C_B], writes=[NXT])
            rcs.append((rc, RC))
        st[("o", qb)] = (nxt, NXT)
        return rcs

    def cmp_steps(qb):
        rhsQ = rq(qb)
        ncc = qb // 16 + 1
        ctx = {}

        def mk_qk(cc):
            def f(S, SB_):
                if cc == 0:
                    ctx["acc"] = [accC_r.next(), accC_r.next()]
                extra = None
                if qb - 16 * cc <= 16:
                    extra = (ident[:], CM[:, qb - 16 * cc, :], [IDB, CMB])
                qk(S, SB_, KC, KCB, cc, rhsQ, extra)
            return f

        def mk_pv(cc):
            def f(p, PB):
                for h in range(4):
                    a, AB = ctx["acc"][h // 2]
                    off = (h % 2) * 193
                    mm(P, a[:, off:off + 193], p[:, h * 128:(h + 1) * 128], VC[:, cc, 0:193],
                       cc == 0 and h % 2 == 0, cc == ncc - 1, [PB, VCB], [AB], skip=True)
            return f

        def after():
            acc = ctx["acc"]
            rcs = finalize(qb, [acc[0][0], acc[1][0]], [acc[0][1], acc[1][1]], 193, 0, True)
            imp, IMP = None, None
            for h in range(4):
                a, AB = acc[h // 2]
                off = (h % 2) * 193
                rc, RC = rcs[h]
                ni, NI = imp_r.next()
                if imp is None:
                    ts(P, "dve", ni[:], a[:, off + 65:off + 193], rc[:, 0:1], None, ALU.mult, None, [AB, RC], [NI])
                else:
                    P.op("dve", lambda e, ni=ni, a=a, off=off, rc=rc, imp=imp: e.scalar_tensor_tensor(
                        out=ni[:], in0=a[:, off + 65:off + 193], scalar=rc[:, 0:1], in1=imp[:], op0=ALU.mult,
                        op1=ALU.add), reads=[AB, RC, IMP], writes=[NI])
                imp, IMP = ni, NI
            sc, SC = sc_r.next()
            tt(P, "dve", sc[:], imp[:], AT[:, 126 - 2 * qb:254 - 2 * qb], ALU.add, [IMP, ATB], [SC])
            ts(P, "dve", sc[:, 0:1], imp[:, 0:1], 1e4, None, ALU.add, None, [IMP, SC], [SC])
            m1, M1 = m8_r.next()
            P.op("dve", lambda e, m1=m1, sc=sc: e.max(out=m1[:], in_=sc[:]), reads=[SC], writes=[M1])
            sc2, SC2 = sc_r.next()
            P.op("dve", lambda e, sc2=sc2, m1=m1, sc=sc: e.match_replace(
                out=sc2[:], in_to_replace=m1[:], in_values=sc[:], imm_value=-1e9), reads=[SC, M1], writes=[SC2])
            m2, M2 = m8_r.next()
            P.op("dve", lambda e, m2=m2, sc2=sc2: e.max(out=m2[:], in_=sc2[:]), reads=[SC2], writes=[M2])
            ns, NS = ns_r.next()
            ts(P, "dve", ns[:], sc[:], m2[:, 7:8], -30000.0, ALU.is_lt, ALU.mult, [SC, M2], [NS])
            pst, PST = pst_r.next()
            tr(P, pst[:], ns[:], ident[:], [NS, IDB], [PST])
            nt, NTB = nt_r.next()
            for h in range(4):
                cp(P, "act" if h % 2 else "dve", nt[:, h, :], pst[:], [PST], [NTB])
            st[("nt", qb)] = (nt, NTB)
            nv = (2 * qb - 1) // 58 + 1 if qb > 0 else 0
            qvs = []
            for v in range(nv):
                qvt, QVB = qv_r.next()
                nr = min(58, 128 - 58 * v)
                P.dma("pool", qvt[0:70], QT_d[:, :, qb * 128:(qb + 1) * 128], writes=[QVB])
                P.dma("pool", qvt[70:70 + nr], nt[58 * v:58 * v + nr], reads=[NTB], writes=[QVB])
                qvs.append((qvt, QVB))
            st[("qv", qb)] = qvs

        return [(mk_qk(cc), mk_pv(cc), after if cc == ncc - 1 else None) for cc in range(ncc)]

    def branch_steps(qb, which):
        rhsQ = rq(qb)
        ctx = {}
        if which == "sel":
            cs = list(range(qb + 1))
            KT, KTB, V, VB, accr, col = KS, KSB, VS, VSB, accS_r, 1
        else:
            cs = list(range(max(0, qb - 4), qb + 1))
            KT, KTB, V, VB, accr, col = KW, KWB, VW, VWB, accW_r, 2

        def mk_qk(c):
            def f(S, SB_):
                if c == cs[0]:
                    ctx["acc"] = accr.next()
                extra = None
                if c == qb:
                    extra = (ident[:], CA[:], [IDB, CAB])
                elif which == "sel":
                    qvt, QVB = st[("qv", qb)][(2 * c) // 58]
                    mm(P, S[:], KS[:, c * 128:(c + 1) * 128], qvt[:].rearrange("p h q -> p (h q)"), True, True,
                       [KSB, QVB], [SB_])
                    return
                elif c == qb - 4:
                    extra = (ident[:], WM[:], [IDB, WMB])
                qk(S, SB_, KT, KTB, c, rhsQ, extra)
            return f

        def mk_pv(c):
            def f(p, PB):
                a, AB = ctx["acc"]
                for h in range(4):
                    mm(P, a[:, h * 65:(h + 1) * 65], p[:, h * 128:(h + 1) * 128], V[:, c, 0:65],
                       c == cs[0] and h == 0, c == qb, [PB, VB], [AB], skip=True)
            return f

        def after():
            a, AB = ctx["acc"]
            finalize(qb, a, AB, 65, col, False)
            if which == "sel":
                cur, CUR = st[("o", qb)]
                P.dma("sp", o_d[qb * 128:(qb + 1) * 128, :], cur[:], reads=[CUR])
                st.pop(("o", qb)); st.pop(("nt", qb)); st.pop(("qv", qb))

        return [(mk_qk(c), mk_pv(c), after if c == qb else None) for c in cs]

    steps = list(cmp_steps(0))
    for qb in range(NQB):
        steps += branch_steps(qb, "win")
        if qb + 1 < NQB:
            steps += cmp_steps(qb + 1)
        steps += branch_steps(qb, "sel")
    run_pipeline(P, steps, s_r, p_r, scale)
    return P.finish()


_PROGS = {}


def _prog(name, fn):
    if name not in _PROGS:
        _PROGS[name] = fn()
    return _PROGS[name]


def _pl(g):
    return np.ascontiguousarray(np.asarray(g, np.float32).reshape(-1, 128).T)


def _bc(g):
    return np.ascontiguousarray(np.tile(np.asarray(g, np.float32)[None, :], (128, 1)))


def _run(nc, maps):
    res = run_bass_kernel_spmd(nc, maps, core_ids=list(range(8)))
    return res.results


def _split2(a):
    a = np.asarray(a, np.float64)
    hi = a.astype(np.float32).astype(NPBF)
    lo = (a - hi.astype(np.float64)).astype(np.float32).astype(NPBF)
    return hi, lo


def _nsa_c_consts():
    r = np.arange(58)[:, None]
    u = np.arange(TT)[None, :]
    Ek = (((u // 64) % 58) == r).astype(np.float32).astype(NPBF)
    jl = np.arange(128)[:, None, None]
    m = np.arange(17)[None, :, None]
    ql = np.tile(np.arange(128), 4)[None, None, :]
    CM = np.where(16 * jl - ql <= 128 * m - 31, 0.0, -30000.0).astype(np.float32).astype(NPBF)
    kl = np.arange(128)[:, None]
    q4 = np.tile(np.arange(128), 4)[None, :]
    CA = np.where(kl <= q4, 0.0, -30000.0).astype(np.float32).astype(NPBF)
    WM = np.where(kl > q4, 0.0, -30000.0).astype(np.float32).astype(NPBF)
    qq = np.arange(128)[:, None]
    rel = np.arange(256)[None, :] - 126
    cur = (qq >= 64).astype(np.int64)
    AT = np.zeros((128, 256), np.float32)
    AT[(rel == cur) | (rel == cur - 1)] = 1e4
    AT[rel > cur] = -1.0
    jj = np.arange(512)[:, None]
    ii = np.arange(128)[None, :]
    Ov = ((jj >= 4 * ii - 1) & (jj <= 4 * ii + 3) & (jj <= 510)).astype(np.float32)
    return dict(Ek=np.ascontiguousarray(Ek), CM=np.ascontiguousarray(CM), CA=np.ascontiguousarray(CA),
                WM=np.ascontiguousarray(WM), AT=AT, Ov=Ov)


def _aug_k(pos):
    hi, lo = _split2(pos)
    one = np.ones_like(hi)
    return np.stack([one, one, hi, lo, hi, lo], 0)


def kernel(**inp):
    f = {k: np.asarray(v) for k, v in inp.items()}
    x = f["x"]
    B, T, D = x.shape
    xf = np.ascontiguousarray(x.reshape(B * T, D))
    ident = np.eye(128, dtype=np.float32).astype(NPBF)
    G4 = 4
    nc = _prog("nsa_a", build_nsa_a)
    maps = [{"x": xf[c * NT:(c + 1) * NT], "g": _pl(f["a_attn_norm"][0]), "w": np.ascontiguousarray(f["a_w_in"][0]),
             "gq": _bc(f["a_q_norm"][0]), "gs": _bc(f["a_kslc_norm"][0]), "gw": _bc(f["a_kwin_norm"][0]),
             "ident": ident} for c in range(8)]
    r = _run(nc, maps)
    big = np.concatenate([q["big"] for q in r], 0)
    gates = np.concatenate([q["gates"] for q in r], 0)
    nc = _prog("nsa_b", build_nsa_b)

    def a2T(col0, b):
        a = big[b * T:(b + 1) * T, col0:col0 + 64]
        o = np.zeros((128, T), NPBF)
        o[0:64] = a.T
        o[64:128, 0:T - 1] = a[1:].T
        return o

    def w1l(w):
        return np.ascontiguousarray(np.asarray(w, np.float32).reshape(16, 2, 64, 256).transpose(1, 2, 0, 3).reshape(128, 16, 256))

    def pel(p):
        return np.ascontiguousarray(np.asarray(p, np.float32).reshape(16, 2, 64).transpose(1, 2, 0).reshape(128, 16))

    maps = []
    for c in range(8):
        b, g = c // 4, c % 4
        maps.append({
            "k_a2T": a2T(1024 + g * 64, b), "v_a2T": a2T(1280 + g * 64, b),
            "k_w1": w1l(f["a_cmp_k_w1"][0]), "v_w1": w1l(f["a_cmp_v_w1"][0]),
            "k_pe": pel(f["a_cmp_pos_k"][0]), "v_pe": pel(f["a_cmp_pos_v"][0]),
            "k_b1": _pl(f["a_cmp_k_b1"][0]), "v_b1": _pl(f["a_cmp_v_b1"][0]),
            "k_w2": np.ascontiguousarray(f["a_cmp_k_w2"][0].reshape(2, 128, 64).transpose(1, 0, 2)),
            "v_w2": np.ascontiguousarray(f["a_cmp_v_w2"][0].reshape(2, 128, 64).transpose(1, 0, 2)),
            "gk": _bc(f["a_kcmp_norm"][0])})
    rB = _run(nc, maps)
    nc = _prog("nsa_c", build_nsa_c)
    C = _nsa_c_consts()
    tpos = np.arange(T, dtype=np.float64)
    kaug_tok = _aug_k(tpos)
    kaug_cmp = _aug_k(16.0 * np.arange(512) + 15.5)
    maps = []
    for c in range(8):
        b, g = c // 4, c % 4
        bb = big[b * T:(b + 1) * T]
        QT = np.zeros((70, 4, T), NPBF)
        for hg in range(4):
            h = g * 4 + hg
            QT[0:64, hg] = bb[:, h * 64:(h + 1) * 64].T
            beta = 8.0 * 2.0 ** (-8.0 * (h + 1) / 16.0)
            bh, bl = _split2(np.full(T, beta))
            th, tl = _split2(beta * tpos)
            QT[64:70, hg] = np.stack([-th.astype(np.float32), -tl.astype(np.float32), bh.astype(np.float32),
                                      bh.astype(np.float32), bl.astype(np.float32), bl.astype(np.float32)], 0).astype(NPBF)
        KS = np.concatenate([bb[:, 1536 + g * 64:1536 + (g + 1) * 64].T, kaug_tok, C["Ek"]], 0)
        KW = np.concatenate([bb[:, 2048 + g * 64:2048 + (g + 1) * 64].T, kaug_tok], 0)
        KC = np.concatenate([rB[c]["k_cmp"].T, kaug_cmp], 0)

        def vaug(v):
            o = np.zeros((128, 64, 66), NPBF)
            o[:, :, 0:64] = v.reshape(64, 128, 64).transpose(1, 0, 2)
            o[:, :, 64] = 1.0
            return o

        VC = np.zeros((128, 4, 194), NPBF)
        VC[:, :, 0:64] = rB[c]["v_cmp"].reshape(4, 128, 64).transpose(1, 0, 2)
        VC[:, :, 64] = 1.0
        VC[:, :, 65:193] = C["Ov"].reshape(4, 128, 128).transpose(1, 0, 2).astype(NPBF)
        Gt = np.ascontiguousarray(gates[b * T:(b + 1) * T, g * 12:(g + 1) * 12].reshape(64, 128, 12).transpose(1, 0, 2))
        maps.append({"QT": QT, "KS": np.ascontiguousarray(KS), "KW": np.ascontiguousarray(KW),
                     "KC": np.ascontiguousarray(KC), "VS": vaug(bb[:, 1792 + g * 64:1792 + (g + 1) * 64]),
                     "VW": vaug(bb[:, 2304 + g * 64:2304 + (g + 1) * 64]), "VC": VC, "G": Gt,
                     "CM": C["CM"], "CA": C["CA"], "WM": C["WM"], "AT": C["AT"], "ident": ident})
    rC = _run(nc, maps)
    o_nsa = np.zeros((B, T, 1024), np.float32)
    for c in range(8):
        o_nsa[c // 4, :, (c % 4) * 256:(c % 4 + 1) * 256] = rC[c]["o"]
    o_nsa = o_nsa.reshape(B * T, 1024)

    def outproj(o, xres, w):
        nc = _prog("outproj", build_outproj)
        r = _run(nc, [{"o": o[c * NT:(c + 1) * NT], "x": xres[c * NT:(c + 1) * NT], "w": np.ascontiguousarray(w),
                       "ident": ident} for c in range(8)])
        return np.concatenate([q["y"] for q in r], 0)

    def ffn(xin, g, wgu, wd):
        nc = _prog("ffn", build_ffn)
        r = _run(nc, [{"x": xin[c * NT:(c + 1) * NT], "g": _pl(g), "wgu": np.ascontiguousarray(wgu),
                       "wd": np.ascontiguousarray(wd), "ident": ident} for c in range(8)])
        return np.concatenate([q["y"] for q in r], 0)

    x1 = outproj(o_nsa, xf, f["a_w_out"][0])
    x2 = ffn(x1, f["ffn_norm"][0], f["ffn_w_gate_up"][0], f["ffn_w_down"][0])
    nc = _prog("kvq", build_kvq)
    inv = (10000.0 ** (-np.arange(0, 64, 2, dtype=np.float32) / 64)).astype(np.float32)
    maps = []
    for c in range(8):
        pos = (np.arange(NT) + (c % 4) * NT).astype(np.float32)
        ang = pos[:, None] * inv[None, :]
        cs = np.concatenate([np.cos(ang), np.sin(ang)], 1).astype(np.float32).reshape(NB, 128, 64).transpose(1, 0, 2)
        maps.append({"x": x2[c * NT:(c + 1) * NT], "g_kv": _pl(f["kv_norm"]), "g_q": _pl(f["b_attn_norm"][0]),
                     "g_c": _pl(f["kv_c_norm"]), "g_qa": _pl(f["b_q_a_norm"][0]), "g_kn": _bc(f["kv_k_norm"]),
                     "g_qn": _bc(f["b_q_norm"][0]), "w_kva": np.ascontiguousarray(f["kv_w_a"]),
                     "w_kvb": np.ascontiguousarray(f["kv_w_b"]), "w_qa": np.ascontiguousarray(f["b_w_q_a"][0]),
                     "w_qb": np.ascontiguousarray(f["b_w_q_b"][0]), "cs": np.ascontiguousarray(cs), "ident": ident})
    r = _run(nc, maps)
    kk = np.concatenate([q["k"] for q in r], 0)
    qq = np.concatenate([q["q"] for q in r], 0)
    vv = np.concatenate([q["v"] for q in r], 0)
    nc = _prog("mla", build_mla)
    r = _run(nc, [mla_host_inputs(kk, qq, vv, c // 4, c % 4) for c in range(8)])
    o_mla = np.zeros((B, T, 1024), np.float32)
    for c in range(8):
        o_mla[c // 4, :, (c % 4) * 256:(c % 4 + 1) * 256] = r[c]["o"]
    o_mla = o_mla.reshape(B * T, 1024)
    x3 = outproj(o_mla, x2, f["b_w_out"][0])
    x4 = ffn(x3, f["ffn_norm"][1], f["ffn_w_gate_up"][1], f["ffn_w_down"][1])
    return x4.reshape(B, T, D).astype(np.float32)
```

```python
import numpy as np
import ml_dtypes
import concourse.bass as bass
import concourse.mybir as mybir
from concourse.bass_utils import run_bass_kernel_spmd

F32 = mybir.dt.float32
BF16 = mybir.dt.bfloat16
AF = mybir.ActivationFunctionType
ALU = mybir.AluOpType
AX = mybir.AxisListType
NPBF = ml_dtypes.bfloat16

ENGS = ["sp", "act", "dve", "pool", "pe"]


class Buf:
    __slots__ = ("w", "r", "dsem", "dcnt", "name")

    def __init__(self, name=""):
        self.w = None
        self.r = []
        self.dsem = None
        self.dcnt = 0
        self.name = name


class Prog:
    def __init__(self):
        self.nc = bass.Bass("TRN2", target_bir_lowering=False)
        nc = self.nc
        self.q = {e: [] for e in ENGS}
        self.cnt = {e: 0 for e in ENGS}
        self.sems = {}
        self.sem = {}
        for e in ("act", "dve", "pool", "pe"):
            self.sem[e] = self.new_sem("eng_" + e)
        self.wm = {e: {} for e in ENGS}
        self.all_dma_tokens = {}
        self.nsb = 0
        self.nps = 0

    def new_sem(self, name):
        h = self.nc.alloc_semaphore(name)
        sid = len(self.sems)
        self.sems[sid] = h
        return sid

    def sb(self, shape, dtype, name=None):
        self.nsb += 1
        return self.nc.alloc_sbuf_tensor(f"S{self.nsb}_" + (name or ""), list(shape), dtype)

    def ps(self, shape, dtype=F32, name=None):
        self.nps += 1
        return self.nc.alloc_psum_tensor(f"P{self.nps}_" + (name or ""), list(shape), dtype)

    def dram(self, name, shape, dtype, kind):
        return self.nc.dram_tensor(name, list(shape), dtype, kind=kind)

    def _deps(self, eng, reads, writes):
        deps = {}

        def need(tok):
            if tok is None:
                return
            s, v = tok
            if eng == "pe" and s == self.sem["pe"]:
                return
            if deps.get(s, 0) < v:
                deps[s] = v

        for b in reads:
            need(b.w)
        for b in writes:
            need(b.w)
            for t in b.r:
                need(t)
        out = []
        wm = self.wm[eng]
        for s, v in deps.items():
            if wm.get(s, 0) < v:
                wm[s] = v
                out.append((s, v))
        return out

    def op(self, eng, fn, reads=(), writes=()):
        waits = self._deps(eng, reads, writes)
        self.cnt[eng] += 1
        tok = (self.sem[eng], self.cnt[eng])
        self.q[eng].append((waits, fn, (tok[0], 1)))
        for b in reads:
            b.r.append(tok)
        for b in writes:
            b.w = tok
            b.r = []
        return tok

    def dma(self, q, out, in_, reads=(), writes=(), **kw):
        waits = self._deps(q, reads, writes)
        tb = (list(writes) + list(reads))[0]
        if tb.dsem is None:
            tb.dsem = self.new_sem("d_" + tb.name + str(len(self.sems)))
        tb.dcnt += 1
        tok = (tb.dsem, 16 * tb.dcnt)
        self.all_dma_tokens[tb.dsem] = tok[1]
        self.q[q].append((waits, lambda e: e.dma_start(out=out, in_=in_, **kw), (tok[0], 16)))
        for b in reads:
            b.r.append(tok)
        for b in writes:
            b.w = tok
            b.r = []
        return tok

    def cc(self, kind, ins_ap, outs_ap, groups, reads=(), writes=()):
        waits = self._deps("pool", reads, writes)
        sem = self.new_sem("cc" + str(len(self.sems)))
        tok = (sem, 16)
        self.all_dma_tokens[sem] = 16
        self.q["pool"].append((waits, lambda e: e.collective_compute(
            kind, ALU.bypass, replica_groups=groups, ins=[ins_ap], outs=[outs_ap]), (sem, 16)))
        for b in reads:
            b.r.append(tok)
        for b in writes:
            b.w = tok
            b.r = []
        return tok

    def finish(self):
        nc = self.nc
        final = [(s, v) for s, v in self.all_dma_tokens.items()]
        for e in ("act", "dve", "pool", "pe"):
            if self.cnt[e] > 0:
                final.append((self.sem[e], self.cnt[e]))
        self.q["sp"].append((final, None, None))
        emap = {"sp": "sync", "act": "scalar", "dve": "vector", "pool": "gpsimd", "pe": "tensor"}
        sems = self.sems

        def replay(lst):
            def run(e):
                for waits, fn, inc in lst:
                    for s, v in waits:
                        e.wait_ge(sems[s], v)
                    if fn is not None:
                        ins = fn(e)
                        ins.then_inc(sems[inc[0]], inc[1])
            return run

        with nc.Block() as block:
            for k in ENGS:
                if self.q[k]:
                    getattr(block, emap[k])(replay(self.q[k]))
        return nc


def mm(P, out, lhsT, rhs, start, stop, reads, writes, skip=False):
    return P.op("pe", lambda e: e.matmul(out, lhsT=lhsT, rhs=rhs, start=start, stop=stop,
                                         skip_group_check=skip), reads=reads, writes=writes)


def tr(P, out, in_, ident, reads, writes):
    return P.op("pe", lambda e: e.transpose(out, in_, ident), reads=reads, writes=writes)


def act(P, out, in_, func, reads, writes, **kw):
    return P.op("act", lambda e: e.activation(out=out, in_=in_, func=func, **kw), reads=reads, writes=writes)


def tt(P, eng, out, in0, in1, op, reads, writes):
    return P.op(eng, lambda e: e.tensor_tensor(out=out, in0=in0, in1=in1, op=op), reads=reads, writes=writes)


def ts(P, eng, out, in0, s1, s2, op0, op1, reads, writes):
    if op1 is None:
        return P.op(eng, lambda e: e.tensor_scalar(out=out, in0=in0, scalar1=s1, scalar2=None, op0=op0),
                    reads=reads, writes=writes)
    return P.op(eng, lambda e: e.tensor_scalar(out=out, in0=in0, scalar1=s1, scalar2=s2, op0=op0, op1=op1),
                reads=reads, writes=writes)


def cp(P, eng, out, in_, reads, writes):
    if eng == "act":
        return P.op("act", lambda e: e.activation(out=out, in_=in_, func=AF.Copy), reads=reads, writes=writes)
    return P.op(eng, lambda e: e.tensor_copy(out=out, in_=in_), reads=reads, writes=writes)


def memset(P, eng, ap, val, writes):
    return P.op(eng, lambda e: e.memset(ap, val), writes=writes)


class Rot:
    def __init__(self, P, n, shape, dtype, name, psum=False):
        self.items = []
        for i in range(n):
            t = P.ps(shape, dtype, f"{name}{i}") if psum else P.sb(shape, dtype, f"{name}{i}")
            self.items.append((t, Buf(f"{name}{i}")))
        self.i = 0

    def next(self):
        it = self.items[self.i % len(self.items)]
        self.i += 1
        return it


def load_const(P, dram_ap, shape, dtype, name, q="sp"):
    t = P.sb(shape, dtype, name)
    b = Buf(name)
    P.dma(q, t[:], dram_ap, writes=[b])
    return t, b


class WBufs:
    def __init__(self):
        self.items = []

    def cols(self, c0, c1):
        return [b for lo, hi, b in self.items if lo < c1 and hi > c0]

    def kc(self, j):
        return [b for lo, hi, b in self.items if lo <= j < hi]


def load_weight(P, w_ap, K, N, name, stg, gain=None, by="col", cw=1024, defer=False):
    KC = K // 128
    wt = P.sb([128, KC, N], BF16, name)
    wb = WBufs()
    cnt = [0]

    def piece(kc, c0, c1, buf):
        st, sbuf = stg.next()
        P.dma("sp", st[:, 0:c1 - c0], w_ap[kc * 128:(kc + 1) * 128, c0:c1], writes=[sbuf])
        eng = "dve" if cnt[0] % 2 == 0 else "act"
        cnt[0] += 1
        if gain is None:
            cp(P, eng, wt[:, kc, c0:c1], st[:, 0:c1 - c0], [sbuf], [buf])
        else:
            gt, gb = gain
            if eng == "dve":
                ts(P, "dve", wt[:, kc, c0:c1], st[:, 0:c1 - c0], gt[:, kc:kc + 1], None, ALU.mult, None,
                   [sbuf, gb], [buf])
            else:
                act(P, wt[:, kc, c0:c1], st[:, 0:c1 - c0], AF.Copy, [sbuf, gb], [buf], scale=gt[:, kc:kc + 1])

    chunks = {}
    keys = []
    if by == "col":
        for c0 in range(0, N, cw):
            c1 = min(N, c0 + cw)
            buf = Buf(f"{name}_{c0}")
            wb.items.append((c0, c1, buf))
            chunks[c0] = [(kc, c0, c1, buf) for kc in range(KC)]
            keys.append(c0)
    else:
        for kc in range(KC):
            buf = Buf(f"{name}_k{kc}")
            wb.items.append((kc, kc + 1, buf))
            chunks[kc] = [(kc, c0, min(N, c0 + cw), buf) for c0 in range(0, N, cw)]
            keys.append(kc)
    done = set()

    def load(key):
        if key in done:
            return
        done.add(key)
        for a in chunks[key]:
            piece(*a)

    if defer:
        return wt, wb, load
    for k in keys:
        load(k)
    return wt, wb


def rms_rstd(P, x_ap, D, ss, scratch, reads, SS, SCR, eps=1e-6):
    act(P, scratch, x_ap, AF.Square, reads, [SCR, SS], accum_out=ss)
    act(P, ss, ss, AF.Ln, [SS], [SS], scale=1.0 / D, bias=eps)
    act(P, ss, ss, AF.Exp, [SS], [SS], scale=-0.5)


def make_ident(P, ident_dram):
    return load_const(P, ident_dram, [128, 128], BF16, "ident_sb")


NT = 2048
NB = NT // 128


def build_outproj():
    P = Prog()
    o_d = P.dram("o", [NT, 1024], F32, "ExternalInput").ap()
    x_d = P.dram("x", [NT, 1024], F32, "ExternalInput").ap()
    w_d = P.dram("w", [1024, 1024], F32, "ExternalInput").ap()
    id_d = P.dram("ident", [128, 128], BF16, "ExternalInput").ap()
    y_d = P.dram("y", [NT, 1024], F32, "ExternalOutput").ap()
    ident, IDB = make_ident(P, id_d)
    stg = Rot(P, 4, [128, 1024], F32, "stg")
    w, WB = load_weight(P, w_d, 1024, 1024, "w", stg)
    oin = Rot(P, 2, [128, 1024], F32, "oin")
    xin = Rot(P, 2, [128, 1024], F32, "xin")
    obf = Rot(P, 2, [128, 1024], BF16, "obf")
    oT = Rot(P, 2, [128, 8, 128], BF16, "oT")
    pT = Rot(P, 2, [128, 8, 128], BF16, "pT", psum=True)
    acc = Rot(P, 4, [128, 512], F32, "acc", psum=True)
    yo = Rot(P, 2, [128, 1024], F32, "yo")
    for b in range(NB):
        rows = slice(b * 128, (b + 1) * 128)
        ot, OB = oin.next()
        xt, XB = xin.next()
        P.dma("sp", ot[:], o_d[rows, :], writes=[OB])
        P.dma("sp", xt[:], x_d[rows, :], writes=[XB])
        bt, BB = obf.next()
        cp(P, "dve", bt[:], ot[:], [OB], [BB])
        pt, PB = pT.next()
        for c in range(8):
            tr(P, pt[:, c, :], bt[:, c * 128:(c + 1) * 128], ident[:], [BB, IDB], [PB])
        tt_, TB = oT.next()
        cp(P, "act", tt_[:], pt[:], [PB], [TB])
        yt, YB = yo.next()
        for half in range(2):
            at, AB = acc.next()
            for c in range(8):
                mm(P, at[:], tt_[:, c, :], w[:, c, half * 512:(half + 1) * 512], c == 0, c == 7,
                   [TB] + WB.cols(half * 512, half * 512 + 512), [AB])
            tt(P, "dve", yt[:, half * 512:(half + 1) * 512], at[:], xt[:, half * 512:(half + 1) * 512], ALU.add,
               [AB, XB], [YB])
        P.dma("pool", y_d[rows, :], yt[:], reads=[YB])
    return P.finish()


FH = 2816


def build_ffn():
    P = Prog()
    x_d = P.dram("x", [NT, 1024], F32, "ExternalInput").ap()
    g_d = P.dram("g", [128, 8], F32, "ExternalInput").ap()
    wgu_d = P.dram("wgu", [1024, 2 * FH], F32, "ExternalInput").ap()
    wd_d = P.dram("wd", [FH, 1024], F32, "ExternalInput").ap()
    id_d = P.dram("ident", [128, 128], BF16, "ExternalInput").ap()
    y_d = P.dram("y", [NT, 1024], F32, "ExternalOutput").ap()
    ident, IDB = make_ident(P, id_d)
    gain = load_const(P, g_d, [128, 8], F32, "gain")
    stg = Rot(P, 4, [128, 1024], F32, "stg")
    wgu, WGU, ld_gu = load_weight(P, wgu_d, 1024, 2 * FH, "wgu", stg, gain=gain, cw=512, defer=True)
    wd, WD, ld_d = load_weight(P, wd_d, FH, 1024, "wd", stg, by="kc", defer=True)
    for j in range(FH // 128):
        ld_gu((j * 128) // 512 * 512)
        ld_gu((FH + j * 128) // 512 * 512)
        ld_d(j)
    xin = Rot(P, 4, [128, 1024], F32, "xin")
    scr = P.sb([128, 1024], F32, "scr"); SCR = Buf("scr")
    ssr = Rot(P, 2, [128, 1], F32, "ss")
    xn = Rot(P, 2, [128, 1024], BF16, "xn")
    pT = Rot(P, 1, [128, 8, 128], BF16, "pT", psum=True)
    xT = Rot(P, 2, [128, 8, 256], BF16, "xT")
    gu = Rot(P, 3, [128, 512], F32, "gu", psum=True)
    accs = Rot(P, 4, [128, 512], F32, "acc", psum=True)
    sg = Rot(P, 2, [128, 256], BF16, "sg")
    aT = Rot(P, 2, [128, 256], BF16, "aT")
    yo = Rot(P, 2, [128, 1024], F32, "yo")
    NJ = FH // 128
    def stageA(sbk):
        xs = []
        xTt, XT = xT.next()
        for t in range(2):
            rows = slice(sbk * 256 + t * 128, sbk * 256 + (t + 1) * 128)
            xt, XB = xin.next()
            xs.append((xt, XB, rows))
            P.dma("sp", xt[:], x_d[rows, :], writes=[XB])
            ss, SS = ssr.next()
            rms_rstd(P, x_ap=xt[:], D=1024, ss=ss[:], scratch=scr[:], reads=[XB], SS=SS, SCR=SCR)
            xnt, XN = xn.next()
            ts(P, "dve", xnt[:], xt[:], ss[:, 0:1], None, ALU.mult, None, [XB, SS], [XN])
            pt, PB = pT.next()
            for c in range(8):
                tr(P, pt[:, c, :], xnt[:, c * 128:(c + 1) * 128], ident[:], [XN, IDB], [PB])
            cp(P, "dve", xTt[:, :, t * 128:(t + 1) * 128], pt[:], [PB], [XT])
        return xs, xTt, XT

    def stageB(sbk, ctx):
        xs, xTt, XT = ctx
        acc = [[accs.next() for _ in range(2)] for _ in range(2)]
        for j in range(NJ):
            gt, GB = gu.next()
            ut, UB = gu.next()
            for c in range(8):
                mm(P, gt[:, 0:256], wgu[:, c, j * 128:(j + 1) * 128], xTt[:, c, :], c == 0, c == 7,
                   WGU.cols(j * 128, j * 128 + 128) + [XT], [GB])
            for c in range(8):
                mm(P, ut[:, 0:256], wgu[:, c, FH + j * 128:FH + (j + 1) * 128], xTt[:, c, :], c == 0, c == 7,
                   WGU.cols(FH + j * 128, FH + j * 128 + 128) + [XT], [UB])
            sgt, SG = sg.next()
            act(P, sgt[:], gt[:, 0:256], AF.Silu, [GB], [SG])
            at, AB = aT.next()
            tt(P, "dve", at[:], sgt[:], ut[:, 0:256], ALU.mult, [SG, UB], [AB])
            for t in range(2):
                for half in range(2):
                    a_t, A_B = acc[t][half]
                    mm(P, a_t[:], at[:, t * 128:(t + 1) * 128], wd[:, j, half * 512:(half + 1) * 512], j == 0,
                       j == NJ - 1, [AB] + WD.kc(j), [A_B])
        for t in range(2):
            xt, XB, rows = xs[t]
            yt, YB = yo.next()
            for half in range(2):
                a_t, A_B = acc[t][half]
                tt(P, "dve", yt[:, half * 512:(half + 1) * 512], a_t[:], xt[:, half * 512:(half + 1) * 512], ALU.add,
                   [A_B, XB], [YB])
            P.dma("pool", y_d[rows, :], yt[:], reads=[YB])

    nsb = NT // 256
    ctx = stageA(0)
    for sbk in range(nsb):
        nxt = stageA(sbk + 1) if sbk + 1 < nsb else None
        stageB(sbk, ctx)
        ctx = nxt
    return P.finish()


def rope_ops(P, src, SRC, dst, DST, cosb, sinb, CS, tmp, TMP, nh):
    x1 = src[:, :, 0:32]
    x2 = src[:, :, 32:64]
    t1, t2 = tmp[:, 0, 0:nh, :], tmp[:, 1, 0:nh, :]
    tt(P, "dve", t1, x1, cosb, ALU.mult, [SRC, CS], [TMP])
    tt(P, "dve", t2, x2, sinb, ALU.mult, [SRC, CS], [TMP])
    tt(P, "dve", dst[:, :, 0:32], t1, t2, ALU.subtract, [TMP], [DST])
    tt(P, "dve", t1, x1, sinb, ALU.mult, [SRC, CS], [TMP])
    tt(P, "dve", t2, x2, cosb, ALU.mult, [SRC, CS], [TMP])
    tt(P, "dve", dst[:, :, 32:64], t1, t2, ALU.add, [TMP], [DST])


def build_kvq(dbg=0):
    P = Prog()
    x_d = P.dram("x", [NT, 1024], F32, "ExternalInput").ap()
    gkv_d = P.dram("g_kv", [128, 8], F32, "ExternalInput").ap()
    gq_d = P.dram("g_q", [128, 8], F32, "ExternalInput").ap()
    gc_d = P.dram("g_c", [128, 2], F32, "ExternalInput").ap()
    gqa_d = P.dram("g_qa", [128, 3], F32, "ExternalInput").ap()
    gkn_d = P.dram("g_kn", [128, 192], F32, "ExternalInput").ap()
    gqn_d = P.dram("g_qn", [128, 192], F32, "ExternalInput").ap()
    wkva_d = P.dram("w_kva", [1024, 320], F32, "ExternalInput").ap()
    wkvb_d = P.dram("w_kvb", [256, 2048], F32, "ExternalInput").ap()
    wqa_d = P.dram("w_qa", [1024, 384], F32, "ExternalInput").ap()
    wqb_d = P.dram("w_qb", [384, 1536], F32, "ExternalInput").ap()
    cs_d = P.dram("cs", [128, NB, 64], F32, "ExternalInput").ap()
    id_d = P.dram("ident", [128, 128], BF16, "ExternalInput").ap()
    kT_d = P.dram("k", [NT, 1536], BF16, "ExternalOutput").ap()
    qT_d = P.dram("q", [NT, 1536], BF16, "ExternalOutput").ap()
    v_d = P.dram("v", [NT, 1024], BF16, "ExternalOutput").ap()
    ident, IDB = make_ident(P, id_d)
    gkv = load_const(P, gkv_d, [128, 8], F32, "gkv")
    gq = load_const(P, gq_d, [128, 8], F32, "gq")
    gc = load_const(P, gc_d, [128, 2], F32, "gc")
    gqa = load_const(P, gqa_d, [128, 3], F32, "gqa")
    gkn, GKN = load_const(P, gkn_d, [128, 192], F32, "gkn")
    gqn, GQN = load_const(P, gqn_d, [128, 192], F32, "gqn")
    cs, CS = load_const(P, cs_d, [128, NB, 64], F32, "cs")
    stg = Rot(P, 4, [128, 1024], F32, "stg")
    wkva, WKVA = load_weight(P, wkva_d, 1024, 320, "wkva", stg, gain=gkv)
    wkvb, WKVB = load_weight(P, wkvb_d, 256, 2048, "wkvb", stg, gain=gc)
    wqa, WQA = load_weight(P, wqa_d, 1024, 384, "wqa", stg, gain=gq)
    wqb, WQB = load_weight(P, wqb_d, 384, 1536, "wqb", stg, gain=gqa)
    xin = Rot(P, 2, [128, 1024], F32, "xin")
    scr = P.sb([128, 1024], F32, "scr"); SCR = Buf("scr")
    ssr = Rot(P, 4, [128, 1], F32, "ss")
    ss8 = Rot(P, 4, [128, 8], F32, "ss8")
    xn = Rot(P, 2, [128, 1024], BF16, "xn")
    pT = Rot(P, 2, [128, 8, 128], BF16, "pT", psum=True)
    xT = Rot(P, 3, [128, 8, 128], BF16, "xT")
    pf = Rot(P, 6, [128, 512], F32, "pf", psum=True)
    cn = Rot(P, 2, [128, 384], BF16, "cn")
    cT = Rot(P, 2, [128, 3, 128], BF16, "cT")
    kpe = Rot(P, 2, [128, 64], F32, "kpe")
    tmpf = Rot(P, 3, [128, 8, 192], F32, "tmpf")
    pe8 = Rot(P, 2, [128, 8, 64], F32, "pe8")
    rtmp = P.sb([128, 2, 8, 32], F32, "rtmp"); RTMP = Buf("rtmp")
    kt = Rot(P, 2, [128, 8, 192], BF16, "kt")
    vt = Rot(P, 2, [128, 8, 128], BF16, "vt")
    kTs = Rot(P, 2, [128, 2, 8, 128], BF16, "kTs")
    def stageA(b):
        rows = slice(b * 128, (b + 1) * 128)
        xt, XB = xin.next()
        P.dma("sp", xt[:], x_d[rows, :], writes=[XB])
        ss, SS = ssr.next()
        rms_rstd(P, xt[:], 1024, ss[:], scr[:], [XB], SS, SCR)
        xnt, XN = xn.next()
        ts(P, "dve", xnt[:], xt[:], ss[:, 0:1], None, ALU.mult, None, [XB, SS], [XN])
        pt, PB = pT.next()
        for c in range(8):
            tr(P, pt[:, c, :], xnt[:, c * 128:(c + 1) * 128], ident[:], [XN, IDB], [PB])
        xTt, XT = xT.next()
        cp(P, "dve", xTt[:], pt[:], [PB], [XT])
        return xTt, XT

    def stageB(b, ctx):
        xTt, XT = ctx
        kva, KVA = pf.next()
        qa, QA = pf.next()
        for c in range(8):
            mm(P, kva[:, 0:320], xTt[:, c, :], wkva[:, c, :], c == 0, c == 7, [XT] + WKVA.cols(0, 320), [KVA])
        for c in range(8):
            mm(P, qa[:, 0:384], xTt[:, c, :], wqa[:, c, :], c == 0, c == 7, [XT] + WQA.cols(0, 384), [QA])
        rows = slice(b * 128, (b + 1) * 128)
        cosb = cs[:, b, 0:32].unsqueeze(1).to_broadcast([128, 8, 32])
        sinb = cs[:, b, 32:64].unsqueeze(1).to_broadcast([128, 8, 32])
        ss2, SS2 = ssr.next()
        rms_rstd(P, kva[:, 0:256], 256, ss2[:], scr[:, 0:256], [KVA], SS2, SCR)
        cnt, CN = cn.next()
        ts(P, "dve", cnt[:, 0:256], kva[:, 0:256], ss2[:, 0:1], None, ALU.mult, None, [KVA, SS2], [CN])
        kp, KP = kpe.next()
        tt(P, "dve", kp[:], kva[:, 256:320], gkn[:, 128:192], ALU.mult, [KVA, GKN], [KP])
        sspe, SSPE = ssr.next()
        act(P, scr[:, 0:64], kva[:, 256:320], AF.Square, [KVA], [SCR, SSPE], accum_out=sspe[:])
        pt2, PB2 = pT.next()
        for c in range(2):
            tr(P, pt2[:, c, :], cnt[:, c * 128:(c + 1) * 128], ident[:], [CN, IDB], [PB2])
        cTt, CT = cT.next()
        cp(P, "act", cTt[:, 0:2, :], pt2[:, 0:2, :], [PB2], [CT])
        kvb = []
        for n in range(4):
            kvp, KVP = pf.next()
            kvb.append((kvp, KVP))
            for c in range(2):
                mm(P, kvp[:], cTt[:, c, :], wkvb[:, c, n * 512:(n + 1) * 512], c == 0, c == 1,
                   [CT] + WKVB.cols(n * 512, n * 512 + 512), [KVP])
        s8, S8 = ss8.next()
        for h in range(8):
            kvp, KVP = kvb[h // 2]
            act(P, scr[:, 0:128], kvp[:, (h % 2) * 256:(h % 2) * 256 + 128], AF.Square, [KVP], [SCR, S8],
                accum_out=s8[:, h:h + 1])
        ts(P, "dve", s8[:], s8[:], sspe[:, 0:1], None, ALU.add, None, [S8, SSPE], [S8])
        act(P, s8[:], s8[:], AF.Ln, [S8], [S8], scale=1.0 / 192, bias=1e-6)
        act(P, s8[:], s8[:], AF.Exp, [S8], [S8], scale=-0.5)
        tf, TF = tmpf.next()
        ktt, KT = kt.next()
        vtt, VT = vt.next()
        for n in range(4):
            kvp, KVP = kvb[n]
            kv3 = kvp[:].rearrange("p (h c) -> p h c", h=2)
            tt(P, "dve", tf[:, 2 * n:2 * n + 2, 0:128], kv3[:, :, 0:128],
               gkn[:, 0:128].unsqueeze(1).to_broadcast([128, 2, 128]), ALU.mult, [KVP, GKN], [TF])
            cp(P, "act", vtt[:, 2 * n:2 * n + 2, :], kv3[:, :, 128:256], [KVP], [VT])
        tt(P, "dve", ktt[:, :, 0:128], tf[:, :, 0:128], s8[:, :].unsqueeze(2).to_broadcast([128, 8, 128]), ALU.mult,
           [TF, S8], [KT])
        p8, P8 = pe8.next()
        tt(P, "dve", p8[:], kp[:, :].unsqueeze(1).to_broadcast([128, 8, 64]),
           s8[:, :].unsqueeze(2).to_broadcast([128, 8, 64]), ALU.mult, [KP, S8], [P8])
        rope_ops(P, p8[:], P8, ktt[:, :, 128:192], KT, cosb, sinb, CS, rtmp, RTMP, 8)
        P.dma("pool", v_d[rows, :], vtt[:].rearrange("p h d -> p (h d)"), reads=[VT])

        def emit_T(src, SRC, dst_d):
            P.dma("pool", dst_d[rows, :], src[:].rearrange("p h d -> p (h d)"), reads=[SRC])

        emit_T(ktt, KT, kT_d)
        ss3, SS3 = ssr.next()
        rms_rstd(P, qa[:, 0:384], 384, ss3[:], scr[:, 0:384], [QA], SS3, SCR)
        qn, QN = cn.next()
        ts(P, "dve", qn[:], qa[:, 0:384], ss3[:, 0:1], None, ALU.mult, None, [QA, SS3], [QN])
        pt3, PB3 = pT.next()
        for c in range(3):
            tr(P, pt3[:, c, :], qn[:, c * 128:(c + 1) * 128], ident[:], [QN, IDB], [PB3])
        qT_, QT = cT.next()
        cp(P, "act", qT_[:], pt3[:, 0:3, :], [PB3], [QT])
        qb = []
        for n in range(4):
            qp, QP = pf.next()
            qb.append((qp, QP))
            for c in range(3):
                mm(P, qp[:, 0:384], qT_[:, c, :], wqb[:, c, n * 384:(n + 1) * 384], c == 0, c == 2,
                   [QT] + WQB.cols(n * 384, n * 384 + 384), [QP])
        s8q, S8Q = ss8.next()
        for h in range(8):
            qp, QP = qb[h // 2]
            act(P, scr[:, 0:192], qp[:, (h % 2) * 192:(h % 2) * 192 + 192], AF.Square, [QP], [SCR, S8Q],
                accum_out=s8q[:, h:h + 1])
        act(P, s8q[:], s8q[:], AF.Ln, [S8Q], [S8Q], scale=1.0 / 192, bias=1e-6)
        act(P, s8q[:], s8q[:], AF.Exp, [S8Q], [S8Q], scale=-0.5)
        tq, TQ = tmpf.next()
        for n in range(4):
            qp, QP = qb[n]
            cp(P, "act", tq[:, 2 * n:2 * n + 2, :].rearrange("p h c -> p (h c)"), qp[:, 0:384], [QP], [TQ])
        tq2, TQ2 = tmpf.next()
        tt(P, "dve", tq2[:], tq[:], gqn[:, :].unsqueeze(1).to_broadcast([128, 8, 192]), ALU.mult, [TQ, GQN], [TQ2])
        tq, TQ = tq2, TQ2
        qtt, QTT = kt.next()
        tt(P, "dve", qtt[:, :, 0:128], tq[:, :, 0:128], s8q[:, :].unsqueeze(2).to_broadcast([128, 8, 128]), ALU.mult,
           [TQ, S8Q], [QTT])
        p8q, P8Q = pe8.next()
        tt(P, "dve", p8q[:], tq[:, :, 128:192], s8q[:, :].unsqueeze(2).to_broadcast([128, 8, 64]), ALU.mult,
           [TQ, S8Q], [P8Q])
        rope_ops(P, p8q[:], P8Q, qtt[:, :, 128:192], QTT, cosb, sinb, CS, rtmp, RTMP, 8)
        emit_T(qtt, QTT, qT_d)

    ctx = stageA(0)
    for b in range(NB):
        nxt = stageA(b + 1) if b + 1 < NB else None
        stageB(b, ctx)
        ctx = nxt
    return P.finish()


def run_pipeline(P, steps, s_r, p_r, scale, look=2):
    n = len(steps)
    Sb = [None] * n

    def do_qk(i):
        S, SB_ = s_r.next()
        Sb[i] = (S, SB_)
        steps[i][0](S, SB_)

    for i in range(min(look, n)):
        do_qk(i)
    for i in range(n):
        if i + look < n:
            do_qk(i + look)
        S, SB_ = Sb[i]
        p, PB = p_r.next()
        act(P, p[:], S[:], AF.Exp, [SB_], [PB], scale=scale)
        steps[i][1](p, PB)
        if steps[i][2] is not None:
            steps[i][2]()


TT = 8192


def build_mla():
    P = Prog()
    kTa_d = P.dram("kTa", [128, 2, TT], BF16, "ExternalInput").ap()
    kTb_d = P.dram("kTb", [128, 2, TT], BF16, "ExternalInput").ap()
    qTa_d = P.dram("qTa", [128, 2, TT], BF16, "ExternalInput").ap()
    qTb_d = P.dram("qTb", [128, 2, TT], BF16, "ExternalInput").ap()
    v_d = P.dram("vaug", [128, 64, 2, 132], BF16, "ExternalInput").ap()
    m_d = P.dram("masks", [128, 4, 512], BF16, "ExternalInput").ap()
    o_d = P.dram("o", [TT, 256], F32, "ExternalOutput").ap()
    kTa = P.sb([128, 2, TT], BF16, "kTa"); KA = [Buf("ka0"), Buf("ka1")]
    kTb = P.sb([128, 2, TT], BF16, "kTb"); KB = [Buf("kb0"), Buf("kb1")]
    va = P.sb([128, 64, 2, 132], BF16, "va"); VA = Buf("va")
    for h in range(2):
        P.dma("sp", kTa[:, h, :], kTa_d[:, h, :], writes=[KA[h]])
        P.dma("sp", kTb[:, h, :], kTb_d[:, h, :], writes=[KB[h]])
    for c0 in range(0, 64, 16):
        P.dma("sp", va[:, c0:c0 + 16], v_d[:, c0:c0 + 16], writes=[VA])
    mk, MK = load_const(P, m_d, [128, 4, 512], BF16, "mk")
    qa_r = Rot(P, 2, [128, 512], BF16, "qa")
    qb_r = Rot(P, 2, [128, 512], BF16, "qb")
    s_r = Rot(P, 3, [128, 512], F32, "S", psum=True)
    acc_r = Rot(P, 4, [128, 512], F32, "acc", psum=True)
    p_r = Rot(P, 4, [128, 512], BF16, "p")
    pm_r = Rot(P, 3, [128, 512], BF16, "pm")
    rs_r = Rot(P, 4, [128, 1], F32, "rs")
    o_r = Rot(P, 2, [128, 4, 128], F32, "osb")
    scale = 192.0 ** -0.5
    steps = []
    for h in range(2):
        for qs in range(TT // 512):
            ctx = {}

            def mk_qk(h=h, qs=qs, ctx=ctx, kc=0):
                def f(S, SB_):
                    if kc == 0:
                        qa, QA = qa_r.next()
                        qb, QB = qb_r.next()
                        P.dma("sp", qa[:], qTa_d[:, h, qs * 512:(qs + 1) * 512], writes=[QA])
                        P.dma("sp", qb[:], qTb_d[:, h, qs * 512:(qs + 1) * 512], writes=[QB])
                        ctx["q"] = (qa, QA, qb, QB)
                        ctx["accs"] = [acc_r.next() for _ in range(4)]
                    qa, QA, qb, QB = ctx["q"]
                    mm(P, S[:], kTa[:, h, kc * 128:(kc + 1) * 128], qa[:], True, False, [KA[h], QA], [SB_])
                    mm(P, S[:], kTb[:, h, kc * 128:(kc + 1) * 128], qb[:], False, True, [KB[h], QB], [SB_])
                return f

            def mk_pv(h=h, qs=qs, ctx=ctx, kc=0):
                def f(p, PB):
                    j = kc - 4 * qs
                    if j >= 0:
                        pm, PM = pm_r.next()
                        tt(P, "dve", pm[:], p[:], mk[:, j, :], ALU.mult, [PB, MK], [PM])
                        p, PB = pm, PM
                    for t in range(4):
                        if j >= 0 and t < j:
                            continue
                        a, AB = ctx["accs"][t]
                        mm(P, a[:, 0:129], p[:, t * 128:(t + 1) * 128], va[:, kc, h, 0:129], kc == 0,
                           kc == 4 * qs + t, [PB, VA], [AB])
                return f

            def mk_after(h=h, qs=qs, ctx=ctx):
                def f():
                    osb, OB = o_r.next()
                    for t in range(4):
                        a, AB = ctx["accs"][t]
                        rs, RS = rs_r.next()
                        P.op("dve", lambda e, rs=rs, a=a: e.reciprocal(out=rs[:], in_=a[:, 128:129]), reads=[AB],
                             writes=[RS])
                        ts(P, "dve", osb[:, t, :], a[:, 0:128], rs[:, 0:1], None, ALU.mult, None, [AB, RS], [OB])
                    P.dma("sp", o_d[qs * 512:(qs + 1) * 512, h * 128:(h + 1) * 128].rearrange("(t p) d -> p t d", p=128),
                          osb[:], reads=[OB])
                return f

            nk = 4 * qs + 4
            for kc in range(nk):
                steps.append((mk_qk(kc=kc), mk_pv(kc=kc), mk_after() if kc == nk - 1 else None))
    run_pipeline(P, steps, s_r, p_r, scale)
    return P.finish()


def mla_host_inputs(k, q, v, b, hp):
    ks = k[b * TT:(b + 1) * TT].reshape(TT, 8, 192)[:, 2 * hp:2 * hp + 2, :]
    qs = q[b * TT:(b + 1) * TT].reshape(TT, 8, 192)[:, 2 * hp:2 * hp + 2, :]
    vs = v[b * TT:(b + 1) * TT].reshape(TT, 8, 128)[:, 2 * hp:2 * hp + 2, :]
    kT = np.ascontiguousarray(ks.transpose(2, 1, 0))
    qT = np.ascontiguousarray(qs.transpose(2, 1, 0))
    vaug = np.zeros((128, 64, 2, 132), dtype=NPBF)
    vaug[:, :, :, 0:128] = vs.reshape(64, 128, 2, 128).transpose(1, 0, 2, 3)
    vaug[:, :, :, 128] = 1.0
    kl = np.arange(128)[:, None, None]
    j = np.arange(4)[None, :, None]
    ql = np.arange(512)[None, None, :]
    masks = (j * 128 + kl <= ql).astype(np.float32).astype(NPBF)
    kTb = np.zeros((128, 2, TT), NPBF)
    kTb[0:64] = kT[128:192]
    qTb = np.zeros((128, 2, TT), NPBF)
    qTb[0:64] = qT[128:192]
    return {"kTa": np.ascontiguousarray(kT[0:128]), "kTb": kTb,
            "qTa": np.ascontiguousarray(qT[0:128]), "qTb": qTb,
            "vaug": vaug, "masks": np.ascontiguousarray(masks)}


NW = 2608


def build_nsa_a():
    P = Prog()
    x_d = P.dram("x", [NT, 1024], F32, "ExternalInput").ap()
    g_d = P.dram("g", [128, 8], F32, "ExternalInput").ap()
    w_d = P.dram("w", [1024, NW], F32, "ExternalInput").ap()
    gq_d = P.dram("gq", [128, 64], F32, "ExternalInput").ap()
    gs_d = P.dram("gs", [128, 64], F32, "ExternalInput").ap()
    gw_d = P.dram("gw", [128, 64], F32, "ExternalInput").ap()
    id_d = P.dram("ident", [128, 128], BF16, "ExternalInput").ap()
    big_d = P.dram("big", [NT, 2560], BF16, "ExternalOutput").ap()
    gates_d = P.dram("gates", [NT, 48], F32, "ExternalOutput").ap()
    ident, IDB = make_ident(P, id_d)
    gain = load_const(P, g_d, [128, 8], F32, "gain")
    gq, GQ = load_const(P, gq_d, [128, 64], F32, "gq")
    gs, GS = load_const(P, gs_d, [128, 64], F32, "gs")
    gw, GW = load_const(P, gw_d, [128, 64], F32, "gw")
    stg = Rot(P, 4, [128, 1024], F32, "stg")
    w, WB = load_weight(P, w_d, 1024, NW, "w", stg, gain=gain)
    xin = Rot(P, 2, [128, 1024], F32, "xin")
    scr = P.sb([128, 1536], F32, "scr"); SCR = Buf("scr")
    scrA = P.sb([128, 1024], F32, "scrA"); SCRA = Buf("scrA")
    ssr = Rot(P, 3, [128, 1], F32, "ss")
    xn = Rot(P, 2, [128, 1024], BF16, "xn")
    pT = Rot(P, 1, [128, 8, 128], BF16, "pT", psum=True)
    xT = Rot(P, 3, [128, 8, 128], BF16, "xT")
    pf = Rot(P, 6, [128, 512], F32, "pf", psum=True)
    y_r = Rot(P, 2, [128, NW], F32, "y")
    ssn_r = Rot(P, 2, [128, 24], F32, "ssn")
    tmp_r = Rot(P, 2, [128, 1024], F32, "tmp")
    big_r = Rot(P, 2, [128, 2560], BF16, "bigs")
    gt_r = Rot(P, 2, [128, 48], F32, "gts")
    def stageA(b):
        rows = slice(b * 128, (b + 1) * 128)
        xt, XB = xin.next()
        P.dma("sp", xt[:], x_d[rows, :], writes=[XB])
        ss, SS = ssr.next()
        rms_rstd(P, xt[:], 1024, ss[:], scrA[:, 0:1024], [XB], SS, SCRA)
        xnt, XN = xn.next()
        ts(P, "dve", xnt[:], xt[:], ss[:, 0:1], None, ALU.mult, None, [XB, SS], [XN])
        pt, PB = pT.next()
        for c in range(8):
            tr(P, pt[:, c, :], xnt[:, c * 128:(c + 1) * 128], ident[:], [XN, IDB], [PB])
        xTt, XT = xT.next()
        cp(P, "dve", xTt[:], pt[:], [PB], [XT])
        return xTt, XT

    def stageB(b, ctx):
        xTt, XT = ctx
        rows = slice(b * 128, (b + 1) * 128)
        y, YB = y_r.next()
        for n in range(6):
            c0, c1 = n * 512, min(NW, (n + 1) * 512)
            pp, PP = pf.next()
            for c in range(8):
                mm(P, pp[:, 0:c1 - c0], xTt[:, c, :], w[:, c, c0:c1], c == 0, c == 7, [XT] + WB.cols(c0, c1), [PP])
            cp(P, "act", y[:, c0:c1], pp[:, 0:c1 - c0], [PP], [YB])
        act(P, scr[:, 0:1024], y[:, 0:1024], AF.Square, [YB], [SCR])
        act(P, scr[:, 1024:1280], y[:, 1536:1792], AF.Square, [YB], [SCR])
        act(P, scr[:, 1280:1536], y[:, 2048:2304], AF.Square, [YB], [SCR])
        ssn, SSN = ssn_r.next()
        P.op("dve", lambda e, ssn=ssn: e.tensor_reduce(out=ssn[:], in_=scr[:, 0:1536].rearrange("p (h d) -> p h d", d=64),
                                                      axis=AX.X, op=ALU.add), reads=[SCR], writes=[SSN])
        act(P, ssn[:], ssn[:], AF.Ln, [SSN], [SSN], scale=1.0 / 64, bias=1e-6)
        act(P, ssn[:], ssn[:], AF.Exp, [SSN], [SSN], scale=-0.5)
        bg, BG = big_r.next()
        tmp, TMP = tmp_r.next()

        def normed(src0, nh, h0, gt, GT, dst0):
            tv = tmp[:, 0:nh * 64].rearrange("p (h d) -> p h d", d=64)
            tt(P, "dve", tv, y[:, src0:src0 + nh * 64].rearrange("p (h d) -> p h d", d=64),
               ssn[:, h0:h0 + nh].unsqueeze(2).to_broadcast([128, nh, 64]), ALU.mult, [YB, SSN], [TMP])
            tt(P, "dve", bg[:, dst0:dst0 + nh * 64].rearrange("p (h d) -> p h d", d=64), tv,
               gt[:, :].unsqueeze(1).to_broadcast([128, nh, 64]), ALU.mult, [TMP, GT], [BG])

        normed(0, 16, 0, gq, GQ, 0)
        normed(1536, 4, 16, gs, GS, 1536)
        normed(2048, 4, 20, gw, GW, 2048)
        cp(P, "dve", bg[:, 1024:1536], y[:, 1024:1536], [YB], [BG])
        cp(P, "dve", bg[:, 1792:2048], y[:, 1792:2048], [YB], [BG])
        cp(P, "dve", bg[:, 2304:2560], y[:, 2304:2560], [YB], [BG])
        gt_, GT_ = gt_r.next()
        act(P, gt_[:], y[:, 2560:2608], AF.Sigmoid, [YB], [GT_])
        P.dma("pool", big_d[rows, :], bg[:], reads=[BG])
        P.dma("pool", gates_d[rows, :], gt_[:], reads=[GT_])

    ctx = stageA(0)
    for b in range(NB):
        nxt = stageA(b + 1) if b + 1 < NB else None
        stageB(b, ctx)
        ctx = nxt
    return P.finish()


def build_nsa_b():
    P = Prog()
    ins = {}
    for nm in ("k", "v"):
        ins[nm] = dict(
            aT=P.dram(nm + "_a2T", [128, TT], BF16, "ExternalInput").ap(),
            w1=P.dram(nm + "_w1", [128, 16, 256], F32, "ExternalInput").ap(),
            pe=P.dram(nm + "_pe", [128, 16], F32, "ExternalInput").ap(),
            b1=P.dram(nm + "_b1", [128, 2], F32, "ExternalInput").ap(),
            w2=P.dram(nm + "_w2", [128, 2, 64], F32, "ExternalInput").ap(),
            out=P.dram(nm + "_cmp", [512, 64], BF16, "ExternalOutput").ap(),
        )
    gk_d = P.dram("gk", [128, 64], F32, "ExternalInput").ap()
    gk, GK = load_const(P, gk_d, [128, 64], F32, "gk")
    ps_r = Rot(P, 4, [128, 512], F32, "ps", psum=True)
    f_r = Rot(P, 6, [128, 512], F32, "f")
    for nm in ("k", "v"):
        I = ins[nm]
        aT, AT = load_const(P, I["aT"], [128, TT], BF16, nm + "aT")
        w1f, W1F = load_const(P, I["w1"], [128, 16, 256], F32, nm + "w1f")
        pef, PEF = load_const(P, I["pe"], [128, 16], F32, nm + "pef")
        b1, B1 = load_const(P, I["b1"], [128, 2], F32, nm + "b1")
        w2f, W2F = load_const(P, I["w2"], [128, 2, 64], F32, nm + "w2f")
        w1 = P.sb([128, 16, 256], BF16, nm + "w1"); W1 = Buf(nm + "w1")
        cp(P, "dve", w1[:], w1f[:], [W1F], [W1])
        peb = P.sb([128, 16], BF16, nm + "peb"); PEB = Buf(nm + "peb")
        cp(P, "dve", peb[:], pef[:], [PEF], [PEB])
        w2 = P.sb([128, 2, 64], BF16, nm + "w2"); W2 = Buf(nm + "w2")
        cp(P, "dve", w2[:], w2f[:], [W2F], [W2])
        av = aT[:, :].rearrange("p (j s) -> p j s", s=16)
        gT = P.sb([128, 2, 512], BF16, nm + "gT"); GT = Buf(nm + "gT")
        memset(P, "dve", gT[:], 0.0, [GT])
        for half in range(2):
            hp, HP = ps_r.next()
            cv, CV = ps_r.next()
            for m in range(16):
                rhs = av[:, 0:511, 2 * m] if m < 8 else av[:, 1:512, 2 * m - 16]
                mm(P, hp[:, 0:511], w1[:, m, half * 128:(half + 1) * 128], rhs, m == 0, m == 15, [W1, AT], [HP])
            for m in range(16):
                mm(P, cv[:, 0:1], w1[:, m, half * 128:(half + 1) * 128], peb[:, m:m + 1], m == 0, m == 15, [W1, PEB], [CV])
            bias, BI = f_r.next()
            tt(P, "dve", bias[:, 0:1], cv[:, 0:1], b1[:, half:half + 1], ALU.add, [CV, B1], [BI])
            u, U = f_r.next()
            act(P, u[:, 0:511], hp[:, 0:511], AF.Identity, [HP, BI], [U], bias=bias[:, 0:1])
            t1, T1 = f_r.next()
            tt(P, "dve", t1[:, 0:511], u[:, 0:511], u[:, 0:511], ALU.mult, [U], [T1])
            t2, T2 = f_r.next()
            ts(P, "dve", t2[:, 0:511], t1[:, 0:511], 0.044715, 1.0, ALU.mult, ALU.add, [T1], [T2])
            t3, T3 = f_r.next()
            tt(P, "dve", t3[:, 0:511], t2[:, 0:511], u[:, 0:511], ALU.mult, [T2, U], [T3])
            t4, T4 = f_r.next()
            act(P, t4[:, 0:511], t3[:, 0:511], AF.Tanh, [T3], [T4], scale=0.7978845608028654)
            t5, T5 = f_r.next()
            ts(P, "dve", t5[:, 0:511], t4[:, 0:511], 1.0, 0.5, ALU.add, ALU.mult, [T4], [T5])
            tt(P, "dve", gT[:, half, 0:511], t5[:, 0:511], u[:, 0:511], ALU.mult, [T5, U], [GT])
        for ch in range(4):
            op_, OP = ps_r.next()
            for half in range(2):
                mm(P, op_[:, 0:64], gT[:, half, ch * 128:(ch + 1) * 128], w2[:, half, :], half == 0, half == 1,
                   [GT, W2], [OP])
            ob = P.sb([128, 64], BF16, f"{nm}ob{ch}"); OB = Buf("ob")
            if nm == "k":
                sq, SQ = f_r.next()
                ssk = P.sb([128, 1], F32, f"ssk{ch}"); SSK = Buf("ssk")
                rms_rstd(P, op_[:, 0:64], 64, ssk[:], sq[:, 0:64], [OP], SSK, SQ)
                kn, KN = f_r.next()
                ts(P, "dve", kn[:, 0:64], op_[:, 0:64], ssk[:, 0:1], None, ALU.mult, None, [OP, SSK], [KN])
                tt(P, "dve", ob[:], kn[:, 0:64], gk[:], ALU.mult, [KN, GK], [OB])
            else:
                cp(P, "act", ob[:], op_[:, 0:64], [OP], [OB])
            P.dma("sp", I["out"][ch * 128:(ch + 1) * 128, :], ob[:], reads=[OB])
    return P.finish()


NQB = TT // 128


def build_nsa_c():
    P = Prog()
    QT_d = P.dram("QT", [70, 4, TT], BF16, "ExternalInput").ap()
    KS_d = P.dram("KS", [128, TT], BF16, "ExternalInput").ap()
    KW_d = P.dram("KW", [70, TT], BF16, "ExternalInput").ap()
    KC_d = P.dram("KC", [70, 512], BF16, "ExternalInput").ap()
    VS_d = P.dram("VS", [128, 64, 66], BF16, "ExternalInput").ap()
    VW_d = P.dram("VW", [128, 64, 66], BF16, "ExternalInput").ap()
    VC_d = P.dram("VC", [128, 4, 194], BF16, "ExternalInput").ap()
    G_d = P.dram("G", [128, 64, 12], F32, "ExternalInput").ap()
    CM_d = P.dram("CM", [128, 17, 512], BF16, "ExternalInput").ap()
    CA_d = P.dram("CA", [128, 512], BF16, "ExternalInput").ap()
    WM_d = P.dram("WM", [128, 512], BF16, "ExternalInput").ap()
    AT_d = P.dram("AT", [128, 256], F32, "ExternalInput").ap()
    id_d = P.dram("ident", [128, 128], BF16, "ExternalInput").ap()
    o_d = P.dram("o", [TT, 256], F32, "ExternalOutput").ap()
    ident, IDB = make_ident(P, id_d)
    QT = P.sb([70, 4, TT], BF16, "QT"); QTB = Buf("QT")
    for h in range(4):
        P.dma("sp", QT[:, h, :], QT_d[:, h, :], writes=[QTB])
    KS, KSB = load_const(P, KS_d, [128, TT], BF16, "KS")
    KW, KWB = load_const(P, KW_d, [70, TT], BF16, "KW")
    KC, KCB = load_const(P, KC_d, [70, 512], BF16, "KC")
    VS, VSB = load_const(P, VS_d, [128, 64, 66], BF16, "VS")
    VW, VWB = load_const(P, VW_d, [128, 64, 66], BF16, "VW")
    VC, VCB = load_const(P, VC_d, [128, 4, 194], BF16, "VC")
    G, GB = load_const(P, G_d, [128, 64, 12], F32, "G")
    CM, CMB = load_const(P, CM_d, [128, 17, 512], BF16, "CM")
    CA, CAB = load_const(P, CA_d, [128, 512], BF16, "CA")
    WM, WMB = load_const(P, WM_d, [128, 512], BF16, "WM")
    AT, ATB = load_const(P, AT_d, [128, 256], F32, "AT")
    s_r = Rot(P, 3, [128, 512], F32, "S", psum=True)
    accC_r = Rot(P, 2, [128, 512], F32, "accC", psum=True)
    accS_r = Rot(P, 1, [128, 512], F32, "accS", psum=True)
    accW_r = Rot(P, 1, [128, 512], F32, "accW", psum=True)
    pst_r = Rot(P, 1, [128, 128], BF16, "pst", psum=True)
    p_r = Rot(P, 4, [128, 512], BF16, "p")
    sm_r = Rot(P, 48, [128, 1], F32, "sm")
    imp_r = Rot(P, 4, [128, 128], F32, "imp")
    sc_r = Rot(P, 4, [128, 128], F32, "sc")
    m8_r = Rot(P, 4, [128, 8], F32, "m8")
    ns_r = Rot(P, 2, [128, 128], BF16, "ns")
    nt_r = Rot(P, 3, [128, 4, 128], BF16, "nt")
    o_r = Rot(P, 8, [128, 256], F32, "ot")
    qv_r = Rot(P, 7, [128, 4, 128], BF16, "qv")
    for qvt, QVB in qv_r.items:
        memset(P, "pool", qvt[64:128], 0.0, [QVB])
    scale = 0.125

    def qk(S, SB_, KT, KTB, c, rhsQ, extra):
        mm(P, S[:], KT[0:70, c * 128:(c + 1) * 128], rhsQ, True, extra is None, [KTB, QTB], [SB_])
        if extra is not None:
            lhsT, rhs, rd = extra
            mm(P, S[:], lhsT, rhs, False, True, rd, [SB_])

    def rq(qb):
        return QT[:, :, qb * 128:(qb + 1) * 128]

    st = {}

    def finalize(qb, a, AB, width, col, first):
        nxt, NXT = o_r.next()
        cur = st.get(("o", qb))
        rcs = []
        for h in range(4):
            off = (h % 2) * 193 if width == 193 else h * 65
            aa, AAB = (a[h // 2], AB[h // 2]) if width == 193 else (a, AB)
            sm, SM = sm_r.next()
            ts(P, "dve", sm[:], aa[:, off + 64:off + 65], 1e-30, None, ALU.max, None, [AAB], [SM])
            rc, RC = sm_r.next()
            P.op("dve", lambda e, rc=rc, sm=sm: e.reciprocal(out=rc[:], in_=sm[:]), reads=[SM], writes=[RC])
            gf, GF = sm_r.next()
            tt(P, "dve", gf[:], rc[:], G[:, qb, h * 3 + col:h * 3 + col + 1], ALU.mult, [RC, GB], [GF])
            if cur is None:
                ts(P, "dve", nxt[:, h * 64:(h + 1) * 64], aa[:, off:off + 64], gf[:, 0:1], None, ALU.mult, None,
                   [AAB, GF], [NXT])
            else:
                c_t, C_B = cur
                P.op("dve", lambda e, nxt=nxt, aa=aa, off=off, h=h, gf=gf, c_t=c_t: e.scalar_tensor_tensor(
                    out=nxt[:, h * 64:(h + 1) * 64], in0=aa[:, off:off + 64], scalar=gf[:, 0:1],
                    in1=c_t[:, h * 64:(h + 1) * 64], op0=ALU.mult, op1=ALU.add), reads=[AAB, GF, C_B], writes=[NXT])
            rcs.append((rc, RC))
        st[("o", qb)] = (nxt, NXT)
        return rcs

    def cmp_steps(qb):
        rhsQ = rq(qb)
        ncc = qb // 16 + 1
        ctx = {}

        def mk_qk(cc):
            def f(S, SB_):
                if cc == 0:
                    ctx["acc"] = [accC_r.next(), accC_r.next()]
                extra = None
                if qb - 16 * cc <= 16:
                    extra = (ident[:], CM[:, qb - 16 * cc, :], [IDB, CMB])
                qk(S, SB_, KC, KCB, cc, rhsQ, extra)
            return f

        def mk_pv(cc):
            def f(p, PB):
                for h in range(4):
                    a, AB = ctx["acc"][h // 2]
                    off = (h % 2) * 193
                    mm(P, a[:, off:off + 193], p[:, h * 128:(h + 1) * 128], VC[:, cc, 0:193],
                       cc == 0 and h % 2 == 0, cc == ncc - 1, [PB, VCB], [AB], skip=True)
            return f

        def after():
            acc = ctx["acc"]
            rcs = finalize(qb, [acc[0][0], acc[1][0]], [acc[0][1], acc[1][1]], 193, 0, True)
            imp, IMP = None, None
            for h in range(4):
                a, AB = acc[h // 2]
                off = (h % 2) * 193
                rc, RC = rcs[h]
                ni, NI = imp_r.next()
                if imp is None:
                    ts(P, "dve", ni[:], a[:, off + 65:off + 193], rc[:, 0:1], None, ALU.mult, None, [AB, RC], [NI])
                else:
                    P.op("dve", lambda e, ni=ni, a=a, off=off, rc=rc, imp=imp: e.scalar_tensor_tensor(
                        out=ni[:], in0=a[:, off + 65:off + 193], scalar=rc[:, 0:1], in1=imp[:], op0=ALU.mult,
                        op1=ALU.add), reads=[AB, RC, IMP], writes=[NI])
                imp, IMP = ni, NI
            sc, SC = sc_r.next()
            tt(P, "dve", sc[:], imp[:], AT[:, 126 - 2 * qb:254 - 2 * qb], ALU.add, [IMP, ATB], [SC])
            ts(P, "dve", sc[:, 0:1], imp[:, 0:1], 1e4, None, ALU.add, None, [IMP, SC], [SC])
            m1, M1 = m8_r.next()
            P.op("dve", lambda e, m1=m1, sc=sc: e.max(out=m1[:], in_=sc[:]), reads=[SC], writes=[M1])
            sc2, SC2 = sc_r.next()
            P.op("dve", lambda e, sc2=sc2, m1=m1, sc=sc: e.match_replace(
                out=sc2[:], in_to_replace=m1[:], in_values=sc[:], imm_value=-1e9), reads=[SC, M1], writes=[SC2])
            m2, M2 = m8_r.next()
            P.op("dve", lambda e, m2=m2, sc2=sc2: e.max(out=m2[:], in_=sc2[:]), reads=[SC2], writes=[M2])
            ns, NS = ns_r.next()
            ts(P, "dve", ns[:], sc[:], m2[:, 7:8], -30000.0, ALU.is_lt, ALU.mult, [SC, M2], [NS])
            pst, PST = pst_r.next()
            tr(P, pst[:], ns[:], ident[:], [NS, IDB], [PST])
            nt, NTB = nt_r.next()
            for h in range(4):
                cp(P, "act" if h % 2 else "dve", nt[:, h, :], pst[:], [PST], [NTB])
            st[("nt", qb)] = (nt, NTB)
            nv = (2 * qb - 1) // 58 + 1 if qb > 0 else 0
            qvs = []
            for v in range(nv):
                qvt, QVB = qv_r.next()
                nr = min(58, 128 - 58 * v)
                P.dma("pool", qvt[0:70], QT_d[:, :, qb * 128:(qb + 1) * 128], writes=[QVB])
                P.dma("pool", qvt[70:70 + nr], nt[58 * v:58 * v + nr], reads=[NTB], writes=[QVB])
                qvs.append((qvt, QVB))
            st[("qv", qb)] = qvs

        return [(mk_qk(cc), mk_pv(cc), after if cc == ncc - 1 else None) for cc in range(ncc)]

    def branch_steps(qb, which):
        rhsQ = rq(qb)
        ctx = {}
        if which == "sel":
            cs = list(range(qb + 1))
            KT, KTB, V, VB, accr, col = KS, KSB, VS, VSB, accS_r, 1
        else:
            cs = list(range(max(0, qb - 4), qb + 1))
            KT, KTB, V, VB, accr, col = KW, KWB, VW, VWB, accW_r, 2

        def mk_qk(c):
            def f(S, SB_):
                if c == cs[0]:
                    ctx["acc"] = accr.next()
                extra = None
                if c == qb:
                    extra = (ident[:], CA[:], [IDB, CAB])
                elif which == "sel":
                    qvt, QVB = st[("qv", qb)][(2 * c) // 58]
                    mm(P, S[:], KS[:, c * 128:(c + 1) * 128], qvt[:].rearrange("p h q -> p (h q)"), True, True,
                       [KSB, QVB], [SB_])
                    return
                elif c == qb - 4:
                    extra = (ident[:], WM[:], [IDB, WMB])
                qk(S, SB_, KT, KTB, c, rhsQ, extra)
            return f

        def mk_pv(c):
            def f(p, PB):
                a, AB = ctx["acc"]
                for h in range(4):
                    mm(P, a[:, h * 65:(h + 1) * 65], p[:, h * 128:(h + 1) * 128], V[:, c, 0:65],
                       c == cs[0] and h == 0, c == qb, [PB, VB], [AB], skip=True)
            return f

        def after():
            a, AB = ctx["acc"]
            finalize(qb, a, AB, 65, col, False)
            if which == "sel":
                cur, CUR = st[("o", qb)]
                P.dma("sp", o_d[qb * 128:(qb + 1) * 128, :], cur[:], reads=[CUR])
                st.pop(("o", qb)); st.pop(("nt", qb)); st.pop(("qv", qb))

        return [(mk_qk(c), mk_pv(c), after if c == qb else None) for c in cs]

    steps = list(cmp_steps(0))
    for qb in range(NQB):
        steps += branch_steps(qb, "win")
        if qb + 1 < NQB:
            steps += cmp_steps(qb + 1)
        steps += branch_steps(qb, "sel")
    run_pipeline(P, steps, s_r, p_r, scale)
    return P.finish()


_PROGS = {}


def _prog(name, fn):
    if name not in _PROGS:
        _PROGS[name] = fn()
    return _PROGS[name]


def _pl(g):
    return np.ascontiguousarray(np.asarray(g, np.float32).reshape(-1, 128).T)


def _bc(g):
    return np.ascontiguousarray(np.tile(np.asarray(g, np.float32)[None, :], (128, 1)))


def _run(nc, maps):
    res = run_bass_kernel_spmd(nc, maps, core_ids=list(range(8)))
    return res.results


def _split2(a):
    a = np.asarray(a, np.float64)
    hi = a.astype(np.float32).astype(NPBF)
    lo = (a - hi.astype(np.float64)).astype(np.float32).astype(NPBF)
    return hi, lo


def _nsa_c_consts():
    r = np.arange(58)[:, None]
    u = np.arange(TT)[None, :]
    Ek = (((u // 64) % 58) == r).astype(np.float32).astype(NPBF)
    jl = np.arange(128)[:, None, None]
    m = np.arange(17)[None, :, None]
    ql = np.tile(np.arange(128), 4)[None, None, :]
    CM = np.where(16 * jl - ql <= 128 * m - 31, 0.0, -30000.0).astype(np.float32).astype(NPBF)
    kl = np.arange(128)[:, None]
    q4 = np.tile(np.arange(128), 4)[None, :]
    CA = np.where(kl <= q4, 0.0, -30000.0).astype(np.float32).astype(NPBF)
    WM = np.where(kl > q4, 0.0, -30000.0).astype(np.float32).astype(NPBF)
    qq = np.arange(128)[:, None]
    rel = np.arange(256)[None, :] - 126
    cur = (qq >= 64).astype(np.int64)
    AT = np.zeros((128, 256), np.float32)
    AT[(rel == cur) | (rel == cur - 1)] = 1e4
    AT[rel > cur] = -1.0
    jj = np.arange(512)[:, None]
    ii = np.arange(128)[None, :]
    Ov = ((jj >= 4 * ii - 1) & (jj <= 4 * ii + 3) & (jj <= 510)).astype(np.float32)
    return dict(Ek=np.ascontiguousarray(Ek), CM=np.ascontiguousarray(CM), CA=np.ascontiguousarray(CA),
                WM=np.ascontiguousarray(WM), AT=AT, Ov=Ov)


def _aug_k(pos):
    hi, lo = _split2(pos)
    one = np.ones_like(hi)
    return np.stack([one, one, hi, lo, hi, lo], 0)


def kernel(**inp):
    f = {k: np.asarray(v) for k, v in inp.items()}
    x = f["x"]
    B, T, D = x.shape
    xf = np.ascontiguousarray(x.reshape(B * T, D))
    ident = np.eye(128, dtype=np.float32).astype(NPBF)
    G4 = 4
    nc = _prog("nsa_a", build_nsa_a)
    maps = [{"x": xf[c * NT:(c + 1) * NT], "g": _pl(f["a_attn_norm"][0]), "w": np.ascontiguousarray(f["a_w_in"][0]),
             "gq": _bc(f["a_q_norm"][0]), "gs": _bc(f["a_kslc_norm"][0]), "gw": _bc(f["a_kwin_norm"][0]),
             "ident": ident} for c in range(8)]
    r = _run(nc, maps)
    big = np.concatenate([q["big"] for q in r], 0)
    gates = np.concatenate([q["gates"] for q in r], 0)
    nc = _prog("nsa_b", build_nsa_b)

    def a2T(col0, b):
        a = big[b * T:(b + 1) * T, col0:col0 + 64]
        o = np.zeros((128, T), NPBF)
        o[0:64] = a.T
        o[64:128, 0:T - 1] = a[1:].T
        return o

    def w1l(w):
        return np.ascontiguousarray(np.asarray(w, np.float32).reshape(16, 2, 64, 256).transpose(1, 2, 0, 3).reshape(128, 16, 256))

    def pel(p):
        return np.ascontiguousarray(np.asarray(p, np.float32).reshape(16, 2, 64).transpose(1, 2, 0).reshape(128, 16))

    maps = []
    for c in range(8):
        b, g = c // 4, c % 4
        maps.append({
            "k_a2T": a2T(1024 + g * 64, b), "v_a2T": a2T(1280 + g * 64, b),
            "k_w1": w1l(f["a_cmp_k_w1"][0]), "v_w1": w1l(f["a_cmp_v_w1"][0]),
            "k_pe": pel(f["a_cmp_pos_k"][0]), "v_pe": pel(f["a_cmp_pos_v"][0]),
            "k_b1": _pl(f["a_cmp_k_b1"][0]), "v_b1": _pl(f["a_cmp_v_b1"][0]),
            "k_w2": np.ascontiguousarray(f["a_cmp_k_w2"][0].reshape(2, 128, 64).transpose(1, 0, 2)),
            "v_w2": np.ascontiguousarray(f["a_cmp_v_w2"][0].reshape(2, 128, 64).transpose(1, 0, 2)),
            "gk": _bc(f["a_kcmp_norm"][0])})
    rB = _run(nc, maps)
    nc = _prog("nsa_c", build_nsa_c)
    C = _nsa_c_consts()
    tpos = np.arange(T, dtype=np.float64)
    kaug_tok = _aug_k(tpos)
    kaug_cmp = _aug_k(16.0 * np.arange(512) + 15.5)
    maps = []
    for c in range(8):
        b, g = c // 4, c % 4
        bb = big[b * T:(b + 1) * T]
        QT = np.zeros((70, 4, T), NPBF)
        for hg in range(4):
            h = g * 4 + hg
            QT[0:64, hg] = bb[:, h * 64:(h + 1) * 64].T
            beta = 8.0 * 2.0 ** (-8.0 * (h + 1) / 16.0)
            bh, bl = _split2(np.full(T, beta))
            th, tl = _split2(beta * tpos)
            QT[64:70, hg] = np.stack([-th.astype(np.float32), -tl.astype(np.float32), bh.astype(np.float32),
                                      bh.astype(np.float32), bl.astype(np.float32), bl.astype(np.float32)], 0).astype(NPBF)
        KS = np.concatenate([bb[:, 1536 + g * 64:1536 + (g + 1) * 64].T, kaug_tok, C["Ek"]], 0)
        KW = np.concatenate([bb[:, 2048 + g * 64:2048 + (g + 1) * 64].T, kaug_tok], 0)
        KC = np.concatenate([rB[c]["k_cmp"].T, kaug_cmp], 0)

        def vaug(v):
            o = np.zeros((128, 64, 66), NPBF)
            o[:, :, 0:64] = v.reshape(64, 128, 64).transpose(1, 0, 2)
            o[:, :, 64] = 1.0
            return o

        VC = np.zeros((128, 4, 194), NPBF)
        VC[:, :, 0:64] = rB[c]["v_cmp"].reshape(4, 128, 64).transpose(1, 0, 2)
        VC[:, :, 64] = 1.0
        VC[:, :, 65:193] = C["Ov"].reshape(4, 128, 128).transpose(1, 0, 2).astype(NPBF)
        Gt = np.ascontiguousarray(gates[b * T:(b + 1) * T, g * 12:(g + 1) * 12].reshape(64, 128, 12).transpose(1, 0, 2))
        maps.append({"QT": QT, "KS": np.ascontiguousarray(KS), "KW": np.ascontiguousarray(KW),
                     "KC": np.ascontiguousarray(KC), "VS": vaug(bb[:, 1792 + g * 64:1792 + (g + 1) * 64]),
                     "VW": vaug(bb[:, 2304 + g * 64:2304 + (g + 1) * 64]), "VC": VC, "G": Gt,
                     "CM": C["CM"], "CA": C["CA"], "WM": C["WM"], "AT": C["AT"], "ident": ident})
    rC = _run(nc, maps)
    o_nsa = np.zeros((B, T, 1024), np.float32)
    for c in range(8):
        o_nsa[c // 4, :, (c % 4) * 256:(c % 4 + 1) * 256] = rC[c]["o"]
    o_nsa = o_nsa.reshape(B * T, 1024)

    def outproj(o, xres, w):
        nc = _prog("outproj", build_outproj)
        r = _run(nc, [{"o": o[c * NT:(c + 1) * NT], "x": xres[c * NT:(c + 1) * NT], "w": np.ascontiguousarray(w),
                       "ident": ident} for c in range(8)])
        return np.concatenate([q["y"] for q in r], 0)

    def ffn(xin, g, wgu, wd):
        nc = _prog("ffn", build_ffn)
        r = _run(nc, [{"x": xin[c * NT:(c + 1) * NT], "g": _pl(g), "wgu": np.ascontiguousarray(wgu),
                       "wd": np.ascontiguousarray(wd), "ident": ident} for c in range(8)])
        return np.concatenate([q["y"] for q in r], 0)

    x1 = outproj(o_nsa, xf, f["a_w_out"][0])
    x2 = ffn(x1, f["ffn_norm"][0], f["ffn_w_gate_up"][0], f["ffn_w_down"][0])
    nc = _prog("kvq", build_kvq)
    inv = (10000.0 ** (-np.arange(0, 64, 2, dtype=np.float32) / 64)).astype(np.float32)
    maps = []
    for c in range(8):
        pos = (np.arange(NT) + (c % 4) * NT).astype(np.float32)
        ang = pos[:, None] * inv[None, :]
        cs = np.concatenate([np.cos(ang), np.sin(ang)], 1).astype(np.float32).reshape(NB, 128, 64).transpose(1, 0, 2)
        maps.append({"x": x2[c * NT:(c + 1) * NT], "g_kv": _pl(f["kv_norm"]), "g_q": _pl(f["b_attn_norm"][0]),
                     "g_c": _pl(f["kv_c_norm"]), "g_qa": _pl(f["b_q_a_norm"][0]), "g_kn": _bc(f["kv_k_norm"]),
                     "g_qn": _bc(f["b_q_norm"][0]), "w_kva": np.ascontiguousarray(f["kv_w_a"]),
                     "w_kvb": np.ascontiguousarray(f["kv_w_b"]), "w_qa": np.ascontiguousarray(f["b_w_q_a"][0]),
                     "w_qb": np.ascontiguousarray(f["b_w_q_b"][0]), "cs": np.ascontiguousarray(cs), "ident": ident})
    r = _run(nc, maps)
    kk = np.concatenate([q["k"] for q in r], 0)
    qq = np.concatenate([q["q"] for q in r], 0)
    vv = np.concatenate([q["v"] for q in r], 0)
    nc = _prog("mla", build_mla)
    r = _run(nc, [mla_host_inputs(kk, qq, vv, c // 4, c % 4) for c in range(8)])
    o_mla = np.zeros((B, T, 1024), np.float32)
    for c in range(8):
        o_mla[c // 4, :, (c % 4) * 256:(c % 4 + 1) * 256] = r[c]["o"]
    o_mla = o_mla.reshape(B * T, 1024)
    x3 = outproj(o_mla, x2, f["b_w_out"][0])
    x4 = ffn(x3, f["ffn_norm"][1], f["ffn_w_gate_up"][1], f["ffn_w_down"][1])
    return x4.reshape(B, T, D).astype(np.float32)
```
